# Optimizing a Trainium2 kernel written in Bass

```python
import jax, jax.numpy as jnp
from jax import lax
import numpy as np

D_MODEL = 2048
BATCH = 4
SEQ = 2048
DEPTH = 1

CONV_CH = 1024
CONV_K = 31
N_HEADS = 8
QK_NOPE = 128
QK_ROPE = 64
V_HEAD = 128
QK_HEAD = QK_NOPE + QK_ROPE
Q_LORA = 768
KV_LORA = 512
ATTN_CH = N_HEADS * V_HEAD
MIX_WIDTH = CONV_CH + ATTN_CH
IN_COLS = 2 * CONV_CH + Q_LORA + KV_LORA + QK_ROPE
ROPE_THETA = 10000.0
Q_BLOCK = 128
D_FF = ((8 * D_MODEL // 3 + 255) // 256) * 256
EPS = 1e-6

kernel_name = "hymba_conformer_mla_sandwich_layer"


def rmsnorm(x, g):
    xf = x.astype(jnp.float32)
    y = xf * lax.rsqrt(jnp.mean(xf * xf, axis=-1, keepdims=True) + EPS)
    return (y * g.astype(jnp.float32)).astype(x.dtype)


def layernorm(x, g, b):
    xf = x.astype(jnp.float32)
    mu = jnp.mean(xf, axis=-1, keepdims=True)
    var = jnp.mean(jnp.square(xf - mu), axis=-1, keepdims=True)
    y = (xf - mu) * lax.rsqrt(var + EPS)
    return (y * g.astype(jnp.float32) + b.astype(jnp.float32)).astype(x.dtype)


def rope_tables(positions, dtype):
    inv_freq = ROPE_THETA ** (-jnp.arange(0, QK_ROPE, 2, dtype=jnp.float32) / QK_ROPE)
    ang = positions.astype(jnp.float32)[..., None] * inv_freq
    return jnp.cos(ang).astype(dtype), jnp.sin(ang).astype(dtype)


def apply_rope(x, cos, sin):
    x1, x2 = jnp.split(x, 2, axis=-1)
    return jnp.concatenate([x1 * cos - x2 * sin, x2 * cos + x1 * sin], axis=-1)


def causal_depthwise_conv(u, w, b):
    y = lax.conv_general_dilated(
        u, w[:, None, :], window_strides=(1,), padding=[(CONV_K - 1, 0)],
        dimension_numbers=("NWC", "WIO", "NWC"), feature_group_count=u.shape[-1])
    return y + b


def causal_attention(q, k, v):
    B, S, H, Dq = q.shape
    nblk = S // Q_BLOCK
    qb = q.reshape(B, nblk, Q_BLOCK, H, Dq).transpose(1, 0, 2, 3, 4)
    kpos = jnp.arange(S)
    scale = Dq ** -0.5
    neg = jnp.finfo(jnp.float32).min

    def one_block(args):
        i, qi = args
        s = jnp.einsum('bqhd,bkhd->bhqk', qi, k).astype(jnp.float32) * scale
        qpos = i * Q_BLOCK + jnp.arange(Q_BLOCK)
        s = jnp.where(kpos[None, :] <= qpos[:, None], s, neg)
        p = jax.nn.softmax(s, axis=-1).astype(v.dtype)
        return jnp.einsum('bhqk,bkhd->bqhd', p, v)

    out = lax.map(one_block, (jnp.arange(nblk), qb))
    return out.transpose(1, 0, 2, 3, 4).reshape(B, S, H * v.shape[-1])


def setup_inputs(seed: int = 0) -> dict:
    key = jax.random.key(seed)
    ks = jax.random.split(key, 24)
    f = jnp.float32
    L = DEPTH

    def w(k, shape, fan_in):
        return jax.random.normal(k, shape, f) * (fan_in ** -0.5)

    def gain(k, n):
        return jnp.ones((L, n), f) + 0.05 * jax.random.normal(k, (L, n), f)

    x = jax.random.normal(ks[0], (BATCH, SEQ, D_MODEL), f)
    offset = jax.random.randint(ks[1], (BATCH, 1), 0, 1024, dtype=jnp.int32)
    positions = offset + jnp.arange(SEQ, dtype=jnp.int32)[None, :]
    return {
        "x": x,
        "positions": positions,
        "pre_mix_norm": gain(ks[2], D_MODEL),
        "w_in": w(ks[3], (L, D_MODEL, IN_COLS), D_MODEL),
        "q_norm": gain(ks[4], Q_LORA),
        "w_uq": w(ks[5], (L, Q_LORA, N_HEADS * QK_HEAD), Q_LORA),
        "kv_norm": gain(ks[6], KV_LORA),
        "w_ukv": w(ks[7], (L, KV_LORA, N_HEADS * (QK_NOPE + V_HEAD)), KV_LORA),
        "conv_w": w(ks[8], (L, CONV_K, CONV_CH), CONV_K),
        "conv_b": 0.02 * jax.random.normal(ks[9], (L, CONV_CH), f),
        "conv_ln_g": gain(ks[10], CONV_CH),
        "conv_ln_b": 0.02 * jax.random.normal(ks[11], (L, CONV_CH), f),
        "conv_out_norm": gain(ks[12], CONV_CH),
        "attn_out_norm": gain(ks[13], ATTN_CH),
        "w_out": w(ks[14], (L, MIX_WIDTH, D_MODEL), MIX_WIDTH),
        "post_mix_norm": gain(ks[15], D_MODEL),
        "pre_ffn_norm": gain(ks[16], D_MODEL),
        "w_gate": w(ks[17], (L, D_MODEL, D_FF), D_MODEL),
        "w_up": w(ks[18], (L, D_MODEL, D_FF), D_MODEL),
        "w_down": w(ks[19], (L, D_FF, D_MODEL), D_FF),
        "post_ffn_norm": gain(ks[20], D_MODEL),
    }


def reference(x, positions, pre_mix_norm, w_in, q_norm, w_uq, kv_norm, w_ukv,
              conv_w, conv_b, conv_ln_g, conv_ln_b, conv_out_norm, attn_out_norm,
              w_out, post_mix_norm, pre_ffn_norm, w_gate, w_up, w_down, post_ffn_norm):
    B, S, _ = x.shape
    cos, sin = rope_tables(positions, x.dtype)
    c1 = 2 * CONV_CH
    c2 = c1 + Q_LORA
    c3 = c2 + KV_LORA
    for l in range(DEPTH):
        h = rmsnorm(x, pre_mix_norm[l])
        z = h @ w_in[l]
        conv_in, q_lat, kv_lat, k_rope = z[..., :c1], z[..., c1:c2], z[..., c2:c3], z[..., c3:]

        a, g = jnp.split(conv_in, 2, axis=-1)
        u = a * jax.nn.sigmoid(g)
        u = causal_depthwise_conv(u, conv_w[l], conv_b[l])
        u = jax.nn.silu(layernorm(u, conv_ln_g[l], conv_ln_b[l]))

        q = (rmsnorm(q_lat, q_norm[l]) @ w_uq[l]).reshape(B, S, N_HEADS, QK_HEAD)
        q_nope, q_pe = q[..., :QK_NOPE], q[..., QK_NOPE:]
        q_pe = apply_rope(q_pe, cos[:, :, None, :], sin[:, :, None, :])
        kv = (rmsnorm(kv_lat, kv_norm[l]) @ w_ukv[l]).reshape(B, S, N_HEADS, QK_NOPE + V_HEAD)
        k_nope, v = kv[..., :QK_NOPE], kv[..., QK_NOPE:]
        k_pe = apply_rope(k_rope, cos, sin)
        k_pe = jnp.broadcast_to(k_pe[:, :, None, :], (B, S, N_HEADS, QK_ROPE))
        q_full = jnp.concatenate([q_nope, q_pe], axis=-1)
        k_full = jnp.concatenate([k_nope, k_pe], axis=-1)
        attn = causal_attention(q_full, k_full, v)

        mix = jnp.concatenate([rmsnorm(u, conv_out_norm[l]),
                               rmsnorm(attn, attn_out_norm[l])], axis=-1) @ w_out[l]
        x = x + rmsnorm(mix, post_mix_norm[l])

        hf = rmsnorm(x, pre_ffn_norm[l])
        ff = (jax.nn.silu(hf @ w_gate[l]) * (hf @ w_up[l])) @ w_down[l]
        x = x + rmsnorm(ff, post_ffn_norm[l])
    return x
```

```python
import os
import numpy as np
import ml_dtypes
import concourse.bass as bass
import concourse.mybir as mybir
from concourse.bass_utils import run_bass_kernel_spmd

F32 = mybir.dt.float32
BF16 = mybir.dt.bfloat16
I32 = mybir.dt.int32
AF = mybir.ActivationFunctionType
ALU = mybir.AluOpType
PI = float(np.pi)

D_MODEL = 2048
SEQ = 2048
BATCH = 4
TOWN = 1024
CONV_CH = 1024
CONV_K = 31
N_HEADS = 8
Q_LORA = 768
KV_LORA = 512
D_FF = 5632
NFB = D_FF // 128
EPS = 1e-6
SCALE = 192 ** -0.5

CW = 0
CB = 248
LG = 256
LB = 264
G2 = 272
GQ = 280
GKV = 286
IFQ = 290
PHS = 291
SGN = 292
PFL = 293
NCV = 320

KB = 1024


class Buf:
    __slots__ = ("space", "lo", "hi", "w", "r", "name")

    def __init__(self, space, lo, hi, name=""):
        self.space, self.lo, self.hi, self.name = space, lo, hi, name
        self.w = None
        self.r = {}


class Ins:
    __slots__ = ("eng", "fn", "is_dma", "key", "signal", "value", "waits", "dma_val", "idx")

    def __init__(self, eng, fn, is_dma, key):
        self.eng, self.fn, self.is_dma, self.key = eng, fn, is_dma, key
        self.idx = -1
        self.signal = False
        self.value = None
        self.waits = []
        self.dma_val = None


ENGS = ["sync", "act", "dve", "pool", "pe"]


class Prog:
    def __init__(self):
        self.q = {e: [] for e in ENGS}
        self.bufs = {}
        self.dma_cnt = {}

    def buf(self, space, lo, hi, name=""):
        b = Buf(space, lo, hi, name)
        self.bufs.setdefault(space, []).append(b)
        return b

    def _overlap(self, b):
        return [x for x in self.bufs[b.space] if x.lo < b.hi and b.lo < x.hi]

    def op(self, eng, fn, reads=(), writes=(), dma_key=None):
        ins = Ins(eng, fn, dma_key is not None, dma_key)
        ins.idx = len(self.q[eng])
        need_eng = {}
        need_dma = {}

        def add(p, kind):
            if p is None or p is ins:
                return
            if p.is_dma:
                need_dma[p.key] = max(need_dma.get(p.key, 0), p.dma_val)
                return
            if p.eng == eng and not ins.is_dma:
                if eng == "pe":
                    return
            cur = need_eng.get(p.eng)
            if cur is None or cur.idx < p.idx:
                need_eng[p.eng] = p

        for b in reads:
            for x in self._overlap(b):
                add(x.w, "RAW")
            if b.space == "ps" and eng != "pe":
                lo = b.lo // 2048 * 2048
                hi = -(-b.hi // 2048) * 2048
                for x in self.bufs["ps"]:
                    if x.lo < hi and lo < x.hi:
                        for r in x.r.values():
                            if r.eng != eng and r.eng != "pe":
                                add(r, "RAR")
        for b in writes:
            for x in self._overlap(b):
                add(x.w, "WAW")
                for r in x.r.values():
                    add(r, "WAR")
        for p in need_eng.values():
            p.signal = True
            ins.waits.append(("eng", p))
        for k in need_dma:
            ins.waits.append(("dma", k, self.dma_cnt[k]))
        if dma_key is not None:
            self.dma_cnt[dma_key] = self.dma_cnt.get(dma_key, 0) + 16
            ins.dma_val = self.dma_cnt[dma_key]
        for b in writes:
            for x in self._overlap(b):
                x.w = ins
                x.r = {}
        rk = ("dma", dma_key) if ins.is_dma else eng
        for b in reads:
            b.r[rk] = ins
        self.q[eng].append(ins)
        return ins


def build_program(stop_after=None, tensors=None):
    nc = bass.Bass("TRN2", target_bir_lowering=False)
    P = Prog()

    def din(name, shape, dt):
        return nc.dram_tensor(name, list(shape), dt, kind="ExternalInput").ap()

    x_own = din("x_own", [TOWN, D_MODEL], F32)
    x_prev = din("x_prev", [TOWN, D_MODEL], F32)
    pos_bc = din("pos_bc", [128, 2048], I32)
    c_bf = din("c_bf", [128, 512], BF16)
    cvec_d = din("cvec", [128, NCV], F32)
    gbc_d = din("gbc", [5, 128, 2048], F32)
    w_in_p = din("w_in_p", [27, 128, 2048], F32)
    w_uq_p = din("w_uq_p", [16, 128, 768], F32)
    w_uk_p = din("w_uk_p", [8, 128, 512], F32)
    w_uv_p = din("w_uv_p", [2, 128, 2048], F32)
    w_out_p = din("w_out_p", [128, 16 * 2048], F32)
    w_gu_p = din("w_gu_p", [88, 128, 2048], F32)
    w_dn_p = din("w_dn_p", [44, 128, 2048], F32)
    out_d = nc.dram_tensor("out", [TOWN, D_MODEL], F32, kind="ExternalOutput").ap()
    x1_d = nc.dram_tensor("x1_scratch", [TOWN, D_MODEL], F32).ap()
    dbg = {}

    base = (nc.sbuf_base + 63) // 64 * 64
    CONST0 = base
    WR0 = CONST0 + 4 * KB
    AA0 = WR0 + 24 * KB
    assert AA0 + 178 * KB <= nc.sbuf_top, (AA0 + 178 * KB, nc.sbuf_top)

    def dsz(dt):
        return 4 if dt in (F32, I32) else 2

    class T:
        def __init__(self, name, shape, dt, off):
            self.h = nc.alloc_sbuf_tensor_at(name, list(shape), dt, offset=off)
            self.off = off
            self.shape = shape
            self.dt = dt
            self.nbytes = int(np.prod(shape[1:])) * dsz(dt)
            self.whole = P.buf("sb", off, off + self.nbytes, name)
            self._subs = {}
            if tensors is not None:
                tensors[name] = self

        def sub(self, i, n=None):
            key = (i, n)
            if key not in self._subs:
                slab = self.nbytes // self.shape[1]
                cnt = 1 if n is None else n
                self._subs[key] = P.buf("sb", self.off + i * slab, self.off + (i + cnt) * slab)
            return self._subs[key]

        def rng(self, lo_el, hi_el):
            key = ("r", lo_el, hi_el)
            if key not in self._subs:
                self._subs[key] = P.buf("sb", self.off + lo_el * dsz(self.dt), self.off + hi_el * dsz(self.dt))
            return self._subs[key]

    _names = [0]

    def sbt(shape, dt, off, name=None):
        _names[0] += 1
        return T(name or f"t{_names[0]}", shape, dt, off)

    cb = sbt([128, 512], BF16, CONST0, "cb")
    ident = cb.h[:, 0:128]
    ones_b = cb.h[:, 128:256]
    f2 = cb.h[:, 256:384]
    tri = cb.h[:, 384:512]
    cv = sbt([128, NCV], F32, CONST0 + 1024, "cv")
    st = sbt([128, 320], F32, CONST0 + 1024 + NCV * 4, "st")
    onesf = sbt([128, 64], F32, CONST0 + 1024 + NCV * 4 + 1280, "onesf")
    assert CONST0 + 1024 + NCV * 4 + 1280 + 256 <= WR0

    def cvc(col):
        return cv.h[:, col:col + 1]

    _stn = [0]

    def stcol(n=1):
        c = _stn[0]
        _stn[0] += n
        assert _stn[0] <= 320
        return c

    NSLOT = 6
    wslots = [sbt([128, 2048], BF16, WR0 + i * 4 * KB, f"ws{i}") for i in range(NSLOT)]
    wpieces = []
    wstate = {"issued": 0}

    psb = [nc.alloc_psum_tensor(f"psb{i}", [128, 512], F32) for i in range(8)]
    psbuf = [P.buf("ps", i * 2048, (i + 1) * 2048, f"ps{i}") for i in range(8)]
    psbf = [psb[i][:, :].bitcast(BF16) for i in range(8)]
    _bank = [0]

    def nbank():
        b = _bank[0] % 8
        _bank[0] += 1
        return b

    _psub = {}

    def psub(bank, lo, hi):
        k = (bank, lo, hi)
        if k not in _psub:
            _psub[k] = P.buf("ps", bank * 2048 + lo * 4, bank * 2048 + hi * 4)
        return _psub[k]

    def E(eng, fn, r=(), w=(), key=None):
        return P.op(eng, fn, r, w, key)

    def mm(out, lhsT, rhs, start, stop, r, w):
        return E("pe", lambda e: e.matmul(out, lhsT=lhsT, rhs=rhs, start=start, stop=stop), r, w)

    def issue_weights(upto):
        while wstate["issued"] < min(upto, len(wpieces)):
            i = wstate["issued"]
            src, n = wpieces[i]
            slot = wslots[i % NSLOT]
            E("pool", (lambda s, n_, sl: (lambda e: e.dma_start(out=sl.h[:, 0:n_], in_=s)))(src, n, slot),
              r=(), w=(slot.whole,), key=f"ws{i % NSLOT}")
            wstate["issued"] += 1

    wcur = [0]

    def next_w(prefetch=4):
        i = wcur[0]
        wcur[0] += 1
        issue_weights(i + 1 + prefetch)
        return wslots[i % NSLOT]

    for j in range(27):
        wpieces.append((w_in_p[j], 2048))
    for j in range(16):
        wpieces.append((w_uq_p[j], 768))
    for j in range(8):
        wpieces.append((w_uk_p[j], 512))
    for j in range(2):
        wpieces.append((w_uv_p[j], 2048))
    for j in range(88):
        wpieces.append((w_gu_p[j], 2048))
    for j in range(44):
        wpieces.append((w_dn_p[j], 2048))

    class _Stop(Exception):
        pass

    def ckpt(name):
        if stop_after == name:
            raise _Stop()

    try:
        E("sync", lambda e: e.dma_start(out=cb.h[:, :], in_=c_bf), w=(cb.whole,), key="const")
        E("sync", lambda e: e.dma_start(out=cv.h[:, :], in_=cvec_d), w=(cv.whole,), key="const")
        E("dve", lambda e: e.memset(st.h[:, :], 0.0), w=(st.whole,))
        E("dve", lambda e: e.memset(onesf.h[:, :], 1.0), w=(onesf.whole,))

        CS = sbt([128, 2048], F32, AA0 + 170 * KB, "CS")
        posi = sbt([128, 2048], I32, AA0 + 64 * KB, "posi")
        posf = sbt([128, 2048], F32, AA0 + 72 * KB, "posf")
        tmpa = sbt([128, 2048], F32, AA0 + 80 * KB, "tmpa")
        E("sync", lambda e: e.dma_start(out=posi.h[:, :], in_=pos_bc), w=(posi.whole,), key="pos")
        E("dve", lambda e: e.tensor_copy(out=posf.h[:, :], in_=posi.h[:, :]), r=(posi.whole,), w=(posf.whole,))
        E("dve", lambda e: e.tensor_scalar(out=CS.h[:, :], in0=posf.h[:, :], scalar1=cvc(IFQ), scalar2=cvc(PHS),
                                           op0=ALU.mult, op1=ALU.add), r=(posf.whole, cv.whole), w=(CS.whole,))
        E("dve", lambda e: e.tensor_scalar(out=tmpa.h[:, :], in0=CS.h[:, :], scalar1=1.0 / (2 * PI), scalar2=None,
                                           op0=ALU.mult), r=(CS.whole,), w=(tmpa.whole,))
        E("dve", lambda e: e.tensor_copy(out=posi.h[:, :], in_=tmpa.h[:, :]), r=(tmpa.whole,), w=(posi.whole,))
        E("dve", lambda e: e.tensor_copy(out=posf.h[:, :], in_=posi.h[:, :]), r=(posi.whole,), w=(posf.whole,))
        C1 = 6.28125
        C2 = 2 * PI - C1
        E("dve", lambda e: e.scalar_tensor_tensor(out=CS.h[:, :], in0=posf.h[:, :], scalar=-C1, in1=CS.h[:, :],
                                                  op0=ALU.mult, op1=ALU.add), r=(posf.whole, CS.whole), w=(CS.whole,))
        E("dve", lambda e: e.scalar_tensor_tensor(out=CS.h[:, :], in0=posf.h[:, :], scalar=-C2, in1=CS.h[:, :],
                                                  op0=ALU.mult, op1=ALU.add), r=(posf.whole, CS.whole), w=(CS.whole,))
        E("dve", lambda e: e.tensor_single_scalar(out=tmpa.h[:, :], in_=CS.h[:, :], scalar=PI, op=ALU.is_gt),
          r=(CS.whole,), w=(tmpa.whole,))
        E("dve", lambda e: e.scalar_tensor_tensor(out=CS.h[:, :], in0=tmpa.h[:, :], scalar=-2 * PI, in1=CS.h[:, :],
                                                  op0=ALU.mult, op1=ALU.add), r=(tmpa.whole, CS.whole), w=(CS.whole,))
        E("dve", lambda e: e.tensor_scalar(out=CS.h[:, :], in0=CS.h[:, :], scalar1=-PI, scalar2=PI,
                                           op0=ALU.max, op1=ALU.min), r=(CS.whole,), w=(CS.whole,))
        E("act", lambda e: e.activation(out=CS.h[:, :], in_=CS.h[:, :], func=AF.Sin), r=(CS.whole,), w=(CS.whole,))
        E("dve", lambda e: e.tensor_scalar(out=CS.h[:, :], in0=CS.h[:, :], scalar1=cvc(SGN), scalar2=None,
                                           op0=ALU.mult), r=(CS.whole, cv.whole), w=(CS.whole,))

        def rstd_from_ss(c_ss, c_out, n, inv_n):
            c_ms = stcol(n)
            c_sq = stcol(n)
            bs = st.rng(c_ss, c_ss + n)
            bm = st.rng(c_ms, c_ms + n)
            bq = st.rng(c_sq, c_sq + n)
            bo = st.rng(c_out, c_out + n)
            E("dve", lambda e: e.tensor_scalar(out=st.h[:, c_ms:c_ms + n], in0=st.h[:, c_ss:c_ss + n], scalar1=inv_n,
                                               scalar2=EPS, op0=ALU.mult, op1=ALU.add), r=(bs,), w=(bm,))
            E("act", lambda e: e.activation(out=st.h[:, c_sq:c_sq + n], in_=st.h[:, c_ms:c_ms + n], func=AF.Sqrt),
              r=(bm,), w=(bq,))
            E("dve", lambda e: e.reciprocal(out=st.h[:, c_out:c_out + n], in_=st.h[:, c_sq:c_sq + n]), r=(bq,), w=(bo,))
            return bo

        def rstd_psum_inplace(bank, n, inv_n):
            b = psbuf[bank]
            ap = psb[bank][:, 0:n]
            E("dve", lambda e: e.tensor_scalar(out=ap, in0=ap, scalar1=inv_n, scalar2=EPS, op0=ALU.mult, op1=ALU.add),
              r=(b,), w=(b,))
            E("act", lambda e: e.activation(out=ap, in_=ap, func=AF.Sqrt), r=(b,), w=(b,))
            E("dve", lambda e: e.reciprocal(out=ap, in_=ap), r=(b,), w=(b,))

        hT = sbt([128, 16, 2048], BF16, AA0 + 0, "hT")
        xa = [sbt([128, 2048], F32, AA0 + (64 + 8 * i) * KB, f"xa{i}") for i in range(2)]
        gbcA = sbt([128, 2048], F32, AA0 + 80 * KB, "gbcA")
        junkA = [sbt([128, 2048], BF16, AA0 + (100 + 4 * i) * KB, f"junkA{i}") for i in range(4)]
        xnA = [sbt([128, 2048], BF16, AA0 + (92 + 4 * i) * KB, f"xnA{i}") for i in range(2)]
        issue_weights(NSLOT)
        E("sync", lambda e: e.dma_start(out=gbcA.h[:, :], in_=gbc_d[0]), w=(gbcA.whole,), key="gbc")

        def norm_transpose(src_ap, src_buf, gb_t, xn_t, junk_t, width, c_ss, c_rs, dst_fn, dst_bufs, evac_engs):
            nchunk = width // 128
            bs = st.rng(c_ss, c_ss + 1)
            E("act", lambda e: e.activation(out=junk_t.h[:, 0:width], in_=src_ap, func=AF.Square,
                                            accum_out=st.h[:, c_ss:c_ss + 1]), r=(src_buf,), w=(junk_t.whole, bs))
            brs = rstd_from_ss(c_ss, c_rs, 1, 1.0 / width)
            E("dve", lambda e: e.scalar_tensor_tensor(out=xn_t.h[:, 0:width], in0=src_ap,
                                                      scalar=st.h[:, c_rs:c_rs + 1], in1=gb_t.h[:, 0:width],
                                                      op0=ALU.mult, op1=ALU.mult),
              r=(src_buf, brs, gb_t.whole), w=(xn_t.whole,))
            for g in range(nchunk // 8):
                bk = nbank()
                for c8 in range(8):
                    c = g * 8 + c8
                    E("pe", (lambda bk_, c8_, c_: (lambda e: e.transpose(psbf[bk_][:, c8_ * 128:(c8_ + 1) * 128],
                                                                       xn_t.h[:, c_ * 128:(c_ + 1) * 128], ident)))(bk, c8, c),
                      r=(xn_t.whole, cb.whole), w=(psbuf[bk],))
                eng = evac_engs[g % len(evac_engs)]
                src = psbf[bk][:, 0:1024].rearrange("p (a b) -> p a b", a=8)
                dst = dst_fn(g)
                if eng == "act":
                    E("act", (lambda d_, s_: (lambda e: e.copy(out=d_, in_=s_)))(dst, src), r=(psbuf[bk],), w=dst_bufs)
                else:
                    E("dve", (lambda d_, s_: (lambda e: e.tensor_copy(out=d_, in_=s_)))(dst, src), r=(psbuf[bk],), w=dst_bufs)

        cA_ss = stcol(16)
        cA_rs = stcol(16)
        for t in range(16):
            s = t % 2
            src = x_prev[t * 128:(t + 1) * 128, :] if t < 8 else x_own[(t - 8) * 128:(t - 7) * 128, :]
            E("sync", (lambda s_, src_: (lambda e: e.dma_start(out=xa[s_].h[:, :], in_=src_)))(s, src),
              w=(xa[s].whole,), key=f"xa{s}")
            norm_transpose(xa[s].h[:, :], xa[s].whole, gbcA, xnA[s], junkA[t % 4], 2048, cA_ss + t, cA_rs + t,
                           (lambda t_: (lambda g: hT.h[:, g * 8:(g + 1) * 8, t_ * 128:(t_ + 1) * 128]))(t),
                           (hT.whole,), ["act", "dve"])

        ckpt('A')
        zz = sbt([128, 4, 2048], F32, AA0 + 64 * KB, "zz")
        zq = sbt([128, 6, 1024], F32, AA0 + 64 * KB, "zq")
        sqz = sbt([128, 4, 2048], BF16, AA0 + 96 * KB, "sqz")
        sqq = sbt([128, 6, 1024], BF16, AA0 + 96 * KB, "sqq")
        t_k = sbt([128, 2048], BF16, AA0 + 112 * KB, "t_k")
        kvn = sbt([128, 4, 2048], BF16, AA0 + 116 * KB, "kvn")
        qln = sbt([128, 6, 1024], BF16, AA0 + 132 * KB, "qln")
        u_bf = sbt([128, 8, 1152], BF16, AA0 + 148 * KB, "u_bf")
        kr2 = sbt([128, 2048], BF16, AA0 + 166 * KB, "kr2")
        sg = [sbt([128, 1152], F32, AA0 + 64 * KB + i * 4608, f"sg{i}") for i in range(2)]

        for b in range(4):
            ws = next_w()
            banks = [nbank() for _ in range(4)]
            for kc in range(16):
                for n in range(4):
                    mm(psb[banks[n]][:, :], ws.h[:, kc * 128:(kc + 1) * 128], hT.h[:, kc, n * 512:(n + 1) * 512],
                       kc == 0, kc == 15, (ws.whole, hT.whole), (psbuf[banks[n]],))
            for n in range(4):
                if os.environ.get("KDBG") == "noevac":
                    break
                bk = banks[n]
                E("dve", (lambda bk_, b_, n_: (lambda e: e.tensor_scalar(
                    out=zz.h[:, b_, n_ * 512:(n_ + 1) * 512], in0=psb[bk_][:, :], scalar1=cvc(GKV + b_), scalar2=None,
                    op0=ALU.mult)))(bk, b, n), r=(psbuf[bk], cv.whole), w=(zz.sub(b),))
                if os.environ.get("KDBG") == "noact":
                    continue
                E("act", (lambda bk_, b_, n_: (lambda e: e.activation(
                    out=sqz.h[:, b_, n_ * 512:(n_ + 1) * 512], in_=psb[bk_][:, :], func=AF.Square)))(bk, b, n),
                  r=(psbuf[bk],), w=(sqz.sub(b),) + ((psbuf[bk],) if os.environ.get("KDBG") == "serial" else ()))
        ckpt('B1')
        for n in range(4):
            bk = nbank()
            for b in range(4):
                mm(psb[bk][:, :], ones_b, sqz.h[:, b, n * 512:(n + 1) * 512], b == 0, b == 3,
                   (cb.whole, sqz.sub(b)), (psbuf[bk],))
            rstd_psum_inplace(bk, 512, 1.0 / KV_LORA)
            for b in range(4):
                E("dve", (lambda bk_, b_, n_: (lambda e: e.tensor_tensor(
                    out=kvn.h[:, b_, n_ * 512:(n_ + 1) * 512], in0=zz.h[:, b_, n_ * 512:(n_ + 1) * 512],
                    in1=psb[bk_][:, :], op=ALU.mult)))(bk, b, n), r=(zz.sub(b), psbuf[bk]), w=(kvn.sub(b),))
        ckpt('B2')
        ws = next_w()
        banks = [nbank() for _ in range(4)]
        for kc in range(16):
            for n in range(4):
                mm(psb[banks[n]][:, :], ws.h[:, kc * 128:(kc + 1) * 128], hT.h[:, kc, n * 512:(n + 1) * 512],
                   kc == 0, kc == 15, (ws.whole, hT.whole), (psbuf[banks[n]],))
        for n in range(4):
            bk = banks[n]
            E("dve", (lambda bk_, n_: (lambda e: e.tensor_tensor(
                out=t_k.h[:, n_ * 512:(n_ + 1) * 512], in0=psb[bk_][:, :], in1=CS.h[:, n_ * 512:(n_ + 1) * 512],
                op=ALU.mult)))(bk, n), r=(psbuf[bk], CS.whole), w=(t_k.rng(n * 512, (n + 1) * 512),))
            bk2 = nbank()
            mm(psb[bk2][:, :], f2, t_k.h[:, n * 512:(n + 1) * 512], True, True,
               (cb.whole, t_k.rng(n * 512, (n + 1) * 512)), (psbuf[bk2],))
            E("act", (lambda bk_, n_: (lambda e: e.copy(out=kr2.h[:, n_ * 512:(n_ + 1) * 512], in_=psb[bk_][:, :])))(bk2, n),
              r=(psbuf[bk2],), w=(kr2.rng(n * 512, (n + 1) * 512),))
        ckpt('B3')
        for b in range(6):
            ws = next_w()
            banks = [nbank() for _ in range(2)]
            for kc in range(16):
                for n in range(2):
                    mm(psb[banks[n]][:, :], ws.h[:, kc * 128:(kc + 1) * 128],
                       hT.h[:, kc, 1024 + n * 512:1024 + (n + 1) * 512],
                       kc == 0, kc == 15, (ws.whole, hT.whole), (psbuf[banks[n]],))
            for n in range(2):
                bk = banks[n]
                E("dve", (lambda bk_, b_, n_: (lambda e: e.tensor_scalar(
                    out=zq.h[:, b_, n_ * 512:(n_ + 1) * 512], in0=psb[bk_][:, :], scalar1=cvc(GQ + b_), scalar2=None,
                    op0=ALU.mult)))(bk, b, n), r=(psbuf[bk], cv.whole), w=(zq.sub(b),))
                E("act", (lambda bk_, b_, n_: (lambda e: e.activation(
                    out=sqq.h[:, b_, n_ * 512:(n_ + 1) * 512], in_=psb[bk_][:, :], func=AF.Square)))(bk, b, n),
                  r=(psbuf[bk],), w=(sqq.sub(b),))
        for n in range(2):
            bk = nbank()
            for b in range(6):
                mm(psb[bk][:, :], ones_b, sqq.h[:, b, n * 512:(n + 1) * 512], b == 0, b == 5,
                   (cb.whole, sqq.sub(b)), (psbuf[bk],))
            rstd_psum_inplace(bk, 512, 1.0 / Q_LORA)
            for b in range(6):
                E("dve", (lambda bk_, b_, n_: (lambda e: e.tensor_tensor(
                    out=qln.h[:, b_, n_ * 512:(n_ + 1) * 512], in0=zq.h[:, b_, n_ * 512:(n_ + 1) * 512],
                    in1=psb[bk_][:, :], op=ALU.mult)))(bk, b, n), r=(zq.sub(b), psbuf[bk]), w=(qln.sub(b),))
        ckpt('B4')
        tokr = [(896, 1024), (1024, 1536), (1536, 2048)]
        for c in range(8):
            sgt = sg[c % 2]
            ws = next_w()
            banks = [nbank() for _ in range(3)]
            for kc in range(16):
                for n, (a0, a1) in enumerate(tokr):
                    mm(psb[banks[n]][:, 0:a1 - a0], ws.h[:, kc * 128:(kc + 1) * 128], hT.h[:, kc, a0:a1],
                       kc == 0, kc == 15, (ws.whole, hT.whole), (psbuf[banks[n]],))
            for n, (a0, a1) in enumerate(tokr):
                E("act", (lambda bk_, a0_, a1_, sg_: (lambda e: e.activation(
                    out=sg_.h[:, a0_ - 896:a1_ - 896], in_=psb[bk_][:, 0:a1_ - a0_], func=AF.Sigmoid)))(banks[n], a0, a1, sgt),
                  r=(psbuf[banks[n]],), w=(sgt.whole,))
            ws = next_w()
            banks = [nbank() for _ in range(3)]
            for kc in range(16):
                for n, (a0, a1) in enumerate(tokr):
                    mm(psb[banks[n]][:, 0:a1 - a0], ws.h[:, kc * 128:(kc + 1) * 128], hT.h[:, kc, a0:a1],
                       kc == 0, kc == 15, (ws.whole, hT.whole), (psbuf[banks[n]],))
            for n, (a0, a1) in enumerate(tokr):
                E("dve", (lambda bk_, a0_, a1_, sg_, c_: (lambda e: e.tensor_tensor(
                    out=u_bf.h[:, c_, a0_ - 896:a1_ - 896], in0=psb[bk_][:, 0:a1_ - a0_], in1=sg_.h[:, a0_ - 896:a1_ - 896],
                    op=ALU.mult)))(banks[n], a0, a1, sgt, c), r=(psbuf[banks[n]], sgt.whole), w=(u_bf.sub(c),))

        ckpt('B')
        yv = sbt([128, 8, 1024], F32, AA0 + 0, "yv")
        Dr = [sbt([128, 31, 128], BF16, AA0 + (32 + 8 * i) * KB, f"Dr{i}") for i in range(2)]
        sqs = [sbt([128, 512], BF16, AA0 + 48 * KB + i * 1024, f"sqs{i}") for i in range(4)]
        bc0 = sbt([128, 1024], F32, AA0 + 52 * KB, "bc0")
        bc1 = sbt([128, 1024], F32, AA0 + 56 * KB, "bc1")
        mixT = sbt([128, 16, 1024], BF16, AA0 + 64 * KB, "mixT")
        mixT_attn = P.buf("sb", mixT.off + 8 * 2048, mixT.off + 16 * 2048, "mixT_attn")
        bS1 = [nbank(), nbank()]
        bS2 = [nbank(), nbank()]
        nsq = [0]
        for c in range(8):
            dr = Dr[c % 2]
            for k in range(CONV_K):
                E("dve", (lambda dr_, k_, c_: (lambda e: e.tensor_scalar(
                    out=dr_.h[:, k_, :], in0=ident, scalar1=cvc(CW + c_ * 31 + k_), scalar2=None, op0=ALU.mult)))(dr, k, c),
                  r=(cb.whole, cv.whole), w=(dr.sub(k),))
            for n in range(2):
                bk = nbank()
                while bk in bS1 or bk in bS2:
                    bk = nbank()
                for k in range(CONV_K):
                    mm(psb[bk][:, :], dr.h[:, k, :], u_bf.h[:, c, 98 + k + n * 512:98 + k + (n + 1) * 512],
                       k == 0, k == CONV_K - 1, (dr.sub(k), u_bf.sub(c)), (psbuf[bk],))
                yb = yv.rng(c * 1024 + n * 512, c * 1024 + (n + 1) * 512)
                E("act", (lambda bk_, c_, n_: (lambda e: e.activation(
                    out=yv.h[:, c_, n_ * 512:(n_ + 1) * 512], in_=psb[bk_][:, :], func=AF.Identity,
                    bias=cvc(CB + c_))))(bk, c, n), r=(psbuf[bk], cv.whole), w=(yb,))
                s1 = sqs[nsq[0] % 4]
                nsq[0] += 1
                s2 = sqs[nsq[0] % 4]
                nsq[0] += 1
                E("act", (lambda bk_, c_, s_: (lambda e: e.activation(
                    out=s_.h[:, :], in_=psb[bk_][:, :], func=AF.Identity, bias=cvc(CB + c_))))(bk, c, s1),
                  r=(psbuf[bk], cv.whole), w=(s1.whole,))
                E("act", (lambda bk_, c_, s_: (lambda e: e.activation(
                    out=s_.h[:, :], in_=psb[bk_][:, :], func=AF.Square, bias=cvc(CB + c_))))(bk, c, s2),
                  r=(psbuf[bk], cv.whole), w=(s2.whole,))
                mm(psb[bS1[n]][:, :], ones_b, s1.h[:, :], c == 0, c == 7, (cb.whole, s1.whole), (psbuf[bS1[n]],))
                mm(psb[bS2[n]][:, :], ones_b, s2.h[:, :], c == 0, c == 7, (cb.whole, s2.whole), (psbuf[bS2[n]],))
        for n in range(2):
            sl = slice(n * 512, (n + 1) * 512)
            b0 = bc0.rng(n * 512, (n + 1) * 512)
            b1 = bc1.rng(n * 512, (n + 1) * 512)
            E("dve", (lambda n_, sl_: (lambda e: e.tensor_scalar(out=bc0.h[:, sl_], in0=psb[bS1[n_]][:, :],
                                                                scalar1=1.0 / CONV_CH, scalar2=None, op0=ALU.mult)))(n, sl),
              r=(psbuf[bS1[n]],), w=(b0,))
            E("dve", (lambda sl_: (lambda e: e.tensor_tensor(out=bc1.h[:, sl_], in0=bc0.h[:, sl_], in1=bc0.h[:, sl_],
                                                            op=ALU.mult)))(sl), r=(b0,), w=(b1,))
            E("dve", (lambda n_, sl_: (lambda e: e.scalar_tensor_tensor(out=bc1.h[:, sl_], in0=psb[bS2[n_]][:, :],
                                                                       scalar=1.0 / CONV_CH, in1=bc1.h[:, sl_],
                                                                       op0=ALU.mult, op1=ALU.subtract)))(n, sl),
              r=(psbuf[bS2[n]], b1), w=(b1,))
            E("dve", (lambda sl_: (lambda e: e.tensor_scalar(out=bc1.h[:, sl_], in0=bc1.h[:, sl_], scalar1=EPS,
                                                            scalar2=None, op0=ALU.add)))(sl), r=(b1,), w=(b1,))
            E("act", (lambda sl_: (lambda e: e.activation(out=bc1.h[:, sl_], in_=bc1.h[:, sl_], func=AF.Sqrt)))(sl),
              r=(b1,), w=(b1,))
            E("dve", (lambda sl_: (lambda e: e.reciprocal(out=bc1.h[:, sl_], in_=bc1.h[:, sl_])))(sl), r=(b1,), w=(b1,))
        bS3 = [bS1[0], bS1[1]]
        for c in range(8):
            for n in range(2):
                sl = slice(n * 512, (n + 1) * 512)
                yb = yv.rng(c * 1024 + n * 512, c * 1024 + (n + 1) * 512)
                b0 = bc0.rng(n * 512, (n + 1) * 512)
                b1 = bc1.rng(n * 512, (n + 1) * 512)
                E("dve", (lambda c_, sl_: (lambda e: e.tensor_tensor(out=yv.h[:, c_, sl_], in0=yv.h[:, c_, sl_],
                                                                    in1=bc0.h[:, sl_], op=ALU.subtract)))(c, sl),
                  r=(yb, b0), w=(yb,))
                E("dve", (lambda c_, sl_: (lambda e: e.tensor_tensor(out=yv.h[:, c_, sl_], in0=yv.h[:, c_, sl_],
                                                                    in1=bc1.h[:, sl_], op=ALU.mult)))(c, sl),
                  r=(yb, b1), w=(yb,))
                E("act", (lambda c_, sl_: (lambda e: e.activation(out=yv.h[:, c_, sl_], in_=yv.h[:, c_, sl_], func=AF.Silu,
                                                                 scale=cvc(LG + c_), bias=cvc(LB + c_))))(c, sl),
                  r=(yb, cv.whole), w=(yb,))
                s2 = sqs[nsq[0] % 4]
                nsq[0] += 1
                E("act", (lambda c_, sl_, s_: (lambda e: e.activation(out=s_.h[:, :], in_=yv.h[:, c_, sl_],
                                                                     func=AF.Square)))(c, sl, s2), r=(yb,), w=(s2.whole,))
                mm(psb[bS3[n]][:, :], ones_b, s2.h[:, :], c == 0, c == 7, (cb.whole, s2.whole), (psbuf[bS3[n]],))
        for n in range(2):
            rstd_psum_inplace(bS3[n], 512, 1.0 / CONV_CH)
        for c in range(8):
            for n in range(2):
                sl = slice(n * 512, (n + 1) * 512)
                yb = yv.rng(c * 1024 + n * 512, c * 1024 + (n + 1) * 512)
                E("dve", (lambda c_, n_, sl_: (lambda e: e.scalar_tensor_tensor(
                    out=mixT.h[:, c_, sl_], in0=yv.h[:, c_, sl_], scalar=cvc(G2 + c_), in1=psb[bS3[n_]][:, :],
                    op0=ALU.mult, op1=ALU.mult)))(c, n, sl), r=(yb, cv.whole, psbuf[bS3[n]]), w=(mixT.sub(c),))

        ckpt('E')
        knT = sbt([128, 8, 2048], BF16, AA0 + 0, "knT")
        qnT = sbt([128, 8, 1024], BF16, AA0 + 32 * KB, "qnT")
        tq = sbt([128, 8, 1024], BF16, AA0 + 48 * KB, "tq")
        VA = sbt([128, 8, 8 * 130], BF16, AA0 + 96 * KB, "VA")
        VB = sbt([128, 8, 8 * 130], BF16, AA0 + 148 * KB, "VB")
        for h in range(N_HEADS):
            ws = next_w()
            banks = [nbank() for _ in range(2)]
            for kc in range(6):
                for n in range(2):
                    mm(psb[banks[n]][:, :], ws.h[:, kc * 128:(kc + 1) * 128], qln.h[:, kc, n * 512:(n + 1) * 512],
                       kc == 0, kc == 5, (ws.whole, qln.sub(kc)), (psbuf[banks[n]],))
            for n in range(2):
                E("act", (lambda bk_, h_, n_: (lambda e: e.copy(out=qnT.h[:, h_, n_ * 512:(n_ + 1) * 512],
                                                               in_=psb[bk_][:, :])))(banks[n], h, n),
                  r=(psbuf[banks[n]],), w=(qnT.sub(h),))
            ws = next_w()
            banks = [nbank() for _ in range(2)]
            for kc in range(6):
                for n in range(2):
                    mm(psb[banks[n]][:, :], ws.h[:, kc * 128:(kc + 1) * 128], qln.h[:, kc, n * 512:(n + 1) * 512],
                       kc == 0, kc == 5, (ws.whole, qln.sub(kc)), (psbuf[banks[n]],))
            for n in range(2):
                E("dve", (lambda bk_, h_, n_: (lambda e: e.tensor_tensor(
                    out=tq.h[:, h_, n_ * 512:(n_ + 1) * 512], in0=psb[bk_][:, :],
                    in1=CS.h[:, 1024 + n_ * 512:1024 + (n_ + 1) * 512], op=ALU.mult)))(banks[n], h, n),
                  r=(psbuf[banks[n]], CS.whole), w=(tq.sub(h),))
        for h in range(N_HEADS):
            ws = next_w()
            banks = [nbank() for _ in range(4)]
            for kc in range(4):
                for n in range(4):
                    mm(psb[banks[n]][:, :], ws.h[:, kc * 128:(kc + 1) * 128], kvn.h[:, kc, n * 512:(n + 1) * 512],
                       kc == 0, kc == 3, (ws.whole, kvn.sub(kc)), (psbuf[banks[n]],))
            for n in range(4):
                if n % 2 == 0:
                    E("act", (lambda bk_, h_, n_: (lambda e: e.copy(out=knT.h[:, h_, n_ * 512:(n_ + 1) * 512],
                                                                   in_=psb[bk_][:, :])))(banks[n], h, n),
                      r=(psbuf[banks[n]],), w=(knT.sub(h),))
                else:
                    E("dve", (lambda bk_, h_, n_: (lambda e: e.tensor_copy(out=knT.h[:, h_, n_ * 512:(n_ + 1) * 512],
                                                                          in_=psb[bk_][:, :])))(banks[n], h, n),
                      r=(psbuf[banks[n]],), w=(knT.sub(h),))
        wv = [next_w(), next_w()]
        E("dve", lambda e: e.tensor_scalar(out=VA.h[:, :, :].rearrange("p t (h d) -> p (t h) d", d=130)[:, :, 128:129],
                                           in0=onesf.h[:, 0:64].rearrange("p (a b) -> p a b", b=1),
                                           scalar1=cvc(PFL), scalar2=None, op0=ALU.mult),
          r=(onesf.whole, cv.whole), w=(VA.whole,))
        E("dve", lambda e: e.memset(VB.h[:, :, :].rearrange("p t (h d) -> p (t h) d", d=130)[:, :, 128:129], 1.0),
          w=(VB.whole,))
        for t in range(16):
            Vt = VA if t < 8 else VB
            tt = t % 8
            banks = [nbank() for _ in range(2)]
            for kc in range(4):
                for hf in range(2):
                    mm(psb[banks[hf]][:, :], kvn.h[:, kc, t * 128:(t + 1) * 128],
                       wv[kc // 2].h[:, (kc % 2) * 1024 + hf * 512:(kc % 2) * 1024 + (hf + 1) * 512],
                       kc == 0, kc == 3, (kvn.sub(kc), wv[kc // 2].whole), (psbuf[banks[hf]],))
            for hf in range(2):
                dst = Vt.h[:, tt, :].rearrange("p (h d) -> p h d", d=130)[:, hf * 4:(hf + 1) * 4, 0:128]
                srcp = psb[banks[hf]][:, :].rearrange("p (h d) -> p h d", d=128)
                if t < 8:
                    E("dve", (lambda d_, s_: (lambda e: e.tensor_scalar(out=d_, in0=s_, scalar1=cvc(PFL), scalar2=None,
                                                                       op0=ALU.mult)))(dst, srcp),
                      r=(psbuf[banks[hf]], cv.whole), w=(Vt.sub(tt),))
                else:
                    E("act", (lambda d_, s_: (lambda e: e.copy(out=d_, in_=s_)))(dst, srcp),
                      r=(psbuf[banks[hf]],), w=(Vt.sub(tt),))

        ckpt('C')
        attn = sbt([128, 8, 1024], F32, AA0 + 116 * KB, "attn")
        PT = [sbt([128, 512], BF16, AA0 + 170 * KB + i * 1024, f"PT{i}") for i in range(3)]
        rc = sbt([128, 64], F32, AA0 + 173 * KB, "rc")
        npt = [0]
        nrc = [0]
        for h in range(N_HEADS):
            for qb in range(2):
                accb = [nbank(), nbank(), nbank(), nbank()]

                def acc_ap(i, lo, hi, accb=accb):
                    return psb[accb[i]][:, lo:hi]

                def acc_buf(i, accb=accb):
                    return psbuf[accb[i]]

                nkc = 8 + 4 * qb + 4
                for kc in range(nkc):
                    j = kc - 8 - 4 * qb
                    i0 = max(j, 0)
                    q0 = i0 * 128
                    bk = nbank()
                    while bk in accb:
                        bk = nbank()
                    Vt = VA if kc < 8 else VB
                    mm(psb[bk][:, q0:512], knT.h[:, h, kc * 128:(kc + 1) * 128],
                       qnT.h[:, h, qb * 512 + q0:qb * 512 + 512], True, False,
                       (knT.sub(h), qnT.sub(h)), (psbuf[bk],))
                    mm(psb[bk][:, q0:512], kr2.h[:, kc * 128:(kc + 1) * 128],
                       tq.h[:, h, qb * 512 + q0:qb * 512 + 512], False, True,
                       (kr2.whole, tq.sub(h)), (psbuf[bk],))
                    pt = PT[npt[0] % 3]
                    npt[0] += 1
                    E("act", (lambda bk_, pt_, q0_: (lambda e: e.activation(out=pt_.h[:, q0_:512], in_=psb[bk_][:, q0_:512],
                                                                          func=AF.Exp, scale=SCALE)))(bk, pt, q0),
                      r=(psbuf[bk],), w=(pt.whole,))
                    if j >= 0:
                        E("dve", (lambda pt_, q0_: (lambda e: e.tensor_tensor(out=pt_.h[:, q0_:q0_ + 128],
                                                                            in0=pt_.h[:, q0_:q0_ + 128], in1=tri,
                                                                            op=ALU.mult)))(pt, q0),
                          r=(pt.whole, cb.whole), w=(pt.whole,))
                    for i in range(i0, 4):
                        last = 8 + 4 * qb + i
                        mm(acc_ap(i, 0, 129), pt.h[:, i * 128:(i + 1) * 128],
                           Vt.h[:, kc % 8, h * 130:h * 130 + 129], kc == 0, kc == last,
                           (pt.whole, Vt.sub(kc % 8)), (acc_buf(i),))
                for i in range(4):
                    col = nrc[0] % 64
                    nrc[0] += 1
                    rb = rc.rng(col, col + 1)
                    E("dve", (lambda i_, col_, f=acc_ap: (lambda e: e.reciprocal(out=rc.h[:, col_:col_ + 1],
                                                                               in_=f(i_, 128, 129))))(i, col),
                      r=(acc_buf(i),), w=(rb,))
                    E("dve", (lambda i_, col_, h_, qb_, f=acc_ap: (lambda e: e.tensor_scalar(
                        out=attn.h[:, qb_ * 4 + i_, h_ * 128:(h_ + 1) * 128], in0=f(i_, 0, 128),
                        scalar1=rc.h[:, col_:col_ + 1], scalar2=None, op0=ALU.mult)))(i, col, h, qb),
                      r=(acc_buf(i), rb), w=(attn.sub(qb * 4 + i),))

        ckpt('D')
        gbcD = sbt([128, 1024], F32, AA0 + 96 * KB, "gbcD")
        junkD = [sbt([128, 1024], BF16, AA0 + (100 + 6 * i) * KB, f"junkD{i}") for i in range(2)]
        xnD = [sbt([128, 1024], BF16, AA0 + (102 + 2 * i) * KB, f"xnD{i}") for i in range(2)]
        E("sync", lambda e: e.dma_start(out=gbcD.h[:, :], in_=gbc_d[4][:, 0:1024]), w=(gbcD.whole,), key="gbc")
        cD_ss = stcol(8)
        cD_rs = stcol(8)
        for i in range(8):
            norm_transpose(attn.h[:, i, :], attn.sub(i), gbcD, xnD[i % 2], junkD[i % 2], 1024, cD_ss + i, cD_rs + i,
                           (lambda i_: (lambda g: mixT.h[:, 8:16, i_ * 128:(i_ + 1) * 128]))(i),
                           (mixT_attn,), ["dve"])

        ckpt('Dn')
        w_out = sbt([128, 16, 2048], BF16, AA0 + 0, "w_out")
        for g in range(4):
            E("pool", (lambda g_: (lambda e: e.dma_start(out=w_out.h[:, g_ * 4:(g_ + 1) * 4, :],
                                                         in_=w_out_p[:, g_ * 8192:(g_ + 1) * 8192].rearrange("p (a b) -> p a b", a=4))))(g),
              w=(w_out.sub(g * 4, 4),), key=f"wout{g}")
        xF = [sbt([128, 2048], F32, AA0 + (96 + 8 * i) * KB, f"xF{i}") for i in range(2)]
        gpost = sbt([128, 2048], F32, AA0 + 112 * KB, "gpost")
        gpre = sbt([128, 2048], F32, AA0 + 120 * KB, "gpre")
        x1t = sbt([128, 2048], F32, AA0 + 128 * KB, "x1t")
        xnF = sbt([128, 2048], BF16, AA0 + 136 * KB, "xnF")
        hfT = sbt([128, 16, 1024], BF16, AA0 + 140 * KB, "hfT")
        junkF = sbt([128, 2048], BF16, AA0 + 172 * KB, "junkF")
        E("sync", lambda e: e.dma_start(out=gpost.h[:, :], in_=gbc_d[1]), w=(gpost.whole,), key="gbc2")
        E("sync", lambda e: e.dma_start(out=gpre.h[:, :], in_=gbc_d[2]), w=(gpre.whole,), key="gbc2")
        cF_p = stcol(32)
        cF_ss = stcol(8)
        cF_rs = stcol(8)
        cF_ss2 = stcol(8)
        cF_rs2 = stcol(8)
        for i in range(8):
            s = i % 2
            E("sync", (lambda s_, i_: (lambda e: e.dma_start(out=xF[s_].h[:, :], in_=x_own[i_ * 128:(i_ + 1) * 128, :])))(s, i),
              w=(xF[s].whole,), key=f"xF{s}")
            banks = []
            for cbk in range(4):
                bk = nbank()
                banks.append(bk)
                for kc in range(16):
                    mm(psb[bk][:, :], mixT.h[:, kc, i * 128:(i + 1) * 128], w_out.h[:, kc, cbk * 512:(cbk + 1) * 512],
                       kc == 0, kc == 15, (mixT.sub(kc), w_out.sub(kc)), (psbuf[bk],))
                pc = cF_p + i * 4 + cbk
                E("act", (lambda bk_, pc_, c_: (lambda e: e.activation(out=junkF.h[:, c_ * 512:(c_ + 1) * 512], in_=psb[bk_][:, :],
                                                                      func=AF.Square, accum_out=st.h[:, pc_:pc_ + 1])))(bk, pc, cbk),
                  r=(psbuf[bk],), w=(junkF.rng(cbk * 512, (cbk + 1) * 512), st.rng(pc, pc + 1)))
            E("dve", (lambda i_: (lambda e: e.tensor_reduce(out=st.h[:, cF_ss + i_:cF_ss + i_ + 1],
                                                           in_=st.h[:, cF_p + 4 * i_:cF_p + 4 * i_ + 4],
                                                           axis=mybir.AxisListType.X, op=ALU.add)))(i),
              r=(st.rng(cF_p + 4 * i, cF_p + 4 * i + 4),), w=(st.rng(cF_ss + i, cF_ss + i + 1),))
            brs = rstd_from_ss(cF_ss + i, cF_rs + i, 1, 1.0 / D_MODEL)
            for cbk in range(4):
                sl = slice(cbk * 512, (cbk + 1) * 512)
                E("dve", (lambda bk_, sl_, i_: (lambda e: e.scalar_tensor_tensor(
                    out=x1t.h[:, sl_], in0=psb[bk_][:, :], scalar=st.h[:, cF_rs + i_:cF_rs + i_ + 1], in1=gpost.h[:, sl_],
                    op0=ALU.mult, op1=ALU.mult)))(banks[cbk], sl, i),
                  r=(psbuf[banks[cbk]], brs, gpost.whole), w=(x1t.whole,))
            E("dve", (lambda s_: (lambda e: e.tensor_tensor(out=x1t.h[:, :], in0=x1t.h[:, :], in1=xF[s_].h[:, :],
                                                           op=ALU.add)))(s), r=(x1t.whole, xF[s].whole), w=(x1t.whole,))
            E("sync", (lambda i_: (lambda e: e.dma_start(out=x1_d[i_ * 128:(i_ + 1) * 128, :], in_=x1t.h[:, :])))(i),
              r=(x1t.whole,), w=(), key="x1w")
            norm_transpose(x1t.h[:, :], x1t.whole, gpre, xnF, junkF, 2048, cF_ss2 + i, cF_rs2 + i,
                           (lambda i_: (lambda g: hfT.h[:, g * 8:(g + 1) * 8, i_ * 128:(i_ + 1) * 128]))(i),
                           (hfT.whole,), ["act", "dve"])
        x1w_done = P.q["sync"][-1]

        ckpt('F')
        actT = sbt([128, NFB, 1024], BF16, AA0 + 0, "actT")
        sgf = [sbt([128, 512], F32, AA0 + 172 * KB + i * 2048, f"sgf{i}") for i in range(2)]
        nsg = [0]
        for f in range(NFB):
            wg = next_w()
            wu = next_w()
            bg = [nbank(), nbank()]
            bu = [nbank(), nbank()]
            for kc in range(16):
                for n in range(2):
                    mm(psb[bg[n]][:, :], wg.h[:, kc * 128:(kc + 1) * 128], hfT.h[:, kc, n * 512:(n + 1) * 512],
                       kc == 0, kc == 15, (wg.whole, hfT.whole), (psbuf[bg[n]],))
            for kc in range(16):
                for n in range(2):
                    mm(psb[bu[n]][:, :], wu.h[:, kc * 128:(kc + 1) * 128], hfT.h[:, kc, n * 512:(n + 1) * 512],
                       kc == 0, kc == 15, (wu.whole, hfT.whole), (psbuf[bu[n]],))
            for n in range(2):
                sgt = sgf[nsg[0] % 2]
                nsg[0] += 1
                E("act", (lambda bk_, s_: (lambda e: e.activation(out=s_.h[:, :], in_=psb[bk_][:, :], func=AF.Silu)))(bg[n], sgt),
                  r=(psbuf[bg[n]],), w=(sgt.whole,))
                E("dve", (lambda bk_, s_, f_, n_: (lambda e: e.tensor_tensor(
                    out=actT.h[:, f_, n_ * 512:(n_ + 1) * 512], in0=psb[bk_][:, :], in1=s_.h[:, :], op=ALU.mult)))(bu[n], sgt, f, n),
                  r=(psbuf[bu[n]], sgt.whole), w=(actT.sub(f),))

        ckpt('G1')
        ff = sbt([128, 8, 2048], F32, AA0 + 88 * KB, "ff")
        xr = [sbt([128, 2048], F32, AA0 + (152 + 8 * i) * KB, f"xr{i}") for i in range(2)]
        gffn = sbt([128, 2048], F32, AA0 + 168 * KB, "gffn")
        junkG = [sbt([128, 512], BF16, AA0 + 176 * KB + i * 1024, f"junkG{i}") for i in range(2)]
        E("sync", lambda e: e.dma_start(out=gffn.h[:, :], in_=gbc_d[3]), w=(gffn.whole,), key="gbc3")
        cG_p = stcol(32)
        cG_ss = stcol(8)
        cG_rs = stcol(8)
        for cbk in range(4):
            for fg in range(11):
                ws = next_w()
                for i in range(8):
                    for fb in range(4):
                        fidx = fg * 4 + fb
                        mm(psb[i][:, :], actT.h[:, fidx, i * 128:(i + 1) * 128], ws.h[:, fb * 512:(fb + 1) * 512],
                           fg == 0 and fb == 0, fg == 10 and fb == 3, (actT.sub(fidx), ws.whole), (psbuf[i],))
            for i in range(8):
                sl = slice(cbk * 512, (cbk + 1) * 512)
                pc = cG_p + i * 4 + cbk
                fb_ = ff.rng(i * 2048 + cbk * 512, i * 2048 + (cbk + 1) * 512)
                E("dve", (lambda i_, sl_: (lambda e: e.tensor_copy(out=ff.h[:, i_, sl_], in_=psb[i_][:, :])))(i, sl),
                  r=(psbuf[i],), w=(fb_,))
                E("act", (lambda i_, pc_: (lambda e: e.activation(out=junkG[i_ % 2].h[:, :], in_=psb[i_][:, :], func=AF.Square,
                                                                 accum_out=st.h[:, pc_:pc_ + 1])))(i, pc),
                  r=(psbuf[i],), w=(junkG[i % 2].whole, st.rng(pc, pc + 1)))
        for i in range(8):
            s = i % 2
            E("sync", (lambda s_, i_: (lambda e: e.dma_start(out=xr[s_].h[:, :], in_=x1_d[i_ * 128:(i_ + 1) * 128, :])))(s, i),
              w=(xr[s].whole,), key=f"xr{s}")
            P.q["sync"][-1].waits.append(("dma", "x1w", P.dma_cnt["x1w"]))
            E("dve", (lambda i_: (lambda e: e.tensor_reduce(out=st.h[:, cG_ss + i_:cG_ss + i_ + 1],
                                                           in_=st.h[:, cG_p + 4 * i_:cG_p + 4 * i_ + 4],
                                                           axis=mybir.AxisListType.X, op=ALU.add)))(i),
              r=(st.rng(cG_p + 4 * i, cG_p + 4 * i + 4),), w=(st.rng(cG_ss + i, cG_ss + i + 1),))
            brs = rstd_from_ss(cG_ss + i, cG_rs + i, 1, 1.0 / D_MODEL)
            E("dve", (lambda i_: (lambda e: e.scalar_tensor_tensor(
                out=ff.h[:, i_, :], in0=ff.h[:, i_, :], scalar=st.h[:, cG_rs + i_:cG_rs + i_ + 1], in1=gffn.h[:, :],
                op0=ALU.mult, op1=ALU.mult)))(i), r=(ff.sub(i), brs, gffn.whole), w=(ff.sub(i),))
            E("dve", (lambda i_, s_: (lambda e: e.tensor_tensor(out=ff.h[:, i_, :], in0=ff.h[:, i_, :], in1=xr[s_].h[:, :],
                                                               op=ALU.add)))(i, s), r=(ff.sub(i), xr[s].whole), w=(ff.sub(i),))
            E("sync", (lambda i_: (lambda e: e.dma_start(out=out_d[i_ * 128:(i_ + 1) * 128, :], in_=ff.h[:, i_, :])))(i),
              r=(ff.sub(i),), w=(), key="outw")
        fin = E("sync", None)
        fin.waits.append(("dma", "outw", P.dma_cnt["outw"]))
        assert wcur[0] == len(wpieces), (wcur[0], len(wpieces))


    except _Stop:
        pass

    fin_all = P.op("sync", None)
    for k_, v_ in P.dma_cnt.items():
        fin_all.waits.append(("dma", k_, v_))

    for e_ in ENGS:
        cnt = 0
        for ins in P.q[e_]:
            if ins.signal and not ins.is_dma:
                cnt += 1
                ins.value = cnt
    keys = sorted(P.dma_cnt.keys())
    sem_ctx = {}
    sems_eng = {e_: nc.alloc_semaphore(f"s_{e_}") for e_ in ENGS}
    sems_key = {k: nc.alloc_semaphore(f"d_{k}") for k in keys}

    def replay(ename, eng):
        waited = {}
        for ins in P.q[ename]:
            for w in ins.waits:
                if w[0] == "eng":
                    p = w[1]
                    sem, val = sems_eng[p.eng], p.value
                else:
                    sem, val = sems_key[w[1]], w[2]
                k = id(sem)
                if waited.get(k, 0) < val:
                    eng.wait_ge(sem, val)
                    waited[k] = val
            if ins.fn is None:
                continue
            bi = ins.fn(eng)
            if ins.is_dma:
                bi.then_inc(sems_key[ins.key], 16)
            elif ins.signal:
                bi.then_inc(sems_eng[ename], 1)

    with nc.Block() as block:
        @block.sync
        def _(e):
            replay("sync", e)

        @block.scalar
        def _(e):
            replay("act", e)

        @block.vector
        def _(e):
            replay("dve", e)

        @block.gpsimd
        def _(e):
            replay("pool", e)

        @block.tensor
        def _(e):
            replay("pe", e)

    stats = {e_: len(P.q[e_]) for e_ in ENGS}
    stats["sig"] = {e_: sum(1 for i in P.q[e_] if i.signal and not i.is_dma) for e_ in ENGS}
    return nc, stats


def _blocks_k(w, ncols_per_block):
    K, N = w.shape
    kc = K // 128
    nb = N // ncols_per_block
    a = w.reshape(kc, 128, nb, ncols_per_block).transpose(2, 1, 0, 3)
    return np.ascontiguousarray(a.reshape(nb, 128, kc * ncols_per_block))


def prepare_inputs(x, positions, pre_mix_norm, w_in, q_norm, w_uq, kv_norm, w_ukv, conv_w, conv_b, conv_ln_g,
                   conv_ln_b, conv_out_norm, attn_out_norm, w_out, post_mix_norm, pre_ffn_norm, w_gate, w_up,
                   w_down, post_ffn_norm):
    f = np.float32
    x = np.asarray(x, f)
    positions = np.asarray(positions, np.int32)
    w_in = np.asarray(w_in, f)[0]
    w_uq = np.asarray(w_uq, f)[0]
    w_ukv = np.asarray(w_ukv, f)[0]
    w_out = np.asarray(w_out, f)[0]
    w_gate = np.asarray(w_gate, f)[0]
    w_up = np.asarray(w_up, f)[0]
    w_down = np.asarray(w_down, f)[0]

    c1 = 2 * CONV_CH
    c2 = c1 + Q_LORA
    c3 = c2 + KV_LORA
    cols = []
    cols += list(range(c2, c3))
    cols += list(range(c3, c3 + 64)) + list(range(c3 + 32, c3 + 64)) + list(range(c3, c3 + 32))
    cols += list(range(c1, c2))
    for c in range(8):
        cols += list(range(CONV_CH + c * 128, CONV_CH + (c + 1) * 128))
        cols += list(range(c * 128, (c + 1) * 128))
    w_in_p = _blocks_k(w_in[:, cols], 128)
    cols = []
    for h in range(N_HEADS):
        b0 = h * 192
        cols += list(range(b0, b0 + 128))
        cols += list(range(b0 + 128, b0 + 192)) + list(range(b0 + 160, b0 + 192)) + list(range(b0 + 128, b0 + 160))
    w_uq_p = _blocks_k(w_uq[:, cols], 128)
    kcols = []
    vcols = []
    for h in range(N_HEADS):
        kcols += list(range(h * 256, h * 256 + 128))
        vcols += list(range(h * 256 + 128, h * 256 + 256))
    w_uk_p = _blocks_k(w_ukv[:, kcols], 128)
    wv = w_ukv[:, vcols]
    w_uv_p = np.ascontiguousarray(wv.reshape(2, 2, 128, 1024).transpose(0, 2, 1, 3).reshape(2, 128, 2048))
    w_out_p = np.ascontiguousarray(w_out.reshape(16, 128, 2048).transpose(1, 0, 2).reshape(128, 16 * 2048))
    g_p = _blocks_k(w_gate, 128)
    u_p = _blocks_k(w_up, 128)
    w_gu_p = np.ascontiguousarray(np.stack([g_p, u_p], axis=1).reshape(88, 128, 2048))
    wd = w_down.reshape(11, 4, 128, 4, 512)
    w_dn_p = np.ascontiguousarray(wd.transpose(3, 0, 2, 1, 4).reshape(44, 128, 2048))

    c_bf = np.zeros((128, 512), np.float32)
    c_bf[:, 0:128] = np.eye(128)
    c_bf[:, 128:256] = 1.0
    e64 = np.eye(64)
    c_bf[:, 256:384] = np.block([[e64, e64], [e64, e64]])
    kk = np.arange(128)[:, None]
    qq = np.arange(128)[None, :]
    c_bf[:, 384:512] = (qq >= kk).astype(np.float32)
    c_bf = c_bf.astype(ml_dtypes.bfloat16)

    cvec = np.zeros((128, NCV), f)
    cw = np.asarray(conv_w, f)[0]
    for c in range(8):
        cvec[:, CW + c * 31:CW + (c + 1) * 31] = cw[:, c * 128:(c + 1) * 128].T
    cvec[:, CB:CB + 8] = np.asarray(conv_b, f)[0].reshape(8, 128).T
    cvec[:, LG:LG + 8] = np.asarray(conv_ln_g, f)[0].reshape(8, 128).T
    cvec[:, LB:LB + 8] = np.asarray(conv_ln_b, f)[0].reshape(8, 128).T
    cvec[:, G2:G2 + 8] = np.asarray(conv_out_norm, f)[0].reshape(8, 128).T
    cvec[:, GQ:GQ + 6] = np.asarray(q_norm, f)[0].reshape(6, 128).T
    cvec[:, GKV:GKV + 4] = np.asarray(kv_norm, f)[0].reshape(4, 128).T
    inv_freq = (np.float32(10000.0) ** (-np.arange(0, 64, 2, dtype=np.float32) / np.float32(64))).astype(f)
    cvec[:, IFQ] = np.tile(inv_freq, 4)
    cvec[0:64, PHS] = np.float32(np.pi / 2)
    cvec[:, SGN] = 1.0
    cvec[64:96, SGN] = -1.0

    gbc = np.zeros((5, 128, 2048), f)
    gbc[0] = np.asarray(pre_mix_norm, f)[0][None, :]
    gbc[1] = np.asarray(post_mix_norm, f)[0][None, :]
    gbc[2] = np.asarray(pre_ffn_norm, f)[0][None, :]
    gbc[3] = np.asarray(post_ffn_norm, f)[0][None, :]
    gbc[4, :, 0:1024] = np.asarray(attn_out_norm, f)[0][None, :]

    shared = dict(c_bf=c_bf, gbc=gbc, w_in_p=w_in_p, w_uq_p=w_uq_p, w_uk_p=w_uk_p, w_uv_p=w_uv_p,
                  w_out_p=w_out_p, w_gu_p=w_gu_p, w_dn_p=w_dn_p)
    in_maps = []
    for core in range(8):
        b, half = core // 2, core % 2
        m = dict(shared)
        m["x_own"] = np.ascontiguousarray(x[b, half * TOWN:(half + 1) * TOWN])
        cvc = cvec.copy()
        pos = np.zeros((2048,), np.int32)
        if half == 1:
            m["x_prev"] = np.ascontiguousarray(x[b, 0:TOWN])
            pos[:] = positions[b, 0:2048]
            cvc[:, PFL] = 1.0
        else:
            m["x_prev"] = np.zeros((TOWN, D_MODEL), f)
            pos[1024:] = positions[b, 0:1024]
            cvc[:, PFL] = 0.0
        m["cvec"] = cvc
        m["pos_bc"] = np.ascontiguousarray(np.broadcast_to(pos[None, :], (128, 2048)))
        in_maps.append(m)
    return in_maps


_CACHE = {}


def kernel(**inputs):
    if "nc" not in _CACHE:
        _CACHE["nc"], _CACHE["stats"] = build_program()
    nc = _CACHE["nc"]
    in_maps = prepare_inputs(**inputs)
    res = run_bass_kernel_spmd(nc, in_maps, core_ids=list(range(8)))
    out = np.zeros((BATCH, SEQ, D_MODEL), np.float32)
    for core in range(8):
        b, half = core // 2, core % 2
        out[b, half * TOWN:(half + 1) * TOWN] = res.results[core]["out"]
    return out
```

```python
import os
import numpy as np
import ml_dtypes
import concourse.bass as bass
import concourse.mybir as mybir
from concourse.bass_utils import run_bass_kernel_spmd

F32 = mybir.dt.float32
BF16 = mybir.dt.bfloat16
I32 = mybir.dt.int32
AF = mybir.ActivationFunctionType
ALU = mybir.AluOpType
PI = float(np.pi)

D_MODEL = 2048
SEQ = 2048
BATCH = 4
TOWN = 1024
CONV_CH = 1024
CONV_K = 31
N_HEADS = 8
Q_LORA = 768
KV_LORA = 512
D_FF = 5632
NFB = D_FF // 128
EPS = 1e-6
SCALE = 192 ** -0.5

CW = 0
CB = 248
LG = 256
LB = 264
G2 = 272
GQ = 280
GKV = 286
IFQ = 290
PHS = 291
SGN = 292
PFL = 293
NCV = 320

KB = 1024


class Buf:
    __slots__ = ("space", "lo", "hi", "name")

    def __init__(self, space, lo, hi, name=""):
        self.space, self.lo, self.hi, self.name = space, lo, hi, name


class Ins:
    __slots__ = ("eng", "fn", "is_dma", "key", "signal", "value", "waits", "dma_val", "idx")

    def __init__(self, eng, fn, is_dma, key):
        self.eng, self.fn, self.is_dma, self.key = eng, fn, is_dma, key
        self.idx = -1
        self.signal = False
        self.value = None
        self.waits = []
        self.dma_val = None


ENGS = ["sync", "act", "dve", "pool", "pe"]


class Prog:
    def __init__(self):
        self.q = {e: [] for e in ENGS}
        self.wr = {"sb": [], "ps": []}
        self.rd = {"sb": [], "ps": []}
        self.dma_cnt = {}

    def buf(self, space, lo, hi, name=""):
        return Buf(space, lo, hi, name)

    def op(self, eng, fn, reads=(), writes=(), dma_key=None):
        ins = Ins(eng, fn, dma_key is not None, dma_key)
        ins.idx = len(self.q[eng])
        need_eng = {}
        need_dma = {}

        def add(p):
            if p is ins:
                return
            if p.is_dma:
                need_dma[p.key] = 1
                return
            if p.eng == eng and not ins.is_dma and eng == "pe":
                return
            cur = need_eng.get(p.eng)
            if cur is None or cur.idx < p.idx:
                need_eng[p.eng] = p

        for b in reads:
            for w in self.wr[b.space]:
                if w[0] < b.hi and b.lo < w[1]:
                    add(w[2])
            if b.space == "ps" and eng != "pe":
                lo = b.lo // 2048 * 2048
                hi = -(-b.hi // 2048) * 2048
                for r in self.rd["ps"]:
                    if r[0] < hi and lo < r[1] and r[3].eng != eng and r[3].eng != "pe":
                        add(r[3])
        for b in writes:
            for w in self.wr[b.space]:
                if w[0] < b.hi and b.lo < w[1]:
                    add(w[2])
            for r in self.rd[b.space]:
                if r[0] < b.hi and b.lo < r[1]:
                    add(r[3])
        for p in need_eng.values():
            p.signal = True
            ins.waits.append(("eng", p))
        for k in need_dma:
            ins.waits.append(("dma", k, self.dma_cnt[k]))
        if dma_key is not None:
            self.dma_cnt[dma_key] = self.dma_cnt.get(dma_key, 0) + 16
            ins.dma_val = self.dma_cnt[dma_key]
        for b in writes:
            sp = b.space
            self.wr[sp] = [w for w in self.wr[sp] if not (b.lo <= w[0] and w[1] <= b.hi)]
            self.wr[sp].append([b.lo, b.hi, ins])
            self.rd[sp] = [r for r in self.rd[sp] if not (b.lo <= r[0] and r[1] <= b.hi)]
        rk = ("dma", dma_key) if ins.is_dma else eng
        for b in reads:
            lst = self.rd[b.space]
            for r in lst:
                if r[0] == b.lo and r[1] == b.hi and r[2] == rk:
                    r[3] = ins
                    break
            else:
                lst.append([b.lo, b.hi, rk, ins])
        self.q[eng].append(ins)
        return ins


def build_program(stop_after=None, tensors=None):
    nc = bass.Bass("TRN2", target_bir_lowering=False)
    P = Prog()

    def din(name, shape, dt):
        return nc.dram_tensor(name, list(shape), dt, kind="ExternalInput").ap()

    x_own = din("x_own", [TOWN, D_MODEL], F32)
    x_prev = din("x_prev", [TOWN, D_MODEL], F32)
    pos_bc = din("pos_bc", [128, 2048], I32)
    c_bf = din("c_bf", [128, 512], BF16)
    cvec_d = din("cvec", [128, NCV], F32)
    gbc_d = din("gbc", [5, 128, 2048], F32)
    w_in_p = din("w_in_p", [27, 128, 2048], F32)
    w_uq_p = din("w_uq_p", [16, 128, 768], F32)
    w_uk_p = din("w_uk_p", [8, 128, 512], F32)
    w_uv_p = din("w_uv_p", [2, 128, 2048], F32)
    w_out_p = din("w_out_p", [128, 16 * 2048], F32)
    w_gu_p = din("w_gu_p", [88, 128, 2048], F32)
    w_dn_p = din("w_dn_p", [44, 128, 2048], F32)
    out_d = nc.dram_tensor("out", [TOWN, D_MODEL], F32, kind="ExternalOutput").ap()
    x1_d = nc.dram_tensor("x1_scratch", [TOWN, D_MODEL], F32).ap()
    dbg = {}

    base = (nc.sbuf_base + 63) // 64 * 64
    CONST0 = base
    WR0 = CONST0 + 4 * KB
    AA0 = WR0 + 24 * KB
    assert AA0 + 178 * KB <= nc.sbuf_top, (AA0 + 178 * KB, nc.sbuf_top)

    def dsz(dt):
        return 4 if dt in (F32, I32) else 2

    class T:
        def __init__(self, name, shape, dt, off):
            self.h = nc.alloc_sbuf_tensor_at(name, list(shape), dt, offset=off)
            self.off = off
            self.shape = shape
            self.dt = dt
            self.nbytes = int(np.prod(shape[1:])) * dsz(dt)
            self.whole = P.buf("sb", off, off + self.nbytes, name)
            self._subs = {}
            if tensors is not None:
                tensors[name] = self

        def sub(self, i, n=None):
            key = (i, n)
            if key not in self._subs:
                slab = self.nbytes // self.shape[1]
                cnt = 1 if n is None else n
                self._subs[key] = P.buf("sb", self.off + i * slab, self.off + (i + cnt) * slab)
            return self._subs[key]

        def rng(self, lo_el, hi_el):
            key = ("r", lo_el, hi_el)
            if key not in self._subs:
                self._subs[key] = P.buf("sb", self.off + lo_el * dsz(self.dt), self.off + hi_el * dsz(self.dt))
            return self._subs[key]

    _names = [0]

    def sbt(shape, dt, off, name=None):
        _names[0] += 1
        return T(name or f"t{_names[0]}", shape, dt, off)

    cb = sbt([128, 512], BF16, CONST0, "cb")
    ident = cb.h[:, 0:128]
    ones_b = cb.h[:, 128:256]
    f2 = cb.h[:, 256:384]
    tri = cb.h[:, 384:512]
    cv = sbt([128, NCV], F32, CONST0 + 1024, "cv")
    st = sbt([128, 320], F32, CONST0 + 1024 + NCV * 4, "st")
    onesf = sbt([128, 64], F32, CONST0 + 1024 + NCV * 4 + 1280, "onesf")
    assert CONST0 + 1024 + NCV * 4 + 1280 + 256 <= WR0

    def cvc(col):
        return cv.h[:, col:col + 1]

    _stn = [0]

    def stcol(n=1):
        c = _stn[0]
        _stn[0] += n
        assert _stn[0] <= 320
        return c

    NSLOT = 6
    wslots = [sbt([128, 2048], BF16, WR0 + i * 4 * KB, f"ws{i}") for i in range(NSLOT)]
    wpieces = []
    wstate = {"issued": 0}

    psb = [nc.alloc_psum_tensor(f"psb{i}", [128, 512], F32) for i in range(8)]
    psbuf = [P.buf("ps", i * 2048, (i + 1) * 2048, f"ps{i}") for i in range(8)]
    psbf = [psb[i][:, :].bitcast(BF16) for i in range(8)]
    _bank = [0]

    def nbank():
        b = _bank[0] % 8
        _bank[0] += 1
        return b

    _psub = {}

    def psub(bank, lo, hi):
        k = (bank, lo, hi)
        if k not in _psub:
            _psub[k] = P.buf("ps", bank * 2048 + lo * 4, bank * 2048 + hi * 4)
        return _psub[k]

    def E(eng, fn, r=(), w=(), key=None):
        return P.op(eng, fn, r, w, key)

    def mm(out, lhsT, rhs, start, stop, r, w):
        return E("pe", lambda e: e.matmul(out, lhsT=lhsT, rhs=rhs, start=start, stop=stop), r, w)

    def issue_weights(upto):
        while wstate["issued"] < min(upto, len(wpieces)):
            i = wstate["issued"]
            src, n = wpieces[i]
            slot = wslots[i % NSLOT]
            E("pool", (lambda s, n_, sl: (lambda e: e.dma_start(out=sl.h[:, 0:n_], in_=s)))(src, n, slot),
              r=(), w=(slot.whole,), key=f"ws{i % NSLOT}")
            wstate["issued"] += 1

    wcur = [0]

    def next_w(prefetch=4):
        i = wcur[0]
        wcur[0] += 1
        issue_weights(i + 1 + prefetch)
        return wslots[i % NSLOT]

    for j in range(27):
        wpieces.append((w_in_p[j], 2048))
    for j in range(16):
        wpieces.append((w_uq_p[j], 768))
    for j in range(8):
        wpieces.append((w_uk_p[j], 512))
    for j in range(2):
        wpieces.append((w_uv_p[j], 2048))
    for j in range(88):
        wpieces.append((w_gu_p[j], 2048))
    for j in range(44):
        wpieces.append((w_dn_p[j], 2048))

    class _Stop(Exception):
        pass

    def ckpt(name):
        if stop_after == name:
            raise _Stop()

    try:
        E("sync", lambda e: e.dma_start(out=cb.h[:, :], in_=c_bf), w=(cb.whole,), key="const")
        E("sync", lambda e: e.dma_start(out=cv.h[:, :], in_=cvec_d), w=(cv.whole,), key="const")
        E("dve", lambda e: e.memset(st.h[:, :], 0.0), w=(st.whole,))
        E("dve", lambda e: e.memset(onesf.h[:, :], 1.0), w=(onesf.whole,))

        CS = sbt([128, 2048], F32, AA0 + 170 * KB, "CS")
        posi = sbt([128, 2048], I32, AA0 + 124 * KB, "posi")
        posf = sbt([128, 2048], F32, AA0 + 132 * KB, "posf")
        tmpa = sbt([128, 2048], F32, AA0 + 140 * KB, "tmpa")
        E("sync", lambda e: e.dma_start(out=posi.h[:, :], in_=pos_bc), w=(posi.whole,), key="pos")
        E("dve", lambda e: e.tensor_copy(out=posf.h[:, :], in_=posi.h[:, :]), r=(posi.whole,), w=(posf.whole,))
        E("dve", lambda e: e.tensor_scalar(out=CS.h[:, :], in0=posf.h[:, :], scalar1=cvc(IFQ), scalar2=cvc(PHS),
                                           op0=ALU.mult, op1=ALU.add), r=(posf.whole, cv.whole), w=(CS.whole,))
        E("dve", lambda e: e.tensor_scalar(out=tmpa.h[:, :], in0=CS.h[:, :], scalar1=1.0 / (2 * PI), scalar2=None,
                                           op0=ALU.mult), r=(CS.whole,), w=(tmpa.whole,))
        E("dve", lambda e: e.tensor_copy(out=posi.h[:, :], in_=tmpa.h[:, :]), r=(tmpa.whole,), w=(posi.whole,))
        E("dve", lambda e: e.tensor_copy(out=posf.h[:, :], in_=posi.h[:, :]), r=(posi.whole,), w=(posf.whole,))
        C1 = 6.28125
        C2 = 2 * PI - C1
        E("dve", lambda e: e.scalar_tensor_tensor(out=CS.h[:, :], in0=posf.h[:, :], scalar=-C1, in1=CS.h[:, :],
                                                  op0=ALU.mult, op1=ALU.add), r=(posf.whole, CS.whole), w=(CS.whole,))
        E("dve", lambda e: e.scalar_tensor_tensor(out=CS.h[:, :], in0=posf.h[:, :], scalar=-C2, in1=CS.h[:, :],
                                                  op0=ALU.mult, op1=ALU.add), r=(posf.whole, CS.whole), w=(CS.whole,))
        E("dve", lambda e: e.tensor_single_scalar(out=tmpa.h[:, :], in_=CS.h[:, :], scalar=PI, op=ALU.is_gt),
          r=(CS.whole,), w=(tmpa.whole,))
        E("dve", lambda e: e.scalar_tensor_tensor(out=CS.h[:, :], in0=tmpa.h[:, :], scalar=-2 * PI, in1=CS.h[:, :],
                                                  op0=ALU.mult, op1=ALU.add), r=(tmpa.whole, CS.whole), w=(CS.whole,))
        E("dve", lambda e: e.tensor_scalar(out=CS.h[:, :], in0=CS.h[:, :], scalar1=-PI, scalar2=PI,
                                           op0=ALU.max, op1=ALU.min), r=(CS.whole,), w=(CS.whole,))
        E("act", lambda e: e.activation(out=CS.h[:, :], in_=CS.h[:, :], func=AF.Sin), r=(CS.whole,), w=(CS.whole,))
        E("dve", lambda e: e.tensor_scalar(out=CS.h[:, :], in0=CS.h[:, :], scalar1=cvc(SGN), scalar2=None,
                                           op0=ALU.mult), r=(CS.whole, cv.whole), w=(CS.whole,))

        def rstd_from_ss(c_ss, c_out, n, inv_n):
            c_ms = stcol(n)
            c_sq = stcol(n)
            bs = st.rng(c_ss, c_ss + n)
            bm = st.rng(c_ms, c_ms + n)
            bq = st.rng(c_sq, c_sq + n)
            bo = st.rng(c_out, c_out + n)
            E("dve", lambda e: e.tensor_scalar(out=st.h[:, c_ms:c_ms + n], in0=st.h[:, c_ss:c_ss + n], scalar1=inv_n,
                                               scalar2=EPS, op0=ALU.mult, op1=ALU.add), r=(bs,), w=(bm,))
            E("act", lambda e: e.activation(out=st.h[:, c_sq:c_sq + n], in_=st.h[:, c_ms:c_ms + n], func=AF.Sqrt),
              r=(bm,), w=(bq,))
            E("dve", lambda e: e.reciprocal(out=st.h[:, c_out:c_out + n], in_=st.h[:, c_sq:c_sq + n]), r=(bq,), w=(bo,))
            return bo

        def rstd_psum_inplace(bank, n, inv_n):
            b = psbuf[bank]
            ap = psb[bank][:, 0:n]
            E("dve", lambda e: e.tensor_scalar(out=ap, in0=ap, scalar1=inv_n, scalar2=EPS, op0=ALU.mult, op1=ALU.add),
              r=(b,), w=(b,))
            E("act", lambda e: e.activation(out=ap, in_=ap, func=AF.Sqrt), r=(b,), w=(b,))
            E("dve", lambda e: e.reciprocal(out=ap, in_=ap), r=(b,), w=(b,))

        def nt_stats(src_ap, src_buf, junk_t, width, c_ss, c_rs):
            bs = st.rng(c_ss, c_ss + 1)
            E("act", lambda e: e.activation(out=junk_t.h[:, 0:width], in_=src_ap, func=AF.Square,
                                            accum_out=st.h[:, c_ss:c_ss + 1]), r=(src_buf,), w=(junk_t.whole, bs))
            return rstd_from_ss(c_ss, c_rs, 1, 1.0 / width)

        def nt_apply(src_ap, src_buf, brs, gb_t, xn_t, width, c_rs, dst_fn, dst_bufs, evac_engs, banks=None):
            nchunk = width // 128
            E("dve", lambda e: e.scalar_tensor_tensor(out=xn_t.h[:, 0:width], in0=src_ap,
                                                      scalar=st.h[:, c_rs:c_rs + 1], in1=gb_t.h[:, 0:width],
                                                      op0=ALU.mult, op1=ALU.mult),
              r=(src_buf, brs, gb_t.whole), w=(xn_t.whole,))
            for g in range(nchunk // 8):
                bk = nbank() if banks is None else banks[g]
                for c8 in range(8):
                    c = g * 8 + c8
                    E("pe", (lambda bk_, c8_, c_: (lambda e: e.transpose(psbf[bk_][:, c8_ * 128:(c8_ + 1) * 128],
                                                                       xn_t.h[:, c_ * 128:(c_ + 1) * 128], ident)))(bk, c8, c),
                      r=(xn_t.whole, cb.whole), w=(psbuf[bk],))
                eng = evac_engs[g % len(evac_engs)]
                src = psbf[bk][:, 0:1024].rearrange("p (a b) -> p a b", a=8)
                dst = dst_fn(g)
                if eng == "act":
                    E("act", (lambda d_, s_: (lambda e: e.copy(out=d_, in_=s_)))(dst, src), r=(psbuf[bk],), w=dst_bufs)
                else:
                    E("dve", (lambda d_, s_: (lambda e: e.tensor_copy(out=d_, in_=s_)))(dst, src), r=(psbuf[bk],), w=dst_bufs)

        hT = sbt([128, 16, 2048], BF16, AA0 + 0, "hT")
        xa = [sbt([128, 2048], F32, AA0 + (64 + 8 * i) * KB, f"xa{i}") for i in range(3)]
        gbcA = sbt([128, 2048], F32, AA0 + 88 * KB, "gbcA")
        xnA = [sbt([128, 2048], BF16, AA0 + (96 + 4 * i) * KB, f"xnA{i}") for i in range(3)]
        junkA = [sbt([128, 2048], BF16, AA0 + (108 + 4 * i) * KB, f"junkA{i}") for i in range(4)]
        issue_weights(NSLOT)
        E("sync", lambda e: e.dma_start(out=gbcA.h[:, :], in_=gbc_d[0]), w=(gbcA.whole,), key="gbc")
        cA_ss = stcol(16)
        cA_rs = stcol(16)
        brsA = {}

        def A1(t):
            s_ = t % 3
            src = x_prev[t * 128:(t + 1) * 128, :] if t < 8 else x_own[(t - 8) * 128:(t - 7) * 128, :]
            E("sync", (lambda s__, src_: (lambda e: e.dma_start(out=xa[s__].h[:, :], in_=src_)))(s_, src),
              w=(xa[s_].whole,), key=f"xa{s_}")
            brsA[t] = nt_stats(xa[s_].h[:, :], xa[s_].whole, junkA[t % 4], 2048, cA_ss + t, cA_rs + t)

        def A2(t):
            s_ = t % 3
            nt_apply(xa[s_].h[:, :], xa[s_].whole, brsA[t], gbcA, xnA[s_], 2048, cA_rs + t,
                     (lambda t_: (lambda g: hT.h[:, g * 8:(g + 1) * 8, t_ * 128:(t_ + 1) * 128]))(t),
                     (hT.whole,), ["act", "dve"])

        for t in range(16):
            A1(t)
            if t >= 1:
                A2(t - 1)
        A2(15)

        ckpt('A')
        zz = sbt([128, 4, 2048], F32, AA0 + 64 * KB, "zz")
        zq = sbt([128, 6, 1024], F32, AA0 + 64 * KB, "zq")
        sqz = sbt([128, 4, 2048], BF16, AA0 + 96 * KB, "sqz")
        sqq = sbt([128, 6, 1024], BF16, AA0 + 96 * KB, "sqq")
        t_k = sbt([128, 2048], BF16, AA0 + 112 * KB, "t_k")
        kvn = sbt([128, 4, 2048], BF16, AA0 + 116 * KB, "kvn")
        qln = sbt([128, 6, 1024], BF16, AA0 + 132 * KB, "qln")
        u_bf = sbt([128, 8, 1152], BF16, AA0 + 148 * KB, "u_bf")
        kr2 = sbt([128, 2048], BF16, AA0 + 166 * KB, "kr2")
        sg = [sbt([128, 1152], F32, AA0 + 64 * KB + i * 4608, f"sg{i}") for i in range(2)]

        for b in range(4):
            ws = next_w()
            banks = [nbank() for _ in range(4)]
            for kc in range(16):
                for n in range(4):
                    mm(psb[banks[n]][:, :], ws.h[:, kc * 128:(kc + 1) * 128], hT.h[:, kc, n * 512:(n + 1) * 512],
                       kc == 0, kc == 15, (ws.whole, hT.whole), (psbuf[banks[n]],))
            for n in range(4):
                if os.environ.get("KDBG") == "noevac":
                    break
                bk = banks[n]
                E("dve", (lambda bk_, b_, n_: (lambda e: e.tensor_scalar(
                    out=zz.h[:, b_, n_ * 512:(n_ + 1) * 512], in0=psb[bk_][:, :], scalar1=cvc(GKV + b_), scalar2=None,
                    op0=ALU.mult)))(bk, b, n), r=(psbuf[bk], cv.whole), w=(zz.sub(b),))
                if os.environ.get("KDBG") == "noact":
                    continue
                E("act", (lambda bk_, b_, n_: (lambda e: e.activation(
                    out=sqz.h[:, b_, n_ * 512:(n_ + 1) * 512], in_=psb[bk_][:, :], func=AF.Square)))(bk, b, n),
                  r=(psbuf[bk],), w=(sqz.sub(b),) + ((psbuf[bk],) if os.environ.get("KDBG") == "serial" else ()))
        ckpt('B1')
        for n in range(4):
            bk = nbank()
            for b in range(4):
                mm(psb[bk][:, :], ones_b, sqz.h[:, b, n * 512:(n + 1) * 512], b == 0, b == 3,
                   (cb.whole, sqz.sub(b)), (psbuf[bk],))
            rstd_psum_inplace(bk, 512, 1.0 / KV_LORA)
            for b in range(4):
                E("dve", (lambda bk_, b_, n_: (lambda e: e.tensor_tensor(
                    out=kvn.h[:, b_, n_ * 512:(n_ + 1) * 512], in0=zz.h[:, b_, n_ * 512:(n_ + 1) * 512],
                    in1=psb[bk_][:, :], op=ALU.mult)))(bk, b, n), r=(zz.sub(b), psbuf[bk]), w=(kvn.sub(b),))
        ckpt('B2')
        ws = next_w()
        banks = [nbank() for _ in range(4)]
        for kc in range(16):
            for n in range(4):
                mm(psb[banks[n]][:, :], ws.h[:, kc * 128:(kc + 1) * 128], hT.h[:, kc, n * 512:(n + 1) * 512],
                   kc == 0, kc == 15, (ws.whole, hT.whole), (psbuf[banks[n]],))
        for n in range(4):
            bk = banks[n]
            E("dve", (lambda bk_, n_: (lambda e: e.tensor_tensor(
                out=t_k.h[:, n_ * 512:(n_ + 1) * 512], in0=psb[bk_][:, :], in1=CS.h[:, n_ * 512:(n_ + 1) * 512],
                op=ALU.mult)))(bk, n), r=(psbuf[bk], CS.whole), w=(t_k.rng(n * 512, (n + 1) * 512),))
            bk2 = nbank()
            mm(psb[bk2][:, :], f2, t_k.h[:, n * 512:(n + 1) * 512], True, True,
               (cb.whole, t_k.rng(n * 512, (n + 1) * 512)), (psbuf[bk2],))
            E("act", (lambda bk_, n_: (lambda e: e.copy(out=kr2.h[:, n_ * 512:(n_ + 1) * 512], in_=psb[bk_][:, :])))(bk2, n),
              r=(psbuf[bk2],), w=(kr2.rng(n * 512, (n + 1) * 512),))
        ckpt('B3')
        for b in range(6):
            ws = next_w()
            banks = [nbank() for _ in range(2)]
            for kc in range(16):
                for n in range(2):
                    mm(psb[banks[n]][:, :], ws.h[:, kc * 128:(kc + 1) * 128],
                       hT.h[:, kc, 1024 + n * 512:1024 + (n + 1) * 512],
                       kc == 0, kc == 15, (ws.whole, hT.whole), (psbuf[banks[n]],))
            for n in range(2):
                bk = banks[n]
                E("dve", (lambda bk_, b_, n_: (lambda e: e.tensor_scalar(
                    out=zq.h[:, b_, n_ * 512:(n_ + 1) * 512], in0=psb[bk_][:, :], scalar1=cvc(GQ + b_), scalar2=None,
                    op0=ALU.mult)))(bk, b, n), r=(psbuf[bk], cv.whole), w=(zq.sub(b),))
                E("act", (lambda bk_, b_, n_: (lambda e: e.activation(
                    out=sqq.h[:, b_, n_ * 512:(n_ + 1) * 512], in_=psb[bk_][:, :], func=AF.Square)))(bk, b, n),
                  r=(psbuf[bk],), w=(sqq.sub(b),))
        for n in range(2):
            bk = nbank()
            for b in range(6):
                mm(psb[bk][:, :], ones_b, sqq.h[:, b, n * 512:(n + 1) * 512], b == 0, b == 5,
                   (cb.whole, sqq.sub(b)), (psbuf[bk],))
            rstd_psum_inplace(bk, 512, 1.0 / Q_LORA)
            for b in range(6):
                E("dve", (lambda bk_, b_, n_: (lambda e: e.tensor_tensor(
                    out=qln.h[:, b_, n_ * 512:(n_ + 1) * 512], in0=zq.h[:, b_, n_ * 512:(n_ + 1) * 512],
                    in1=psb[bk_][:, :], op=ALU.mult)))(bk, b, n), r=(zq.sub(b), psbuf[bk]), w=(qln.sub(b),))
        ckpt('B4')
        tokr = [(896, 1024), (1024, 1536), (1536, 2048)]
        for c in range(8):
            sgt = sg[c % 2]
            ws = next_w()
            banks = [nbank() for _ in range(3)]
            for kc in range(16):
                for n, (a0, a1) in enumerate(tokr):
                    mm(psb[banks[n]][:, 0:a1 - a0], ws.h[:, kc * 128:(kc + 1) * 128], hT.h[:, kc, a0:a1],
                       kc == 0, kc == 15, (ws.whole, hT.whole), (psbuf[banks[n]],))
            for n, (a0, a1) in enumerate(tokr):
                E("act", (lambda bk_, a0_, a1_, sg_: (lambda e: e.activation(
                    out=sg_.h[:, a0_ - 896:a1_ - 896], in_=psb[bk_][:, 0:a1_ - a0_], func=AF.Sigmoid)))(banks[n], a0, a1, sgt),
                  r=(psbuf[banks[n]],), w=(sgt.whole,))
            ws = next_w()
            banks = [nbank() for _ in range(3)]
            for kc in range(16):
                for n, (a0, a1) in enumerate(tokr):
                    mm(psb[banks[n]][:, 0:a1 - a0], ws.h[:, kc * 128:(kc + 1) * 128], hT.h[:, kc, a0:a1],
                       kc == 0, kc == 15, (ws.whole, hT.whole), (psbuf[banks[n]],))
            for n, (a0, a1) in enumerate(tokr):
                E("dve", (lambda bk_, a0_, a1_, sg_, c_: (lambda e: e.tensor_tensor(
                    out=u_bf.h[:, c_, a0_ - 896:a1_ - 896], in0=psb[bk_][:, 0:a1_ - a0_], in1=sg_.h[:, a0_ - 896:a1_ - 896],
                    op=ALU.mult)))(banks[n], a0, a1, sgt, c), r=(psbuf[banks[n]], sgt.whole), w=(u_bf.sub(c),))

        ckpt('B')
        yv = sbt([128, 8, 1024], F32, AA0 + 0, "yv")
        Dr = [sbt([128, 31, 128], BF16, AA0 + (32 + 8 * i) * KB, f"Dr{i}") for i in range(2)]
        sqs = [sbt([128, 512], BF16, AA0 + 48 * KB + i * 1024, f"sqs{i}") for i in range(4)]
        bc0 = sbt([128, 1024], F32, AA0 + 52 * KB, "bc0")
        bc1 = sbt([128, 1024], F32, AA0 + 56 * KB, "bc1")
        mixT = sbt([128, 16, 1024], BF16, AA0 + 64 * KB, "mixT")
        mixT_attn = P.buf("sb", mixT.off + 8 * 2048, mixT.off + 16 * 2048, "mixT_attn")
        bS1 = [nbank(), nbank()]
        bS2 = [nbank(), nbank()]
        nsq = [0]
        for c in range(8):
            dr = Dr[c % 2]
            for k in range(CONV_K):
                E("dve", (lambda dr_, k_, c_: (lambda e: e.tensor_scalar(
                    out=dr_.h[:, k_, :], in0=ident, scalar1=cvc(CW + c_ * 31 + k_), scalar2=None, op0=ALU.mult)))(dr, k, c),
                  r=(cb.whole, cv.whole), w=(dr.sub(k),))
            for n in range(2):
                bk = nbank()
                while bk in bS1 or bk in bS2:
                    bk = nbank()
                for k in range(CONV_K):
                    mm(psb[bk][:, :], dr.h[:, k, :], u_bf.h[:, c, 98 + k + n * 512:98 + k + (n + 1) * 512],
                       k == 0, k == CONV_K - 1, (dr.sub(k), u_bf.sub(c)), (psbuf[bk],))
                yb = yv.rng(c * 1024 + n * 512, c * 1024 + (n + 1) * 512)
                E("act", (lambda bk_, c_, n_: (lambda e: e.activation(
                    out=yv.h[:, c_, n_ * 512:(n_ + 1) * 512], in_=psb[bk_][:, :], func=AF.Identity,
                    bias=cvc(CB + c_))))(bk, c, n), r=(psbuf[bk], cv.whole), w=(yb,))
                s1 = sqs[nsq[0] % 4]
                nsq[0] += 1
                s2 = sqs[nsq[0] % 4]
                nsq[0] += 1
                E("act", (lambda bk_, c_, s_: (lambda e: e.activation(
                    out=s_.h[:, :], in_=psb[bk_][:, :], func=AF.Identity, bias=cvc(CB + c_))))(bk, c, s1),
                  r=(psbuf[bk], cv.whole), w=(s1.whole,))
                E("act", (lambda bk_, c_, s_: (lambda e: e.activation(
                    out=s_.h[:, :], in_=psb[bk_][:, :], func=AF.Square, bias=cvc(CB + c_))))(bk, c, s2),
                  r=(psbuf[bk], cv.whole), w=(s2.whole,))
                mm(psb[bS1[n]][:, :], ones_b, s1.h[:, :], c == 0, c == 7, (cb.whole, s1.whole), (psbuf[bS1[n]],))
                mm(psb[bS2[n]][:, :], ones_b, s2.h[:, :], c == 0, c == 7, (cb.whole, s2.whole), (psbuf[bS2[n]],))
        for n in range(2):
            sl = slice(n * 512, (n + 1) * 512)
            b0 = bc0.rng(n * 512, (n + 1) * 512)
            b1 = bc1.rng(n * 512, (n + 1) * 512)
            E("dve", (lambda n_, sl_: (lambda e: e.tensor_scalar(out=bc0.h[:, sl_], in0=psb[bS1[n_]][:, :],
                                                                scalar1=1.0 / CONV_CH, scalar2=None, op0=ALU.mult)))(n, sl),
              r=(psbuf[bS1[n]],), w=(b0,))
            E("dve", (lambda sl_: (lambda e: e.tensor_tensor(out=bc1.h[:, sl_], in0=bc0.h[:, sl_], in1=bc0.h[:, sl_],
                                                            op=ALU.mult)))(sl), r=(b0,), w=(b1,))
            E("dve", (lambda n_, sl_: (lambda e: e.scalar_tensor_tensor(out=bc1.h[:, sl_], in0=psb[bS2[n_]][:, :],
                                                                       scalar=1.0 / CONV_CH, in1=bc1.h[:, sl_],
                                                                       op0=ALU.mult, op1=ALU.subtract)))(n, sl),
              r=(psbuf[bS2[n]], b1), w=(b1,))
            E("dve", (lambda sl_: (lambda e: e.tensor_scalar(out=bc1.h[:, sl_], in0=bc1.h[:, sl_], scalar1=EPS,
                                                            scalar2=None, op0=ALU.add)))(sl), r=(b1,), w=(b1,))
            E("act", (lambda sl_: (lambda e: e.activation(out=bc1.h[:, sl_], in_=bc1.h[:, sl_], func=AF.Sqrt)))(sl),
              r=(b1,), w=(b1,))
            E("dve", (lambda sl_: (lambda e: e.reciprocal(out=bc1.h[:, sl_], in_=bc1.h[:, sl_])))(sl), r=(b1,), w=(b1,))
        bS3 = [bS1[0], bS1[1]]
        for c in range(8):
            for n in range(2):
                sl = slice(n * 512, (n + 1) * 512)
                yb = yv.rng(c * 1024 + n * 512, c * 1024 + (n + 1) * 512)
                b0 = bc0.rng(n * 512, (n + 1) * 512)
                b1 = bc1.rng(n * 512, (n + 1) * 512)
                E("dve", (lambda c_, sl_: (lambda e: e.tensor_tensor(out=yv.h[:, c_, sl_], in0=yv.h[:, c_, sl_],
                                                                    in1=bc0.h[:, sl_], op=ALU.subtract)))(c, sl),
                  r=(yb, b0), w=(yb,))
                E("dve", (lambda c_, sl_: (lambda e: e.tensor_tensor(out=yv.h[:, c_, sl_], in0=yv.h[:, c_, sl_],
                                                                    in1=bc1.h[:, sl_], op=ALU.mult)))(c, sl),
                  r=(yb, b1), w=(yb,))
                E("act", (lambda c_, sl_: (lambda e: e.activation(out=yv.h[:, c_, sl_], in_=yv.h[:, c_, sl_], func=AF.Silu,
                                                                 scale=cvc(LG + c_), bias=cvc(LB + c_))))(c, sl),
                  r=(yb, cv.whole), w=(yb,))
                s2 = sqs[nsq[0] % 4]
                nsq[0] += 1
                E("act", (lambda c_, sl_, s_: (lambda e: e.activation(out=s_.h[:, :], in_=yv.h[:, c_, sl_],
                                                                     func=AF.Square)))(c, sl, s2), r=(yb,), w=(s2.whole,))
                mm(psb[bS3[n]][:, :], ones_b, s2.h[:, :], c == 0, c == 7, (cb.whole, s2.whole), (psbuf[bS3[n]],))
        for n in range(2):
            rstd_psum_inplace(bS3[n], 512, 1.0 / CONV_CH)
        for c in range(8):
            for n in range(2):
                sl = slice(n * 512, (n + 1) * 512)
                yb = yv.rng(c * 1024 + n * 512, c * 1024 + (n + 1) * 512)
                E("dve", (lambda c_, n_, sl_: (lambda e: e.scalar_tensor_tensor(
                    out=mixT.h[:, c_, sl_], in0=yv.h[:, c_, sl_], scalar=cvc(G2 + c_), in1=psb[bS3[n_]][:, :],
                    op0=ALU.mult, op1=ALU.mult)))(c, n, sl), r=(yb, cv.whole, psbuf[bS3[n]]), w=(mixT.sub(c),))

        ckpt('E')
        knT = sbt([128, 8, 2048], BF16, AA0 + 0, "knT")
        qnT = sbt([128, 8, 1024], BF16, AA0 + 32 * KB, "qnT")
        tq = sbt([128, 8, 1024], BF16, AA0 + 48 * KB, "tq")
        VA = sbt([128, 8, 8 * 130], BF16, AA0 + 96 * KB, "VA")
        VB = sbt([128, 8, 8 * 130], BF16, AA0 + 148 * KB, "VB")
        for h in range(N_HEADS):
            ws = next_w()
            banks = [nbank() for _ in range(2)]
            for kc in range(6):
                for n in range(2):
                    mm(psb[banks[n]][:, :], ws.h[:, kc * 128:(kc + 1) * 128], qln.h[:, kc, n * 512:(n + 1) * 512],
                       kc == 0, kc == 5, (ws.whole, qln.sub(kc)), (psbuf[banks[n]],))
            for n in range(2):
                E("act", (lambda bk_, h_, n_: (lambda e: e.copy(out=qnT.h[:, h_, n_ * 512:(n_ + 1) * 512],
                                                               in_=psb[bk_][:, :])))(banks[n], h, n),
                  r=(psbuf[banks[n]],), w=(qnT.sub(h),))
            ws = next_w()
            banks = [nbank() for _ in range(2)]
            for kc in range(6):
                for n in range(2):
                    mm(psb[banks[n]][:, :], ws.h[:, kc * 128:(kc + 1) * 128], qln.h[:, kc, n * 512:(n + 1) * 512],
                       kc == 0, kc == 5, (ws.whole, qln.sub(kc)), (psbuf[banks[n]],))
            for n in range(2):
                E("dve", (lambda bk_, h_, n_: (lambda e: e.tensor_tensor(
                    out=tq.h[:, h_, n_ * 512:(n_ + 1) * 512], in0=psb[bk_][:, :],
                    in1=CS.h[:, 1024 + n_ * 512:1024 + (n_ + 1) * 512], op=ALU.mult)))(banks[n], h, n),
                  r=(psbuf[banks[n]], CS.whole), w=(tq.sub(h),))
        for h in range(N_HEADS):
            ws = next_w()
            banks = [nbank() for _ in range(4)]
            for kc in range(4):
                for n in range(4):
                    mm(psb[banks[n]][:, :], ws.h[:, kc * 128:(kc + 1) * 128], kvn.h[:, kc, n * 512:(n + 1) * 512],
                       kc == 0, kc == 3, (ws.whole, kvn.sub(kc)), (psbuf[banks[n]],))
            for n in range(4):
                if n % 2 == 0:
                    E("act", (lambda bk_, h_, n_: (lambda e: e.copy(out=knT.h[:, h_, n_ * 512:(n_ + 1) * 512],
                                                                   in_=psb[bk_][:, :])))(banks[n], h, n),
                      r=(psbuf[banks[n]],), w=(knT.sub(h),))
                else:
                    E("dve", (lambda bk_, h_, n_: (lambda e: e.tensor_copy(out=knT.h[:, h_, n_ * 512:(n_ + 1) * 512],
                                                                          in_=psb[bk_][:, :])))(banks[n], h, n),
                      r=(psbuf[banks[n]],), w=(knT.sub(h),))
        wv = [next_w(), next_w()]
        E("dve", lambda e: e.tensor_scalar(out=VA.h[:, :, :].rearrange("p t (h d) -> p (t h) d", d=130)[:, :, 128:129],
                                           in0=onesf.h[:, 0:64].rearrange("p (a b) -> p a b", b=1),
                                           scalar1=cvc(PFL), scalar2=None, op0=ALU.mult),
          r=(onesf.whole, cv.whole), w=(VA.whole,))
        E("dve", lambda e: e.memset(VB.h[:, :, :].rearrange("p t (h d) -> p (t h) d", d=130)[:, :, 128:129], 1.0),
          w=(VB.whole,))
        for t in range(16):
            Vt = VA if t < 8 else VB
            tt = t % 8
            banks = [nbank() for _ in range(2)]
            for kc in range(4):
                for hf in range(2):
                    mm(psb[banks[hf]][:, :], kvn.h[:, kc, t * 128:(t + 1) * 128],
                       wv[kc // 2].h[:, (kc % 2) * 1024 + hf * 512:(kc % 2) * 1024 + (hf + 1) * 512],
                       kc == 0, kc == 3, (kvn.sub(kc), wv[kc // 2].whole), (psbuf[banks[hf]],))
            for hf in range(2):
                dst = Vt.h[:, tt, :].rearrange("p (h d) -> p h d", d=130)[:, hf * 4:(hf + 1) * 4, 0:128]
                srcp = psb[banks[hf]][:, :].rearrange("p (h d) -> p h d", d=128)
                if t < 8:
                    E("dve", (lambda d_, s_: (lambda e: e.tensor_scalar(out=d_, in0=s_, scalar1=cvc(PFL), scalar2=None,
                                                                       op0=ALU.mult)))(dst, srcp),
                      r=(psbuf[banks[hf]], cv.whole), w=(Vt.sub(tt),))
                else:
                    E("act", (lambda d_, s_: (lambda e: e.copy(out=d_, in_=s_)))(dst, srcp),
                      r=(psbuf[banks[hf]],), w=(Vt.sub(tt),))

        ckpt('C')
        attn = sbt([128, 8, 1024], F32, AA0 + 116 * KB, "attn")
        PT = [sbt([128, 512], BF16, AA0 + 170 * KB + i * 1024, f"PT{i}") for i in range(4)]
        rc = sbt([128, 64], F32, AA0 + 174 * KB, "rc")
        nrc = [0]
        ST_B = [4, 5, 6, 7]
        ACC_B = [0, 1, 2, 3]
        items = []
        for h in range(N_HEADS):
            for qb in range(2):
                for kc in range(8 + 4 * qb + 4):
                    items.append((h, qb, kc))
        LOOK = 3

        def d_qk(n):
            h, qb, kc = items[n]
            q0 = max(kc - 8 - 4 * qb, 0) * 128
            bk = ST_B[n % 4]
            mm(psb[bk][:, q0:512], knT.h[:, h, kc * 128:(kc + 1) * 128],
               qnT.h[:, h, qb * 512 + q0:qb * 512 + 512], True, False,
               (knT.sub(h), qnT.sub(h)), (psbuf[bk],))
            mm(psb[bk][:, q0:512], kr2.h[:, kc * 128:(kc + 1) * 128],
               tq.h[:, h, qb * 512 + q0:qb * 512 + 512], False, True,
               (kr2.whole, tq.sub(h)), (psbuf[bk],))

        def d_rest(n):
            h, qb, kc = items[n]
            j = kc - 8 - 4 * qb
            i0 = max(j, 0)
            q0 = i0 * 128
            bk = ST_B[n % 4]
            pt = PT[n % 4]
            Vt = VA if kc < 8 else VB
            E("act", lambda e: e.activation(out=pt.h[:, q0:512], in_=psb[bk][:, q0:512], func=AF.Exp, scale=SCALE),
              r=(psbuf[bk],), w=(pt.whole,))
            if j >= 0:
                E("dve", lambda e: e.tensor_tensor(out=pt.h[:, q0:q0 + 128], in0=pt.h[:, q0:q0 + 128], in1=tri,
                                                   op=ALU.mult), r=(pt.whole, cb.whole), w=(pt.whole,))
            for i in range(i0, 4):
                last = 8 + 4 * qb + i
                ab = ACC_B[i]
                mm(psb[ab][:, 0:129], pt.h[:, i * 128:(i + 1) * 128], Vt.h[:, kc % 8, h * 130:h * 130 + 129],
                   kc == 0, kc == last, (pt.whole, Vt.sub(kc % 8)), (psbuf[ab],))
                if kc == last:
                    col = nrc[0] % 64
                    nrc[0] += 1
                    rb = rc.rng(col, col + 1)
                    E("dve", (lambda ab_, col_: (lambda e: e.reciprocal(out=rc.h[:, col_:col_ + 1],
                                                                       in_=psb[ab_][:, 128:129])))(ab, col),
                      r=(psbuf[ab],), w=(rb,))
                    E("dve", (lambda ab_, col_, i_: (lambda e: e.tensor_scalar(
                        out=attn.h[:, qb * 4 + i_, h * 128:(h + 1) * 128], in0=psb[ab_][:, 0:128],
                        scalar1=rc.h[:, col_:col_ + 1], scalar2=None, op0=ALU.mult)))(ab, col, i),
                      r=(psbuf[ab], rb), w=(attn.rng((qb * 4 + i) * 1024 + h * 128, (qb * 4 + i) * 1024 + (h + 1) * 128),))

        for n in range(len(items) + LOOK):
            if n < len(items):
                d_qk(n)
            if n >= LOOK:
                d_rest(n - LOOK)

        ckpt('D')
        gbcD = sbt([128, 1024], F32, AA0 + 96 * KB, "gbcD")
        junkD = [sbt([128, 1024], BF16, AA0 + (100 + 6 * i) * KB, f"junkD{i}") for i in range(2)]
        xnD = [sbt([128, 1024], BF16, AA0 + (102 + 2 * i) * KB, f"xnD{i}") for i in range(2)]
        E("sync", lambda e: e.dma_start(out=gbcD.h[:, :], in_=gbc_d[4][:, 0:1024]), w=(gbcD.whole,), key="gbc")
        cD_ss = stcol(8)
        cD_rs = stcol(8)
        brsD = {}
        for i in range(8):
            brsD[i] = nt_stats(attn.h[:, i, :], attn.sub(i), junkD[i % 2], 1024, cD_ss + i, cD_rs + i)
            if i >= 1:
                nt_apply(attn.h[:, i - 1, :], attn.sub(i - 1), brsD[i - 1], gbcD, xnD[(i - 1) % 2], 1024, cD_rs + i - 1,
                         (lambda i_: (lambda g: mixT.h[:, 8:16, i_ * 128:(i_ + 1) * 128]))(i - 1),
                         (mixT_attn,), ["dve"])
        nt_apply(attn.h[:, 7, :], attn.sub(7), brsD[7], gbcD, xnD[1], 1024, cD_rs + 7,
                 (lambda g: mixT.h[:, 8:16, 7 * 128:8 * 128]), (mixT_attn,), ["dve"])

        ckpt('Dn')
        w_out = sbt([128, 16, 2048], BF16, AA0 + 0, "w_out")
        for g in range(4):
            E("pool", (lambda g_: (lambda e: e.dma_start(out=w_out.h[:, g_ * 4:(g_ + 1) * 4, :],
                                                         in_=w_out_p[:, g_ * 8192:(g_ + 1) * 8192].rearrange("p (a b) -> p a b", a=4))))(g),
              w=(w_out.sub(g * 4, 4),), key=f"wout{g}")
        xF = [sbt([128, 2048], F32, AA0 + (96 + 8 * i) * KB, f"xF{i}") for i in range(2)]
        gpost = sbt([128, 2048], F32, AA0 + 112 * KB, "gpost")
        gpre = sbt([128, 2048], F32, AA0 + 120 * KB, "gpre")
        x1t = sbt([128, 2048], F32, AA0 + 128 * KB, "x1t")
        xnF = sbt([128, 2048], BF16, AA0 + 136 * KB, "xnF")
        hfT = sbt([128, 16, 1024], BF16, AA0 + 140 * KB, "hfT")
        junkF = sbt([128, 2048], BF16, AA0 + 172 * KB, "junkF")
        E("sync", lambda e: e.dma_start(out=gpost.h[:, :], in_=gbc_d[1]), w=(gpost.whole,), key="gbc2")
        E("sync", lambda e: e.dma_start(out=gpre.h[:, :], in_=gbc_d[2]), w=(gpre.whole,), key="gbc2")
        cF_p = stcol(32)
        cF_ss = stcol(8)
        cF_rs = stcol(8)
        cF_ss2 = stcol(8)
        cF_rs2 = stcol(8)
        brsF = {}
        brsF2 = {}

        def F1(i):
            s_ = i % 2
            E("sync", lambda e: e.dma_start(out=xF[s_].h[:, :], in_=x_own[i * 128:(i + 1) * 128, :]),
              w=(xF[s_].whole,), key=f"xF{s_}")
            for cbk in range(4):
                bk = (i % 2) * 4 + cbk
                for kc in range(16):
                    mm(psb[bk][:, :], mixT.h[:, kc, i * 128:(i + 1) * 128], w_out.h[:, kc, cbk * 512:(cbk + 1) * 512],
                       kc == 0, kc == 15, (mixT.sub(kc), w_out.sub(kc)), (psbuf[bk],))
                pc = cF_p + i * 4 + cbk
                E("act", (lambda bk_, pc_, c_: (lambda e: e.activation(
                    out=junkF.h[:, c_ * 512:(c_ + 1) * 512], in_=psb[bk_][:, :], func=AF.Square,
                    accum_out=st.h[:, pc_:pc_ + 1])))(bk, pc, cbk),
                  r=(psbuf[bk],), w=(junkF.rng(cbk * 512, (cbk + 1) * 512), st.rng(pc, pc + 1)))
            E("dve", lambda e: e.tensor_reduce(out=st.h[:, cF_ss + i:cF_ss + i + 1],
                                               in_=st.h[:, cF_p + 4 * i:cF_p + 4 * i + 4],
                                               axis=mybir.AxisListType.X, op=ALU.add),
              r=(st.rng(cF_p + 4 * i, cF_p + 4 * i + 4),), w=(st.rng(cF_ss + i, cF_ss + i + 1),))
            brsF[i] = rstd_from_ss(cF_ss + i, cF_rs + i, 1, 1.0 / D_MODEL)

        def F2(i):
            s_ = i % 2
            for cbk in range(4):
                bk = (i % 2) * 4 + cbk
                sl = slice(cbk * 512, (cbk + 1) * 512)
                E("dve", (lambda bk_, sl_: (lambda e: e.scalar_tensor_tensor(
                    out=x1t.h[:, sl_], in0=psb[bk_][:, :], scalar=st.h[:, cF_rs + i:cF_rs + i + 1], in1=gpost.h[:, sl_],
                    op0=ALU.mult, op1=ALU.mult)))(bk, sl),
                  r=(psbuf[bk], brsF[i], gpost.whole), w=(x1t.rng(cbk * 512, (cbk + 1) * 512),))
            E("dve", lambda e: e.tensor_tensor(out=x1t.h[:, :], in0=x1t.h[:, :], in1=xF[s_].h[:, :], op=ALU.add),
              r=(x1t.whole, xF[s_].whole), w=(x1t.whole,))
            E("sync", lambda e: e.dma_start(out=x1_d[i * 128:(i + 1) * 128, :], in_=x1t.h[:, :]),
              r=(x1t.whole,), w=(), key="x1w")
            brsF2[i] = nt_stats(x1t.h[:, :], x1t.whole, junkF, 2048, cF_ss2 + i, cF_rs2 + i)

        def F3(i):
            nt_apply(x1t.h[:, :], x1t.whole, brsF2[i], gpre, xnF, 2048, cF_rs2 + i,
                     (lambda g: hfT.h[:, g * 8:(g + 1) * 8, i * 128:(i + 1) * 128]),
                     (hfT.whole,), ["act", "dve"], banks=[(i % 2) * 4, (i % 2) * 4 + 1])

        F1(0)
        for i in range(8):
            if i + 1 < 8:
                F1(i + 1)
            F2(i)
            F3(i)
        ckpt('F')
        actT = sbt([128, NFB, 1024], BF16, AA0 + 0, "actT")
        sgf = [sbt([128, 512], F32, AA0 + 172 * KB + i * 2048, f"sgf{i}") for i in range(2)]
        nsg = [0]
        for f in range(NFB):
            wg = next_w()
            wu = next_w()
            bg = [nbank(), nbank()]
            bu = [nbank(), nbank()]
            for kc in range(16):
                for n in range(2):
                    mm(psb[bg[n]][:, :], wg.h[:, kc * 128:(kc + 1) * 128], hfT.h[:, kc, n * 512:(n + 1) * 512],
                       kc == 0, kc == 15, (wg.whole, hfT.whole), (psbuf[bg[n]],))
            for kc in range(16):
                for n in range(2):
                    mm(psb[bu[n]][:, :], wu.h[:, kc * 128:(kc + 1) * 128], hfT.h[:, kc, n * 512:(n + 1) * 512],
                       kc == 0, kc == 15, (wu.whole, hfT.whole), (psbuf[bu[n]],))
            for n in range(2):
                sgt = sgf[nsg[0] % 2]
                nsg[0] += 1
                E("act", (lambda bk_, s_: (lambda e: e.activation(out=s_.h[:, :], in_=psb[bk_][:, :], func=AF.Silu)))(bg[n], sgt),
                  r=(psbuf[bg[n]],), w=(sgt.whole,))
                E("dve", (lambda bk_, s_, f_, n_: (lambda e: e.tensor_tensor(
                    out=actT.h[:, f_, n_ * 512:(n_ + 1) * 512], in0=psb[bk_][:, :], in1=s_.h[:, :], op=ALU.mult)))(bu[n], sgt, f, n),
                  r=(psbuf[bu[n]], sgt.whole), w=(actT.sub(f),))

        ckpt('G1')
        ff = sbt([128, 8, 2048], F32, AA0 + 88 * KB, "ff")
        xr = [sbt([128, 2048], F32, AA0 + (152 + 8 * i) * KB, f"xr{i}") for i in range(2)]
        gffn = sbt([128, 2048], F32, AA0 + 168 * KB, "gffn")
        junkG = [sbt([128, 512], BF16, AA0 + 176 * KB + i * 1024, f"junkG{i}") for i in range(2)]
        E("sync", lambda e: e.dma_start(out=gffn.h[:, :], in_=gbc_d[3]), w=(gffn.whole,), key="gbc3")
        cG_p = stcol(32)
        cG_ss = stcol(8)
        cG_rs = stcol(8)
        for cbk in range(4):
            for fg in range(11):
                ws = next_w()
                for i in range(8):
                    for fb in range(4):
                        fidx = fg * 4 + fb
                        mm(psb[i][:, :], actT.h[:, fidx, i * 128:(i + 1) * 128], ws.h[:, fb * 512:(fb + 1) * 512],
                           fg == 0 and fb == 0, fg == 10 and fb == 3, (actT.sub(fidx), ws.whole), (psbuf[i],))
            for i in range(8):
                sl = slice(cbk * 512, (cbk + 1) * 512)
                pc = cG_p + i * 4 + cbk
                fb_ = ff.rng(i * 2048 + cbk * 512, i * 2048 + (cbk + 1) * 512)
                E("dve", (lambda i_, sl_: (lambda e: e.tensor_tensor(out=ff.h[:, i_, sl_], in0=psb[i_][:, :],
                                                                     in1=gffn.h[:, sl_], op=ALU.mult)))(i, sl),
                  r=(psbuf[i], gffn.whole), w=(fb_,))
                E("act", (lambda i_, pc_: (lambda e: e.activation(out=junkG[i_ % 2].h[:, :], in_=psb[i_][:, :], func=AF.Square,
                                                                 accum_out=st.h[:, pc_:pc_ + 1])))(i, pc),
                  r=(psbuf[i],), w=(junkG[i % 2].whole, st.rng(pc, pc + 1)))
        for i in range(8):
            s = i % 2
            E("sync", (lambda s_, i_: (lambda e: e.dma_start(out=xr[s_].h[:, :], in_=x1_d[i_ * 128:(i_ + 1) * 128, :])))(s, i),
              w=(xr[s].whole,), key=f"xr{s}")
            P.q["sync"][-1].waits.append(("dma", "x1w", P.dma_cnt["x1w"]))
            E("dve", (lambda i_: (lambda e: e.tensor_reduce(out=st.h[:, cG_ss + i_:cG_ss + i_ + 1],
                                                           in_=st.h[:, cG_p + 4 * i_:cG_p + 4 * i_ + 4],
                                                           axis=mybir.AxisListType.X, op=ALU.add)))(i),
              r=(st.rng(cG_p + 4 * i, cG_p + 4 * i + 4),), w=(st.rng(cG_ss + i, cG_ss + i + 1),))
            brs = rstd_from_ss(cG_ss + i, cG_rs + i, 1, 1.0 / D_MODEL)
            E("dve", (lambda i_, s_: (lambda e: e.scalar_tensor_tensor(
                out=ff.h[:, i_, :], in0=ff.h[:, i_, :], scalar=st.h[:, cG_rs + i_:cG_rs + i_ + 1], in1=xr[s_].h[:, :],
                op0=ALU.mult, op1=ALU.add)))(i, s), r=(ff.sub(i), brs, xr[s].whole), w=(ff.sub(i),))
            E("sync", (lambda i_: (lambda e: e.dma_start(out=out_d[i_ * 128:(i_ + 1) * 128, :], in_=ff.h[:, i_, :])))(i),
              r=(ff.sub(i),), w=(), key="outw")
        fin = E("sync", None)
        fin.waits.append(("dma", "outw", P.dma_cnt["outw"]))
        assert wcur[0] == len(wpieces), (wcur[0], len(wpieces))


    except _Stop:
        pass

    fin_all = P.op("sync", None)
    for k_, v_ in P.dma_cnt.items():
        fin_all.waits.append(("dma", k_, v_))

    for e_ in ENGS:
        cnt = 0
        for ins in P.q[e_]:
            if ins.signal and not ins.is_dma:
                cnt += 1
                ins.value = cnt
    keys = sorted(P.dma_cnt.keys())
    sem_ctx = {}
    sems_eng = {e_: nc.alloc_semaphore(f"s_{e_}") for e_ in ENGS}
    sems_key = {k: nc.alloc_semaphore(f"d_{k}") for k in keys}

    def replay(ename, eng):
        waited = {}
        for ins in P.q[ename]:
            for w in ins.waits:
                if w[0] == "eng":
                    p = w[1]
                    sem, val = sems_eng[p.eng], p.value
                else:
                    sem, val = sems_key[w[1]], w[2]
                k = id(sem)
                if waited.get(k, 0) < val:
                    eng.wait_ge(sem, val)
                    waited[k] = val
            if ins.fn is None:
                continue
            bi = ins.fn(eng)
            if ins.is_dma:
                bi.then_inc(sems_key[ins.key], 16)
            elif ins.signal:
                bi.then_inc(sems_eng[ename], 1)

    with nc.Block() as block:
        @block.sync
        def _(e):
            replay("sync", e)

        @block.scalar
        def _(e):
            replay("act", e)

        @block.vector
        def _(e):
            replay("dve", e)

        @block.gpsimd
        def _(e):
            replay("pool", e)

        @block.tensor
        def _(e):
            replay("pe", e)

    stats = {e_: len(P.q[e_]) for e_ in ENGS}
    stats["sig"] = {e_: sum(1 for i in P.q[e_] if i.signal and not i.is_dma) for e_ in ENGS}
    return nc, stats


def _blocks_k(w, ncols_per_block):
    K, N = w.shape
    kc = K // 128
    nb = N // ncols_per_block
    a = w.reshape(kc, 128, nb, ncols_per_block).transpose(2, 1, 0, 3)
    return np.ascontiguousarray(a.reshape(nb, 128, kc * ncols_per_block))


def prepare_inputs(x, positions, pre_mix_norm, w_in, q_norm, w_uq, kv_norm, w_ukv, conv_w, conv_b, conv_ln_g,
                   conv_ln_b, conv_out_norm, attn_out_norm, w_out, post_mix_norm, pre_ffn_norm, w_gate, w_up,
                   w_down, post_ffn_norm):
    f = np.float32
    x = np.asarray(x, f)
    positions = np.asarray(positions, np.int32)
    w_in = np.asarray(w_in, f)[0]
    w_uq = np.asarray(w_uq, f)[0]
    w_ukv = np.asarray(w_ukv, f)[0]
    w_out = np.asarray(w_out, f)[0]
    w_gate = np.asarray(w_gate, f)[0]
    w_up = np.asarray(w_up, f)[0]
    w_down = np.asarray(w_down, f)[0]

    c1 = 2 * CONV_CH
    c2 = c1 + Q_LORA
    c3 = c2 + KV_LORA
    cols = []
    cols += list(range(c2, c3))
    cols += list(range(c3, c3 + 64)) + list(range(c3 + 32, c3 + 64)) + list(range(c3, c3 + 32))
    cols += list(range(c1, c2))
    for c in range(8):
        cols += list(range(CONV_CH + c * 128, CONV_CH + (c + 1) * 128))
        cols += list(range(c * 128, (c + 1) * 128))
    w_in_p = _blocks_k(w_in[:, cols], 128)
    cols = []
    for h in range(N_HEADS):
        b0 = h * 192
        cols += list(range(b0, b0 + 128))
        cols += list(range(b0 + 128, b0 + 192)) + list(range(b0 + 160, b0 + 192)) + list(range(b0 + 128, b0 + 160))
    w_uq_p = _blocks_k(w_uq[:, cols], 128)
    kcols = []
    vcols = []
    for h in range(N_HEADS):
        kcols += list(range(h * 256, h * 256 + 128))
        vcols += list(range(h * 256 + 128, h * 256 + 256))
    w_uk_p = _blocks_k(w_ukv[:, kcols], 128)
    wv = w_ukv[:, vcols]
    w_uv_p = np.ascontiguousarray(wv.reshape(2, 2, 128, 1024).transpose(0, 2, 1, 3).reshape(2, 128, 2048))
    w_out_p = np.ascontiguousarray(w_out.reshape(16, 128, 2048).transpose(1, 0, 2).reshape(128, 16 * 2048))
    g_p = _blocks_k(w_gate, 128)
    u_p = _blocks_k(w_up, 128)
    w_gu_p = np.ascontiguousarray(np.stack([g_p, u_p], axis=1).reshape(88, 128, 2048))
    wd = w_down.reshape(11, 4, 128, 4, 512)
    w_dn_p = np.ascontiguousarray(wd.transpose(3, 0, 2, 1, 4).reshape(44, 128, 2048))

    c_bf = np.zeros((128, 512), np.float32)
    c_bf[:, 0:128] = np.eye(128)
    c_bf[:, 128:256] = 1.0
    e64 = np.eye(64)
    c_bf[:, 256:384] = np.block([[e64, e64], [e64, e64]])
    kk = np.arange(128)[:, None]
    qq = np.arange(128)[None, :]
    c_bf[:, 384:512] = (qq >= kk).astype(np.float32)
    c_bf = c_bf.astype(ml_dtypes.bfloat16)

    cvec = np.zeros((128, NCV), f)
    cw = np.asarray(conv_w, f)[0]
    for c in range(8):
        cvec[:, CW + c * 31:CW + (c + 1) * 31] = cw[:, c * 128:(c + 1) * 128].T
    cvec[:, CB:CB + 8] = np.asarray(conv_b, f)[0].reshape(8, 128).T
    cvec[:, LG:LG + 8] = np.asarray(conv_ln_g, f)[0].reshape(8, 128).T
    cvec[:, LB:LB + 8] = np.asarray(conv_ln_b, f)[0].reshape(8, 128).T
    cvec[:, G2:G2 + 8] = np.asarray(conv_out_norm, f)[0].reshape(8, 128).T
    cvec[:, GQ:GQ + 6] = np.asarray(q_norm, f)[0].reshape(6, 128).T
    cvec[:, GKV:GKV + 4] = np.asarray(kv_norm, f)[0].reshape(4, 128).T
    inv_freq = (np.float32(10000.0) ** (-np.arange(0, 64, 2, dtype=np.float32) / np.float32(64))).astype(f)
    cvec[:, IFQ] = np.tile(inv_freq, 4)
    cvec[0:64, PHS] = np.float32(np.pi / 2)
    cvec[:, SGN] = 1.0
    cvec[64:96, SGN] = -1.0

    gbc = np.zeros((5, 128, 2048), f)
    gbc[0] = np.asarray(pre_mix_norm, f)[0][None, :]
    gbc[1] = np.asarray(post_mix_norm, f)[0][None, :]
    gbc[2] = np.asarray(pre_ffn_norm, f)[0][None, :]
    gbc[3] = np.asarray(post_ffn_norm, f)[0][None, :]
    gbc[4, :, 0:1024] = np.asarray(attn_out_norm, f)[0][None, :]

    shared = dict(c_bf=c_bf, gbc=gbc, w_in_p=w_in_p, w_uq_p=w_uq_p, w_uk_p=w_uk_p, w_uv_p=w_uv_p,
                  w_out_p=w_out_p, w_gu_p=w_gu_p, w_dn_p=w_dn_p)
    in_maps = []
    for core in range(8):
        b, half = core // 2, core % 2
        m = dict(shared)
        m["x_own"] = np.ascontiguousarray(x[b, half * TOWN:(half + 1) * TOWN])
        cvc = cvec.copy()
        pos = np.zeros((2048,), np.int32)
        if half == 1:
            m["x_prev"] = np.ascontiguousarray(x[b, 0:TOWN])
            pos[:] = positions[b, 0:2048]
            cvc[:, PFL] = 1.0
        else:
            m["x_prev"] = np.zeros((TOWN, D_MODEL), f)
            pos[1024:] = positions[b, 0:1024]
            cvc[:, PFL] = 0.0
        m["cvec"] = cvc
        m["pos_bc"] = np.ascontiguousarray(np.broadcast_to(pos[None, :], (128, 2048)))
        in_maps.append(m)
    return in_maps


_CACHE = {}


def kernel(**inputs):
    if "nc" not in _CACHE:
        _CACHE["nc"], _CACHE["stats"] = build_program()
    nc = _CACHE["nc"]
    in_maps = prepare_inputs(**inputs)
    res = run_bass_kernel_spmd(nc, in_maps, core_ids=list(range(8)))
    out = np.zeros((BATCH, SEQ, D_MODEL), np.float32)
    for core in range(8):
        b, half = core // 2, core % 2
        out[b, half * TOWN:(half + 1) * TOWN] = res.results[core]["out"]
    return out
```

```python
import os
import numpy as np
import ml_dtypes
import concourse.bass as bass
import concourse.mybir as mybir
from concourse.bass_utils import run_bass_kernel_spmd

F32 = mybir.dt.float32
BF16 = mybir.dt.bfloat16
I32 = mybir.dt.int32
AF = mybir.ActivationFunctionType
ALU = mybir.AluOpType
PI = float(np.pi)

D_MODEL = 2048
SEQ = 2048
BATCH = 4
TOWN = 1024
CONV_CH = 1024
CONV_K = 31
N_HEADS = 8
Q_LORA = 768
KV_LORA = 512
D_FF = 5632
NFB = D_FF // 128
EPS = 1e-6
SCALE = 192 ** -0.5

CW = 0
CB = 248
LG = 256
LB = 264
G2 = 272
GQ = 280
GKV = 286
IFQ = 290
PHS = 291
SGN = 292
PFL = 293
NCV = 320

KB = 1024


class Buf:
    __slots__ = ("space", "lo", "hi", "name")

    def __init__(self, space, lo, hi, name=""):
        self.space, self.lo, self.hi, self.name = space, lo, hi, name


class Ins:
    __slots__ = ("eng", "fn", "is_dma", "key", "signal", "value", "waits", "dma_val", "idx")

    def __init__(self, eng, fn, is_dma, key):
        self.eng, self.fn, self.is_dma, self.key = eng, fn, is_dma, key
        self.idx = -1
        self.signal = False
        self.value = None
        self.waits = []
        self.dma_val = None


ENGS = ["sync", "act", "dve", "pool", "pe"]


class Prog:
    def __init__(self):
        self.q = {e: [] for e in ENGS}
        self.wr = {"sb": [], "ps": []}
        self.rd = {"sb": [], "ps": []}
        self.dma_cnt = {}

    def buf(self, space, lo, hi, name=""):
        return Buf(space, lo, hi, name)

    def op(self, eng, fn, reads=(), writes=(), dma_key=None):
        ins = Ins(eng, fn, dma_key is not None, dma_key)
        ins.idx = len(self.q[eng])
        need_eng = {}
        need_dma = {}

        def add(p):
            if p is ins:
                return
            if p.is_dma:
                need_dma[p.key] = 1
                return
            if p.eng == eng and not ins.is_dma and eng == "pe":
                return
            cur = need_eng.get(p.eng)
            if cur is None or cur.idx < p.idx:
                need_eng[p.eng] = p

        for b in reads:
            for w in self.wr[b.space]:
                if w[0] < b.hi and b.lo < w[1]:
                    add(w[2])
            if b.space == "ps" and eng != "pe":
                lo = b.lo // 2048 * 2048
                hi = -(-b.hi // 2048) * 2048
                for r in self.rd["ps"]:
                    if r[0] < hi and lo < r[1] and r[3].eng != eng and r[3].eng != "pe":
                        add(r[3])
        for b in writes:
            for w in self.wr[b.space]:
                if w[0] < b.hi and b.lo < w[1]:
                    add(w[2])
            for r in self.rd[b.space]:
                if r[0] < b.hi and b.lo < r[1]:
                    add(r[3])
        for p in need_eng.values():
            p.signal = True
            ins.waits.append(("eng", p))
        for k in need_dma:
            ins.waits.append(("dma", k, self.dma_cnt[k]))
        if dma_key is not None:
            self.dma_cnt[dma_key] = self.dma_cnt.get(dma_key, 0) + 16
            ins.dma_val = self.dma_cnt[dma_key]
        for b in writes:
            sp = b.space
            self.wr[sp] = [w for w in self.wr[sp] if not (b.lo <= w[0] and w[1] <= b.hi)]
            self.wr[sp].append([b.lo, b.hi, ins])
            self.rd[sp] = [r for r in self.rd[sp] if not (b.lo <= r[0] and r[1] <= b.hi)]
        rk = ("dma", dma_key) if ins.is_dma else eng
        for b in reads:
            lst = self.rd[b.space]
            for r in lst:
                if r[0] == b.lo and r[1] == b.hi and r[2] == rk:
                    r[3] = ins
                    break
            else:
                lst.append([b.lo, b.hi, rk, ins])
        self.q[eng].append(ins)
        return ins


def build_program(stop_after=None, tensors=None):
    nc = bass.Bass("TRN2", target_bir_lowering=False)
    P = Prog()

    def din(name, shape, dt):
        return nc.dram_tensor(name, list(shape), dt, kind="ExternalInput").ap()

    x_own = din("x_own", [TOWN, D_MODEL], F32)
    x_prev = din("x_prev", [TOWN, D_MODEL], F32)
    pos_bc = din("pos_bc", [128, 2048], I32)
    c_bf = din("c_bf", [128, 512], BF16)
    cvec_d = din("cvec", [128, NCV], F32)
    gbc_d = din("gbc", [5, 128, 2048], F32)
    w_in_p = din("w_in_p", [27, 128, 2048], F32)
    w_uq_p = din("w_uq_p", [16, 128, 768], F32)
    w_uk_p = din("w_uk_p", [8, 128, 512], F32)
    w_uv_p = din("w_uv_p", [2, 128, 2048], F32)
    w_out_p = din("w_out_p", [128, 16 * 2048], F32)
    w_gu_p = din("w_gu_p", [88, 128, 2048], F32)
    w_dn_p = din("w_dn_p", [44, 128, 2048], F32)
    out_d = nc.dram_tensor("out", [TOWN, D_MODEL], F32, kind="ExternalOutput").ap()
    x1_d = nc.dram_tensor("x1_scratch", [TOWN, D_MODEL], F32).ap()
    dbg = {}

    base = (nc.sbuf_base + 63) // 64 * 64
    CONST0 = base
    WR0 = CONST0 + 4 * KB
    AA0 = WR0 + 24 * KB
    assert AA0 + 178 * KB <= nc.sbuf_top, (AA0 + 178 * KB, nc.sbuf_top)

    def dsz(dt):
        return 4 if dt in (F32, I32) else 2

    class T:
        def __init__(self, name, shape, dt, off):
            self.h = nc.alloc_sbuf_tensor_at(name, list(shape), dt, offset=off)
            self.off = off
            self.shape = shape
            self.dt = dt
            self.nbytes = int(np.prod(shape[1:])) * dsz(dt)
            self.whole = P.buf("sb", off, off + self.nbytes, name)
            self._subs = {}
            if tensors is not None:
                tensors[name] = self

        def sub(self, i, n=None):
            key = (i, n)
            if key not in self._subs:
                slab = self.nbytes // self.shape[1]
                cnt = 1 if n is None else n
                self._subs[key] = P.buf("sb", self.off + i * slab, self.off + (i + cnt) * slab)
            return self._subs[key]

        def rng(self, lo_el, hi_el):
            key = ("r", lo_el, hi_el)
            if key not in self._subs:
                self._subs[key] = P.buf("sb", self.off + lo_el * dsz(self.dt), self.off + hi_el * dsz(self.dt))
            return self._subs[key]

    _names = [0]

    def sbt(shape, dt, off, name=None):
        _names[0] += 1
        return T(name or f"t{_names[0]}", shape, dt, off)

    cb = sbt([128, 512], BF16, CONST0, "cb")
    ident = cb.h[:, 0:128]
    ones_b = cb.h[:, 128:256]
    f2 = cb.h[:, 256:384]
    tri = cb.h[:, 384:512]
    cv = sbt([128, NCV], F32, CONST0 + 1024, "cv")
    st = sbt([128, 320], F32, CONST0 + 1024 + NCV * 4, "st")
    onesf = sbt([128, 64], F32, CONST0 + 1024 + NCV * 4 + 1280, "onesf")
    assert CONST0 + 1024 + NCV * 4 + 1280 + 256 <= WR0

    def cvc(col):
        return cv.h[:, col:col + 1]

    _stn = [0]

    def stcol(n=1):
        c = _stn[0]
        _stn[0] += n
        assert _stn[0] <= 320
        return c

    NSLOT = 6
    wslots = [sbt([128, 2048], BF16, WR0 + i * 4 * KB, f"ws{i}") for i in range(NSLOT)]
    wpieces = []
    wstate = {"issued": 0}

    psb = [nc.alloc_psum_tensor(f"psb{i}", [128, 512], F32) for i in range(8)]
    psbuf = [P.buf("ps", i * 2048, (i + 1) * 2048, f"ps{i}") for i in range(8)]
    psbf = [psb[i][:, :].bitcast(BF16) for i in range(8)]
    _bank = [0]

    def nbank():
        b = _bank[0] % 8
        _bank[0] += 1
        return b

    _psub = {}

    def psub(bank, lo, hi):
        k = (bank, lo, hi)
        if k not in _psub:
            _psub[k] = P.buf("ps", bank * 2048 + lo * 4, bank * 2048 + hi * 4)
        return _psub[k]

    def E(eng, fn, r=(), w=(), key=None):
        return P.op(eng, fn, r, w, key)

    def mm(out, lhsT, rhs, start, stop, r, w):
        return E("pe", lambda e: e.matmul(out, lhsT=lhsT, rhs=rhs, start=start, stop=stop), r, w)

    def issue_weights(upto):
        while wstate["issued"] < min(upto, len(wpieces)):
            i = wstate["issued"]
            src, n = wpieces[i]
            slot = wslots[i % NSLOT]
            E("pool", (lambda s, n_, sl: (lambda e: e.dma_start(out=sl.h[:, 0:n_], in_=s)))(src, n, slot),
              r=(), w=(slot.whole,), key=f"ws{i % NSLOT}")
            wstate["issued"] += 1

    wcur = [0]

    def next_w(prefetch=4):
        i = wcur[0]
        wcur[0] += 1
        issue_weights(i + 1 + prefetch)
        return wslots[i % NSLOT]

    for j in range(27):
        wpieces.append((w_in_p[j], 2048))
    for j in range(16):
        wpieces.append((w_uq_p[j], 768))
    for j in range(8):
        wpieces.append((w_uk_p[j], 512))
    for j in range(2):
        wpieces.append((w_uv_p[j], 2048))
    for j in range(88):
        wpieces.append((w_gu_p[j], 2048))
    for j in range(44):
        wpieces.append((w_dn_p[j], 2048))

    class _Stop(Exception):
        pass

    def ckpt(name):
        if stop_after == name:
            raise _Stop()

    try:
        E("sync", lambda e: e.dma_start(out=cb.h[:, :], in_=c_bf), w=(cb.whole,), key="const")
        E("sync", lambda e: e.dma_start(out=cv.h[:, :], in_=cvec_d), w=(cv.whole,), key="const")
        E("dve", lambda e: e.memset(st.h[:, :], 0.0), w=(st.whole,))
        E("dve", lambda e: e.memset(onesf.h[:, :], 1.0), w=(onesf.whole,))

        CS = sbt([128, 2048], F32, AA0 + 170 * KB, "CS")
        posi = sbt([128, 2048], I32, AA0 + 132 * KB, "posi")
        posf = sbt([128, 2048], F32, AA0 + 140 * KB, "posf")
        tmpa = sbt([128, 2048], F32, AA0 + 148 * KB, "tmpa")
        E("sync", lambda e: e.dma_start(out=posi.h[:, :], in_=pos_bc), w=(posi.whole,), key="pos")
        E("dve", lambda e: e.tensor_copy(out=posf.h[:, :], in_=posi.h[:, :]), r=(posi.whole,), w=(posf.whole,))
        E("dve", lambda e: e.tensor_scalar(out=CS.h[:, :], in0=posf.h[:, :], scalar1=cvc(IFQ), scalar2=cvc(PHS),
                                           op0=ALU.mult, op1=ALU.add), r=(posf.whole, cv.whole), w=(CS.whole,))
        E("dve", lambda e: e.tensor_scalar(out=tmpa.h[:, :], in0=CS.h[:, :], scalar1=1.0 / (2 * PI), scalar2=None,
                                           op0=ALU.mult), r=(CS.whole,), w=(tmpa.whole,))
        E("dve", lambda e: e.tensor_copy(out=posi.h[:, :], in_=tmpa.h[:, :]), r=(tmpa.whole,), w=(posi.whole,))
        E("dve", lambda e: e.tensor_copy(out=posf.h[:, :], in_=posi.h[:, :]), r=(posi.whole,), w=(posf.whole,))
        C1 = 6.28125
        C2 = 2 * PI - C1
        E("dve", lambda e: e.scalar_tensor_tensor(out=CS.h[:, :], in0=posf.h[:, :], scalar=-C1, in1=CS.h[:, :],
                                                  op0=ALU.mult, op1=ALU.add), r=(posf.whole, CS.whole), w=(CS.whole,))
        E("dve", lambda e: e.scalar_tensor_tensor(out=CS.h[:, :], in0=posf.h[:, :], scalar=-C2, in1=CS.h[:, :],
                                                  op0=ALU.mult, op1=ALU.add), r=(posf.whole, CS.whole), w=(CS.whole,))
        E("dve", lambda e: e.tensor_single_scalar(out=tmpa.h[:, :], in_=CS.h[:, :], scalar=PI, op=ALU.is_gt),
          r=(CS.whole,), w=(tmpa.whole,))
        E("dve", lambda e: e.scalar_tensor_tensor(out=CS.h[:, :], in0=tmpa.h[:, :], scalar=-2 * PI, in1=CS.h[:, :],
                                                  op0=ALU.mult, op1=ALU.add), r=(tmpa.whole, CS.whole), w=(CS.whole,))
        E("dve", lambda e: e.tensor_scalar(out=CS.h[:, :], in0=CS.h[:, :], scalar1=-PI, scalar2=PI,
                                           op0=ALU.max, op1=ALU.min), r=(CS.whole,), w=(CS.whole,))
        E("act", lambda e: e.activation(out=CS.h[:, :], in_=CS.h[:, :], func=AF.Sin), r=(CS.whole,), w=(CS.whole,))
        E("dve", lambda e: e.tensor_scalar(out=CS.h[:, :], in0=CS.h[:, :], scalar1=cvc(SGN), scalar2=None,
                                           op0=ALU.mult), r=(CS.whole, cv.whole), w=(CS.whole,))

        def rstd_from_ss(c_ss, c_out, n, inv_n):
            c_ms = stcol(n)
            c_sq = stcol(n)
            bs = st.rng(c_ss, c_ss + n)
            bm = st.rng(c_ms, c_ms + n)
            bq = st.rng(c_sq, c_sq + n)
            bo = st.rng(c_out, c_out + n)
            E("dve", lambda e: e.tensor_scalar(out=st.h[:, c_ms:c_ms + n], in0=st.h[:, c_ss:c_ss + n], scalar1=inv_n,
                                               scalar2=EPS, op0=ALU.mult, op1=ALU.add), r=(bs,), w=(bm,))
            E("act", lambda e: e.activation(out=st.h[:, c_sq:c_sq + n], in_=st.h[:, c_ms:c_ms + n], func=AF.Sqrt),
              r=(bm,), w=(bq,))
            E("dve", lambda e: e.reciprocal(out=st.h[:, c_out:c_out + n], in_=st.h[:, c_sq:c_sq + n]), r=(bq,), w=(bo,))
            return bo

        def rstd_psum_inplace(bank, n, inv_n):
            b = psbuf[bank]
            ap = psb[bank][:, 0:n]
            E("dve", lambda e: e.tensor_scalar(out=ap, in0=ap, scalar1=inv_n, scalar2=EPS, op0=ALU.mult, op1=ALU.add),
              r=(b,), w=(b,))
            E("act", lambda e: e.activation(out=ap, in_=ap, func=AF.Sqrt), r=(b,), w=(b,))
            E("dve", lambda e: e.reciprocal(out=ap, in_=ap), r=(b,), w=(b,))

        def nt_stats(src_ap, src_buf, junk_t, width, c_ss, c_rs):
            bs = st.rng(c_ss, c_ss + 1)
            E("act", lambda e: e.activation(out=junk_t.h[:, 0:width], in_=src_ap, func=AF.Square,
                                            accum_out=st.h[:, c_ss:c_ss + 1]), r=(src_buf,), w=(junk_t.whole, bs))
            return rstd_from_ss(c_ss, c_rs, 1, 1.0 / width)

        def nt_apply_a(src_ap, src_buf, brs, gb_t, xn_t, width, c_rs, banks=None):
            nchunk = width // 128
            E("dve", lambda e: e.scalar_tensor_tensor(out=xn_t.h[:, 0:width], in0=src_ap,
                                                      scalar=st.h[:, c_rs:c_rs + 1], in1=gb_t.h[:, 0:width],
                                                      op0=ALU.mult, op1=ALU.mult),
              r=(src_buf, brs, gb_t.whole), w=(xn_t.whole,))
            used = []
            for g in range(nchunk // 8):
                bk = nbank() if banks is None else banks[g]
                used.append(bk)
                for c8 in range(8):
                    c = g * 8 + c8
                    E("pe", (lambda bk_, c8_, c_: (lambda e: e.transpose(psbf[bk_][:, c8_ * 128:(c8_ + 1) * 128],
                                                                       xn_t.h[:, c_ * 128:(c_ + 1) * 128], ident)))(bk, c8, c),
                      r=(xn_t.whole, cb.whole), w=(psbuf[bk],))
            return used

        def nt_apply_b(used, dst_fn, dst_bufs, evac_engs):
            for g, bk in enumerate(used):
                eng = evac_engs[g % len(evac_engs)]
                src = psbf[bk][:, 0:1024].rearrange("p (a b) -> p a b", a=8)
                dst = dst_fn(g)
                if eng == "act":
                    E("act", (lambda d_, s_: (lambda e: e.copy(out=d_, in_=s_)))(dst, src), r=(psbuf[bk],), w=dst_bufs)
                else:
                    E("dve", (lambda d_, s_: (lambda e: e.tensor_copy(out=d_, in_=s_)))(dst, src), r=(psbuf[bk],), w=dst_bufs)

        def nt_apply(src_ap, src_buf, brs, gb_t, xn_t, width, c_rs, dst_fn, dst_bufs, evac_engs, banks=None):
            used = nt_apply_a(src_ap, src_buf, brs, gb_t, xn_t, width, c_rs, banks)
            nt_apply_b(used, dst_fn, dst_bufs, evac_engs)

        hT = sbt([128, 16, 2048], BF16, AA0 + 0, "hT")
        xa = [sbt([128, 2048], F32, AA0 + (64 + 8 * i) * KB, f"xa{i}") for i in range(4)]
        gbcA = sbt([128, 2048], F32, AA0 + 96 * KB, "gbcA")
        xnA = [sbt([128, 2048], BF16, AA0 + (104 + 4 * i) * KB, f"xnA{i}") for i in range(3)]
        junkA = [sbt([128, 2048], BF16, AA0 + (116 + 4 * i) * KB, f"junkA{i}") for i in range(4)]
        issue_weights(NSLOT)
        E("sync", lambda e: e.dma_start(out=gbcA.h[:, :], in_=gbc_d[0]), w=(gbcA.whole,), key="gbc")
        cA_ss = stcol(16)
        cA_rs = stcol(16)
        brsA = {}

        def A1(t):
            s_ = t % 4
            src = x_prev[t * 128:(t + 1) * 128, :] if t < 8 else x_own[(t - 8) * 128:(t - 7) * 128, :]
            E("sync", (lambda s__, src_: (lambda e: e.dma_start(out=xa[s__].h[:, :], in_=src_)))(s_, src),
              w=(xa[s_].whole,), key=f"xa{s_}")
            brsA[t] = nt_stats(xa[s_].h[:, :], xa[s_].whole, junkA[t % 4], 2048, cA_ss + t, cA_rs + t)

        usedA = {}

        def A2a(t):
            s_ = t % 4
            usedA[t] = nt_apply_a(xa[s_].h[:, :], xa[s_].whole, brsA[t], gbcA, xnA[t % 3], 2048, cA_rs + t)

        def A2b(t):
            nt_apply_b(usedA[t], (lambda g: hT.h[:, g * 8:(g + 1) * 8, t * 128:(t + 1) * 128]),
                       (hT.whole,), ["act", "dve"])

        A1(0)
        A1(1)
        A1(2)
        A2a(0)
        for t in range(16):
            if t + 1 < 16:
                A2a(t + 1)
            if t + 3 < 16:
                A1(t + 3)
            A2b(t)

        ckpt('A')
        zz = sbt([128, 4, 2048], F32, AA0 + 64 * KB, "zz")
        zq = sbt([128, 6, 1024], F32, AA0 + 64 * KB, "zq")
        sqz = sbt([128, 4, 2048], BF16, AA0 + 96 * KB, "sqz")
        sqq = sbt([128, 6, 1024], BF16, AA0 + 96 * KB, "sqq")
        t_k = sbt([128, 2048], BF16, AA0 + 112 * KB, "t_k")
        kvn = sbt([128, 4, 2048], BF16, AA0 + 116 * KB, "kvn")
        qln = sbt([128, 6, 1024], BF16, AA0 + 132 * KB, "qln")
        u_bf = sbt([128, 8, 1152], BF16, AA0 + 148 * KB, "u_bf")
        kr2 = sbt([128, 2048], BF16, AA0 + 166 * KB, "kr2")
        sg = [sbt([128, 1152], F32, AA0 + 64 * KB + i * 4608, f"sg{i}") for i in range(2)]

        for b in range(4):
            ws = next_w()
            banks = [nbank() for _ in range(4)]
            for kc in range(16):
                for n in range(4):
                    mm(psb[banks[n]][:, :], ws.h[:, kc * 128:(kc + 1) * 128], hT.h[:, kc, n * 512:(n + 1) * 512],
                       kc == 0, kc == 15, (ws.whole, hT.whole), (psbuf[banks[n]],))
            for n in range(4):
                if os.environ.get("KDBG") == "noevac":
                    break
                bk = banks[n]
                E("dve", (lambda bk_, b_, n_: (lambda e: e.tensor_scalar(
                    out=zz.h[:, b_, n_ * 512:(n_ + 1) * 512], in0=psb[bk_][:, :], scalar1=cvc(GKV + b_), scalar2=None,
                    op0=ALU.mult)))(bk, b, n), r=(psbuf[bk], cv.whole), w=(zz.sub(b),))
                if os.environ.get("KDBG") == "noact":
                    continue
                E("act", (lambda bk_, b_, n_: (lambda e: e.activation(
                    out=sqz.h[:, b_, n_ * 512:(n_ + 1) * 512], in_=psb[bk_][:, :], func=AF.Square)))(bk, b, n),
                  r=(psbuf[bk],), w=(sqz.sub(b),) + ((psbuf[bk],) if os.environ.get("KDBG") == "serial" else ()))
        ckpt('B1')
        for n in range(4):
            bk = nbank()
            for b in range(4):
                mm(psb[bk][:, :], ones_b, sqz.h[:, b, n * 512:(n + 1) * 512], b == 0, b == 3,
                   (cb.whole, sqz.sub(b)), (psbuf[bk],))
            rstd_psum_inplace(bk, 512, 1.0 / KV_LORA)
            for b in range(4):
                E("dve", (lambda bk_, b_, n_: (lambda e: e.tensor_tensor(
                    out=kvn.h[:, b_, n_ * 512:(n_ + 1) * 512], in0=zz.h[:, b_, n_ * 512:(n_ + 1) * 512],
                    in1=psb[bk_][:, :], op=ALU.mult)))(bk, b, n), r=(zz.sub(b), psbuf[bk]), w=(kvn.sub(b),))
        ckpt('B2')
        ws = next_w()
        banks = [nbank() for _ in range(4)]
        for kc in range(16):
            for n in range(4):
                mm(psb[banks[n]][:, :], ws.h[:, kc * 128:(kc + 1) * 128], hT.h[:, kc, n * 512:(n + 1) * 512],
                   kc == 0, kc == 15, (ws.whole, hT.whole), (psbuf[banks[n]],))
        for n in range(4):
            bk = banks[n]
            E("dve", (lambda bk_, n_: (lambda e: e.tensor_tensor(
                out=t_k.h[:, n_ * 512:(n_ + 1) * 512], in0=psb[bk_][:, :], in1=CS.h[:, n_ * 512:(n_ + 1) * 512],
                op=ALU.mult)))(bk, n), r=(psbuf[bk], CS.whole), w=(t_k.rng(n * 512, (n + 1) * 512),))
            bk2 = nbank()
            mm(psb[bk2][:, :], f2, t_k.h[:, n * 512:(n + 1) * 512], True, True,
               (cb.whole, t_k.rng(n * 512, (n + 1) * 512)), (psbuf[bk2],))
            E("act", (lambda bk_, n_: (lambda e: e.copy(out=kr2.h[:, n_ * 512:(n_ + 1) * 512], in_=psb[bk_][:, :])))(bk2, n),
              r=(psbuf[bk2],), w=(kr2.rng(n * 512, (n + 1) * 512),))
        ckpt('B3')
        for b in range(6):
            ws = next_w()
            banks = [nbank() for _ in range(2)]
            for kc in range(16):
                for n in range(2):
                    mm(psb[banks[n]][:, :], ws.h[:, kc * 128:(kc + 1) * 128],
                       hT.h[:, kc, 1024 + n * 512:1024 + (n + 1) * 512],
                       kc == 0, kc == 15, (ws.whole, hT.whole), (psbuf[banks[n]],))
            for n in range(2):
                bk = banks[n]
                E("dve", (lambda bk_, b_, n_: (lambda e: e.tensor_scalar(
                    out=zq.h[:, b_, n_ * 512:(n_ + 1) * 512], in0=psb[bk_][:, :], scalar1=cvc(GQ + b_), scalar2=None,
                    op0=ALU.mult)))(bk, b, n), r=(psbuf[bk], cv.whole), w=(zq.sub(b),))
                E("act", (lambda bk_, b_, n_: (lambda e: e.activation(
                    out=sqq.h[:, b_, n_ * 512:(n_ + 1) * 512], in_=psb[bk_][:, :], func=AF.Square)))(bk, b, n),
                  r=(psbuf[bk],), w=(sqq.sub(b),))
        for n in range(2):
            bk = nbank()
            for b in range(6):
                mm(psb[bk][:, :], ones_b, sqq.h[:, b, n * 512:(n + 1) * 512], b == 0, b == 5,
                   (cb.whole, sqq.sub(b)), (psbuf[bk],))
            rstd_psum_inplace(bk, 512, 1.0 / Q_LORA)
            for b in range(6):
                E("dve", (lambda bk_, b_, n_: (lambda e: e.tensor_tensor(
                    out=qln.h[:, b_, n_ * 512:(n_ + 1) * 512], in0=zq.h[:, b_, n_ * 512:(n_ + 1) * 512],
                    in1=psb[bk_][:, :], op=ALU.mult)))(bk, b, n), r=(zq.sub(b), psbuf[bk]), w=(qln.sub(b),))
        ckpt('B4')
        tokr = [(896, 1024), (1024, 1536), (1536, 2048)]
        for c in range(8):
            sgt = sg[c % 2]
            ws = next_w()
            banks = [nbank() for _ in range(3)]
            for kc in range(16):
                for n, (a0, a1) in enumerate(tokr):
                    mm(psb[banks[n]][:, 0:a1 - a0], ws.h[:, kc * 128:(kc + 1) * 128], hT.h[:, kc, a0:a1],
                       kc == 0, kc == 15, (ws.whole, hT.whole), (psbuf[banks[n]],))
            for n, (a0, a1) in enumerate(tokr):
                E("act", (lambda bk_, a0_, a1_, sg_: (lambda e: e.activation(
                    out=sg_.h[:, a0_ - 896:a1_ - 896], in_=psb[bk_][:, 0:a1_ - a0_], func=AF.Sigmoid)))(banks[n], a0, a1, sgt),
                  r=(psbuf[banks[n]],), w=(sgt.whole,))
            ws = next_w()
            banks = [nbank() for _ in range(3)]
            for kc in range(16):
                for n, (a0, a1) in enumerate(tokr):
                    mm(psb[banks[n]][:, 0:a1 - a0], ws.h[:, kc * 128:(kc + 1) * 128], hT.h[:, kc, a0:a1],
                       kc == 0, kc == 15, (ws.whole, hT.whole), (psbuf[banks[n]],))
            for n, (a0, a1) in enumerate(tokr):
                E("dve", (lambda bk_, a0_, a1_, sg_, c_: (lambda e: e.tensor_tensor(
                    out=u_bf.h[:, c_, a0_ - 896:a1_ - 896], in0=psb[bk_][:, 0:a1_ - a0_], in1=sg_.h[:, a0_ - 896:a1_ - 896],
                    op=ALU.mult)))(banks[n], a0, a1, sgt, c), r=(psbuf[banks[n]], sgt.whole), w=(u_bf.sub(c),))

        ckpt('B')
        yv = sbt([128, 8, 1024], F32, AA0 + 0, "yv")
        Dr = [sbt([128, 31, 128], BF16, AA0 + (32 + 8 * i) * KB, f"Dr{i}") for i in range(2)]
        sqs = [sbt([128, 512], BF16, AA0 + 48 * KB + i * 1024, f"sqs{i}") for i in range(4)]
        bc0 = sbt([128, 1024], F32, AA0 + 52 * KB, "bc0")
        bc1 = sbt([128, 1024], F32, AA0 + 56 * KB, "bc1")
        mixT = sbt([128, 16, 1024], BF16, AA0 + 64 * KB, "mixT")
        mixT_attn = P.buf("sb", mixT.off + 8 * 2048, mixT.off + 16 * 2048, "mixT_attn")
        bS1 = [nbank(), nbank()]
        bS2 = [nbank(), nbank()]
        nsq = [0]
        for c in range(8):
            dr = Dr[c % 2]
            for k in range(CONV_K):
                E("dve", (lambda dr_, k_, c_: (lambda e: e.tensor_scalar(
                    out=dr_.h[:, k_, :], in0=ident, scalar1=cvc(CW + c_ * 31 + k_), scalar2=None, op0=ALU.mult)))(dr, k, c),
                  r=(cb.whole, cv.whole), w=(dr.sub(k),))
            for n in range(2):
                bk = nbank()
                while bk in bS1 or bk in bS2:
                    bk = nbank()
                for k in range(CONV_K):
                    mm(psb[bk][:, :], dr.h[:, k, :], u_bf.h[:, c, 98 + k + n * 512:98 + k + (n + 1) * 512],
                       k == 0, k == CONV_K - 1, (dr.sub(k), u_bf.sub(c)), (psbuf[bk],))
                yb = yv.rng(c * 1024 + n * 512, c * 1024 + (n + 1) * 512)
                E("act", (lambda bk_, c_, n_: (lambda e: e.activation(
                    out=yv.h[:, c_, n_ * 512:(n_ + 1) * 512], in_=psb[bk_][:, :], func=AF.Identity,
                    bias=cvc(CB + c_))))(bk, c, n), r=(psbuf[bk], cv.whole), w=(yb,))
                s1 = sqs[nsq[0] % 4]
                nsq[0] += 1
                s2 = sqs[nsq[0] % 4]
                nsq[0] += 1
                E("act", (lambda bk_, c_, s_: (lambda e: e.activation(
                    out=s_.h[:, :], in_=psb[bk_][:, :], func=AF.Identity, bias=cvc(CB + c_))))(bk, c, s1),
                  r=(psbuf[bk], cv.whole), w=(s1.whole,))
                E("act", (lambda bk_, c_, s_: (lambda e: e.activation(
                    out=s_.h[:, :], in_=psb[bk_][:, :], func=AF.Square, bias=cvc(CB + c_))))(bk, c, s2),
                  r=(psbuf[bk], cv.whole), w=(s2.whole,))
                mm(psb[bS1[n]][:, :], ones_b, s1.h[:, :], c == 0, c == 7, (cb.whole, s1.whole), (psbuf[bS1[n]],))
                mm(psb[bS2[n]][:, :], ones_b, s2.h[:, :], c == 0, c == 7, (cb.whole, s2.whole), (psbuf[bS2[n]],))
        for n in range(2):
            sl = slice(n * 512, (n + 1) * 512)
            b0 = bc0.rng(n * 512, (n + 1) * 512)
            b1 = bc1.rng(n * 512, (n + 1) * 512)
            E("dve", (lambda n_, sl_: (lambda e: e.tensor_scalar(out=bc0.h[:, sl_], in0=psb[bS1[n_]][:, :],
                                                                scalar1=1.0 / CONV_CH, scalar2=None, op0=ALU.mult)))(n, sl),
              r=(psbuf[bS1[n]],), w=(b0,))
            E("dve", (lambda sl_: (lambda e: e.tensor_tensor(out=bc1.h[:, sl_], in0=bc0.h[:, sl_], in1=bc0.h[:, sl_],
                                                            op=ALU.mult)))(sl), r=(b0,), w=(b1,))
            E("dve", (lambda n_, sl_: (lambda e: e.scalar_tensor_tensor(out=bc1.h[:, sl_], in0=psb[bS2[n_]][:, :],
                                                                       scalar=1.0 / CONV_CH, in1=bc1.h[:, sl_],
                                                                       op0=ALU.mult, op1=ALU.subtract)))(n, sl),
              r=(psbuf[bS2[n]], b1), w=(b1,))
            E("dve", (lambda sl_: (lambda e: e.tensor_scalar(out=bc1.h[:, sl_], in0=bc1.h[:, sl_], scalar1=EPS,
                                                            scalar2=None, op0=ALU.add)))(sl), r=(b1,), w=(b1,))
            E("act", (lambda sl_: (lambda e: e.activation(out=bc1.h[:, sl_], in_=bc1.h[:, sl_], func=AF.Sqrt)))(sl),
              r=(b1,), w=(b1,))
            E("dve", (lambda sl_: (lambda e: e.reciprocal(out=bc1.h[:, sl_], in_=bc1.h[:, sl_])))(sl), r=(b1,), w=(b1,))
        bS3 = [bS1[0], bS1[1]]
        for c in range(8):
            for n in range(2):
                sl = slice(n * 512, (n + 1) * 512)
                yb = yv.rng(c * 1024 + n * 512, c * 1024 + (n + 1) * 512)
                b0 = bc0.rng(n * 512, (n + 1) * 512)
                b1 = bc1.rng(n * 512, (n + 1) * 512)
                E("dve", (lambda c_, sl_: (lambda e: e.tensor_tensor(out=yv.h[:, c_, sl_], in0=yv.h[:, c_, sl_],
                                                                    in1=bc0.h[:, sl_], op=ALU.subtract)))(c, sl),
                  r=(yb, b0), w=(yb,))
                E("dve", (lambda c_, sl_: (lambda e: e.tensor_tensor(out=yv.h[:, c_, sl_], in0=yv.h[:, c_, sl_],
                                                                    in1=bc1.h[:, sl_], op=ALU.mult)))(c, sl),
                  r=(yb, b1), w=(yb,))
                E("act", (lambda c_, sl_: (lambda e: e.activation(out=yv.h[:, c_, sl_], in_=yv.h[:, c_, sl_], func=AF.Silu,
                                                                 scale=cvc(LG + c_), bias=cvc(LB + c_))))(c, sl),
                  r=(yb, cv.whole), w=(yb,))
                s2 = sqs[nsq[0] % 4]
                nsq[0] += 1
                E("act", (lambda c_, sl_, s_: (lambda e: e.activation(out=s_.h[:, :], in_=yv.h[:, c_, sl_],
                                                                     func=AF.Square)))(c, sl, s2), r=(yb,), w=(s2.whole,))
                mm(psb[bS3[n]][:, :], ones_b, s2.h[:, :], c == 0, c == 7, (cb.whole, s2.whole), (psbuf[bS3[n]],))
        for n in range(2):
            rstd_psum_inplace(bS3[n], 512, 1.0 / CONV_CH)
        for c in range(8):
            for n in range(2):
                sl = slice(n * 512, (n + 1) * 512)
                yb = yv.rng(c * 1024 + n * 512, c * 1024 + (n + 1) * 512)
                E("dve", (lambda c_, n_, sl_: (lambda e: e.scalar_tensor_tensor(
                    out=mixT.h[:, c_, sl_], in0=yv.h[:, c_, sl_], scalar=cvc(G2 + c_), in1=psb[bS3[n_]][:, :],
                    op0=ALU.mult, op1=ALU.mult)))(c, n, sl), r=(yb, cv.whole, psbuf[bS3[n]]), w=(mixT.sub(c),))

        ckpt('E')
        knT = sbt([128, 8, 2048], BF16, AA0 + 0, "knT")
        qnT = sbt([128, 8, 1024], BF16, AA0 + 32 * KB, "qnT")
        tq = sbt([128, 8, 1024], BF16, AA0 + 48 * KB, "tq")
        VA = sbt([128, 8, 8 * 130], BF16, AA0 + 96 * KB, "VA")
        VB = sbt([128, 8, 8 * 130], BF16, AA0 + 148 * KB, "VB")
        for h in range(N_HEADS):
            ws = next_w()
            banks = [nbank() for _ in range(2)]
            for kc in range(6):
                for n in range(2):
                    mm(psb[banks[n]][:, :], ws.h[:, kc * 128:(kc + 1) * 128], qln.h[:, kc, n * 512:(n + 1) * 512],
                       kc == 0, kc == 5, (ws.whole, qln.sub(kc)), (psbuf[banks[n]],))
            for n in range(2):
                E("act", (lambda bk_, h_, n_: (lambda e: e.copy(out=qnT.h[:, h_, n_ * 512:(n_ + 1) * 512],
                                                               in_=psb[bk_][:, :])))(banks[n], h, n),
                  r=(psbuf[banks[n]],), w=(qnT.sub(h),))
            ws = next_w()
            banks = [nbank() for _ in range(2)]
            for kc in range(6):
                for n in range(2):
                    mm(psb[banks[n]][:, :], ws.h[:, kc * 128:(kc + 1) * 128], qln.h[:, kc, n * 512:(n + 1) * 512],
                       kc == 0, kc == 5, (ws.whole, qln.sub(kc)), (psbuf[banks[n]],))
            for n in range(2):
                E("dve", (lambda bk_, h_, n_: (lambda e: e.tensor_tensor(
                    out=tq.h[:, h_, n_ * 512:(n_ + 1) * 512], in0=psb[bk_][:, :],
                    in1=CS.h[:, 1024 + n_ * 512:1024 + (n_ + 1) * 512], op=ALU.mult)))(banks[n], h, n),
                  r=(psbuf[banks[n]], CS.whole), w=(tq.sub(h),))
        for h in range(N_HEADS):
            ws = next_w()
            banks = [nbank() for _ in range(4)]
            for kc in range(4):
                for n in range(4):
                    mm(psb[banks[n]][:, :], ws.h[:, kc * 128:(kc + 1) * 128], kvn.h[:, kc, n * 512:(n + 1) * 512],
                       kc == 0, kc == 3, (ws.whole, kvn.sub(kc)), (psbuf[banks[n]],))
            for n in range(4):
                if n % 2 == 0:
                    E("act", (lambda bk_, h_, n_: (lambda e: e.copy(out=knT.h[:, h_, n_ * 512:(n_ + 1) * 512],
                                                                   in_=psb[bk_][:, :])))(banks[n], h, n),
                      r=(psbuf[banks[n]],), w=(knT.sub(h),))
                else:
                    E("dve", (lambda bk_, h_, n_: (lambda e: e.tensor_copy(out=knT.h[:, h_, n_ * 512:(n_ + 1) * 512],
                                                                          in_=psb[bk_][:, :])))(banks[n], h, n),
                      r=(psbuf[banks[n]],), w=(knT.sub(h),))
        wv = [next_w(), next_w()]
        E("dve", lambda e: e.tensor_scalar(out=VA.h[:, :, :].rearrange("p t (h d) -> p (t h) d", d=130)[:, :, 128:129],
                                           in0=onesf.h[:, 0:64].rearrange("p (a b) -> p a b", b=1),
                                           scalar1=cvc(PFL), scalar2=None, op0=ALU.mult),
          r=(onesf.whole, cv.whole), w=(VA.whole,))
        E("dve", lambda e: e.memset(VB.h[:, :, :].rearrange("p t (h d) -> p (t h) d", d=130)[:, :, 128:129], 1.0),
          w=(VB.whole,))
        for t in range(16):
            Vt = VA if t < 8 else VB
            tt = t % 8
            banks = [nbank() for _ in range(2)]
            for kc in range(4):
                for hf in range(2):
                    mm(psb[banks[hf]][:, :], kvn.h[:, kc, t * 128:(t + 1) * 128],
                       wv[kc // 2].h[:, (kc % 2) * 1024 + hf * 512:(kc % 2) * 1024 + (hf + 1) * 512],
                       kc == 0, kc == 3, (kvn.sub(kc), wv[kc // 2].whole), (psbuf[banks[hf]],))
            for hf in range(2):
                dst = Vt.h[:, tt, :].rearrange("p (h d) -> p h d", d=130)[:, hf * 4:(hf + 1) * 4, 0:128]
                srcp = psb[banks[hf]][:, :].rearrange("p (h d) -> p h d", d=128)
                if t < 8:
                    E("dve", (lambda d_, s_: (lambda e: e.tensor_scalar(out=d_, in0=s_, scalar1=cvc(PFL), scalar2=None,
                                                                       op0=ALU.mult)))(dst, srcp),
                      r=(psbuf[banks[hf]], cv.whole), w=(Vt.sub(tt),))
                else:
                    E("act", (lambda d_, s_: (lambda e: e.copy(out=d_, in_=s_)))(dst, srcp),
                      r=(psbuf[banks[hf]],), w=(Vt.sub(tt),))

        ckpt('C')
        attn = sbt([128, 8, 1024], F32, AA0 + 116 * KB, "attn")
        PT = [sbt([128, 512], BF16, AA0 + 170 * KB + i * 1024, f"PT{i}") for i in range(4)]
        rc = sbt([128, 64], F32, AA0 + 174 * KB, "rc")
        nrc = [0]
        ST_B = [4, 5, 6, 7]
        ACC_B = [0, 1, 2, 3]
        items = []
        for h in range(N_HEADS):
            for qb in range(2):
                for kc in range(8 + 4 * qb + 4):
                    items.append((h, qb, kc))
        LOOK = 3

        def d_qk(n):
            h, qb, kc = items[n]
            q0 = max(kc - 8 - 4 * qb, 0) * 128
            bk = ST_B[n % 4]
            mm(psb[bk][:, q0:512], knT.h[:, h, kc * 128:(kc + 1) * 128],
               qnT.h[:, h, qb * 512 + q0:qb * 512 + 512], True, False,
               (knT.sub(h), qnT.sub(h)), (psbuf[bk],))
            mm(psb[bk][:, q0:512], kr2.h[:, kc * 128:(kc + 1) * 128],
               tq.h[:, h, qb * 512 + q0:qb * 512 + 512], False, True,
               (kr2.whole, tq.sub(h)), (psbuf[bk],))

        def d_rest(n):
            h, qb, kc = items[n]
            j = kc - 8 - 4 * qb
            i0 = max(j, 0)
            q0 = i0 * 128
            bk = ST_B[n % 4]
            pt = PT[n % 4]
            Vt = VA if kc < 8 else VB
            E("act", lambda e: e.activation(out=pt.h[:, q0:512], in_=psb[bk][:, q0:512], func=AF.Exp, scale=SCALE),
              r=(psbuf[bk],), w=(pt.whole,))
            if j >= 0:
                E("dve", lambda e: e.tensor_tensor(out=pt.h[:, q0:q0 + 128], in0=pt.h[:, q0:q0 + 128], in1=tri,
                                                   op=ALU.mult), r=(pt.whole, cb.whole), w=(pt.whole,))
            for i in range(i0, 4):
                last = 8 + 4 * qb + i
                ab = ACC_B[i]
                mm(psb[ab][:, 0:129], pt.h[:, i * 128:(i + 1) * 128], Vt.h[:, kc % 8, h * 130:h * 130 + 129],
                   kc == 0, kc == last, (pt.whole, Vt.sub(kc % 8)), (psbuf[ab],))
                if kc == last:
                    col = nrc[0] % 64
                    nrc[0] += 1
                    rb = rc.rng(col, col + 1)
                    E("dve", (lambda ab_, col_: (lambda e: e.reciprocal(out=rc.h[:, col_:col_ + 1],
                                                                       in_=psb[ab_][:, 128:129])))(ab, col),
                      r=(psbuf[ab],), w=(rb,))
                    E("dve", (lambda ab_, col_, i_: (lambda e: e.tensor_scalar(
                        out=attn.h[:, qb * 4 + i_, h * 128:(h + 1) * 128], in0=psb[ab_][:, 0:128],
                        scalar1=rc.h[:, col_:col_ + 1], scalar2=None, op0=ALU.mult)))(ab, col, i),
                      r=(psbuf[ab], rb), w=(attn.rng((qb * 4 + i) * 1024 + h * 128, (qb * 4 + i) * 1024 + (h + 1) * 128),))

        for n in range(len(items) + LOOK):
            if n < len(items):
                d_qk(n)
            if n >= LOOK:
                d_rest(n - LOOK)

        ckpt('D')
        gbcD = sbt([128, 1024], F32, AA0 + 96 * KB, "gbcD")
        junkD = [sbt([128, 1024], BF16, AA0 + (100 + 6 * i) * KB, f"junkD{i}") for i in range(2)]
        xnD = [sbt([128, 1024], BF16, AA0 + (102 + 2 * i) * KB, f"xnD{i}") for i in range(2)]
        E("sync", lambda e: e.dma_start(out=gbcD.h[:, :], in_=gbc_d[4][:, 0:1024]), w=(gbcD.whole,), key="gbc")
        cD_ss = stcol(8)
        cD_rs = stcol(8)
        brsD = {}
        for i in range(8):
            brsD[i] = nt_stats(attn.h[:, i, :], attn.sub(i), junkD[i % 2], 1024, cD_ss + i, cD_rs + i)
            if i >= 1:
                nt_apply(attn.h[:, i - 1, :], attn.sub(i - 1), brsD[i - 1], gbcD, xnD[(i - 1) % 2], 1024, cD_rs + i - 1,
                         (lambda i_: (lambda g: mixT.h[:, 8:16, i_ * 128:(i_ + 1) * 128]))(i - 1),
                         (mixT_attn,), ["dve"])
        nt_apply(attn.h[:, 7, :], attn.sub(7), brsD[7], gbcD, xnD[1], 1024, cD_rs + 7,
                 (lambda g: mixT.h[:, 8:16, 7 * 128:8 * 128]), (mixT_attn,), ["dve"])

        ckpt('Dn')
        w_out = sbt([128, 16, 2048], BF16, AA0 + 0, "w_out")
        for g in range(4):
            E("pool", (lambda g_: (lambda e: e.dma_start(out=w_out.h[:, g_ * 4:(g_ + 1) * 4, :],
                                                         in_=w_out_p[:, g_ * 8192:(g_ + 1) * 8192].rearrange("p (a b) -> p a b", a=4))))(g),
              w=(w_out.sub(g * 4, 4),), key=f"wout{g}")
        xF = [sbt([128, 2048], F32, AA0 + (96 + 8 * i) * KB, f"xF{i}") for i in range(2)]
        gpost = sbt([128, 2048], F32, AA0 + 112 * KB, "gpost")
        gpre = sbt([128, 2048], F32, AA0 + 120 * KB, "gpre")
        x1t = sbt([128, 2048], F32, AA0 + 128 * KB, "x1t")
        xnF = sbt([128, 2048], BF16, AA0 + 136 * KB, "xnF")
        hfT = sbt([128, 16, 1024], BF16, AA0 + 140 * KB, "hfT")
        junkF = sbt([128, 2048], BF16, AA0 + 172 * KB, "junkF")
        E("sync", lambda e: e.dma_start(out=gpost.h[:, :], in_=gbc_d[1]), w=(gpost.whole,), key="gbc2")
        E("sync", lambda e: e.dma_start(out=gpre.h[:, :], in_=gbc_d[2]), w=(gpre.whole,), key="gbc2")
        cF_p = stcol(32)
        cF_ss = stcol(8)
        cF_rs = stcol(8)
        cF_ss2 = stcol(8)
        cF_rs2 = stcol(8)
        brsF = {}
        brsF2 = {}

        def F1a(i):
            s_ = i % 2
            E("sync", lambda e: e.dma_start(out=xF[s_].h[:, :], in_=x_own[i * 128:(i + 1) * 128, :]),
              w=(xF[s_].whole,), key=f"xF{s_}")
            for cbk in (2, 3, 0, 1):
                bk = (i % 2) * 4 + cbk
                for kc in range(16):
                    mm(psb[bk][:, :], mixT.h[:, kc, i * 128:(i + 1) * 128], w_out.h[:, kc, cbk * 512:(cbk + 1) * 512],
                       kc == 0, kc == 15, (mixT.sub(kc), w_out.sub(kc)), (psbuf[bk],))

        def F1b(i):
            for cbk in (2, 3, 0, 1):
                bk = (i % 2) * 4 + cbk
                pc = cF_p + i * 4 + cbk
                E("act", (lambda bk_, pc_, c_: (lambda e: e.activation(
                    out=junkF.h[:, c_ * 512:(c_ + 1) * 512], in_=psb[bk_][:, :], func=AF.Square,
                    accum_out=st.h[:, pc_:pc_ + 1])))(bk, pc, cbk),
                  r=(psbuf[bk],), w=(junkF.rng(cbk * 512, (cbk + 1) * 512), st.rng(pc, pc + 1)))
            E("dve", lambda e: e.tensor_reduce(out=st.h[:, cF_ss + i:cF_ss + i + 1],
                                               in_=st.h[:, cF_p + 4 * i:cF_p + 4 * i + 4],
                                               axis=mybir.AxisListType.X, op=ALU.add),
              r=(st.rng(cF_p + 4 * i, cF_p + 4 * i + 4),), w=(st.rng(cF_ss + i, cF_ss + i + 1),))
            brsF[i] = rstd_from_ss(cF_ss + i, cF_rs + i, 1, 1.0 / D_MODEL)

        def F2(i):
            s_ = i % 2
            for cbk in range(4):
                bk = (i % 2) * 4 + cbk
                sl = slice(cbk * 512, (cbk + 1) * 512)
                E("dve", (lambda bk_, sl_: (lambda e: e.scalar_tensor_tensor(
                    out=x1t.h[:, sl_], in0=psb[bk_][:, :], scalar=st.h[:, cF_rs + i:cF_rs + i + 1], in1=gpost.h[:, sl_],
                    op0=ALU.mult, op1=ALU.mult)))(bk, sl),
                  r=(psbuf[bk], brsF[i], gpost.whole), w=(x1t.rng(cbk * 512, (cbk + 1) * 512),))
            E("dve", lambda e: e.tensor_tensor(out=x1t.h[:, :], in0=x1t.h[:, :], in1=xF[s_].h[:, :], op=ALU.add),
              r=(x1t.whole, xF[s_].whole), w=(x1t.whole,))
            E("sync", lambda e: e.dma_start(out=x1_d[i * 128:(i + 1) * 128, :], in_=x1t.h[:, :]),
              r=(x1t.whole,), w=(), key="x1w")
            brsF2[i] = nt_stats(x1t.h[:, :], x1t.whole, xnF, 2048, cF_ss2 + i, cF_rs2 + i)

        def F3(i):
            nt_apply(x1t.h[:, :], x1t.whole, brsF2[i], gpre, xnF, 2048, cF_rs2 + i,
                     (lambda g: hfT.h[:, g * 8:(g + 1) * 8, i * 128:(i + 1) * 128]),
                     (hfT.whole,), ["act", "dve"], banks=[(i % 2) * 4, (i % 2) * 4 + 1])

        F1a(0)
        F1b(0)
        for i in range(8):
            if i + 1 < 8:
                F1a(i + 1)
            F2(i)
            F3(i)
            if i + 1 < 8:
                F1b(i + 1)
        ckpt('F')
        actT = sbt([128, NFB, 1024], BF16, AA0 + 0, "actT")
        sgf = [sbt([128, 512], F32, AA0 + 172 * KB + i * 2048, f"sgf{i}") for i in range(2)]
        nsg = [0]
        for f in range(NFB):
            wg = next_w()
            wu = next_w()
            bg = [nbank(), nbank()]
            bu = [nbank(), nbank()]
            for kc in range(16):
                for n in range(2):
                    mm(psb[bg[n]][:, :], wg.h[:, kc * 128:(kc + 1) * 128], hfT.h[:, kc, n * 512:(n + 1) * 512],
                       kc == 0, kc == 15, (wg.whole, hfT.whole), (psbuf[bg[n]],))
            for kc in range(16):
                for n in range(2):
                    mm(psb[bu[n]][:, :], wu.h[:, kc * 128:(kc + 1) * 128], hfT.h[:, kc, n * 512:(n + 1) * 512],
                       kc == 0, kc == 15, (wu.whole, hfT.whole), (psbuf[bu[n]],))
            for n in range(2):
                sgt = sgf[nsg[0] % 2]
                nsg[0] += 1
                E("act", (lambda bk_, s_: (lambda e: e.activation(out=s_.h[:, :], in_=psb[bk_][:, :], func=AF.Silu)))(bg[n], sgt),
                  r=(psbuf[bg[n]],), w=(sgt.whole,))
                E("dve", (lambda bk_, s_, f_, n_: (lambda e: e.tensor_tensor(
                    out=actT.h[:, f_, n_ * 512:(n_ + 1) * 512], in0=psb[bk_][:, :], in1=s_.h[:, :], op=ALU.mult)))(bu[n], sgt, f, n),
                  r=(psbuf[bu[n]], sgt.whole), w=(actT.sub(f),))

        ckpt('G1')
        ff = sbt([128, 8, 2048], F32, AA0 + 88 * KB, "ff")
        xr = [sbt([128, 2048], F32, AA0 + (152 + 8 * i) * KB, f"xr{i}") for i in range(2)]
        gffn = sbt([128, 2048], F32, AA0 + 168 * KB, "gffn")
        junkG = [sbt([128, 512], BF16, AA0 + 176 * KB + i * 1024, f"junkG{i}") for i in range(2)]
        E("sync", lambda e: e.dma_start(out=gffn.h[:, :], in_=gbc_d[3]), w=(gffn.whole,), key="gbc3")
        cG_p = stcol(32)
        cG_ss = stcol(8)
        cG_rs = stcol(8)
        for cbk in range(4):
            for fg in range(11):
                ws = next_w()
                for i in range(8):
                    for fb in range(4):
                        fidx = fg * 4 + fb
                        mm(psb[i][:, :], actT.h[:, fidx, i * 128:(i + 1) * 128], ws.h[:, fb * 512:(fb + 1) * 512],
                           fg == 0 and fb == 0, fg == 10 and fb == 3, (actT.sub(fidx), ws.whole), (psbuf[i],))
            for i in range(8):
                sl = slice(cbk * 512, (cbk + 1) * 512)
                pc = cG_p + i * 4 + cbk
                fb_ = ff.rng(i * 2048 + cbk * 512, i * 2048 + (cbk + 1) * 512)
                E("dve", (lambda i_, sl_: (lambda e: e.tensor_tensor(out=ff.h[:, i_, sl_], in0=psb[i_][:, :],
                                                                     in1=gffn.h[:, sl_], op=ALU.mult)))(i, sl),
                  r=(psbuf[i], gffn.whole), w=(fb_,))
                E("act", (lambda i_, pc_: (lambda e: e.activation(out=junkG[i_ % 2].h[:, :], in_=psb[i_][:, :], func=AF.Square,
                                                                 accum_out=st.h[:, pc_:pc_ + 1])))(i, pc),
                  r=(psbuf[i],), w=(junkG[i % 2].whole, st.rng(pc, pc + 1)))
        for i in range(8):
            s = i % 2
            E("sync", (lambda s_, i_: (lambda e: e.dma_start(out=xr[s_].h[:, :], in_=x1_d[i_ * 128:(i_ + 1) * 128, :])))(s, i),
              w=(xr[s].whole,), key=f"xr{s}")
            P.q["sync"][-1].waits.append(("dma", "x1w", P.dma_cnt["x1w"]))
            E("dve", (lambda i_: (lambda e: e.tensor_reduce(out=st.h[:, cG_ss + i_:cG_ss + i_ + 1],
                                                           in_=st.h[:, cG_p + 4 * i_:cG_p + 4 * i_ + 4],
                                                           axis=mybir.AxisListType.X, op=ALU.add)))(i),
              r=(st.rng(cG_p + 4 * i, cG_p + 4 * i + 4),), w=(st.rng(cG_ss + i, cG_ss + i + 1),))
            brs = rstd_from_ss(cG_ss + i, cG_rs + i, 1, 1.0 / D_MODEL)
            E("dve", (lambda i_, s_: (lambda e: e.scalar_tensor_tensor(
                out=ff.h[:, i_, :], in0=ff.h[:, i_, :], scalar=st.h[:, cG_rs + i_:cG_rs + i_ + 1], in1=xr[s_].h[:, :],
                op0=ALU.mult, op1=ALU.add)))(i, s), r=(ff.sub(i), brs, xr[s].whole), w=(ff.sub(i),))
            E("sync", (lambda i_: (lambda e: e.dma_start(out=out_d[i_ * 128:(i_ + 1) * 128, :], in_=ff.h[:, i_, :])))(i),
              r=(ff.sub(i),), w=(), key="outw")
        fin = E("sync", None)
        fin.waits.append(("dma", "outw", P.dma_cnt["outw"]))
        assert wcur[0] == len(wpieces), (wcur[0], len(wpieces))


    except _Stop:
        pass

    fin_all = P.op("sync", None)
    for k_, v_ in P.dma_cnt.items():
        fin_all.waits.append(("dma", k_, v_))

    for e_ in ENGS:
        cnt = 0
        for ins in P.q[e_]:
            if ins.signal and not ins.is_dma:
                cnt += 1
                ins.value = cnt
    keys = sorted(P.dma_cnt.keys())
    sem_ctx = {}
    sems_eng = {e_: nc.alloc_semaphore(f"s_{e_}") for e_ in ENGS}
    sems_key = {k: nc.alloc_semaphore(f"d_{k}") for k in keys}

    def replay(ename, eng):
        waited = {}
        for ins in P.q[ename]:
            for w in ins.waits:
                if w[0] == "eng":
                    p = w[1]
                    sem, val = sems_eng[p.eng], p.value
                else:
                    sem, val = sems_key[w[1]], w[2]
                k = id(sem)
                if waited.get(k, 0) < val:
                    eng.wait_ge(sem, val)
                    waited[k] = val
            if ins.fn is None:
                continue
            bi = ins.fn(eng)
            if ins.is_dma:
                bi.then_inc(sems_key[ins.key], 16)
            elif ins.signal:
                bi.then_inc(sems_eng[ename], 1)

    with nc.Block() as block:
        @block.sync
        def _(e):
            replay("sync", e)

        @block.scalar
        def _(e):
            replay("act", e)

        @block.vector
        def _(e):
            replay("dve", e)

        @block.gpsimd
        def _(e):
            replay("pool", e)

        @block.tensor
        def _(e):
            replay("pe", e)

    stats = {e_: len(P.q[e_]) for e_ in ENGS}
    stats["sig"] = {e_: sum(1 for i in P.q[e_] if i.signal and not i.is_dma) for e_ in ENGS}
    return nc, stats


def _blocks_k(w, ncols_per_block):
    K, N = w.shape
    kc = K // 128
    nb = N // ncols_per_block
    a = w.reshape(kc, 128, nb, ncols_per_block).transpose(2, 1, 0, 3)
    return np.ascontiguousarray(a.reshape(nb, 128, kc * ncols_per_block))


def prepare_inputs(x, positions, pre_mix_norm, w_in, q_norm, w_uq, kv_norm, w_ukv, conv_w, conv_b, conv_ln_g,
                   conv_ln_b, conv_out_norm, attn_out_norm, w_out, post_mix_norm, pre_ffn_norm, w_gate, w_up,
                   w_down, post_ffn_norm):
    f = np.float32
    x = np.asarray(x, f)
    positions = np.asarray(positions, np.int32)
    w_in = np.asarray(w_in, f)[0]
    w_uq = np.asarray(w_uq, f)[0]
    w_ukv = np.asarray(w_ukv, f)[0]
    w_out = np.asarray(w_out, f)[0]
    w_gate = np.asarray(w_gate, f)[0]
    w_up = np.asarray(w_up, f)[0]
    w_down = np.asarray(w_down, f)[0]

    c1 = 2 * CONV_CH
    c2 = c1 + Q_LORA
    c3 = c2 + KV_LORA
    cols = []
    cols += list(range(c2, c3))
    cols += list(range(c3, c3 + 64)) + list(range(c3 + 32, c3 + 64)) + list(range(c3, c3 + 32))
    cols += list(range(c1, c2))
    for c in range(8):
        cols += list(range(CONV_CH + c * 128, CONV_CH + (c + 1) * 128))
        cols += list(range(c * 128, (c + 1) * 128))
    w_in_p = _blocks_k(w_in[:, cols], 128)
    cols = []
    for h in range(N_HEADS):
        b0 = h * 192
        cols += list(range(b0, b0 + 128))
        cols += list(range(b0 + 128, b0 + 192)) + list(range(b0 + 160, b0 + 192)) + list(range(b0 + 128, b0 + 160))
    w_uq_p = _blocks_k(w_uq[:, cols], 128)
    kcols = []
    vcols = []
    for h in range(N_HEADS):
        kcols += list(range(h * 256, h * 256 + 128))
        vcols += list(range(h * 256 + 128, h * 256 + 256))
    w_uk_p = _blocks_k(w_ukv[:, kcols], 128)
    wv = w_ukv[:, vcols]
    w_uv_p = np.ascontiguousarray(wv.reshape(2, 2, 128, 1024).transpose(0, 2, 1, 3).reshape(2, 128, 2048))
    w_out_p = np.ascontiguousarray(w_out.reshape(16, 128, 2048).transpose(1, 0, 2).reshape(128, 16 * 2048))
    g_p = _blocks_k(w_gate, 128)
    u_p = _blocks_k(w_up, 128)
    w_gu_p = np.ascontiguousarray(np.stack([g_p, u_p], axis=1).reshape(88, 128, 2048))
    wd = w_down.reshape(11, 4, 128, 4, 512)
    w_dn_p = np.ascontiguousarray(wd.transpose(3, 0, 2, 1, 4).reshape(44, 128, 2048))

    c_bf = np.zeros((128, 512), np.float32)
    c_bf[:, 0:128] = np.eye(128)
    c_bf[:, 128:256] = 1.0
    e64 = np.eye(64)
    c_bf[:, 256:384] = np.block([[e64, e64], [e64, e64]])
    kk = np.arange(128)[:, None]
    qq = np.arange(128)[None, :]
    c_bf[:, 384:512] = (qq >= kk).astype(np.float32)
    c_bf = c_bf.astype(ml_dtypes.bfloat16)

    cvec = np.zeros((128, NCV), f)
    cw = np.asarray(conv_w, f)[0]
    for c in range(8):
        cvec[:, CW + c * 31:CW + (c + 1) * 31] = cw[:, c * 128:(c + 1) * 128].T
    cvec[:, CB:CB + 8] = np.asarray(conv_b, f)[0].reshape(8, 128).T
    cvec[:, LG:LG + 8] = np.asarray(conv_ln_g, f)[0].reshape(8, 128).T
    cvec[:, LB:LB + 8] = np.asarray(conv_ln_b, f)[0].reshape(8, 128).T
    cvec[:, G2:G2 + 8] = np.asarray(conv_out_norm, f)[0].reshape(8, 128).T
    cvec[:, GQ:GQ + 6] = np.asarray(q_norm, f)[0].reshape(6, 128).T
    cvec[:, GKV:GKV + 4] = np.asarray(kv_norm, f)[0].reshape(4, 128).T
    inv_freq = (np.float32(10000.0) ** (-np.arange(0, 64, 2, dtype=np.float32) / np.float32(64))).astype(f)
    cvec[:, IFQ] = np.tile(inv_freq, 4)
    cvec[0:64, PHS] = np.float32(np.pi / 2)
    cvec[:, SGN] = 1.0
    cvec[64:96, SGN] = -1.0

    gbc = np.zeros((5, 128, 2048), f)
    gbc[0] = np.asarray(pre_mix_norm, f)[0][None, :]
    gbc[1] = np.asarray(post_mix_norm, f)[0][None, :]
    gbc[2] = np.asarray(pre_ffn_norm, f)[0][None, :]
    gbc[3] = np.asarray(post_ffn_norm, f)[0][None, :]
    gbc[4, :, 0:1024] = np.asarray(attn_out_norm, f)[0][None, :]

    shared = dict(c_bf=c_bf, gbc=gbc, w_in_p=w_in_p, w_uq_p=w_uq_p, w_uk_p=w_uk_p, w_uv_p=w_uv_p,
                  w_out_p=w_out_p, w_gu_p=w_gu_p, w_dn_p=w_dn_p)
    in_maps = []
    for core in range(8):
        b, half = core // 2, core % 2
        m = dict(shared)
        m["x_own"] = np.ascontiguousarray(x[b, half * TOWN:(half + 1) * TOWN])
        cvc = cvec.copy()
        pos = np.zeros((2048,), np.int32)
        if half == 1:
            m["x_prev"] = np.ascontiguousarray(x[b, 0:TOWN])
            pos[:] = positions[b, 0:2048]
            cvc[:, PFL] = 1.0
        else:
            m["x_prev"] = np.zeros((TOWN, D_MODEL), f)
            pos[1024:] = positions[b, 0:1024]
            cvc[:, PFL] = 0.0
        m["cvec"] = cvc
        m["pos_bc"] = np.ascontiguousarray(np.broadcast_to(pos[None, :], (128, 2048)))
        in_maps.append(m)
    return in_maps


_CACHE = {}


def kernel(**inputs):
    if "nc" not in _CACHE:
        _CACHE["nc"], _CACHE["stats"] = build_program()
    nc = _CACHE["nc"]
    in_maps = prepare_inputs(**inputs)
    res = run_bass_kernel_spmd(nc, in_maps, core_ids=list(range(8)))
    out = np.zeros((BATCH, SEQ, D_MODEL), np.float32)
    for core in range(8):
        b, half = core // 2, core % 2
        out[b, half * TOWN:(half + 1) * TOWN] = res.results[core]["out"]
    return out
```

```python
import os
import numpy as np
import ml_dtypes
import concourse.bass as bass
import concourse.mybir as mybir
from concourse.bass_utils import run_bass_kernel_spmd

F32 = mybir.dt.float32
BF16 = mybir.dt.bfloat16
I32 = mybir.dt.int32
AF = mybir.ActivationFunctionType
ALU = mybir.AluOpType
PI = float(np.pi)

D_MODEL = 2048
SEQ = 2048
BATCH = 4
TOWN = 1024
CONV_CH = 1024
CONV_K = 31
N_HEADS = 8
Q_LORA = 768
KV_LORA = 512
D_FF = 5632
NFB = D_FF // 128
EPS = 1e-6
SCALE = 192 ** -0.5

CW = 0
CB = 248
LG = 256
LB = 264
G2 = 272
GQ = 280
GKV = 286
IFQ = 290
PHS = 291
SGN = 292
PFL = 293
NCV = 320

KB = 1024


class Buf:
    __slots__ = ("space", "lo", "hi", "name")

    def __init__(self, space, lo, hi, name=""):
        self.space, self.lo, self.hi, self.name = space, lo, hi, name


class Ins:
    __slots__ = ("eng", "fn", "is_dma", "key", "signal", "value", "waits", "dma_val", "idx")

    def __init__(self, eng, fn, is_dma, key):
        self.eng, self.fn, self.is_dma, self.key = eng, fn, is_dma, key
        self.idx = -1
        self.signal = False
        self.value = None
        self.waits = []
        self.dma_val = None


ENGS = ["sync", "act", "dve", "pool", "pe"]


class Prog:
    def __init__(self):
        self.q = {e: [] for e in ENGS}
        self.wr = {"sb": [], "ps": []}
        self.rd = {"sb": [], "ps": []}
        self.dma_cnt = {}

    def buf(self, space, lo, hi, name=""):
        return Buf(space, lo, hi, name)

    def op(self, eng, fn, reads=(), writes=(), dma_key=None):
        ins = Ins(eng, fn, dma_key is not None, dma_key)
        ins.idx = len(self.q[eng])
        need_eng = {}
        need_dma = {}

        def add(p):
            if p is ins:
                return
            if p.is_dma:
                need_dma[p.key] = 1
                return
            if p.eng == eng and not ins.is_dma and eng == "pe":
                return
            cur = need_eng.get(p.eng)
            if cur is None or cur.idx < p.idx:
                need_eng[p.eng] = p

        for b in reads:
            for w in self.wr[b.space]:
                if w[0] < b.hi and b.lo < w[1]:
                    add(w[2])
            if b.space == "ps" and eng != "pe":
                lo = b.lo // 2048 * 2048
                hi = -(-b.hi // 2048) * 2048
                for r in self.rd["ps"]:
                    if r[0] < hi and lo < r[1] and r[3].eng != eng and r[3].eng != "pe":
                        add(r[3])
        for b in writes:
            for w in self.wr[b.space]:
                if w[0] < b.hi and b.lo < w[1]:
                    add(w[2])
            for r in self.rd[b.space]:
                if r[0] < b.hi and b.lo < r[1]:
                    add(r[3])
        for p in need_eng.values():
            p.signal = True
            ins.waits.append(("eng", p))
        for k in need_dma:
            ins.waits.append(("dma", k, self.dma_cnt[k]))
        if dma_key is not None:
            self.dma_cnt[dma_key] = self.dma_cnt.get(dma_key, 0) + 16
            ins.dma_val = self.dma_cnt[dma_key]
        for b in writes:
            sp = b.space
            self.wr[sp] = [w for w in self.wr[sp] if not (b.lo <= w[0] and w[1] <= b.hi)]
            self.wr[sp].append([b.lo, b.hi, ins])
            self.rd[sp] = [r for r in self.rd[sp] if not (b.lo <= r[0] and r[1] <= b.hi)]
        rk = ("dma", dma_key) if ins.is_dma else eng
        for b in reads:
            lst = self.rd[b.space]
            for r in lst:
                if r[0] == b.lo and r[1] == b.hi and r[2] == rk:
                    r[3] = ins
                    break
            else:
                lst.append([b.lo, b.hi, rk, ins])
        self.q[eng].append(ins)
        return ins


def build_program(stop_after=None, tensors=None):
    nc = bass.Bass("TRN2", target_bir_lowering=False)
    P = Prog()

    def din(name, shape, dt):
        return nc.dram_tensor(name, list(shape), dt, kind="ExternalInput").ap()

    x_own = din("x_own", [TOWN, D_MODEL], F32)
    x_prev = din("x_prev", [TOWN, D_MODEL], F32)
    pos_bc = din("pos_bc", [128, 2048], I32)
    c_bf = din("c_bf", [128, 512], BF16)
    cvec_d = din("cvec", [128, NCV], F32)
    gbc_d = din("gbc", [5, 128, 2048], F32)
    w_in_p = din("w_in_p", [27, 128, 2048], F32)
    w_uq_p = din("w_uq_p", [16, 128, 768], F32)
    w_uk_p = din("w_uk_p", [8, 128, 512], F32)
    w_uv_p = din("w_uv_p", [2, 128, 2048], F32)
    w_out_p = din("w_out_p", [128, 16 * 2048], F32)
    w_gu_p = din("w_gu_p", [88, 128, 2048], F32)
    w_dn_p = din("w_dn_p", [44, 128, 2048], F32)
    out_d = nc.dram_tensor("out", [TOWN, D_MODEL], F32, kind="ExternalOutput").ap()
    x1_d = nc.dram_tensor("x1_scratch", [TOWN, D_MODEL], F32).ap()
    dbg = {}

    base = (nc.sbuf_base + 63) // 64 * 64
    CONST0 = base
    WR0 = CONST0 + 4 * KB
    AA0 = WR0 + 24 * KB
    assert AA0 + 178 * KB <= nc.sbuf_top, (AA0 + 178 * KB, nc.sbuf_top)

    def dsz(dt):
        return 4 if dt in (F32, I32) else 2

    class T:
        def __init__(self, name, shape, dt, off):
            self.h = nc.alloc_sbuf_tensor_at(name, list(shape), dt, offset=off)
            self.off = off
            self.shape = shape
            self.dt = dt
            self.nbytes = int(np.prod(shape[1:])) * dsz(dt)
            self.whole = P.buf("sb", off, off + self.nbytes, name)
            self._subs = {}
            if tensors is not None:
                tensors[name] = self

        def sub(self, i, n=None):
            key = (i, n)
            if key not in self._subs:
                slab = self.nbytes // self.shape[1]
                cnt = 1 if n is None else n
                self._subs[key] = P.buf("sb", self.off + i * slab, self.off + (i + cnt) * slab)
            return self._subs[key]

        def rng(self, lo_el, hi_el):
            key = ("r", lo_el, hi_el)
            if key not in self._subs:
                self._subs[key] = P.buf("sb", self.off + lo_el * dsz(self.dt), self.off + hi_el * dsz(self.dt))
            return self._subs[key]

    _names = [0]

    def sbt(shape, dt, off, name=None):
        _names[0] += 1
        return T(name or f"t{_names[0]}", shape, dt, off)

    cb = sbt([128, 512], BF16, CONST0, "cb")
    ident = cb.h[:, 0:128]
    ones_b = cb.h[:, 128:256]
    f2 = cb.h[:, 256:384]
    tri = cb.h[:, 384:512]
    cv = sbt([128, NCV], F32, CONST0 + 1024, "cv")
    st = sbt([128, 320], F32, CONST0 + 1024 + NCV * 4, "st")
    onesf = sbt([128, 64], F32, CONST0 + 1024 + NCV * 4 + 1280, "onesf")
    assert CONST0 + 1024 + NCV * 4 + 1280 + 256 <= WR0

    def cvc(col):
        return cv.h[:, col:col + 1]

    _stn = [0]

    def stcol(n=1):
        c = _stn[0]
        _stn[0] += n
        assert _stn[0] <= 320
        return c

    NSLOT = 6
    wslots = [sbt([128, 2048], BF16, WR0 + i * 4 * KB, f"ws{i}") for i in range(NSLOT)]
    wpieces = []
    wstate = {"issued": 0}

    psb = [nc.alloc_psum_tensor(f"psb{i}", [128, 512], F32) for i in range(8)]
    psbuf = [P.buf("ps", i * 2048, (i + 1) * 2048, f"ps{i}") for i in range(8)]
    psbf = [psb[i][:, :].bitcast(BF16) for i in range(8)]
    _bank = [0]

    def nbank():
        b = _bank[0] % 8
        _bank[0] += 1
        return b

    _psub = {}

    def psub(bank, lo, hi):
        k = (bank, lo, hi)
        if k not in _psub:
            _psub[k] = P.buf("ps", bank * 2048 + lo * 4, bank * 2048 + hi * 4)
        return _psub[k]

    def E(eng, fn, r=(), w=(), key=None):
        return P.op(eng, fn, r, w, key)

    def mm(out, lhsT, rhs, start, stop, r, w):
        return E("pe", lambda e: e.matmul(out, lhsT=lhsT, rhs=rhs, start=start, stop=stop), r, w)

    def issue_weights(upto):
        while wstate["issued"] < min(upto, len(wpieces)):
            i = wstate["issued"]
            src, n = wpieces[i]
            slot = wslots[i % NSLOT]
            E("pool", (lambda s, n_, sl: (lambda e: e.dma_start(out=sl.h[:, 0:n_], in_=s)))(src, n, slot),
              r=(), w=(slot.whole,), key=f"ws{i % NSLOT}")
            wstate["issued"] += 1

    wcur = [0]

    def next_w(prefetch=4):
        i = wcur[0]
        wcur[0] += 1
        issue_weights(i + 1 + prefetch)
        return wslots[i % NSLOT]

    for j in range(27):
        wpieces.append((w_in_p[j], 2048))
    for j in range(16):
        wpieces.append((w_uq_p[j], 768))
    for j in range(8):
        wpieces.append((w_uk_p[j], 512))
    for j in range(2):
        wpieces.append((w_uv_p[j], 2048))
    for j in range(88):
        wpieces.append((w_gu_p[j], 2048))
    for j in range(44):
        wpieces.append((w_dn_p[j], 2048))

    class _Stop(Exception):
        pass

    def ckpt(name):
        if stop_after == name:
            raise _Stop()

    try:
        E("sync", lambda e: e.dma_start(out=cb.h[:, :], in_=c_bf), w=(cb.whole,), key="const")
        E("sync", lambda e: e.dma_start(out=cv.h[:, :], in_=cvec_d), w=(cv.whole,), key="const")
        E("dve", lambda e: e.memset(st.h[:, :], 0.0), w=(st.whole,))
        E("dve", lambda e: e.memset(onesf.h[:, :], 1.0), w=(onesf.whole,))

        def rstd_from_ss(c_ss, c_out, n, inv_n):
            c_ms = stcol(n)
            c_sq = stcol(n)
            bs = st.rng(c_ss, c_ss + n)
            bm = st.rng(c_ms, c_ms + n)
            bq = st.rng(c_sq, c_sq + n)
            bo = st.rng(c_out, c_out + n)
            E("dve", lambda e: e.tensor_scalar(out=st.h[:, c_ms:c_ms + n], in0=st.h[:, c_ss:c_ss + n], scalar1=inv_n,
                                               scalar2=EPS, op0=ALU.mult, op1=ALU.add), r=(bs,), w=(bm,))
            E("act", lambda e: e.activation(out=st.h[:, c_sq:c_sq + n], in_=st.h[:, c_ms:c_ms + n], func=AF.Sqrt),
              r=(bm,), w=(bq,))
            E("dve", lambda e: e.reciprocal(out=st.h[:, c_out:c_out + n], in_=st.h[:, c_sq:c_sq + n]), r=(bq,), w=(bo,))
            return bo

        def rstd_psum_inplace(bank, n, inv_n):
            b = psbuf[bank]
            ap = psb[bank][:, 0:n]
            E("dve", lambda e: e.tensor_scalar(out=ap, in0=ap, scalar1=inv_n, scalar2=EPS, op0=ALU.mult, op1=ALU.add),
              r=(b,), w=(b,))
            E("act", lambda e: e.activation(out=ap, in_=ap, func=AF.Sqrt), r=(b,), w=(b,))
            E("dve", lambda e: e.reciprocal(out=ap, in_=ap), r=(b,), w=(b,))

        def nt_stats(src_ap, src_buf, junk_t, width, c_ss, c_rs):
            bs = st.rng(c_ss, c_ss + 1)
            E("act", lambda e: e.activation(out=junk_t.h[:, 0:width], in_=src_ap, func=AF.Square,
                                            accum_out=st.h[:, c_ss:c_ss + 1]), r=(src_buf,), w=(junk_t.whole, bs))
            return rstd_from_ss(c_ss, c_rs, 1, 1.0 / width)

        def nt_apply_a(src_ap, src_buf, brs, gb_t, xn_t, width, c_rs, banks=None):
            nchunk = width // 128
            E("dve", lambda e: e.scalar_tensor_tensor(out=xn_t.h[:, 0:width], in0=src_ap,
                                                      scalar=st.h[:, c_rs:c_rs + 1], in1=gb_t.h[:, 0:width],
                                                      op0=ALU.mult, op1=ALU.mult),
              r=(src_buf, brs, gb_t.whole), w=(xn_t.whole,))
            used = []
            for g in range(nchunk // 8):
                bk = nbank() if banks is None else banks[g]
                used.append(bk)
                for c8 in range(8):
                    c = g * 8 + c8
                    E("pe", (lambda bk_, c8_, c_: (lambda e: e.transpose(psbf[bk_][:, c8_ * 128:(c8_ + 1) * 128],
                                                                       xn_t.h[:, c_ * 128:(c_ + 1) * 128], ident)))(bk, c8, c),
                      r=(xn_t.whole, cb.whole), w=(psbuf[bk],))
            return used

        def nt_apply_b(used, dst_fn, dst_bufs, evac_engs):
            for g, bk in enumerate(used):
                eng = evac_engs[g % len(evac_engs)]
                src = psbf[bk][:, 0:1024].rearrange("p (a b) -> p a b", a=8)
                dst = dst_fn(g)
                if eng == "act":
                    E("act", (lambda d_, s_: (lambda e: e.copy(out=d_, in_=s_)))(dst, src), r=(psbuf[bk],), w=dst_bufs)
                else:
                    E("dve", (lambda d_, s_: (lambda e: e.tensor_copy(out=d_, in_=s_)))(dst, src), r=(psbuf[bk],), w=dst_bufs)

        def nt_apply(src_ap, src_buf, brs, gb_t, xn_t, width, c_rs, dst_fn, dst_bufs, evac_engs, banks=None):
            used = nt_apply_a(src_ap, src_buf, brs, gb_t, xn_t, width, c_rs, banks)
            nt_apply_b(used, dst_fn, dst_bufs, evac_engs)

        hT = sbt([128, 16, 2048], BF16, AA0 + 0, "hT")
        xa = [sbt([128, 2048], F32, AA0 + (64 + 8 * i) * KB, f"xa{i}") for i in range(4)]
        gbcA = sbt([128, 2048], F32, AA0 + 96 * KB, "gbcA")
        xnA = [sbt([128, 2048], BF16, AA0 + (104 + 4 * i) * KB, f"xnA{i}") for i in range(3)]
        junkA = [sbt([128, 2048], BF16, AA0 + (116 + 4 * i) * KB, f"junkA{i}") for i in range(4)]
        issue_weights(NSLOT)
        E("sync", lambda e: e.dma_start(out=gbcA.h[:, :], in_=gbc_d[0]), w=(gbcA.whole,), key="gbc")
        cA_ss = stcol(16)
        cA_rs = stcol(16)
        brsA = {}

        def A1(t):
            s_ = t % 4
            src = x_prev[t * 128:(t + 1) * 128, :] if t < 8 else x_own[(t - 8) * 128:(t - 7) * 128, :]
            E("sync", (lambda s__, src_: (lambda e: e.dma_start(out=xa[s__].h[:, :], in_=src_)))(s_, src),
              w=(xa[s_].whole,), key=f"xa{s_}")
            brsA[t] = nt_stats(xa[s_].h[:, :], xa[s_].whole, junkA[t % 4], 2048, cA_ss + t, cA_rs + t)

        usedA = {}

        def A2a(t):
            s_ = t % 4
            usedA[t] = nt_apply_a(xa[s_].h[:, :], xa[s_].whole, brsA[t], gbcA, xnA[t % 3], 2048, cA_rs + t)

        def A2b(t):
            nt_apply_b(usedA[t], (lambda g: hT.h[:, g * 8:(g + 1) * 8, t * 128:(t + 1) * 128]),
                       (hT.whole,), ["act", "dve"])

        A1(0)
        A1(1)
        A1(2)
        A2a(0)
        for t in range(16):
            if t + 1 < 16:
                A2a(t + 1)
            if t + 3 < 16:
                A1(t + 3)
            A2b(t)

        CS = sbt([128, 2048], F32, AA0 + 170 * KB, "CS")
        posi = sbt([128, 2048], I32, AA0 + 132 * KB, "posi")
        posf = sbt([128, 2048], F32, AA0 + 140 * KB, "posf")
        tmpa = sbt([128, 2048], F32, AA0 + 148 * KB, "tmpa")
        E("sync", lambda e: e.dma_start(out=posi.h[:, :], in_=pos_bc), w=(posi.whole,), key="pos")
        E("dve", lambda e: e.tensor_copy(out=posf.h[:, :], in_=posi.h[:, :]), r=(posi.whole,), w=(posf.whole,))
        E("dve", lambda e: e.tensor_scalar(out=CS.h[:, :], in0=posf.h[:, :], scalar1=cvc(IFQ), scalar2=cvc(PHS),
                                           op0=ALU.mult, op1=ALU.add), r=(posf.whole, cv.whole), w=(CS.whole,))
        E("dve", lambda e: e.tensor_scalar(out=tmpa.h[:, :], in0=CS.h[:, :], scalar1=1.0 / (2 * PI), scalar2=None,
                                           op0=ALU.mult), r=(CS.whole,), w=(tmpa.whole,))
        E("dve", lambda e: e.tensor_copy(out=posi.h[:, :], in_=tmpa.h[:, :]), r=(tmpa.whole,), w=(posi.whole,))
        E("dve", lambda e: e.tensor_copy(out=posf.h[:, :], in_=posi.h[:, :]), r=(posi.whole,), w=(posf.whole,))
        C1 = 6.28125
        C2 = 2 * PI - C1
        E("dve", lambda e: e.scalar_tensor_tensor(out=CS.h[:, :], in0=posf.h[:, :], scalar=-C1, in1=CS.h[:, :],
                                                  op0=ALU.mult, op1=ALU.add), r=(posf.whole, CS.whole), w=(CS.whole,))
        E("dve", lambda e: e.scalar_tensor_tensor(out=CS.h[:, :], in0=posf.h[:, :], scalar=-C2, in1=CS.h[:, :],
                                                  op0=ALU.mult, op1=ALU.add), r=(posf.whole, CS.whole), w=(CS.whole,))
        E("dve", lambda e: e.tensor_single_scalar(out=tmpa.h[:, :], in_=CS.h[:, :], scalar=PI, op=ALU.is_gt),
          r=(CS.whole,), w=(tmpa.whole,))
        E("dve", lambda e: e.scalar_tensor_tensor(out=CS.h[:, :], in0=tmpa.h[:, :], scalar=-2 * PI, in1=CS.h[:, :],
                                                  op0=ALU.mult, op1=ALU.add), r=(tmpa.whole, CS.whole), w=(CS.whole,))
        E("dve", lambda e: e.tensor_scalar(out=CS.h[:, :], in0=CS.h[:, :], scalar1=-PI, scalar2=PI,
                                           op0=ALU.max, op1=ALU.min), r=(CS.whole,), w=(CS.whole,))
        E("act", lambda e: e.activation(out=CS.h[:, :], in_=CS.h[:, :], func=AF.Sin), r=(CS.whole,), w=(CS.whole,))
        E("dve", lambda e: e.tensor_scalar(out=CS.h[:, :], in0=CS.h[:, :], scalar1=cvc(SGN), scalar2=None,
                                           op0=ALU.mult), r=(CS.whole, cv.whole), w=(CS.whole,))

        ckpt('A')
        zz = sbt([128, 4, 2048], F32, AA0 + 64 * KB, "zz")
        zq = sbt([128, 6, 1024], F32, AA0 + 64 * KB, "zq")
        sqz = sbt([128, 4, 2048], BF16, AA0 + 96 * KB, "sqz")
        sqq = sbt([128, 6, 1024], BF16, AA0 + 96 * KB, "sqq")
        t_k = sbt([128, 2048], BF16, AA0 + 112 * KB, "t_k")
        kvn = sbt([128, 4, 2048], BF16, AA0 + 116 * KB, "kvn")
        qln = sbt([128, 6, 1024], BF16, AA0 + 132 * KB, "qln")
        u_bf = sbt([128, 8, 1152], BF16, AA0 + 148 * KB, "u_bf")
        kr2 = sbt([128, 2048], BF16, AA0 + 166 * KB, "kr2")
        sg = [sbt([128, 1152], F32, AA0 + 64 * KB + i * 4608, f"sg{i}") for i in range(2)]

        for b in range(4):
            ws = next_w()
            banks = [nbank() for _ in range(4)]
            for kc in range(16):
                for n in range(4):
                    mm(psb[banks[n]][:, :], ws.h[:, kc * 128:(kc + 1) * 128], hT.h[:, kc, n * 512:(n + 1) * 512],
                       kc == 0, kc == 15, (ws.whole, hT.whole), (psbuf[banks[n]],))
            for n in range(4):
                if os.environ.get("KDBG") == "noevac":
                    break
                bk = banks[n]
                E("dve", (lambda bk_, b_, n_: (lambda e: e.tensor_scalar(
                    out=zz.h[:, b_, n_ * 512:(n_ + 1) * 512], in0=psb[bk_][:, :], scalar1=cvc(GKV + b_), scalar2=None,
                    op0=ALU.mult)))(bk, b, n), r=(psbuf[bk], cv.whole), w=(zz.sub(b),))
                if os.environ.get("KDBG") == "noact":
                    continue
                E("act", (lambda bk_, b_, n_: (lambda e: e.activation(
                    out=sqz.h[:, b_, n_ * 512:(n_ + 1) * 512], in_=psb[bk_][:, :], func=AF.Square)))(bk, b, n),
                  r=(psbuf[bk],), w=(sqz.sub(b),) + ((psbuf[bk],) if os.environ.get("KDBG") == "serial" else ()))
        ckpt('B1')
        for n in range(4):
            bk = nbank()
            for b in range(4):
                mm(psb[bk][:, :], ones_b, sqz.h[:, b, n * 512:(n + 1) * 512], b == 0, b == 3,
                   (cb.whole, sqz.sub(b)), (psbuf[bk],))
            rstd_psum_inplace(bk, 512, 1.0 / KV_LORA)
            for b in range(4):
                E("dve", (lambda bk_, b_, n_: (lambda e: e.tensor_tensor(
                    out=kvn.h[:, b_, n_ * 512:(n_ + 1) * 512], in0=zz.h[:, b_, n_ * 512:(n_ + 1) * 512],
                    in1=psb[bk_][:, :], op=ALU.mult)))(bk, b, n), r=(zz.sub(b), psbuf[bk]), w=(kvn.sub(b),))
        ckpt('B2')
        ws = next_w()
        banks = [nbank() for _ in range(4)]
        for kc in range(16):
            for n in range(4):
                mm(psb[banks[n]][:, :], ws.h[:, kc * 128:(kc + 1) * 128], hT.h[:, kc, n * 512:(n + 1) * 512],
                   kc == 0, kc == 15, (ws.whole, hT.whole), (psbuf[banks[n]],))
        for n in range(4):
            bk = banks[n]
            E("dve", (lambda bk_, n_: (lambda e: e.tensor_tensor(
                out=t_k.h[:, n_ * 512:(n_ + 1) * 512], in0=psb[bk_][:, :], in1=CS.h[:, n_ * 512:(n_ + 1) * 512],
                op=ALU.mult)))(bk, n), r=(psbuf[bk], CS.whole), w=(t_k.rng(n * 512, (n + 1) * 512),))
            bk2 = nbank()
            mm(psb[bk2][:, :], f2, t_k.h[:, n * 512:(n + 1) * 512], True, True,
               (cb.whole, t_k.rng(n * 512, (n + 1) * 512)), (psbuf[bk2],))
            E("act", (lambda bk_, n_: (lambda e: e.copy(out=kr2.h[:, n_ * 512:(n_ + 1) * 512], in_=psb[bk_][:, :])))(bk2, n),
              r=(psbuf[bk2],), w=(kr2.rng(n * 512, (n + 1) * 512),))
        ckpt('B3')
        for b in range(6):
            ws = next_w()
            banks = [nbank() for _ in range(2)]
            for kc in range(16):
                for n in range(2):
                    mm(psb[banks[n]][:, :], ws.h[:, kc * 128:(kc + 1) * 128],
                       hT.h[:, kc, 1024 + n * 512:1024 + (n + 1) * 512],
                       kc == 0, kc == 15, (ws.whole, hT.whole), (psbuf[banks[n]],))
            for n in range(2):
                bk = banks[n]
                E("dve", (lambda bk_, b_, n_: (lambda e: e.tensor_scalar(
                    out=zq.h[:, b_, n_ * 512:(n_ + 1) * 512], in0=psb[bk_][:, :], scalar1=cvc(GQ + b_), scalar2=None,
                    op0=ALU.mult)))(bk, b, n), r=(psbuf[bk], cv.whole), w=(zq.sub(b),))
                E("act", (lambda bk_, b_, n_: (lambda e: e.activation(
                    out=sqq.h[:, b_, n_ * 512:(n_ + 1) * 512], in_=psb[bk_][:, :], func=AF.Square)))(bk, b, n),
                  r=(psbuf[bk],), w=(sqq.sub(b),))
        for n in range(2):
            bk = nbank()
            for b in range(6):
                mm(psb[bk][:, :], ones_b, sqq.h[:, b, n * 512:(n + 1) * 512], b == 0, b == 5,
                   (cb.whole, sqq.sub(b)), (psbuf[bk],))
            rstd_psum_inplace(bk, 512, 1.0 / Q_LORA)
            for b in range(6):
                E("dve", (lambda bk_, b_, n_: (lambda e: e.tensor_tensor(
                    out=qln.h[:, b_, n_ * 512:(n_ + 1) * 512], in0=zq.h[:, b_, n_ * 512:(n_ + 1) * 512],
                    in1=psb[bk_][:, :], op=ALU.mult)))(bk, b, n), r=(zq.sub(b), psbuf[bk]), w=(qln.sub(b),))
        ckpt('B4')
        tokr = [(896, 1024), (1024, 1536), (1536, 2048)]
        for c in range(8):
            sgt = sg[c % 2]
            ws = next_w()
            banks = [nbank() for _ in range(3)]
            for kc in range(16):
                for n, (a0, a1) in enumerate(tokr):
                    mm(psb[banks[n]][:, 0:a1 - a0], ws.h[:, kc * 128:(kc + 1) * 128], hT.h[:, kc, a0:a1],
                       kc == 0, kc == 15, (ws.whole, hT.whole), (psbuf[banks[n]],))
            for n, (a0, a1) in enumerate(tokr):
                E("act", (lambda bk_, a0_, a1_, sg_: (lambda e: e.activation(
                    out=sg_.h[:, a0_ - 896:a1_ - 896], in_=psb[bk_][:, 0:a1_ - a0_], func=AF.Sigmoid)))(banks[n], a0, a1, sgt),
                  r=(psbuf[banks[n]],), w=(sgt.whole,))
            ws = next_w()
            banks = [nbank() for _ in range(3)]
            for kc in range(16):
                for n, (a0, a1) in enumerate(tokr):
                    mm(psb[banks[n]][:, 0:a1 - a0], ws.h[:, kc * 128:(kc + 1) * 128], hT.h[:, kc, a0:a1],
                       kc == 0, kc == 15, (ws.whole, hT.whole), (psbuf[banks[n]],))
            for n, (a0, a1) in enumerate(tokr):
                E("dve", (lambda bk_, a0_, a1_, sg_, c_: (lambda e: e.tensor_tensor(
                    out=u_bf.h[:, c_, a0_ - 896:a1_ - 896], in0=psb[bk_][:, 0:a1_ - a0_], in1=sg_.h[:, a0_ - 896:a1_ - 896],
                    op=ALU.mult)))(banks[n], a0, a1, sgt, c), r=(psbuf[banks[n]], sgt.whole), w=(u_bf.sub(c),))

        ckpt('B')
        yv = sbt([128, 8, 1024], F32, AA0 + 0, "yv")
        Dr = [sbt([128, 31, 128], BF16, AA0 + (32 + 8 * i) * KB, f"Dr{i}") for i in range(2)]
        sqs = [sbt([128, 512], BF16, AA0 + 48 * KB + i * 1024, f"sqs{i}") for i in range(4)]
        bc0 = sbt([128, 1024], F32, AA0 + 52 * KB, "bc0")
        bc1 = sbt([128, 1024], F32, AA0 + 56 * KB, "bc1")
        mixT = sbt([128, 16, 1024], BF16, AA0 + 64 * KB, "mixT")
        mixT_attn = P.buf("sb", mixT.off + 8 * 2048, mixT.off + 16 * 2048, "mixT_attn")
        bS1 = [nbank(), nbank()]
        bS2 = [nbank(), nbank()]
        nsq = [0]
        for c in range(8):
            dr = Dr[c % 2]
            for k in range(CONV_K):
                E("dve", (lambda dr_, k_, c_: (lambda e: e.tensor_scalar(
                    out=dr_.h[:, k_, :], in0=ident, scalar1=cvc(CW + c_ * 31 + k_), scalar2=None, op0=ALU.mult)))(dr, k, c),
                  r=(cb.whole, cv.whole), w=(dr.sub(k),))
            for n in range(2):
                bk = nbank()
                while bk in bS1 or bk in bS2:
                    bk = nbank()
                for k in range(CONV_K):
                    mm(psb[bk][:, :], dr.h[:, k, :], u_bf.h[:, c, 98 + k + n * 512:98 + k + (n + 1) * 512],
                       k == 0, k == CONV_K - 1, (dr.sub(k), u_bf.sub(c)), (psbuf[bk],))
                yb = yv.rng(c * 1024 + n * 512, c * 1024 + (n + 1) * 512)
                E("act", (lambda bk_, c_, n_: (lambda e: e.activation(
                    out=yv.h[:, c_, n_ * 512:(n_ + 1) * 512], in_=psb[bk_][:, :], func=AF.Identity,
                    bias=cvc(CB + c_))))(bk, c, n), r=(psbuf[bk], cv.whole), w=(yb,))
                s1 = sqs[nsq[0] % 4]
                nsq[0] += 1
                s2 = sqs[nsq[0] % 4]
                nsq[0] += 1
                E("act", (lambda bk_, c_, s_: (lambda e: e.activation(
                    out=s_.h[:, :], in_=psb[bk_][:, :], func=AF.Identity, bias=cvc(CB + c_))))(bk, c, s1),
                  r=(psbuf[bk], cv.whole), w=(s1.whole,))
                E("act", (lambda bk_, c_, s_: (lambda e: e.activation(
                    out=s_.h[:, :], in_=psb[bk_][:, :], func=AF.Square, bias=cvc(CB + c_))))(bk, c, s2),
                  r=(psbuf[bk], cv.whole), w=(s2.whole,))
                mm(psb[bS1[n]][:, :], ones_b, s1.h[:, :], c == 0, c == 7, (cb.whole, s1.whole), (psbuf[bS1[n]],))
                mm(psb[bS2[n]][:, :], ones_b, s2.h[:, :], c == 0, c == 7, (cb.whole, s2.whole), (psbuf[bS2[n]],))
        for n in range(2):
            sl = slice(n * 512, (n + 1) * 512)
            b0 = bc0.rng(n * 512, (n + 1) * 512)
            b1 = bc1.rng(n * 512, (n + 1) * 512)
            E("dve", (lambda n_, sl_: (lambda e: e.tensor_scalar(out=bc0.h[:, sl_], in0=psb[bS1[n_]][:, :],
                                                                scalar1=1.0 / CONV_CH, scalar2=None, op0=ALU.mult)))(n, sl),
              r=(psbuf[bS1[n]],), w=(b0,))
            E("dve", (lambda sl_: (lambda e: e.tensor_tensor(out=bc1.h[:, sl_], in0=bc0.h[:, sl_], in1=bc0.h[:, sl_],
                                                            op=ALU.mult)))(sl), r=(b0,), w=(b1,))
            E("dve", (lambda n_, sl_: (lambda e: e.scalar_tensor_tensor(out=bc1.h[:, sl_], in0=psb[bS2[n_]][:, :],
                                                                       scalar=1.0 / CONV_CH, in1=bc1.h[:, sl_],
                                                                       op0=ALU.mult, op1=ALU.subtract)))(n, sl),
              r=(psbuf[bS2[n]], b1), w=(b1,))
            E("dve", (lambda sl_: (lambda e: e.tensor_scalar(out=bc1.h[:, sl_], in0=bc1.h[:, sl_], scalar1=EPS,
                                                            scalar2=None, op0=ALU.add)))(sl), r=(b1,), w=(b1,))
            E("act", (lambda sl_: (lambda e: e.activation(out=bc1.h[:, sl_], in_=bc1.h[:, sl_], func=AF.Sqrt)))(sl),
              r=(b1,), w=(b1,))
            E("dve", (lambda sl_: (lambda e: e.reciprocal(out=bc1.h[:, sl_], in_=bc1.h[:, sl_])))(sl), r=(b1,), w=(b1,))
        bS3 = [bS1[0], bS1[1]]
        for c in range(8):
            for n in range(2):
                sl = slice(n * 512, (n + 1) * 512)
                yb = yv.rng(c * 1024 + n * 512, c * 1024 + (n + 1) * 512)
                b0 = bc0.rng(n * 512, (n + 1) * 512)
                b1 = bc1.rng(n * 512, (n + 1) * 512)
                E("dve", (lambda c_, sl_: (lambda e: e.tensor_tensor(out=yv.h[:, c_, sl_], in0=yv.h[:, c_, sl_],
                                                                    in1=bc0.h[:, sl_], op=ALU.subtract)))(c, sl),
                  r=(yb, b0), w=(yb,))
                E("dve", (lambda c_, sl_: (lambda e: e.tensor_tensor(out=yv.h[:, c_, sl_], in0=yv.h[:, c_, sl_],
                                                                    in1=bc1.h[:, sl_], op=ALU.mult)))(c, sl),
                  r=(yb, b1), w=(yb,))
                E("act", (lambda c_, sl_: (lambda e: e.activation(out=yv.h[:, c_, sl_], in_=yv.h[:, c_, sl_], func=AF.Silu,
                                                                 scale=cvc(LG + c_), bias=cvc(LB + c_))))(c, sl),
                  r=(yb, cv.whole), w=(yb,))
                s2 = sqs[nsq[0] % 4]
                nsq[0] += 1
                E("act", (lambda c_, sl_, s_: (lambda e: e.activation(out=s_.h[:, :], in_=yv.h[:, c_, sl_],
                                                                     func=AF.Square)))(c, sl, s2), r=(yb,), w=(s2.whole,))
                mm(psb[bS3[n]][:, :], ones_b, s2.h[:, :], c == 0, c == 7, (cb.whole, s2.whole), (psbuf[bS3[n]],))
        for n in range(2):
            rstd_psum_inplace(bS3[n], 512, 1.0 / CONV_CH)
        for c in range(8):
            for n in range(2):
                sl = slice(n * 512, (n + 1) * 512)
                yb = yv.rng(c * 1024 + n * 512, c * 1024 + (n + 1) * 512)
                E("dve", (lambda c_, n_, sl_: (lambda e: e.scalar_tensor_tensor(
                    out=mixT.h[:, c_, sl_], in0=yv.h[:, c_, sl_], scalar=cvc(G2 + c_), in1=psb[bS3[n_]][:, :],
                    op0=ALU.mult, op1=ALU.mult)))(c, n, sl), r=(yb, cv.whole, psbuf[bS3[n]]), w=(mixT.sub(c),))

        ckpt('E')
        knT = sbt([128, 8, 2048], BF16, AA0 + 0, "knT")
        qnT = sbt([128, 8, 1024], BF16, AA0 + 32 * KB, "qnT")
        tq = sbt([128, 8, 1024], BF16, AA0 + 48 * KB, "tq")
        VA = sbt([128, 8, 8 * 130], BF16, AA0 + 96 * KB, "VA")
        VB = sbt([128, 8, 8 * 130], BF16, AA0 + 148 * KB, "VB")
        for h in range(N_HEADS):
            ws = next_w()
            banks = [nbank() for _ in range(2)]
            for kc in range(6):
                for n in range(2):
                    mm(psb[banks[n]][:, :], ws.h[:, kc * 128:(kc + 1) * 128], qln.h[:, kc, n * 512:(n + 1) * 512],
                       kc == 0, kc == 5, (ws.whole, qln.sub(kc)), (psbuf[banks[n]],))
            for n in range(2):
                E("act", (lambda bk_, h_, n_: (lambda e: e.copy(out=qnT.h[:, h_, n_ * 512:(n_ + 1) * 512],
                                                               in_=psb[bk_][:, :])))(banks[n], h, n),
                  r=(psbuf[banks[n]],), w=(qnT.sub(h),))
            ws = next_w()
            banks = [nbank() for _ in range(2)]
            for kc in range(6):
                for n in range(2):
                    mm(psb[banks[n]][:, :], ws.h[:, kc * 128:(kc + 1) * 128], qln.h[:, kc, n * 512:(n + 1) * 512],
                       kc == 0, kc == 5, (ws.whole, qln.sub(kc)), (psbuf[banks[n]],))
            for n in range(2):
                E("dve", (lambda bk_, h_, n_: (lambda e: e.tensor_tensor(
                    out=tq.h[:, h_, n_ * 512:(n_ + 1) * 512], in0=psb[bk_][:, :],
                    in1=CS.h[:, 1024 + n_ * 512:1024 + (n_ + 1) * 512], op=ALU.mult)))(banks[n], h, n),
                  r=(psbuf[banks[n]], CS.whole), w=(tq.sub(h),))
        for h in range(N_HEADS):
            ws = next_w()
            banks = [nbank() for _ in range(4)]
            for kc in range(4):
                for n in range(4):
                    mm(psb[banks[n]][:, :], ws.h[:, kc * 128:(kc + 1) * 128], kvn.h[:, kc, n * 512:(n + 1) * 512],
                       kc == 0, kc == 3, (ws.whole, kvn.sub(kc)), (psbuf[banks[n]],))
            for n in range(4):
                if n % 2 == 0:
                    E("act", (lambda bk_, h_, n_: (lambda e: e.copy(out=knT.h[:, h_, n_ * 512:(n_ + 1) * 512],
                                                                   in_=psb[bk_][:, :])))(banks[n], h, n),
                      r=(psbuf[banks[n]],), w=(knT.sub(h),))
                else:
                    E("dve", (lambda bk_, h_, n_: (lambda e: e.tensor_copy(out=knT.h[:, h_, n_ * 512:(n_ + 1) * 512],
                                                                          in_=psb[bk_][:, :])))(banks[n], h, n),
                      r=(psbuf[banks[n]],), w=(knT.sub(h),))
        wv = [next_w(), next_w()]
        E("dve", lambda e: e.tensor_scalar(out=VA.h[:, :, :].rearrange("p t (h d) -> p (t h) d", d=130)[:, :, 128:129],
                                           in0=onesf.h[:, 0:64].rearrange("p (a b) -> p a b", b=1),
                                           scalar1=cvc(PFL), scalar2=None, op0=ALU.mult),
          r=(onesf.whole, cv.whole), w=(VA.whole,))
        E("dve", lambda e: e.memset(VB.h[:, :, :].rearrange("p t (h d) -> p (t h) d", d=130)[:, :, 128:129], 1.0),
          w=(VB.whole,))
        for t in range(16):
            Vt = VA if t < 8 else VB
            tt = t % 8
            banks = [nbank() for _ in range(2)]
            for kc in range(4):
                for hf in range(2):
                    mm(psb[banks[hf]][:, :], kvn.h[:, kc, t * 128:(t + 1) * 128],
                       wv[kc // 2].h[:, (kc % 2) * 1024 + hf * 512:(kc % 2) * 1024 + (hf + 1) * 512],
                       kc == 0, kc == 3, (kvn.sub(kc), wv[kc // 2].whole), (psbuf[banks[hf]],))
            for hf in range(2):
                dst = Vt.h[:, tt, :].rearrange("p (h d) -> p h d", d=130)[:, hf * 4:(hf + 1) * 4, 0:128]
                srcp = psb[banks[hf]][:, :].rearrange("p (h d) -> p h d", d=128)
                if t < 8:
                    E("dve", (lambda d_, s_: (lambda e: e.tensor_scalar(out=d_, in0=s_, scalar1=cvc(PFL), scalar2=None,
                                                                       op0=ALU.mult)))(dst, srcp),
                      r=(psbuf[banks[hf]], cv.whole), w=(Vt.sub(tt),))
                else:
                    E("act", (lambda d_, s_: (lambda e: e.copy(out=d_, in_=s_)))(dst, srcp),
                      r=(psbuf[banks[hf]],), w=(Vt.sub(tt),))

        ckpt('C')
        attn = sbt([128, 8, 1024], F32, AA0 + 116 * KB, "attn")
        PT = [sbt([128, 512], BF16, AA0 + 170 * KB + i * 1024, f"PT{i}") for i in range(4)]
        rc = sbt([128, 64], F32, AA0 + 174 * KB, "rc")
        nrc = [0]
        ST_B = [4, 5, 6, 7]
        ACC_B = [0, 1, 2, 3]
        items = []
        for h in range(N_HEADS):
            for qb in range(2):
                for kc in range(8 + 4 * qb + 4):
                    items.append((h, qb, kc))
        LOOK = 3

        def d_qk(n):
            h, qb, kc = items[n]
            q0 = max(kc - 8 - 4 * qb, 0) * 128
            bk = ST_B[n % 4]
            mm(psb[bk][:, q0:512], knT.h[:, h, kc * 128:(kc + 1) * 128],
               qnT.h[:, h, qb * 512 + q0:qb * 512 + 512], True, False,
               (knT.sub(h), qnT.sub(h)), (psbuf[bk],))
            mm(psb[bk][:, q0:512], kr2.h[:, kc * 128:(kc + 1) * 128],
               tq.h[:, h, qb * 512 + q0:qb * 512 + 512], False, True,
               (kr2.whole, tq.sub(h)), (psbuf[bk],))

        def d_rest(n):
            h, qb, kc = items[n]
            j = kc - 8 - 4 * qb
            i0 = max(j, 0)
            q0 = i0 * 128
            bk = ST_B[n % 4]
            pt = PT[n % 4]
            Vt = VA if kc < 8 else VB
            E("act", lambda e: e.activation(out=pt.h[:, q0:512], in_=psb[bk][:, q0:512], func=AF.Exp, scale=SCALE),
              r=(psbuf[bk],), w=(pt.whole,))
            if j >= 0:
                E("dve", lambda e: e.tensor_tensor(out=pt.h[:, q0:q0 + 128], in0=pt.h[:, q0:q0 + 128], in1=tri,
                                                   op=ALU.mult), r=(pt.whole, cb.whole), w=(pt.whole,))
            for i in range(i0, 4):
                last = 8 + 4 * qb + i
                ab = ACC_B[i]
                mm(psb[ab][:, 0:129], pt.h[:, i * 128:(i + 1) * 128], Vt.h[:, kc % 8, h * 130:h * 130 + 129],
                   kc == 0, kc == last, (pt.whole, Vt.sub(kc % 8)), (psbuf[ab],))
                if kc == last:
                    col = nrc[0] % 64
                    nrc[0] += 1
                    rb = rc.rng(col, col + 1)
                    E("dve", (lambda ab_, col_: (lambda e: e.reciprocal(out=rc.h[:, col_:col_ + 1],
                                                                       in_=psb[ab_][:, 128:129])))(ab, col),
                      r=(psbuf[ab],), w=(rb,))
                    E("dve", (lambda ab_, col_, i_: (lambda e: e.tensor_scalar(
                        out=attn.h[:, qb * 4 + i_, h * 128:(h + 1) * 128], in0=psb[ab_][:, 0:128],
                        scalar1=rc.h[:, col_:col_ + 1], scalar2=None, op0=ALU.mult)))(ab, col, i),
                      r=(psbuf[ab], rb), w=(attn.rng((qb * 4 + i) * 1024 + h * 128, (qb * 4 + i) * 1024 + (h + 1) * 128),))

        for n in range(len(items) + LOOK):
            if n < len(items):
                d_qk(n)
            if n >= LOOK:
                d_rest(n - LOOK)

        ckpt('D')
        gbcD = sbt([128, 1024], F32, AA0 + 148 * KB, "gbcD")
        junkD = [sbt([128, 1024], BF16, AA0 + (152 + 2 * i) * KB, f"junkD{i}") for i in range(2)]
        xnD = [sbt([128, 1024], BF16, AA0 + (156 + 2 * i) * KB, f"xnD{i}") for i in range(2)]
        E("sync", lambda e: e.dma_start(out=gbcD.h[:, :], in_=gbc_d[4][:, 0:1024]), w=(gbcD.whole,), key="gbc")
        cD_ss = stcol(8)
        cD_rs = stcol(8)
        brsD = {}
        usedD = {}

        def mixT_attn_tile(i):
            return tuple(P.buf("sb", mixT.off + kc * 2048 + i * 256, mixT.off + kc * 2048 + (i + 1) * 256)
                         for kc in range(8, 16))

        def Dn_s(i):
            brsD[i] = nt_stats(attn.h[:, i, :], attn.sub(i), junkD[i % 2], 1024, cD_ss + i, cD_rs + i)

        def Dn_a(i):
            usedD[i] = nt_apply_a(attn.h[:, i, :], attn.sub(i), brsD[i], gbcD, xnD[i % 2], 1024, cD_rs + i,
                                  banks=[4 + i % 4])
            nt_apply_b(usedD[i], (lambda g: mixT.h[:, 8:16, i * 128:(i + 1) * 128]), mixT_attn_tile(i), ["dve"])

        ckpt('Dn')
        w_out = sbt([128, 16, 2048], BF16, AA0 + 0, "w_out")
        for kc in (0, 1, 8, 12, 2, 3, 9, 13, 4, 5, 10, 14, 6, 7, 11, 15):
            E("pool", (lambda kc_: (lambda e: e.dma_start(out=w_out.h[:, kc_, :],
                                                          in_=w_out_p[:, kc_ * 2048:(kc_ + 1) * 2048])))(kc),
              w=(w_out.sub(kc),), key=f"wout{kc % 4}")
        xF = [sbt([128, 2048], F32, AA0 + (96 + 8 * i) * KB, f"xF{i}") for i in range(2)]
        gpost = sbt([128, 2048], F32, AA0 + 112 * KB, "gpost")
        gpre = sbt([128, 2048], F32, AA0 + 120 * KB, "gpre")
        x1t = sbt([128, 2048], F32, AA0 + 128 * KB, "x1t")
        xnF = sbt([128, 2048], BF16, AA0 + 136 * KB, "xnF")
        hfT = sbt([128, 16, 1024], BF16, AA0 + 140 * KB, "hfT")
        junkF = sbt([128, 2048], BF16, AA0 + 172 * KB, "junkF")
        cF_p = stcol(32)
        cF_ss = stcol(8)
        cF_rs = stcol(8)
        cF_ss2 = stcol(8)
        cF_rs2 = stcol(8)
        brsF = {}
        brsF2 = {}

        KC_ORDER = [0, 1, 2, 3, 4, 5, 6, 8, 9, 10, 12, 13, 14, 7, 11, 15]

        def F1a(i):
            s_ = i % 2
            E("sync", lambda e: e.dma_start(out=xF[s_].h[:, :], in_=x_own[i * 128:(i + 1) * 128, :]),
              w=(xF[s_].whole,), key=f"xF{s_}")
            for cbk in (2, 3, 0, 1):
                bk = (i % 2) * 4 + cbk
                for kc in KC_ORDER:
                    mb = mixT.sub(kc) if kc < 8 else P.buf("sb", mixT.off + kc * 2048 + i * 256,
                                                           mixT.off + kc * 2048 + (i + 1) * 256)
                    mm(psb[bk][:, :], mixT.h[:, kc, i * 128:(i + 1) * 128], w_out.h[:, kc, cbk * 512:(cbk + 1) * 512],
                       kc == KC_ORDER[0], kc == KC_ORDER[-1], (mb, w_out.sub(kc)), (psbuf[bk],))

        def F1b(i):
            for cbk in (2, 3, 0, 1):
                bk = (i % 2) * 4 + cbk
                pc = cF_p + i * 4 + cbk
                E("act", (lambda bk_, pc_, c_: (lambda e: e.activation(
                    out=junkF.h[:, c_ * 512:(c_ + 1) * 512], in_=psb[bk_][:, :], func=AF.Square,
                    accum_out=st.h[:, pc_:pc_ + 1])))(bk, pc, cbk),
                  r=(psbuf[bk],), w=(junkF.rng(cbk * 512, (cbk + 1) * 512), st.rng(pc, pc + 1)))
            E("dve", lambda e: e.tensor_reduce(out=st.h[:, cF_ss + i:cF_ss + i + 1],
                                               in_=st.h[:, cF_p + 4 * i:cF_p + 4 * i + 4],
                                               axis=mybir.AxisListType.X, op=ALU.add),
              r=(st.rng(cF_p + 4 * i, cF_p + 4 * i + 4),), w=(st.rng(cF_ss + i, cF_ss + i + 1),))
            brsF[i] = rstd_from_ss(cF_ss + i, cF_rs + i, 1, 1.0 / D_MODEL)

        def F2(i):
            s_ = i % 2
            for cbk in range(4):
                bk = (i % 2) * 4 + cbk
                sl = slice(cbk * 512, (cbk + 1) * 512)
                E("dve", (lambda bk_, sl_: (lambda e: e.scalar_tensor_tensor(
                    out=x1t.h[:, sl_], in0=psb[bk_][:, :], scalar=st.h[:, cF_rs + i:cF_rs + i + 1], in1=gpost.h[:, sl_],
                    op0=ALU.mult, op1=ALU.mult)))(bk, sl),
                  r=(psbuf[bk], brsF[i], gpost.whole), w=(x1t.rng(cbk * 512, (cbk + 1) * 512),))
            E("dve", lambda e: e.tensor_tensor(out=x1t.h[:, :], in0=x1t.h[:, :], in1=xF[s_].h[:, :], op=ALU.add),
              r=(x1t.whole, xF[s_].whole), w=(x1t.whole,))
            E("sync", lambda e: e.dma_start(out=x1_d[i * 128:(i + 1) * 128, :], in_=x1t.h[:, :]),
              r=(x1t.whole,), w=(), key="x1w")
            brsF2[i] = nt_stats(x1t.h[:, :], x1t.whole, xnF, 2048, cF_ss2 + i, cF_rs2 + i)

        def F3(i):
            nt_apply(x1t.h[:, :], x1t.whole, brsF2[i], gpre, xnF, 2048, cF_rs2 + i,
                     (lambda g: hfT.h[:, g * 8:(g + 1) * 8, i * 128:(i + 1) * 128]),
                     (hfT.whole,), ["act", "dve"], banks=[(i % 2) * 4, (i % 2) * 4 + 1])

        Dn_s(0)
        Dn_s(1)
        Dn_a(0)
        Dn_s(2)
        Dn_a(1)
        F1a(0)
        for i in range(2, 8):
            if i + 1 < 8:
                Dn_s(i + 1)
            Dn_a(i)
        E("sync", lambda e: e.dma_start(out=gpost.h[:, :], in_=gbc_d[1]), w=(gpost.whole,), key="gbc2")
        E("sync", lambda e: e.dma_start(out=gpre.h[:, :], in_=gbc_d[2]), w=(gpre.whole,), key="gbc2")
        F1b(0)
        for i in range(8):
            if i + 1 < 8:
                F1a(i + 1)
            F2(i)
            F3(i)
            if i + 1 < 8:
                F1b(i + 1)
        ckpt('F')
        actT = sbt([128, NFB, 1024], BF16, AA0 + 0, "actT")
        sgf = [sbt([128, 512], F32, AA0 + 172 * KB + i * 2048, f"sgf{i}") for i in range(2)]
        nsg = [0]
        for f in range(NFB):
            wg = next_w()
            wu = next_w()
            bg = [nbank(), nbank()]
            bu = [nbank(), nbank()]
            for kc in range(16):
                for n in range(2):
                    mm(psb[bg[n]][:, :], wg.h[:, kc * 128:(kc + 1) * 128], hfT.h[:, kc, n * 512:(n + 1) * 512],
                       kc == 0, kc == 15, (wg.whole, hfT.whole), (psbuf[bg[n]],))
            for kc in range(16):
                for n in range(2):
                    mm(psb[bu[n]][:, :], wu.h[:, kc * 128:(kc + 1) * 128], hfT.h[:, kc, n * 512:(n + 1) * 512],
                       kc == 0, kc == 15, (wu.whole, hfT.whole), (psbuf[bu[n]],))
            for n in range(2):
                sgt = sgf[nsg[0] % 2]
                nsg[0] += 1
                E("act", (lambda bk_, s_: (lambda e: e.activation(out=s_.h[:, :], in_=psb[bk_][:, :], func=AF.Silu)))(bg[n], sgt),
                  r=(psbuf[bg[n]],), w=(sgt.whole,))
                E("dve", (lambda bk_, s_, f_, n_: (lambda e: e.tensor_tensor(
                    out=actT.h[:, f_, n_ * 512:(n_ + 1) * 512], in0=psb[bk_][:, :], in1=s_.h[:, :], op=ALU.mult)))(bu[n], sgt, f, n),
                  r=(psbuf[bu[n]], sgt.whole), w=(actT.sub(f),))

        ckpt('G1')
        ff = sbt([128, 8, 2048], F32, AA0 + 88 * KB, "ff")
        xr = [sbt([128, 2048], F32, AA0 + (152 + 8 * i) * KB, f"xr{i}") for i in range(2)]
        gffn = sbt([128, 2048], F32, AA0 + 168 * KB, "gffn")
        junkG = [sbt([128, 512], BF16, AA0 + 176 * KB + i * 1024, f"junkG{i}") for i in range(2)]
        E("sync", lambda e: e.dma_start(out=gffn.h[:, :], in_=gbc_d[3]), w=(gffn.whole,), key="gbc3")
        cG_p = stcol(32)
        cG_ss = stcol(8)
        cG_rs = stcol(8)
        for cbk in range(4):
            for fg in range(11):
                ws = next_w()
                for i in range(8):
                    for fb in range(4):
                        fidx = fg * 4 + fb
                        mm(psb[i][:, :], actT.h[:, fidx, i * 128:(i + 1) * 128], ws.h[:, fb * 512:(fb + 1) * 512],
                           fg == 0 and fb == 0, fg == 10 and fb == 3, (actT.sub(fidx), ws.whole), (psbuf[i],))
            for i in range(8):
                sl = slice(cbk * 512, (cbk + 1) * 512)
                pc = cG_p + i * 4 + cbk
                fb_ = ff.rng(i * 2048 + cbk * 512, i * 2048 + (cbk + 1) * 512)
                E("dve", (lambda i_, sl_: (lambda e: e.tensor_tensor(out=ff.h[:, i_, sl_], in0=psb[i_][:, :],
                                                                     in1=gffn.h[:, sl_], op=ALU.mult)))(i, sl),
                  r=(psbuf[i], gffn.whole), w=(fb_,))
                E("act", (lambda i_, pc_: (lambda e: e.activation(out=junkG[i_ % 2].h[:, :], in_=psb[i_][:, :], func=AF.Square,
                                                                 accum_out=st.h[:, pc_:pc_ + 1])))(i, pc),
                  r=(psbuf[i],), w=(junkG[i % 2].whole, st.rng(pc, pc + 1)))
        for i in range(8):
            s = i % 2
            E("sync", (lambda s_, i_: (lambda e: e.dma_start(out=xr[s_].h[:, :], in_=x1_d[i_ * 128:(i_ + 1) * 128, :])))(s, i),
              w=(xr[s].whole,), key=f"xr{s}")
            P.q["sync"][-1].waits.append(("dma", "x1w", P.dma_cnt["x1w"]))
            E("dve", (lambda i_: (lambda e: e.tensor_reduce(out=st.h[:, cG_ss + i_:cG_ss + i_ + 1],
                                                           in_=st.h[:, cG_p + 4 * i_:cG_p + 4 * i_ + 4],
                                                           axis=mybir.AxisListType.X, op=ALU.add)))(i),
              r=(st.rng(cG_p + 4 * i, cG_p + 4 * i + 4),), w=(st.rng(cG_ss + i, cG_ss + i + 1),))
            brs = rstd_from_ss(cG_ss + i, cG_rs + i, 1, 1.0 / D_MODEL)
            E("dve", (lambda i_, s_: (lambda e: e.scalar_tensor_tensor(
                out=ff.h[:, i_, :], in0=ff.h[:, i_, :], scalar=st.h[:, cG_rs + i_:cG_rs + i_ + 1], in1=xr[s_].h[:, :],
                op0=ALU.mult, op1=ALU.add)))(i, s), r=(ff.sub(i), brs, xr[s].whole), w=(ff.sub(i),))
            E("sync", (lambda i_: (lambda e: e.dma_start(out=out_d[i_ * 128:(i_ + 1) * 128, :], in_=ff.h[:, i_, :])))(i),
              r=(ff.sub(i),), w=(), key="outw")
        fin = E("sync", None)
        fin.waits.append(("dma", "outw", P.dma_cnt["outw"]))
        assert wcur[0] == len(wpieces), (wcur[0], len(wpieces))


    except _Stop:
        pass

    fin_all = P.op("sync", None)
    for k_, v_ in P.dma_cnt.items():
        fin_all.waits.append(("dma", k_, v_))

    for e_ in ENGS:
        cnt = 0
        for ins in P.q[e_]:
            if ins.signal and not ins.is_dma:
                cnt += 1
                ins.value = cnt
    keys = sorted(P.dma_cnt.keys())
    sem_ctx = {}
    sems_eng = {e_: nc.alloc_semaphore(f"s_{e_}") for e_ in ENGS}
    sems_key = {k: nc.alloc_semaphore(f"d_{k}") for k in keys}

    def replay(ename, eng):
        waited = {}
        for ins in P.q[ename]:
            for w in ins.waits:
                if w[0] == "eng":
                    p = w[1]
                    sem, val = sems_eng[p.eng], p.value
                else:
                    sem, val = sems_key[w[1]], w[2]
                k = id(sem)
                if waited.get(k, 0) < val:
                    eng.wait_ge(sem, val)
                    waited[k] = val
            if ins.fn is None:
                continue
            bi = ins.fn(eng)
            if ins.is_dma:
                bi.then_inc(sems_key[ins.key], 16)
            elif ins.signal:
                bi.then_inc(sems_eng[ename], 1)

    with nc.Block() as block:
        @block.sync
        def _(e):
            replay("sync", e)

        @block.scalar
        def _(e):
            replay("act", e)

        @block.vector
        def _(e):
            replay("dve", e)

        @block.gpsimd
        def _(e):
            replay("pool", e)

        @block.tensor
        def _(e):
            replay("pe", e)

    stats = {e_: len(P.q[e_]) for e_ in ENGS}
    stats["sig"] = {e_: sum(1 for i in P.q[e_] if i.signal and not i.is_dma) for e_ in ENGS}
    return nc, stats


def _blocks_k(w, ncols_per_block):
    K, N = w.shape
    kc = K // 128
    nb = N // ncols_per_block
    a = w.reshape(kc, 128, nb, ncols_per_block).transpose(2, 1, 0, 3)
    return np.ascontiguousarray(a.reshape(nb, 128, kc * ncols_per_block))


def prepare_inputs(x, positions, pre_mix_norm, w_in, q_norm, w_uq, kv_norm, w_ukv, conv_w, conv_b, conv_ln_g,
                   conv_ln_b, conv_out_norm, attn_out_norm, w_out, post_mix_norm, pre_ffn_norm, w_gate, w_up,
                   w_down, post_ffn_norm):
    f = np.float32
    x = np.asarray(x, f)
    positions = np.asarray(positions, np.int32)
    w_in = np.asarray(w_in, f)[0]
    w_uq = np.asarray(w_uq, f)[0]
    w_ukv = np.asarray(w_ukv, f)[0]
    w_out = np.asarray(w_out, f)[0]
    w_gate = np.asarray(w_gate, f)[0]
    w_up = np.asarray(w_up, f)[0]
    w_down = np.asarray(w_down, f)[0]

    c1 = 2 * CONV_CH
    c2 = c1 + Q_LORA
    c3 = c2 + KV_LORA
    cols = []
    cols += list(range(c2, c3))
    cols += list(range(c3, c3 + 64)) + list(range(c3 + 32, c3 + 64)) + list(range(c3, c3 + 32))
    cols += list(range(c1, c2))
    for c in range(8):
        cols += list(range(CONV_CH + c * 128, CONV_CH + (c + 1) * 128))
        cols += list(range(c * 128, (c + 1) * 128))
    w_in_p = _blocks_k(w_in[:, cols], 128)
    cols = []
    for h in range(N_HEADS):
        b0 = h * 192
        cols += list(range(b0, b0 + 128))
        cols += list(range(b0 + 128, b0 + 192)) + list(range(b0 + 160, b0 + 192)) + list(range(b0 + 128, b0 + 160))
    w_uq_p = _blocks_k(w_uq[:, cols], 128)
    kcols = []
    vcols = []
    for h in range(N_HEADS):
        kcols += list(range(h * 256, h * 256 + 128))
        vcols += list(range(h * 256 + 128, h * 256 + 256))
    w_uk_p = _blocks_k(w_ukv[:, kcols], 128)
    wv = w_ukv[:, vcols]
    w_uv_p = np.ascontiguousarray(wv.reshape(2, 2, 128, 1024).transpose(0, 2, 1, 3).reshape(2, 128, 2048))
    w_out_p = np.ascontiguousarray(w_out.reshape(16, 128, 2048).transpose(1, 0, 2).reshape(128, 16 * 2048))
    g_p = _blocks_k(w_gate, 128)
    u_p = _blocks_k(w_up, 128)
    w_gu_p = np.ascontiguousarray(np.stack([g_p, u_p], axis=1).reshape(88, 128, 2048))
    wd = w_down.reshape(11, 4, 128, 4, 512)
    w_dn_p = np.ascontiguousarray(wd.transpose(3, 0, 2, 1, 4).reshape(44, 128, 2048))

    c_bf = np.zeros((128, 512), np.float32)
    c_bf[:, 0:128] = np.eye(128)
    c_bf[:, 128:256] = 1.0
    e64 = np.eye(64)
    c_bf[:, 256:384] = np.block([[e64, e64], [e64, e64]])
    kk = np.arange(128)[:, None]
    qq = np.arange(128)[None, :]
    c_bf[:, 384:512] = (qq >= kk).astype(np.float32)
    c_bf = c_bf.astype(ml_dtypes.bfloat16)

    cvec = np.zeros((128, NCV), f)
    cw = np.asarray(conv_w, f)[0]
    for c in range(8):
        cvec[:, CW + c * 31:CW + (c + 1) * 31] = cw[:, c * 128:(c + 1) * 128].T
    cvec[:, CB:CB + 8] = np.asarray(conv_b, f)[0].reshape(8, 128).T
    cvec[:, LG:LG + 8] = np.asarray(conv_ln_g, f)[0].reshape(8, 128).T
    cvec[:, LB:LB + 8] = np.asarray(conv_ln_b, f)[0].reshape(8, 128).T
    cvec[:, G2:G2 + 8] = np.asarray(conv_out_norm, f)[0].reshape(8, 128).T
    cvec[:, GQ:GQ + 6] = np.asarray(q_norm, f)[0].reshape(6, 128).T
    cvec[:, GKV:GKV + 4] = np.asarray(kv_norm, f)[0].reshape(4, 128).T
    inv_freq = (np.float32(10000.0) ** (-np.arange(0, 64, 2, dtype=np.float32) / np.float32(64))).astype(f)
    cvec[:, IFQ] = np.tile(inv_freq, 4)
    cvec[0:64, PHS] = np.float32(np.pi / 2)
    cvec[:, SGN] = 1.0
    cvec[64:96, SGN] = -1.0

    gbc = np.zeros((5, 128, 2048), f)
    gbc[0] = np.asarray(pre_mix_norm, f)[0][None, :]
    gbc[1] = np.asarray(post_mix_norm, f)[0][None, :]
    gbc[2] = np.asarray(pre_ffn_norm, f)[0][None, :]
    gbc[3] = np.asarray(post_ffn_norm, f)[0][None, :]
    gbc[4, :, 0:1024] = np.asarray(attn_out_norm, f)[0][None, :]

    shared = dict(c_bf=c_bf, gbc=gbc, w_in_p=w_in_p, w_uq_p=w_uq_p, w_uk_p=w_uk_p, w_uv_p=w_uv_p,
                  w_out_p=w_out_p, w_gu_p=w_gu_p, w_dn_p=w_dn_p)
    in_maps = []
    for core in range(8):
        b, half = core // 2, core % 2
        m = dict(shared)
        m["x_own"] = np.ascontiguousarray(x[b, half * TOWN:(half + 1) * TOWN])
        cvc = cvec.copy()
        pos = np.zeros((2048,), np.int32)
        if half == 1:
            m["x_prev"] = np.ascontiguousarray(x[b, 0:TOWN])
            pos[:] = positions[b, 0:2048]
            cvc[:, PFL] = 1.0
        else:
            m["x_prev"] = np.zeros((TOWN, D_MODEL), f)
            pos[1024:] = positions[b, 0:1024]
            cvc[:, PFL] = 0.0
        m["cvec"] = cvc
        m["pos_bc"] = np.ascontiguousarray(np.broadcast_to(pos[None, :], (128, 2048)))
        in_maps.append(m)
    return in_maps


_CACHE = {}


def kernel(**inputs):
    if "nc" not in _CACHE:
        _CACHE["nc"], _CACHE["stats"] = build_program()
    nc = _CACHE["nc"]
    in_maps = prepare_inputs(**inputs)
    res = run_bass_kernel_spmd(nc, in_maps, core_ids=list(range(8)))
    out = np.zeros((BATCH, SEQ, D_MODEL), np.float32)
    for core in range(8):
        b, half = core // 2, core % 2
        out[b, half * TOWN:(half + 1) * TOWN] = res.results[core]["out"]
    return out
```

```python
import os
import numpy as np
import ml_dtypes
import concourse.bass as bass
import concourse.mybir as mybir
from concourse.bass_utils import run_bass_kernel_spmd

F32 = mybir.dt.float32
BF16 = mybir.dt.bfloat16
I32 = mybir.dt.int32
AF = mybir.ActivationFunctionType
ALU = mybir.AluOpType
PI = float(np.pi)

D_MODEL = 2048
SEQ = 2048
BATCH = 4
TOWN = 1024
CONV_CH = 1024
CONV_K = 31
N_HEADS = 8
Q_LORA = 768
KV_LORA = 512
D_FF = 5632
NFB = D_FF // 128
EPS = 1e-6
SCALE = 192 ** -0.5

CW = 0
CB = 248
LG = 256
LB = 264
G2 = 272
GQ = 280
GKV = 286
IFQ = 290
PHS = 291
SGN = 292
PFL = 293
NCV = 320

KB = 1024


class Buf:
    __slots__ = ("space", "lo", "hi", "name")

    def __init__(self, space, lo, hi, name=""):
        self.space, self.lo, self.hi, self.name = space, lo, hi, name


class Ins:
    __slots__ = ("eng", "fn", "is_dma", "key", "signal", "value", "waits", "dma_val", "idx")

    def __init__(self, eng, fn, is_dma, key):
        self.eng, self.fn, self.is_dma, self.key = eng, fn, is_dma, key
        self.idx = -1
        self.signal = False
        self.value = None
        self.waits = []
        self.dma_val = None


ENGS = ["sync", "act", "dve", "pool", "pe"]


class Prog:
    def __init__(self):
        self.q = {e: [] for e in ENGS}
        self.wr = {"sb": [], "ps": []}
        self.rd = {"sb": [], "ps": []}
        self.dma_cnt = {}

    def buf(self, space, lo, hi, name=""):
        return Buf(space, lo, hi, name)

    def op(self, eng, fn, reads=(), writes=(), dma_key=None):
        ins = Ins(eng, fn, dma_key is not None, dma_key)
        ins.idx = len(self.q[eng])
        need_eng = {}
        need_dma = {}

        def add(p):
            if p is ins:
                return
            if p.is_dma:
                need_dma[p.key] = 1
                return
            if p.eng == eng and not ins.is_dma and eng == "pe":
                return
            cur = need_eng.get(p.eng)
            if cur is None or cur.idx < p.idx:
                need_eng[p.eng] = p

        for b in reads:
            for w in self.wr[b.space]:
                if w[0] < b.hi and b.lo < w[1]:
                    add(w[2])
            if b.space == "ps" and eng != "pe":
                lo = b.lo // 2048 * 2048
                hi = -(-b.hi // 2048) * 2048
                for r in self.rd["ps"]:
                    if r[0] < hi and lo < r[1] and r[3].eng != eng and r[3].eng != "pe":
                        add(r[3])
        for b in writes:
            for w in self.wr[b.space]:
                if w[0] < b.hi and b.lo < w[1]:
                    add(w[2])
            for r in self.rd[b.space]:
                if r[0] < b.hi and b.lo < r[1]:
                    add(r[3])
        for p in need_eng.values():
            p.signal = True
            ins.waits.append(("eng", p))
        for k in need_dma:
            ins.waits.append(("dma", k, self.dma_cnt[k]))
        if dma_key is not None:
            self.dma_cnt[dma_key] = self.dma_cnt.get(dma_key, 0) + 16
            ins.dma_val = self.dma_cnt[dma_key]
        for b in writes:
            sp = b.space
            self.wr[sp] = [w for w in self.wr[sp] if not (b.lo <= w[0] and w[1] <= b.hi)]
            self.wr[sp].append([b.lo, b.hi, ins])
            self.rd[sp] = [r for r in self.rd[sp] if not (b.lo <= r[0] and r[1] <= b.hi)]
        rk = ("dma", dma_key) if ins.is_dma else eng
        for b in reads:
            lst = self.rd[b.space]
            for r in lst:
                if r[0] == b.lo and r[1] == b.hi and r[2] == rk:
                    r[3] = ins
                    break
            else:
                lst.append([b.lo, b.hi, rk, ins])
        self.q[eng].append(ins)
        return ins


def build_program(stop_after=None, tensors=None):
    nc = bass.Bass("TRN2", target_bir_lowering=False)
    P = Prog()

    def din(name, shape, dt):
        return nc.dram_tensor(name, list(shape), dt, kind="ExternalInput").ap()

    x_own = din("x_own", [TOWN, D_MODEL], F32)
    x_prev = din("x_prev", [TOWN, D_MODEL], F32)
    pos_bc = din("pos_bc", [128, 2048], I32)
    c_bf = din("c_bf", [128, 512], BF16)
    cvec_d = din("cvec", [128, NCV], F32)
    gbc_d = din("gbc", [5, 128, 2048], F32)
    w_in_p = din("w_in_p", [27, 128, 2048], F32)
    w_uq_p = din("w_uq_p", [16, 128, 768], F32)
    w_uk_p = din("w_uk_p", [8, 128, 512], F32)
    w_uv_p = din("w_uv_p", [2, 128, 2048], F32)
    w_out_p = din("w_out_p", [128, 16 * 2048], F32)
    w_gu_p = din("w_gu_p", [88, 128, 2048], F32)
    w_dn_p = din("w_dn_p", [44, 128, 2048], F32)
    out_d = nc.dram_tensor("out", [TOWN, D_MODEL], F32, kind="ExternalOutput").ap()
    x1_d = nc.dram_tensor("x1_scratch", [TOWN, D_MODEL], F32).ap()
    dbg = {}

    base = (nc.sbuf_base + 63) // 64 * 64
    CONST0 = base
    WR0 = CONST0 + 4 * KB
    AA0 = WR0 + 24 * KB
    assert AA0 + 178 * KB <= nc.sbuf_top, (AA0 + 178 * KB, nc.sbuf_top)

    def dsz(dt):
        return 4 if dt in (F32, I32) else 2

    class T:
        def __init__(self, name, shape, dt, off):
            self.h = nc.alloc_sbuf_tensor_at(name, list(shape), dt, offset=off)
            self.off = off
            self.shape = shape
            self.dt = dt
            self.nbytes = int(np.prod(shape[1:])) * dsz(dt)
            self.whole = P.buf("sb", off, off + self.nbytes, name)
            self._subs = {}
            if tensors is not None:
                tensors[name] = self

        def sub(self, i, n=None):
            key = (i, n)
            if key not in self._subs:
                slab = self.nbytes // self.shape[1]
                cnt = 1 if n is None else n
                self._subs[key] = P.buf("sb", self.off + i * slab, self.off + (i + cnt) * slab)
            return self._subs[key]

        def rng(self, lo_el, hi_el):
            key = ("r", lo_el, hi_el)
            if key not in self._subs:
                self._subs[key] = P.buf("sb", self.off + lo_el * dsz(self.dt), self.off + hi_el * dsz(self.dt))
            return self._subs[key]

    _names = [0]

    def sbt(shape, dt, off, name=None):
        _names[0] += 1
        return T(name or f"t{_names[0]}", shape, dt, off)

    cb = sbt([128, 512], BF16, CONST0, "cb")
    ident = cb.h[:, 0:128]
    ones_b = cb.h[:, 128:256]
    f2 = cb.h[:, 256:384]
    tri = cb.h[:, 384:512]
    cv = sbt([128, NCV], F32, CONST0 + 1024, "cv")
    st = sbt([128, 320], F32, CONST0 + 1024 + NCV * 4, "st")
    onesf = sbt([128, 64], F32, CONST0 + 1024 + NCV * 4 + 1280, "onesf")
    assert CONST0 + 1024 + NCV * 4 + 1280 + 256 <= WR0

    def cvc(col):
        return cv.h[:, col:col + 1]

    _stn = [0]

    def stcol(n=1):
        c = _stn[0]
        _stn[0] += n
        assert _stn[0] <= 320
        return c

    NSLOT = 6
    wslots = [sbt([128, 2048], BF16, WR0 + i * 4 * KB, f"ws{i}") for i in range(NSLOT)]
    wpieces = []
    wstate = {"issued": 0}

    psb = [nc.alloc_psum_tensor(f"psb{i}", [128, 512], F32) for i in range(8)]
    psbuf = [P.buf("ps", i * 2048, (i + 1) * 2048, f"ps{i}") for i in range(8)]
    psbf = [psb[i][:, :].bitcast(BF16) for i in range(8)]
    _bank = [0]

    def nbank():
        b = _bank[0] % 8
        _bank[0] += 1
        return b

    _psub = {}

    def psub(bank, lo, hi):
        k = (bank, lo, hi)
        if k not in _psub:
            _psub[k] = P.buf("ps", bank * 2048 + lo * 4, bank * 2048 + hi * 4)
        return _psub[k]

    def E(eng, fn, r=(), w=(), key=None):
        return P.op(eng, fn, r, w, key)

    def mm(out, lhsT, rhs, start, stop, r, w):
        return E("pe", lambda e: e.matmul(out, lhsT=lhsT, rhs=rhs, start=start, stop=stop), r, w)

    def issue_weights(upto):
        while wstate["issued"] < min(upto, len(wpieces)):
            i = wstate["issued"]
            src, n = wpieces[i]
            slot = wslots[i % NSLOT]
            E("pool", (lambda s, n_, sl: (lambda e: e.dma_start(out=sl.h[:, 0:n_], in_=s)))(src, n, slot),
              r=(), w=(slot.whole,), key=f"ws{i % NSLOT}")
            wstate["issued"] += 1

    wcur = [0]

    def next_w(prefetch=4):
        i = wcur[0]
        wcur[0] += 1
        issue_weights(i + 1 + prefetch)
        return wslots[i % NSLOT]

    for j in range(27):
        wpieces.append((w_in_p[j], 2048))
    for j in range(16):
        wpieces.append((w_uq_p[j], 768))
    for j in range(8):
        wpieces.append((w_uk_p[j], 512))
    for j in range(2):
        wpieces.append((w_uv_p[j], 2048))
    for j in range(88):
        wpieces.append((w_gu_p[j], 2048))
    for j in range(44):
        wpieces.append((w_dn_p[j], 2048))

    class _Stop(Exception):
        pass

    def ckpt(name):
        if stop_after == name:
            raise _Stop()

    try:
        E("sync", lambda e: e.dma_start(out=cb.h[:, :], in_=c_bf), w=(cb.whole,), key="const")
        E("sync", lambda e: e.dma_start(out=cv.h[:, :], in_=cvec_d), w=(cv.whole,), key="const")
        E("dve", lambda e: e.memset(st.h[:, :], 0.0), w=(st.whole,))
        E("dve", lambda e: e.memset(onesf.h[:, :], 1.0), w=(onesf.whole,))

        def rstd_from_ss(c_ss, c_out, n, inv_n):
            c_ms = stcol(n)
            c_sq = stcol(n)
            bs = st.rng(c_ss, c_ss + n)
            bm = st.rng(c_ms, c_ms + n)
            bq = st.rng(c_sq, c_sq + n)
            bo = st.rng(c_out, c_out + n)
            E("dve", lambda e: e.tensor_scalar(out=st.h[:, c_ms:c_ms + n], in0=st.h[:, c_ss:c_ss + n], scalar1=inv_n,
                                               scalar2=EPS, op0=ALU.mult, op1=ALU.add), r=(bs,), w=(bm,))
            E("act", lambda e: e.activation(out=st.h[:, c_sq:c_sq + n], in_=st.h[:, c_ms:c_ms + n], func=AF.Sqrt),
              r=(bm,), w=(bq,))
            E("dve", lambda e: e.reciprocal(out=st.h[:, c_out:c_out + n], in_=st.h[:, c_sq:c_sq + n]), r=(bq,), w=(bo,))
            return bo

        def rstd_psum_inplace(bank, n, inv_n):
            b = psbuf[bank]
            ap = psb[bank][:, 0:n]
            E("dve", lambda e: e.tensor_scalar(out=ap, in0=ap, scalar1=inv_n, scalar2=EPS, op0=ALU.mult, op1=ALU.add),
              r=(b,), w=(b,))
            E("act", lambda e: e.activation(out=ap, in_=ap, func=AF.Sqrt), r=(b,), w=(b,))
            E("dve", lambda e: e.reciprocal(out=ap, in_=ap), r=(b,), w=(b,))

        def nt_stats(src_ap, src_buf, junk_t, width, c_ss, c_rs):
            bs = st.rng(c_ss, c_ss + 1)
            E("act", lambda e: e.activation(out=junk_t.h[:, 0:width], in_=src_ap, func=AF.Square,
                                            accum_out=st.h[:, c_ss:c_ss + 1]), r=(src_buf,), w=(junk_t.whole, bs))
            return rstd_from_ss(c_ss, c_rs, 1, 1.0 / width)

        def nt_apply_a(src_ap, src_buf, brs, gb_t, xn_t, width, c_rs, banks=None):
            nchunk = width // 128
            E("dve", lambda e: e.scalar_tensor_tensor(out=xn_t.h[:, 0:width], in0=src_ap,
                                                      scalar=st.h[:, c_rs:c_rs + 1], in1=gb_t.h[:, 0:width],
                                                      op0=ALU.mult, op1=ALU.mult),
              r=(src_buf, brs, gb_t.whole), w=(xn_t.whole,))
            used = []
            for g in range(nchunk // 8):
                bk = nbank() if banks is None else banks[g]
                used.append(bk)
                for c8 in range(8):
                    c = g * 8 + c8
                    E("pe", (lambda bk_, c8_, c_: (lambda e: e.transpose(psbf[bk_][:, c8_ * 128:(c8_ + 1) * 128],
                                                                       xn_t.h[:, c_ * 128:(c_ + 1) * 128], ident)))(bk, c8, c),
                      r=(xn_t.whole, cb.whole), w=(psbuf[bk],))
            return used

        def nt_apply_b(used, dst_fn, dst_bufs, evac_engs):
            for g, bk in enumerate(used):
                eng = evac_engs[g % len(evac_engs)]
                src = psbf[bk][:, 0:1024].rearrange("p (a b) -> p a b", a=8)
                dst = dst_fn(g)
                wb = dst_bufs(g) if callable(dst_bufs) else dst_bufs
                if eng == "act":
                    E("act", (lambda d_, s_: (lambda e: e.copy(out=d_, in_=s_)))(dst, src), r=(psbuf[bk],), w=wb)
                else:
                    E("dve", (lambda d_, s_: (lambda e: e.tensor_copy(out=d_, in_=s_)))(dst, src), r=(psbuf[bk],), w=wb)

        def nt_apply(src_ap, src_buf, brs, gb_t, xn_t, width, c_rs, dst_fn, dst_bufs, evac_engs, banks=None):
            used = nt_apply_a(src_ap, src_buf, brs, gb_t, xn_t, width, c_rs, banks)
            nt_apply_b(used, dst_fn, dst_bufs, evac_engs)

        hT = sbt([128, 16, 2048], BF16, AA0 + 0, "hT")
        xa = [sbt([128, 2048], F32, AA0 + (64 + 8 * i) * KB, f"xa{i}") for i in range(4)]
        gbcA = sbt([128, 2048], F32, AA0 + 96 * KB, "gbcA")
        xnA = [sbt([128, 2048], BF16, AA0 + (104 + 4 * i) * KB, f"xnA{i}") for i in range(3)]
        junkA = [sbt([128, 2048], BF16, AA0 + (116 + 4 * i) * KB, f"junkA{i}") for i in range(4)]
        issue_weights(NSLOT)
        E("sync", lambda e: e.dma_start(out=gbcA.h[:, :], in_=gbc_d[0]), w=(gbcA.whole,), key="gbc")
        def hT_rd(kc, a0, a1):
            return hT.rng(kc * 2048 + a0, kc * 2048 + a1)

        def hT_wr(t, g):
            return tuple(hT.rng(kc * 2048 + t * 128, kc * 2048 + (t + 1) * 128) for kc in range(g * 8, g * 8 + 8))

        cA_ss = stcol(16)
        cA_rs = stcol(16)
        brsA = {}

        def A1(t):
            s_ = t % 4
            src = x_prev[t * 128:(t + 1) * 128, :] if t < 8 else x_own[(t - 8) * 128:(t - 7) * 128, :]
            E("sync", (lambda s__, src_: (lambda e: e.dma_start(out=xa[s__].h[:, :], in_=src_)))(s_, src),
              w=(xa[s_].whole,), key=f"xa{s_}")
            brsA[t] = nt_stats(xa[s_].h[:, :], xa[s_].whole, junkA[t % 4], 2048, cA_ss + t, cA_rs + t)

        usedA = {}

        def A2a(t):
            s_ = t % 4
            usedA[t] = nt_apply_a(xa[s_].h[:, :], xa[s_].whole, brsA[t], gbcA, xnA[t % 3], 2048, cA_rs + t)

        def A2b(t):
            nt_apply_b(usedA[t], (lambda g: hT.h[:, g * 8:(g + 1) * 8, t * 128:(t + 1) * 128]),
                       (lambda g: hT_wr(t, g)), ["act", "dve"])

        A1(0)
        A1(1)
        A1(2)
        A2a(0)
        for t in range(16):
            if t + 1 < 16:
                A2a(t + 1)
            if t + 3 < 16:
                A1(t + 3)
            A2b(t)

        ckpt('A')
        zz = sbt([128, 4, 2048], F32, AA0 + 64 * KB, "zz")
        zq = sbt([128, 6, 1024], F32, AA0 + 64 * KB, "zq")
        sqz = sbt([128, 4, 2048], BF16, AA0 + 96 * KB, "sqz")
        sqq = sbt([128, 6, 1024], BF16, AA0 + 96 * KB, "sqq")
        t_k = sbt([128, 2048], BF16, AA0 + 112 * KB, "t_k")
        kvn = sbt([128, 4, 2048], BF16, AA0 + 116 * KB, "kvn")
        qln = sbt([128, 6, 1024], BF16, AA0 + 132 * KB, "qln")
        u_bf = sbt([128, 8, 1152], BF16, AA0 + 148 * KB, "u_bf")
        kr2 = sbt([128, 2048], BF16, AA0 + 166 * KB, "kr2")
        sg = [sbt([128, 1152], F32, AA0 + 64 * KB + i * 4608, f"sg{i}") for i in range(2)]

        CS = sbt([128, 2048], F32, AA0 + 170 * KB, "CS")

        def emit_rope():
            posi = sbt([128, 2048], I32, AA0 + 132 * KB, "posi")
            posf = sbt([128, 2048], F32, AA0 + 140 * KB, "posf")
            tmpa = sbt([128, 2048], F32, AA0 + 148 * KB, "tmpa")
            E("sync", lambda e: e.dma_start(out=posi.h[:, :], in_=pos_bc), w=(posi.whole,), key="pos")
            E("dve", lambda e: e.tensor_copy(out=posf.h[:, :], in_=posi.h[:, :]), r=(posi.whole,), w=(posf.whole,))
            E("dve", lambda e: e.tensor_scalar(out=CS.h[:, :], in0=posf.h[:, :], scalar1=cvc(IFQ), scalar2=cvc(PHS),
                                               op0=ALU.mult, op1=ALU.add), r=(posf.whole, cv.whole), w=(CS.whole,))
            E("dve", lambda e: e.tensor_scalar(out=tmpa.h[:, :], in0=CS.h[:, :], scalar1=1.0 / (2 * PI), scalar2=None,
                                               op0=ALU.mult), r=(CS.whole,), w=(tmpa.whole,))
            E("dve", lambda e: e.tensor_copy(out=posi.h[:, :], in_=tmpa.h[:, :]), r=(tmpa.whole,), w=(posi.whole,))
            E("dve", lambda e: e.tensor_copy(out=posf.h[:, :], in_=posi.h[:, :]), r=(posi.whole,), w=(posf.whole,))
            C1 = 6.28125
            C2 = 2 * PI - C1
            E("dve", lambda e: e.scalar_tensor_tensor(out=CS.h[:, :], in0=posf.h[:, :], scalar=-C1, in1=CS.h[:, :],
                                                      op0=ALU.mult, op1=ALU.add), r=(posf.whole, CS.whole), w=(CS.whole,))
            E("dve", lambda e: e.scalar_tensor_tensor(out=CS.h[:, :], in0=posf.h[:, :], scalar=-C2, in1=CS.h[:, :],
                                                      op0=ALU.mult, op1=ALU.add), r=(posf.whole, CS.whole), w=(CS.whole,))
            E("dve", lambda e: e.tensor_single_scalar(out=tmpa.h[:, :], in_=CS.h[:, :], scalar=PI, op=ALU.is_gt),
              r=(CS.whole,), w=(tmpa.whole,))
            E("dve", lambda e: e.scalar_tensor_tensor(out=CS.h[:, :], in0=tmpa.h[:, :], scalar=-2 * PI, in1=CS.h[:, :],
                                                      op0=ALU.mult, op1=ALU.add), r=(tmpa.whole, CS.whole), w=(CS.whole,))
            E("dve", lambda e: e.tensor_scalar(out=CS.h[:, :], in0=CS.h[:, :], scalar1=-PI, scalar2=PI,
                                               op0=ALU.max, op1=ALU.min), r=(CS.whole,), w=(CS.whole,))
            E("act", lambda e: e.activation(out=CS.h[:, :], in_=CS.h[:, :], func=AF.Sin), r=(CS.whole,), w=(CS.whole,))
            E("dve", lambda e: e.tensor_scalar(out=CS.h[:, :], in0=CS.h[:, :], scalar1=cvc(SGN), scalar2=None,
                                               op0=ALU.mult), r=(CS.whole, cv.whole), w=(CS.whole,))


        for b in range(4):
            if b == 1:
                emit_rope()
            ws = next_w()
            banks = [nbank() for _ in range(4)]
            for kc in range(16):
                for n in range(4):
                    mm(psb[banks[n]][:, :], ws.h[:, kc * 128:(kc + 1) * 128], hT.h[:, kc, n * 512:(n + 1) * 512],
                       kc == 0, kc == 15, (ws.whole, hT_rd(kc, n * 512, (n + 1) * 512)), (psbuf[banks[n]],))
            for n in range(4):
                if os.environ.get("KDBG") == "noevac":
                    break
                bk = banks[n]
                E("dve", (lambda bk_, b_, n_: (lambda e: e.tensor_scalar(
                    out=zz.h[:, b_, n_ * 512:(n_ + 1) * 512], in0=psb[bk_][:, :], scalar1=cvc(GKV + b_), scalar2=None,
                    op0=ALU.mult)))(bk, b, n), r=(psbuf[bk], cv.whole), w=(zz.sub(b),))
                if os.environ.get("KDBG") == "noact":
                    continue
                E("act", (lambda bk_, b_, n_: (lambda e: e.activation(
                    out=sqz.h[:, b_, n_ * 512:(n_ + 1) * 512], in_=psb[bk_][:, :], func=AF.Square)))(bk, b, n),
                  r=(psbuf[bk],), w=(sqz.sub(b),) + ((psbuf[bk],) if os.environ.get("KDBG") == "serial" else ()))
        ckpt('B1')
        for n in range(4):
            bk = nbank()
            for b in range(4):
                mm(psb[bk][:, :], ones_b, sqz.h[:, b, n * 512:(n + 1) * 512], b == 0, b == 3,
                   (cb.whole, sqz.sub(b)), (psbuf[bk],))
            rstd_psum_inplace(bk, 512, 1.0 / KV_LORA)
            for b in range(4):
                E("dve", (lambda bk_, b_, n_: (lambda e: e.tensor_tensor(
                    out=kvn.h[:, b_, n_ * 512:(n_ + 1) * 512], in0=zz.h[:, b_, n_ * 512:(n_ + 1) * 512],
                    in1=psb[bk_][:, :], op=ALU.mult)))(bk, b, n), r=(zz.sub(b), psbuf[bk]), w=(kvn.sub(b),))
        ckpt('B2')
        ws = next_w()
        banks = [nbank() for _ in range(4)]
        for kc in range(16):
            for n in range(4):
                mm(psb[banks[n]][:, :], ws.h[:, kc * 128:(kc + 1) * 128], hT.h[:, kc, n * 512:(n + 1) * 512],
                   kc == 0, kc == 15, (ws.whole, hT_rd(kc, n * 512, (n + 1) * 512)), (psbuf[banks[n]],))
        for n in range(4):
            bk = banks[n]
            E("dve", (lambda bk_, n_: (lambda e: e.tensor_tensor(
                out=t_k.h[:, n_ * 512:(n_ + 1) * 512], in0=psb[bk_][:, :], in1=CS.h[:, n_ * 512:(n_ + 1) * 512],
                op=ALU.mult)))(bk, n), r=(psbuf[bk], CS.whole), w=(t_k.rng(n * 512, (n + 1) * 512),))
            bk2 = nbank()
            mm(psb[bk2][:, :], f2, t_k.h[:, n * 512:(n + 1) * 512], True, True,
               (cb.whole, t_k.rng(n * 512, (n + 1) * 512)), (psbuf[bk2],))
            E("act", (lambda bk_, n_: (lambda e: e.copy(out=kr2.h[:, n_ * 512:(n_ + 1) * 512], in_=psb[bk_][:, :])))(bk2, n),
              r=(psbuf[bk2],), w=(kr2.rng(n * 512, (n + 1) * 512),))
        ckpt('B3')
        for b in range(6):
            ws = next_w()
            banks = [nbank() for _ in range(2)]
            for kc in range(16):
                for n in range(2):
                    mm(psb[banks[n]][:, :], ws.h[:, kc * 128:(kc + 1) * 128],
                       hT.h[:, kc, 1024 + n * 512:1024 + (n + 1) * 512],
                       kc == 0, kc == 15, (ws.whole, hT_rd(kc, 1024 + n * 512, 1024 + (n + 1) * 512)), (psbuf[banks[n]],))
            for n in range(2):
                bk = banks[n]
                E("dve", (lambda bk_, b_, n_: (lambda e: e.tensor_scalar(
                    out=zq.h[:, b_, n_ * 512:(n_ + 1) * 512], in0=psb[bk_][:, :], scalar1=cvc(GQ + b_), scalar2=None,
                    op0=ALU.mult)))(bk, b, n), r=(psbuf[bk], cv.whole), w=(zq.sub(b),))
                E("act", (lambda bk_, b_, n_: (lambda e: e.activation(
                    out=sqq.h[:, b_, n_ * 512:(n_ + 1) * 512], in_=psb[bk_][:, :], func=AF.Square)))(bk, b, n),
                  r=(psbuf[bk],), w=(sqq.sub(b),))
        for n in range(2):
            bk = nbank()
            for b in range(6):
                mm(psb[bk][:, :], ones_b, sqq.h[:, b, n * 512:(n + 1) * 512], b == 0, b == 5,
                   (cb.whole, sqq.sub(b)), (psbuf[bk],))
            rstd_psum_inplace(bk, 512, 1.0 / Q_LORA)
            for b in range(6):
                E("dve", (lambda bk_, b_, n_: (lambda e: e.tensor_tensor(
                    out=qln.h[:, b_, n_ * 512:(n_ + 1) * 512], in0=zq.h[:, b_, n_ * 512:(n_ + 1) * 512],
                    in1=psb[bk_][:, :], op=ALU.mult)))(bk, b, n), r=(zq.sub(b), psbuf[bk]), w=(qln.sub(b),))
        ckpt('B4')
        tokr = [(896, 1024), (1024, 1536), (1536, 2048)]
        for c in range(8):
            sgt = sg[c % 2]
            ws = next_w()
            banks = [nbank() for _ in range(3)]
            for kc in range(16):
                for n, (a0, a1) in enumerate(tokr):
                    mm(psb[banks[n]][:, 0:a1 - a0], ws.h[:, kc * 128:(kc + 1) * 128], hT.h[:, kc, a0:a1],
                       kc == 0, kc == 15, (ws.whole, hT_rd(kc, a0, a1)), (psbuf[banks[n]],))
            for n, (a0, a1) in enumerate(tokr):
                E("act", (lambda bk_, a0_, a1_, sg_: (lambda e: e.activation(
                    out=sg_.h[:, a0_ - 896:a1_ - 896], in_=psb[bk_][:, 0:a1_ - a0_], func=AF.Sigmoid)))(banks[n], a0, a1, sgt),
                  r=(psbuf[banks[n]],), w=(sgt.whole,))
            ws = next_w()
            banks = [nbank() for _ in range(3)]
            for kc in range(16):
                for n, (a0, a1) in enumerate(tokr):
                    mm(psb[banks[n]][:, 0:a1 - a0], ws.h[:, kc * 128:(kc + 1) * 128], hT.h[:, kc, a0:a1],
                       kc == 0, kc == 15, (ws.whole, hT_rd(kc, a0, a1)), (psbuf[banks[n]],))
            for n, (a0, a1) in enumerate(tokr):
                E("dve", (lambda bk_, a0_, a1_, sg_, c_: (lambda e: e.tensor_tensor(
                    out=u_bf.h[:, c_, a0_ - 896:a1_ - 896], in0=psb[bk_][:, 0:a1_ - a0_], in1=sg_.h[:, a0_ - 896:a1_ - 896],
                    op=ALU.mult)))(banks[n], a0, a1, sgt, c), r=(psbuf[banks[n]], sgt.whole), w=(u_bf.sub(c),))

        ckpt('B')
        yv = sbt([128, 8, 1024], F32, AA0 + 0, "yv")
        Dr = [sbt([128, 31, 128], BF16, AA0 + (32 + 8 * i) * KB, f"Dr{i}") for i in range(2)]
        sqs = [sbt([128, 512], BF16, AA0 + 48 * KB + i * 1024, f"sqs{i}") for i in range(4)]
        bc0 = sbt([128, 1024], F32, AA0 + 52 * KB, "bc0")
        bc1 = sbt([128, 1024], F32, AA0 + 56 * KB, "bc1")
        mixT = sbt([128, 16, 1024], BF16, AA0 + 64 * KB, "mixT")
        mixT_attn = P.buf("sb", mixT.off + 8 * 2048, mixT.off + 16 * 2048, "mixT_attn")
        bS1 = [nbank(), nbank()]
        bS2 = [nbank(), nbank()]
        nsq = [0]
        for c in range(8):
            dr = Dr[c % 2]
            for k in range(CONV_K):
                E("dve", (lambda dr_, k_, c_: (lambda e: e.tensor_scalar(
                    out=dr_.h[:, k_, :], in0=ident, scalar1=cvc(CW + c_ * 31 + k_), scalar2=None, op0=ALU.mult)))(dr, k, c),
                  r=(cb.whole, cv.whole), w=(dr.sub(k),))
            for n in range(2):
                bk = nbank()
                while bk in bS1 or bk in bS2:
                    bk = nbank()
                for k in range(CONV_K):
                    mm(psb[bk][:, :], dr.h[:, k, :], u_bf.h[:, c, 98 + k + n * 512:98 + k + (n + 1) * 512],
                       k == 0, k == CONV_K - 1, (dr.sub(k), u_bf.sub(c)), (psbuf[bk],))
                yb = yv.rng(c * 1024 + n * 512, c * 1024 + (n + 1) * 512)
                E("act", (lambda bk_, c_, n_: (lambda e: e.activation(
                    out=yv.h[:, c_, n_ * 512:(n_ + 1) * 512], in_=psb[bk_][:, :], func=AF.Identity,
                    bias=cvc(CB + c_))))(bk, c, n), r=(psbuf[bk], cv.whole), w=(yb,))
                s1 = sqs[nsq[0] % 4]
                nsq[0] += 1
                s2 = sqs[nsq[0] % 4]
                nsq[0] += 1
                E("act", (lambda bk_, c_, s_: (lambda e: e.activation(
                    out=s_.h[:, :], in_=psb[bk_][:, :], func=AF.Identity, bias=cvc(CB + c_))))(bk, c, s1),
                  r=(psbuf[bk], cv.whole), w=(s1.whole,))
                E("act", (lambda bk_, c_, s_: (lambda e: e.activation(
                    out=s_.h[:, :], in_=psb[bk_][:, :], func=AF.Square, bias=cvc(CB + c_))))(bk, c, s2),
                  r=(psbuf[bk], cv.whole), w=(s2.whole,))
                mm(psb[bS1[n]][:, :], ones_b, s1.h[:, :], c == 0, c == 7, (cb.whole, s1.whole), (psbuf[bS1[n]],))
                mm(psb[bS2[n]][:, :], ones_b, s2.h[:, :], c == 0, c == 7, (cb.whole, s2.whole), (psbuf[bS2[n]],))
        for n in range(2):
            sl = slice(n * 512, (n + 1) * 512)
            b0 = bc0.rng(n * 512, (n + 1) * 512)
            b1 = bc1.rng(n * 512, (n + 1) * 512)
            E("dve", (lambda n_, sl_: (lambda e: e.tensor_scalar(out=bc0.h[:, sl_], in0=psb[bS1[n_]][:, :],
                                                                scalar1=1.0 / CONV_CH, scalar2=None, op0=ALU.mult)))(n, sl),
              r=(psbuf[bS1[n]],), w=(b0,))
            E("dve", (lambda sl_: (lambda e: e.tensor_tensor(out=bc1.h[:, sl_], in0=bc0.h[:, sl_], in1=bc0.h[:, sl_],
                                                            op=ALU.mult)))(sl), r=(b0,), w=(b1,))
            E("dve", (lambda n_, sl_: (lambda e: e.scalar_tensor_tensor(out=bc1.h[:, sl_], in0=psb[bS2[n_]][:, :],
                                                                       scalar=1.0 / CONV_CH, in1=bc1.h[:, sl_],
                                                                       op0=ALU.mult, op1=ALU.subtract)))(n, sl),
              r=(psbuf[bS2[n]], b1), w=(b1,))
            E("dve", (lambda sl_: (lambda e: e.tensor_scalar(out=bc1.h[:, sl_], in0=bc1.h[:, sl_], scalar1=EPS,
                                                            scalar2=None, op0=ALU.add)))(sl), r=(b1,), w=(b1,))
            E("act", (lambda sl_: (lambda e: e.activation(out=bc1.h[:, sl_], in_=bc1.h[:, sl_], func=AF.Sqrt)))(sl),
              r=(b1,), w=(b1,))
            E("dve", (lambda sl_: (lambda e: e.reciprocal(out=bc1.h[:, sl_], in_=bc1.h[:, sl_])))(sl), r=(b1,), w=(b1,))
        bS3 = [bS1[0], bS1[1]]
        for c in range(8):
            for n in range(2):
                sl = slice(n * 512, (n + 1) * 512)
                yb = yv.rng(c * 1024 + n * 512, c * 1024 + (n + 1) * 512)
                b0 = bc0.rng(n * 512, (n + 1) * 512)
                b1 = bc1.rng(n * 512, (n + 1) * 512)
                E("dve", (lambda c_, sl_: (lambda e: e.tensor_tensor(out=yv.h[:, c_, sl_], in0=yv.h[:, c_, sl_],
                                                                    in1=bc0.h[:, sl_], op=ALU.subtract)))(c, sl),
                  r=(yb, b0), w=(yb,))
                E("dve", (lambda c_, sl_: (lambda e: e.tensor_tensor(out=yv.h[:, c_, sl_], in0=yv.h[:, c_, sl_],
                                                                    in1=bc1.h[:, sl_], op=ALU.mult)))(c, sl),
                  r=(yb, b1), w=(yb,))
                E("act", (lambda c_, sl_: (lambda e: e.activation(out=yv.h[:, c_, sl_], in_=yv.h[:, c_, sl_], func=AF.Silu,
                                                                 scale=cvc(LG + c_), bias=cvc(LB + c_))))(c, sl),
                  r=(yb, cv.whole), w=(yb,))
                s2 = sqs[nsq[0] % 4]
                nsq[0] += 1
                E("act", (lambda c_, sl_, s_: (lambda e: e.activation(out=s_.h[:, :], in_=yv.h[:, c_, sl_],
                                                                     func=AF.Square)))(c, sl, s2), r=(yb,), w=(s2.whole,))
                mm(psb[bS3[n]][:, :], ones_b, s2.h[:, :], c == 0, c == 7, (cb.whole, s2.whole), (psbuf[bS3[n]],))
        for n in range(2):
            rstd_psum_inplace(bS3[n], 512, 1.0 / CONV_CH)
        for c in range(8):
            for n in range(2):
                sl = slice(n * 512, (n + 1) * 512)
                yb = yv.rng(c * 1024 + n * 512, c * 1024 + (n + 1) * 512)
                E("dve", (lambda c_, n_, sl_: (lambda e: e.scalar_tensor_tensor(
                    out=mixT.h[:, c_, sl_], in0=yv.h[:, c_, sl_], scalar=cvc(G2 + c_), in1=psb[bS3[n_]][:, :],
                    op0=ALU.mult, op1=ALU.mult)))(c, n, sl), r=(yb, cv.whole, psbuf[bS3[n]]), w=(mixT.sub(c),))

        ckpt('E')
        knT = sbt([128, 8, 2048], BF16, AA0 + 0, "knT")
        qnT = sbt([128, 8, 1024], BF16, AA0 + 32 * KB, "qnT")
        tq = sbt([128, 8, 1024], BF16, AA0 + 48 * KB, "tq")
        VA = sbt([128, 8, 8 * 130], BF16, AA0 + 96 * KB, "VA")
        VB = sbt([128, 8, 8 * 130], BF16, AA0 + 148 * KB, "VB")
        for h in range(N_HEADS):
            ws = next_w()
            banks = [nbank() for _ in range(2)]
            for kc in range(6):
                for n in range(2):
                    mm(psb[banks[n]][:, :], ws.h[:, kc * 128:(kc + 1) * 128], qln.h[:, kc, n * 512:(n + 1) * 512],
                       kc == 0, kc == 5, (ws.whole, qln.sub(kc)), (psbuf[banks[n]],))
            for n in range(2):
                E("act", (lambda bk_, h_, n_: (lambda e: e.copy(out=qnT.h[:, h_, n_ * 512:(n_ + 1) * 512],
                                                               in_=psb[bk_][:, :])))(banks[n], h, n),
                  r=(psbuf[banks[n]],), w=(qnT.sub(h),))
            ws = next_w()
            banks = [nbank() for _ in range(2)]
            for kc in range(6):
                for n in range(2):
                    mm(psb[banks[n]][:, :], ws.h[:, kc * 128:(kc + 1) * 128], qln.h[:, kc, n * 512:(n + 1) * 512],
                       kc == 0, kc == 5, (ws.whole, qln.sub(kc)), (psbuf[banks[n]],))
            for n in range(2):
                E("dve", (lambda bk_, h_, n_: (lambda e: e.tensor_tensor(
                    out=tq.h[:, h_, n_ * 512:(n_ + 1) * 512], in0=psb[bk_][:, :],
                    in1=CS.h[:, 1024 + n_ * 512:1024 + (n_ + 1) * 512], op=ALU.mult)))(banks[n], h, n),
                  r=(psbuf[banks[n]], CS.whole), w=(tq.sub(h),))
        for h in range(N_HEADS):
            ws = next_w()
            banks = [nbank() for _ in range(4)]
            for kc in range(4):
                for n in range(4):
                    mm(psb[banks[n]][:, :], ws.h[:, kc * 128:(kc + 1) * 128], kvn.h[:, kc, n * 512:(n + 1) * 512],
                       kc == 0, kc == 3, (ws.whole, kvn.sub(kc)), (psbuf[banks[n]],))
            for n in range(4):
                if n % 2 == 0:
                    E("act", (lambda bk_, h_, n_: (lambda e: e.copy(out=knT.h[:, h_, n_ * 512:(n_ + 1) * 512],
                                                                   in_=psb[bk_][:, :])))(banks[n], h, n),
                      r=(psbuf[banks[n]],), w=(knT.sub(h),))
                else:
                    E("dve", (lambda bk_, h_, n_: (lambda e: e.tensor_copy(out=knT.h[:, h_, n_ * 512:(n_ + 1) * 512],
                                                                          in_=psb[bk_][:, :])))(banks[n], h, n),
                      r=(psbuf[banks[n]],), w=(knT.sub(h),))
        wv = [next_w(), next_w()]
        E("dve", lambda e: e.tensor_scalar(out=VA.h[:, :, :].rearrange("p t (h d) -> p (t h) d", d=130)[:, :, 128:129],
                                           in0=onesf.h[:, 0:64].rearrange("p (a b) -> p a b", b=1),
                                           scalar1=cvc(PFL), scalar2=None, op0=ALU.mult),
          r=(onesf.whole, cv.whole), w=(VA.whole,))
        E("dve", lambda e: e.memset(VB.h[:, :, :].rearrange("p t (h d) -> p (t h) d", d=130)[:, :, 128:129], 1.0),
          w=(VB.whole,))
        for t in range(16):
            Vt = VA if t < 8 else VB
            tt = t % 8
            banks = [nbank() for _ in range(2)]
            for kc in range(4):
                for hf in range(2):
                    mm(psb[banks[hf]][:, :], kvn.h[:, kc, t * 128:(t + 1) * 128],
                       wv[kc // 2].h[:, (kc % 2) * 1024 + hf * 512:(kc % 2) * 1024 + (hf + 1) * 512],
                       kc == 0, kc == 3, (kvn.sub(kc), wv[kc // 2].whole), (psbuf[banks[hf]],))
            for hf in range(2):
                dst = Vt.h[:, tt, :].rearrange("p (h d) -> p h d", d=130)[:, hf * 4:(hf + 1) * 4, 0:128]
                srcp = psb[banks[hf]][:, :].rearrange("p (h d) -> p h d", d=128)
                if t < 8:
                    E("dve", (lambda d_, s_: (lambda e: e.tensor_scalar(out=d_, in0=s_, scalar1=cvc(PFL), scalar2=None,
                                                                       op0=ALU.mult)))(dst, srcp),
                      r=(psbuf[banks[hf]], cv.whole), w=(Vt.sub(tt),))
                else:
                    E("act", (lambda d_, s_: (lambda e: e.copy(out=d_, in_=s_)))(dst, srcp),
                      r=(psbuf[banks[hf]],), w=(Vt.sub(tt),))

        ckpt('C')
        attn = sbt([128, 8, 1024], F32, AA0 + 116 * KB, "attn")
        PT = [sbt([128, 512], BF16, AA0 + 170 * KB + i * 1024, f"PT{i}") for i in range(4)]
        rc = sbt([128, 64], F32, AA0 + 174 * KB, "rc")
        nrc = [0]
        ST_B = [4, 5, 6, 7]
        ACC_B = [0, 1, 2, 3]
        items = []
        for h in range(N_HEADS):
            for qb in range(2):
                for kc in range(8 + 4 * qb + 4):
                    items.append((h, qb, kc))
        LOOK = 3

        def d_qk(n):
            h, qb, kc = items[n]
            q0 = max(kc - 8 - 4 * qb, 0) * 128
            bk = ST_B[n % 4]
            mm(psb[bk][:, q0:512], knT.h[:, h, kc * 128:(kc + 1) * 128],
               qnT.h[:, h, qb * 512 + q0:qb * 512 + 512], True, False,
               (knT.sub(h), qnT.sub(h)), (psbuf[bk],))
            mm(psb[bk][:, q0:512], kr2.h[:, kc * 128:(kc + 1) * 128],
               tq.h[:, h, qb * 512 + q0:qb * 512 + 512], False, True,
               (kr2.whole, tq.sub(h)), (psbuf[bk],))

        def d_rest(n):
            h, qb, kc = items[n]
            j = kc - 8 - 4 * qb
            i0 = max(j, 0)
            q0 = i0 * 128
            bk = ST_B[n % 4]
            pt = PT[n % 4]
            Vt = VA if kc < 8 else VB
            E("act", lambda e: e.activation(out=pt.h[:, q0:512], in_=psb[bk][:, q0:512], func=AF.Exp, scale=SCALE),
              r=(psbuf[bk],), w=(pt.whole,))
            if j >= 0:
                E("dve", lambda e: e.tensor_tensor(out=pt.h[:, q0:q0 + 128], in0=pt.h[:, q0:q0 + 128], in1=tri,
                                                   op=ALU.mult), r=(pt.whole, cb.whole), w=(pt.whole,))
            for i in range(i0, 4):
                last = 8 + 4 * qb + i
                ab = ACC_B[i]
                mm(psb[ab][:, 0:129], pt.h[:, i * 128:(i + 1) * 128], Vt.h[:, kc % 8, h * 130:h * 130 + 129],
                   kc == 0, kc == last, (pt.whole, Vt.sub(kc % 8)), (psbuf[ab],))
                if kc == last:
                    col = nrc[0] % 64
                    nrc[0] += 1
                    rb = rc.rng(col, col + 1)
                    E("dve", (lambda ab_, col_: (lambda e: e.reciprocal(out=rc.h[:, col_:col_ + 1],
                                                                       in_=psb[ab_][:, 128:129])))(ab, col),
                      r=(psbuf[ab],), w=(rb,))
                    E("dve", (lambda ab_, col_, i_: (lambda e: e.tensor_scalar(
                        out=attn.h[:, qb * 4 + i_, h * 128:(h + 1) * 128], in0=psb[ab_][:, 0:128],
                        scalar1=rc.h[:, col_:col_ + 1], scalar2=None, op0=ALU.mult)))(ab, col, i),
                      r=(psbuf[ab], rb), w=(attn.rng((qb * 4 + i) * 1024 + h * 128, (qb * 4 + i) * 1024 + (h + 1) * 128),))

        for n in range(len(items) + LOOK):
            if n < len(items):
                d_qk(n)
            if n >= LOOK:
                d_rest(n - LOOK)

        ckpt('D')
        gbcD = sbt([128, 1024], F32, AA0 + 148 * KB, "gbcD")
        junkD = [sbt([128, 1024], BF16, AA0 + (152 + 2 * i) * KB, f"junkD{i}") for i in range(2)]
        xnD = [sbt([128, 1024], BF16, AA0 + (156 + 2 * i) * KB, f"xnD{i}") for i in range(2)]
        E("sync", lambda e: e.dma_start(out=gbcD.h[:, :], in_=gbc_d[4][:, 0:1024]), w=(gbcD.whole,), key="gbc")
        cD_ss = stcol(8)
        cD_rs = stcol(8)
        brsD = {}
        usedD = {}

        def mixT_attn_tile(i):
            return tuple(P.buf("sb", mixT.off + kc * 2048 + i * 256, mixT.off + kc * 2048 + (i + 1) * 256)
                         for kc in range(8, 16))

        def Dn_s(i):
            brsD[i] = nt_stats(attn.h[:, i, :], attn.sub(i), junkD[i % 2], 1024, cD_ss + i, cD_rs + i)

        def Dn_a(i):
            usedD[i] = nt_apply_a(attn.h[:, i, :], attn.sub(i), brsD[i], gbcD, xnD[i % 2], 1024, cD_rs + i,
                                  banks=[4 + i % 4])
            nt_apply_b(usedD[i], (lambda g: mixT.h[:, 8:16, i * 128:(i + 1) * 128]), mixT_attn_tile(i), ["dve"])

        ckpt('Dn')
        w_out = sbt([128, 16, 2048], BF16, AA0 + 0, "w_out")
        for kc in (0, 1, 8, 12, 2, 3, 9, 13, 4, 5, 10, 14, 6, 7, 11, 15):
            E("pool", (lambda kc_: (lambda e: e.dma_start(out=w_out.h[:, kc_, :],
                                                          in_=w_out_p[:, kc_ * 2048:(kc_ + 1) * 2048])))(kc),
              w=(w_out.sub(kc),), key=f"wout{kc % 4}")
        xF = [sbt([128, 2048], F32, AA0 + (96 + 8 * i) * KB, f"xF{i}") for i in range(2)]
        gpost = sbt([128, 2048], F32, AA0 + 112 * KB, "gpost")
        gpre = sbt([128, 2048], F32, AA0 + 120 * KB, "gpre")
        x1t = sbt([128, 2048], F32, AA0 + 128 * KB, "x1t")
        xnF = sbt([128, 2048], BF16, AA0 + 136 * KB, "xnF")
        hfT = sbt([128, 16, 1024], BF16, AA0 + 140 * KB, "hfT")
        junkF = sbt([128, 2048], BF16, AA0 + 172 * KB, "junkF")
        cF_p = stcol(32)
        cF_ss = stcol(8)
        cF_rs = stcol(8)
        cF_ss2 = stcol(8)
        cF_rs2 = stcol(8)
        brsF = {}
        brsF2 = {}

        KC_ORDER = [0, 1, 2, 3, 4, 5, 6, 8, 9, 10, 12, 13, 14, 7, 11, 15]

        def F1a(i):
            s_ = i % 2
            E("sync", lambda e: e.dma_start(out=xF[s_].h[:, :], in_=x_own[i * 128:(i + 1) * 128, :]),
              w=(xF[s_].whole,), key=f"xF{s_}")
            for cbk in (2, 3, 0, 1):
                bk = (i % 2) * 4 + cbk
                for kc in KC_ORDER:
                    mb = mixT.sub(kc) if kc < 8 else P.buf("sb", mixT.off + kc * 2048 + i * 256,
                                                           mixT.off + kc * 2048 + (i + 1) * 256)
                    mm(psb[bk][:, :], mixT.h[:, kc, i * 128:(i + 1) * 128], w_out.h[:, kc, cbk * 512:(cbk + 1) * 512],
                       kc == KC_ORDER[0], kc == KC_ORDER[-1], (mb, w_out.sub(kc)), (psbuf[bk],))

        def F1b(i):
            for cbk in (2, 3, 0, 1):
                bk = (i % 2) * 4 + cbk
                pc = cF_p + i * 4 + cbk
                E("act", (lambda bk_, pc_, c_: (lambda e: e.activation(
                    out=junkF.h[:, c_ * 512:(c_ + 1) * 512], in_=psb[bk_][:, :], func=AF.Square,
                    accum_out=st.h[:, pc_:pc_ + 1])))(bk, pc, cbk),
                  r=(psbuf[bk],), w=(junkF.rng(cbk * 512, (cbk + 1) * 512), st.rng(pc, pc + 1)))
            E("dve", lambda e: e.tensor_reduce(out=st.h[:, cF_ss + i:cF_ss + i + 1],
                                               in_=st.h[:, cF_p + 4 * i:cF_p + 4 * i + 4],
                                               axis=mybir.AxisListType.X, op=ALU.add),
              r=(st.rng(cF_p + 4 * i, cF_p + 4 * i + 4),), w=(st.rng(cF_ss + i, cF_ss + i + 1),))
            brsF[i] = rstd_from_ss(cF_ss + i, cF_rs + i, 1, 1.0 / D_MODEL)

        def F2(i):
            s_ = i % 2
            for cbk in range(4):
                bk = (i % 2) * 4 + cbk
                sl = slice(cbk * 512, (cbk + 1) * 512)
                E("dve", (lambda bk_, sl_: (lambda e: e.scalar_tensor_tensor(
                    out=x1t.h[:, sl_], in0=psb[bk_][:, :], scalar=st.h[:, cF_rs + i:cF_rs + i + 1], in1=gpost.h[:, sl_],
                    op0=ALU.mult, op1=ALU.mult)))(bk, sl),
                  r=(psbuf[bk], brsF[i], gpost.whole), w=(x1t.rng(cbk * 512, (cbk + 1) * 512),))
            E("dve", lambda e: e.tensor_tensor(out=x1t.h[:, :], in0=x1t.h[:, :], in1=xF[s_].h[:, :], op=ALU.add),
              r=(x1t.whole, xF[s_].whole), w=(x1t.whole,))
            E("sync", lambda e: e.dma_start(out=x1_d[i * 128:(i + 1) * 128, :], in_=x1t.h[:, :]),
              r=(x1t.whole,), w=(), key="x1w")
            brsF2[i] = nt_stats(x1t.h[:, :], x1t.whole, xnF, 2048, cF_ss2 + i, cF_rs2 + i)

        def F3(i):
            nt_apply(x1t.h[:, :], x1t.whole, brsF2[i], gpre, xnF, 2048, cF_rs2 + i,
                     (lambda g: hfT.h[:, g * 8:(g + 1) * 8, i * 128:(i + 1) * 128]),
                     (hfT.whole,), ["act", "dve"], banks=[(i % 2) * 4, (i % 2) * 4 + 1])

        Dn_s(0)
        Dn_s(1)
        Dn_a(0)
        Dn_s(2)
        Dn_a(1)
        F1a(0)
        for i in range(2, 8):
            if i + 1 < 8:
                Dn_s(i + 1)
            Dn_a(i)
        E("sync", lambda e: e.dma_start(out=gpost.h[:, :], in_=gbc_d[1]), w=(gpost.whole,), key="gbc2")
        E("sync", lambda e: e.dma_start(out=gpre.h[:, :], in_=gbc_d[2]), w=(gpre.whole,), key="gbc2")
        F1b(0)
        for i in range(8):
            if i + 1 < 8:
                F1a(i + 1)
            F2(i)
            F3(i)
            if i + 1 < 8:
                F1b(i + 1)
        ckpt('F')
        actT = sbt([128, NFB, 1024], BF16, AA0 + 0, "actT")
        sgf = [sbt([128, 512], F32, AA0 + 172 * KB + i * 2048, f"sgf{i}") for i in range(2)]
        nsg = [0]
        for f in range(NFB):
            wg = next_w()
            wu = next_w()
            bg = [nbank(), nbank()]
            bu = [nbank(), nbank()]
            for kc in range(16):
                for n in range(2):
                    mm(psb[bg[n]][:, :], wg.h[:, kc * 128:(kc + 1) * 128], hfT.h[:, kc, n * 512:(n + 1) * 512],
                       kc == 0, kc == 15, (wg.whole, hfT.whole), (psbuf[bg[n]],))
            for kc in range(16):
                for n in range(2):
                    mm(psb[bu[n]][:, :], wu.h[:, kc * 128:(kc + 1) * 128], hfT.h[:, kc, n * 512:(n + 1) * 512],
                       kc == 0, kc == 15, (wu.whole, hfT.whole), (psbuf[bu[n]],))
            for n in range(2):
                sgt = sgf[nsg[0] % 2]
                nsg[0] += 1
                E("act", (lambda bk_, s_: (lambda e: e.activation(out=s_.h[:, :], in_=psb[bk_][:, :], func=AF.Silu)))(bg[n], sgt),
                  r=(psbuf[bg[n]],), w=(sgt.whole,))
                E("dve", (lambda bk_, s_, f_, n_: (lambda e: e.tensor_tensor(
                    out=actT.h[:, f_, n_ * 512:(n_ + 1) * 512], in0=psb[bk_][:, :], in1=s_.h[:, :], op=ALU.mult)))(bu[n], sgt, f, n),
                  r=(psbuf[bu[n]], sgt.whole), w=(actT.sub(f),))

        ckpt('G1')
        ff = sbt([128, 8, 2048], F32, AA0 + 88 * KB, "ff")
        xr = [sbt([128, 2048], F32, AA0 + (152 + 8 * i) * KB, f"xr{i}") for i in range(2)]
        gffn = sbt([128, 2048], F32, AA0 + 168 * KB, "gffn")
        junkG = [sbt([128, 512], BF16, AA0 + 176 * KB + i * 1024, f"junkG{i}") for i in range(2)]
        E("sync", lambda e: e.dma_start(out=gffn.h[:, :], in_=gbc_d[3]), w=(gffn.whole,), key="gbc3")
        cG_p = stcol(32)
        cG_ss = stcol(8)
        cG_rs = stcol(8)
        for cbk in range(4):
            for fg in range(11):
                ws = next_w()
                for i in range(8):
                    for fb in range(4):
                        fidx = fg * 4 + fb
                        mm(psb[i][:, :], actT.h[:, fidx, i * 128:(i + 1) * 128], ws.h[:, fb * 512:(fb + 1) * 512],
                           fg == 0 and fb == 0, fg == 10 and fb == 3, (actT.sub(fidx), ws.whole), (psbuf[i],))
            for i in range(8):
                sl = slice(cbk * 512, (cbk + 1) * 512)
                pc = cG_p + i * 4 + cbk
                fb_ = ff.rng(i * 2048 + cbk * 512, i * 2048 + (cbk + 1) * 512)
                E("dve", (lambda i_, sl_: (lambda e: e.tensor_tensor(out=ff.h[:, i_, sl_], in0=psb[i_][:, :],
                                                                     in1=gffn.h[:, sl_], op=ALU.mult)))(i, sl),
                  r=(psbuf[i], gffn.whole), w=(fb_,))
                E("act", (lambda i_, pc_: (lambda e: e.activation(out=junkG[i_ % 2].h[:, :], in_=psb[i_][:, :], func=AF.Square,
                                                                 accum_out=st.h[:, pc_:pc_ + 1])))(i, pc),
                  r=(psbuf[i],), w=(junkG[i % 2].whole, st.rng(pc, pc + 1)))
        E("dve", lambda e: e.tensor_reduce(out=st.h[:, cG_ss:cG_ss + 8],
                                           in_=st.h[:, cG_p:cG_p + 32].rearrange("p (a b) -> p a b", b=4),
                                           axis=mybir.AxisListType.X, op=ALU.add),
          r=(st.rng(cG_p, cG_p + 32),), w=(st.rng(cG_ss, cG_ss + 8),))
        brsG = rstd_from_ss(cG_ss, cG_rs, 8, 1.0 / D_MODEL)
        for i in range(8):
            s = i % 2
            E("sync", (lambda s_, i_: (lambda e: e.dma_start(out=xr[s_].h[:, :], in_=x1_d[i_ * 128:(i_ + 1) * 128, :])))(s, i),
              w=(xr[s].whole,), key=f"xr{s}")
            P.q["sync"][-1].waits.append(("dma", "x1w", P.dma_cnt["x1w"]))
            E("dve", (lambda i_, s_: (lambda e: e.scalar_tensor_tensor(
                out=ff.h[:, i_, :], in0=ff.h[:, i_, :], scalar=st.h[:, cG_rs + i_:cG_rs + i_ + 1], in1=xr[s_].h[:, :],
                op0=ALU.mult, op1=ALU.add)))(i, s), r=(ff.sub(i), brsG, xr[s].whole), w=(ff.sub(i),))
            E("sync", (lambda i_: (lambda e: e.dma_start(out=out_d[i_ * 128:(i_ + 1) * 128, :], in_=ff.h[:, i_, :])))(i),
              r=(ff.sub(i),), w=(), key="outw")
        fin = E("sync", None)
        fin.waits.append(("dma", "outw", P.dma_cnt["outw"]))
        assert wcur[0] == len(wpieces), (wcur[0], len(wpieces))


    except _Stop:
        pass

    fin_all = P.op("sync", None)
    for k_, v_ in P.dma_cnt.items():
        fin_all.waits.append(("dma", k_, v_))

    for e_ in ENGS:
        cnt = 0
        for ins in P.q[e_]:
            if ins.signal and not ins.is_dma:
                cnt += 1
                ins.value = cnt
    keys = sorted(P.dma_cnt.keys())
    sem_ctx = {}
    sems_eng = {e_: nc.alloc_semaphore(f"s_{e_}") for e_ in ENGS}
    sems_key = {k: nc.alloc_semaphore(f"d_{k}") for k in keys}

    def replay(ename, eng):
        waited = {}
        for ins in P.q[ename]:
            for w in ins.waits:
                if w[0] == "eng":
                    p = w[1]
                    sem, val = sems_eng[p.eng], p.value
                else:
                    sem, val = sems_key[w[1]], w[2]
                k = id(sem)
                if waited.get(k, 0) < val:
                    eng.wait_ge(sem, val)
                    waited[k] = val
            if ins.fn is None:
                continue
            bi = ins.fn(eng)
            if ins.is_dma:
                bi.then_inc(sems_key[ins.key], 16)
            elif ins.signal:
                bi.then_inc(sems_eng[ename], 1)

    with nc.Block() as block:
        @block.sync
        def _(e):
            replay("sync", e)

        @block.scalar
        def _(e):
            replay("act", e)

        @block.vector
        def _(e):
            replay("dve", e)

        @block.gpsimd
        def _(e):
            replay("pool", e)

        @block.tensor
        def _(e):
            replay("pe", e)

    stats = {e_: len(P.q[e_]) for e_ in ENGS}
    stats["sig"] = {e_: sum(1 for i in P.q[e_] if i.signal and not i.is_dma) for e_ in ENGS}
    return nc, stats


def _blocks_k(w, ncols_per_block):
    K, N = w.shape
    kc = K // 128
    nb = N // ncols_per_block
    a = w.reshape(kc, 128, nb, ncols_per_block).transpose(2, 1, 0, 3)
    return np.ascontiguousarray(a.reshape(nb, 128, kc * ncols_per_block))


def prepare_inputs(x, positions, pre_mix_norm, w_in, q_norm, w_uq, kv_norm, w_ukv, conv_w, conv_b, conv_ln_g,
                   conv_ln_b, conv_out_norm, attn_out_norm, w_out, post_mix_norm, pre_ffn_norm, w_gate, w_up,
                   w_down, post_ffn_norm):
    f = np.float32
    x = np.asarray(x, f)
    positions = np.asarray(positions, np.int32)
    w_in = np.asarray(w_in, f)[0]
    w_uq = np.asarray(w_uq, f)[0]
    w_ukv = np.asarray(w_ukv, f)[0]
    w_out = np.asarray(w_out, f)[0]
    w_gate = np.asarray(w_gate, f)[0]
    w_up = np.asarray(w_up, f)[0]
    w_down = np.asarray(w_down, f)[0]

    c1 = 2 * CONV_CH
    c2 = c1 + Q_LORA
    c3 = c2 + KV_LORA
    cols = []
    cols += list(range(c2, c3))
    cols += list(range(c3, c3 + 64)) + list(range(c3 + 32, c3 + 64)) + list(range(c3, c3 + 32))
    cols += list(range(c1, c2))
    for c in range(8):
        cols += list(range(CONV_CH + c * 128, CONV_CH + (c + 1) * 128))
        cols += list(range(c * 128, (c + 1) * 128))
    w_in_p = _blocks_k(w_in[:, cols], 128)
    cols = []
    for h in range(N_HEADS):
        b0 = h * 192
        cols += list(range(b0, b0 + 128))
        cols += list(range(b0 + 128, b0 + 192)) + list(range(b0 + 160, b0 + 192)) + list(range(b0 + 128, b0 + 160))
    w_uq_p = _blocks_k(w_uq[:, cols], 128)
    kcols = []
    vcols = []
    for h in range(N_HEADS):
        kcols += list(range(h * 256, h * 256 + 128))
        vcols += list(range(h * 256 + 128, h * 256 + 256))
    w_uk_p = _blocks_k(w_ukv[:, kcols], 128)
    wv = w_ukv[:, vcols]
    w_uv_p = np.ascontiguousarray(wv.reshape(2, 2, 128, 1024).transpose(0, 2, 1, 3).reshape(2, 128, 2048))
    w_out_p = np.ascontiguousarray(w_out.reshape(16, 128, 2048).transpose(1, 0, 2).reshape(128, 16 * 2048))
    g_p = _blocks_k(w_gate, 128)
    u_p = _blocks_k(w_up, 128)
    w_gu_p = np.ascontiguousarray(np.stack([g_p, u_p], axis=1).reshape(88, 128, 2048))
    wd = w_down.reshape(11, 4, 128, 4, 512)
    w_dn_p = np.ascontiguousarray(wd.transpose(3, 0, 2, 1, 4).reshape(44, 128, 2048))

    c_bf = np.zeros((128, 512), np.float32)
    c_bf[:, 0:128] = np.eye(128)
    c_bf[:, 128:256] = 1.0
    e64 = np.eye(64)
    c_bf[:, 256:384] = np.block([[e64, e64], [e64, e64]])
    kk = np.arange(128)[:, None]
    qq = np.arange(128)[None, :]
    c_bf[:, 384:512] = (qq >= kk).astype(np.float32)
    c_bf = c_bf.astype(ml_dtypes.bfloat16)

    cvec = np.zeros((128, NCV), f)
    cw = np.asarray(conv_w, f)[0]
    for c in range(8):
        cvec[:, CW + c * 31:CW + (c + 1) * 31] = cw[:, c * 128:(c + 1) * 128].T
    cvec[:, CB:CB + 8] = np.asarray(conv_b, f)[0].reshape(8, 128).T
    cvec[:, LG:LG + 8] = np.asarray(conv_ln_g, f)[0].reshape(8, 128).T
    cvec[:, LB:LB + 8] = np.asarray(conv_ln_b, f)[0].reshape(8, 128).T
    cvec[:, G2:G2 + 8] = np.asarray(conv_out_norm, f)[0].reshape(8, 128).T
    cvec[:, GQ:GQ + 6] = np.asarray(q_norm, f)[0].reshape(6, 128).T
    cvec[:, GKV:GKV + 4] = np.asarray(kv_norm, f)[0].reshape(4, 128).T
    inv_freq = (np.float32(10000.0) ** (-np.arange(0, 64, 2, dtype=np.float32) / np.float32(64))).astype(f)
    cvec[:, IFQ] = np.tile(inv_freq, 4)
    cvec[0:64, PHS] = np.float32(np.pi / 2)
    cvec[:, SGN] = 1.0
    cvec[64:96, SGN] = -1.0

    gbc = np.zeros((5, 128, 2048), f)
    gbc[0] = np.asarray(pre_mix_norm, f)[0][None, :]
    gbc[1] = np.asarray(post_mix_norm, f)[0][None, :]
    gbc[2] = np.asarray(pre_ffn_norm, f)[0][None, :]
    gbc[3] = np.asarray(post_ffn_norm, f)[0][None, :]
    gbc[4, :, 0:1024] = np.asarray(attn_out_norm, f)[0][None, :]

    shared = dict(c_bf=c_bf, gbc=gbc, w_in_p=w_in_p, w_uq_p=w_uq_p, w_uk_p=w_uk_p, w_uv_p=w_uv_p,
                  w_out_p=w_out_p, w_gu_p=w_gu_p, w_dn_p=w_dn_p)
    in_maps = []
    for core in range(8):
        b, half = core // 2, core % 2
        m = dict(shared)
        m["x_own"] = np.ascontiguousarray(x[b, half * TOWN:(half + 1) * TOWN])
        cvc = cvec.copy()
        pos = np.zeros((2048,), np.int32)
        if half == 1:
            m["x_prev"] = np.ascontiguousarray(x[b, 0:TOWN])
            pos[:] = positions[b, 0:2048]
            cvc[:, PFL] = 1.0
        else:
            m["x_prev"] = np.zeros((TOWN, D_MODEL), f)
            pos[1024:] = positions[b, 0:1024]
            cvc[:, PFL] = 0.0
        m["cvec"] = cvc
        m["pos_bc"] = np.ascontiguousarray(np.broadcast_to(pos[None, :], (128, 2048)))
        in_maps.append(m)
    return in_maps


_CACHE = {}


def kernel(**inputs):
    if "nc" not in _CACHE:
        _CACHE["nc"], _CACHE["stats"] = build_program()
    nc = _CACHE["nc"]
    in_maps = prepare_inputs(**inputs)
    res = run_bass_kernel_spmd(nc, in_maps, core_ids=list(range(8)))
    out = np.zeros((BATCH, SEQ, D_MODEL), np.float32)
    for core in range(8):
        b, half = core // 2, core % 2
        out[b, half * TOWN:(half + 1) * TOWN] = res.results[core]["out"]
    return out
```

```python
import os
import numpy as np
import ml_dtypes
import concourse.bass as bass
import concourse.mybir as mybir
from concourse.bass_utils import run_bass_kernel_spmd

F32 = mybir.dt.float32
BF16 = mybir.dt.bfloat16
I32 = mybir.dt.int32
AF = mybir.ActivationFunctionType
ALU = mybir.AluOpType
PI = float(np.pi)

D_MODEL = 2048
SEQ = 2048
BATCH = 4
TOWN = 1024
CONV_CH = 1024
CONV_K = 31
N_HEADS = 8
Q_LORA = 768
KV_LORA = 512
D_FF = 5632
NFB = D_FF // 128
EPS = 1e-6
SCALE = 192 ** -0.5

CW = 0
CB = 248
LG = 256
LB = 264
G2 = 272
GQ = 280
GKV = 286
IFQ = 290
PHS = 291
SGN = 292
PFL = 293
NCV = 320

KB = 1024


class Buf:
    __slots__ = ("space", "lo", "hi", "name")

    def __init__(self, space, lo, hi, name=""):
        self.space, self.lo, self.hi, self.name = space, lo, hi, name


class Ins:
    __slots__ = ("eng", "fn", "is_dma", "key", "signal", "value", "waits", "dma_val", "idx")

    def __init__(self, eng, fn, is_dma, key):
        self.eng, self.fn, self.is_dma, self.key = eng, fn, is_dma, key
        self.idx = -1
        self.signal = False
        self.value = None
        self.waits = []
        self.dma_val = None


ENGS = ["sync", "act", "dve", "pool", "pe"]


class Prog:
    def __init__(self):
        self.q = {e: [] for e in ENGS}
        self.wr = {"sb": [], "ps": []}
        self.rd = {"sb": [], "ps": []}
        self.dma_cnt = {}

    def buf(self, space, lo, hi, name=""):
        return Buf(space, lo, hi, name)

    def op(self, eng, fn, reads=(), writes=(), dma_key=None):
        ins = Ins(eng, fn, dma_key is not None, dma_key)
        ins.idx = len(self.q[eng])
        need_eng = {}
        need_dma = {}

        def add(p):
            if p is ins:
                return
            if p.is_dma:
                need_dma[p.key] = 1
                return
            if p.eng == eng and not ins.is_dma and eng == "pe":
                return
            cur = need_eng.get(p.eng)
            if cur is None or cur.idx < p.idx:
                need_eng[p.eng] = p

        for b in reads:
            for w in self.wr[b.space]:
                if w[0] < b.hi and b.lo < w[1]:
                    add(w[2])
            if b.space == "ps" and eng != "pe":
                lo = b.lo // 2048 * 2048
                hi = -(-b.hi // 2048) * 2048
                for r in self.rd["ps"]:
                    if r[0] < hi and lo < r[1] and r[3].eng != eng and r[3].eng != "pe":
                        add(r[3])
        for b in writes:
            for w in self.wr[b.space]:
                if w[0] < b.hi and b.lo < w[1]:
                    add(w[2])
            for r in self.rd[b.space]:
                if r[0] < b.hi and b.lo < r[1]:
                    add(r[3])
        for p in need_eng.values():
            p.signal = True
            ins.waits.append(("eng", p))
        for k in need_dma:
            ins.waits.append(("dma", k, self.dma_cnt[k]))
        if dma_key is not None:
            self.dma_cnt[dma_key] = self.dma_cnt.get(dma_key, 0) + 16
            ins.dma_val = self.dma_cnt[dma_key]
        for b in writes:
            sp = b.space
            self.wr[sp] = [w for w in self.wr[sp] if not (b.lo <= w[0] and w[1] <= b.hi)]
            self.wr[sp].append([b.lo, b.hi, ins])
            self.rd[sp] = [r for r in self.rd[sp] if not (b.lo <= r[0] and r[1] <= b.hi)]
        rk = ("dma", dma_key) if ins.is_dma else eng
        for b in reads:
            lst = self.rd[b.space]
            for r in lst:
                if r[0] == b.lo and r[1] == b.hi and r[2] == rk:
                    r[3] = ins
                    break
            else:
                lst.append([b.lo, b.hi, rk, ins])
        self.q[eng].append(ins)
        return ins


def build_program(stop_after=None, tensors=None):
    nc = bass.Bass("TRN2", target_bir_lowering=False)
    P = Prog()

    def din(name, shape, dt):
        return nc.dram_tensor(name, list(shape), dt, kind="ExternalInput").ap()

    x_own = din("x_own", [TOWN, D_MODEL], F32)
    x_prev = din("x_prev", [TOWN, D_MODEL], F32)
    pos_bc = din("pos_bc", [128, 2048], I32)
    c_bf = din("c_bf", [128, 512], BF16)
    cvec_d = din("cvec", [128, NCV], F32)
    gbc_d = din("gbc", [5, 128, 2048], F32)
    w_in_p = din("w_in_p", [27, 128, 2048], F32)
    w_uq_p = din("w_uq_p", [16, 128, 768], F32)
    w_uk_p = din("w_uk_p", [8, 128, 512], F32)
    w_uv_p = din("w_uv_p", [2, 128, 2048], F32)
    w_out_p = din("w_out_p", [128, 16 * 2048], F32)
    w_gu_p = din("w_gu_p", [88, 128, 2048], F32)
    w_dn_p = din("w_dn_p", [44, 128, 2048], F32)
    out_d = nc.dram_tensor("out", [TOWN, D_MODEL], F32, kind="ExternalOutput").ap()
    x1_d = nc.dram_tensor("x1_scratch", [TOWN, D_MODEL], F32).ap()
    dbg = {}

    base = (nc.sbuf_base + 63) // 64 * 64
    CONST0 = base
    WR0 = CONST0 + 4 * KB
    AA0 = WR0 + 24 * KB
    assert AA0 + 178 * KB <= nc.sbuf_top, (AA0 + 178 * KB, nc.sbuf_top)

    def dsz(dt):
        return 4 if dt in (F32, I32) else 2

    class T:
        def __init__(self, name, shape, dt, off):
            self.h = nc.alloc_sbuf_tensor_at(name, list(shape), dt, offset=off)
            self.off = off
            self.shape = shape
            self.dt = dt
            self.nbytes = int(np.prod(shape[1:])) * dsz(dt)
            self.whole = P.buf("sb", off, off + self.nbytes, name)
            self._subs = {}
            if tensors is not None:
                tensors[name] = self

        def sub(self, i, n=None):
            key = (i, n)
            if key not in self._subs:
                slab = self.nbytes // self.shape[1]
                cnt = 1 if n is None else n
                self._subs[key] = P.buf("sb", self.off + i * slab, self.off + (i + cnt) * slab)
            return self._subs[key]

        def rng(self, lo_el, hi_el):
            key = ("r", lo_el, hi_el)
            if key not in self._subs:
                self._subs[key] = P.buf("sb", self.off + lo_el * dsz(self.dt), self.off + hi_el * dsz(self.dt))
            return self._subs[key]

    _names = [0]

    def sbt(shape, dt, off, name=None):
        _names[0] += 1
        return T(name or f"t{_names[0]}", shape, dt, off)

    cb = sbt([128, 512], BF16, CONST0, "cb")
    ident = cb.h[:, 0:128]
    ones_b = cb.h[:, 128:256]
    f2 = cb.h[:, 256:384]
    tri = cb.h[:, 384:512]
    cv = sbt([128, NCV], F32, CONST0 + 1024, "cv")
    st = sbt([128, 320], F32, CONST0 + 1024 + NCV * 4, "st")
    onesf = sbt([128, 64], F32, CONST0 + 1024 + NCV * 4 + 1280, "onesf")
    assert CONST0 + 1024 + NCV * 4 + 1280 + 256 <= WR0

    def cvc(col):
        return cv.h[:, col:col + 1]

    _stn = [0]

    def stcol(n=1):
        c = _stn[0]
        _stn[0] += n
        assert _stn[0] <= 320
        return c

    NSLOT = 6
    wslots = [sbt([128, 2048], BF16, WR0 + i * 4 * KB, f"ws{i}") for i in range(NSLOT)]
    wpieces = []
    wstate = {"issued": 0}

    psb = [nc.alloc_psum_tensor(f"psb{i}", [128, 512], F32) for i in range(8)]
    psbuf = [P.buf("ps", i * 2048, (i + 1) * 2048, f"ps{i}") for i in range(8)]
    psbf = [psb[i][:, :].bitcast(BF16) for i in range(8)]
    _bank = [0]

    def nbank():
        b = _bank[0] % 8
        _bank[0] += 1
        return b

    _psub = {}

    def psub(bank, lo, hi):
        k = (bank, lo, hi)
        if k not in _psub:
            _psub[k] = P.buf("ps", bank * 2048 + lo * 4, bank * 2048 + hi * 4)
        return _psub[k]

    def E(eng, fn, r=(), w=(), key=None):
        return P.op(eng, fn, r, w, key)

    def mm(out, lhsT, rhs, start, stop, r, w):
        return E("pe", lambda e: e.matmul(out, lhsT=lhsT, rhs=rhs, start=start, stop=stop), r, w)

    def issue_weights(upto, after=()):
        while wstate["issued"] < min(upto, len(wpieces)):
            i = wstate["issued"]
            src, n = wpieces[i]
            slot = wslots[i % NSLOT]
            E("pool", (lambda s, n_, sl: (lambda e: e.dma_start(out=sl.h[:, 0:n_], in_=s)))(src, n, slot),
              r=after, w=(slot.whole,), key=f"ws{i % NSLOT}")
            wstate["issued"] += 1

    wcur = [0]

    def next_w(prefetch=4):
        i = wcur[0]
        wcur[0] += 1
        issue_weights(i + 1 + prefetch)
        return wslots[i % NSLOT]

    for j in range(27):
        wpieces.append((w_in_p[j], 2048))
    for j in range(16):
        wpieces.append((w_uq_p[j], 768))
    for j in range(8):
        wpieces.append((w_uk_p[j], 512))
    for j in range(2):
        wpieces.append((w_uv_p[j], 2048))
    for j in range(88):
        wpieces.append((w_gu_p[j], 2048))
    for j in range(44):
        wpieces.append((w_dn_p[j], 2048))

    class _Stop(Exception):
        pass

    def ckpt(name):
        if stop_after == name:
            raise _Stop()

    try:
        E("sync", lambda e: e.dma_start(out=cb.h[:, :], in_=c_bf), w=(cb.whole,), key="const")
        E("sync", lambda e: e.dma_start(out=cv.h[:, :], in_=cvec_d), w=(cv.whole,), key="const")
        E("dve", lambda e: e.memset(st.h[:, :], 0.0), w=(st.whole,))
        E("dve", lambda e: e.memset(onesf.h[:, :], 1.0), w=(onesf.whole,))

        def rstd_from_ss(c_ss, c_out, n, inv_n):
            c_ms = stcol(n)
            c_sq = stcol(n)
            bs = st.rng(c_ss, c_ss + n)
            bm = st.rng(c_ms, c_ms + n)
            bq = st.rng(c_sq, c_sq + n)
            bo = st.rng(c_out, c_out + n)
            E("dve", lambda e: e.tensor_scalar(out=st.h[:, c_ms:c_ms + n], in0=st.h[:, c_ss:c_ss + n], scalar1=inv_n,
                                               scalar2=EPS, op0=ALU.mult, op1=ALU.add), r=(bs,), w=(bm,))
            E("act", lambda e: e.activation(out=st.h[:, c_sq:c_sq + n], in_=st.h[:, c_ms:c_ms + n], func=AF.Sqrt),
              r=(bm,), w=(bq,))
            E("dve", lambda e: e.reciprocal(out=st.h[:, c_out:c_out + n], in_=st.h[:, c_sq:c_sq + n]), r=(bq,), w=(bo,))
            return bo

        def rstd_psum_inplace(bank, n, inv_n):
            b = psbuf[bank]
            ap = psb[bank][:, 0:n]
            E("dve", lambda e: e.tensor_scalar(out=ap, in0=ap, scalar1=inv_n, scalar2=EPS, op0=ALU.mult, op1=ALU.add),
              r=(b,), w=(b,))
            E("act", lambda e: e.activation(out=ap, in_=ap, func=AF.Sqrt), r=(b,), w=(b,))
            E("dve", lambda e: e.reciprocal(out=ap, in_=ap), r=(b,), w=(b,))

        def nt_stats(src_ap, src_buf, junk_t, width, c_ss, c_rs):
            bs = st.rng(c_ss, c_ss + 1)
            E("act", lambda e: e.activation(out=junk_t.h[:, 0:width], in_=src_ap, func=AF.Square,
                                            accum_out=st.h[:, c_ss:c_ss + 1]), r=(src_buf,), w=(junk_t.whole, bs))
            return rstd_from_ss(c_ss, c_rs, 1, 1.0 / width)

        def nt_apply_a(src_ap, src_buf, brs, gb_t, xn_t, width, c_rs, banks=None):
            nchunk = width // 128
            E("dve", lambda e: e.scalar_tensor_tensor(out=xn_t.h[:, 0:width], in0=src_ap,
                                                      scalar=st.h[:, c_rs:c_rs + 1], in1=gb_t.h[:, 0:width],
                                                      op0=ALU.mult, op1=ALU.mult),
              r=(src_buf, brs, gb_t.whole), w=(xn_t.whole,))
            used = []
            for g in range(nchunk // 8):
                bk = nbank() if banks is None else banks[g]
                used.append(bk)
                for c8 in range(8):
                    c = g * 8 + c8
                    E("pe", (lambda bk_, c8_, c_: (lambda e: e.transpose(psbf[bk_][:, c8_ * 128:(c8_ + 1) * 128],
                                                                       xn_t.h[:, c_ * 128:(c_ + 1) * 128], ident)))(bk, c8, c),
                      r=(xn_t.whole, cb.whole), w=(psbuf[bk],))
            return used

        def nt_apply_b(used, dst_fn, dst_bufs, evac_engs):
            for g, bk in enumerate(used):
                eng = evac_engs[g % len(evac_engs)]
                src = psbf[bk][:, 0:1024].rearrange("p (a b) -> p a b", a=8)
                dst = dst_fn(g)
                wb = dst_bufs(g) if callable(dst_bufs) else dst_bufs
                if eng == "act":
                    E("act", (lambda d_, s_: (lambda e: e.copy(out=d_, in_=s_)))(dst, src), r=(psbuf[bk],), w=wb)
                else:
                    E("dve", (lambda d_, s_: (lambda e: e.tensor_copy(out=d_, in_=s_)))(dst, src), r=(psbuf[bk],), w=wb)

        def nt_apply(src_ap, src_buf, brs, gb_t, xn_t, width, c_rs, dst_fn, dst_bufs, evac_engs, banks=None):
            used = nt_apply_a(src_ap, src_buf, brs, gb_t, xn_t, width, c_rs, banks)
            nt_apply_b(used, dst_fn, dst_bufs, evac_engs)

        hT = sbt([128, 16, 2048], BF16, AA0 + 0, "hT")
        xa = [sbt([128, 2048], F32, AA0 + (64 + 8 * i) * KB, f"xa{i}") for i in range(4)]
        gbcA = sbt([128, 2048], F32, AA0 + 96 * KB, "gbcA")
        xnA = [sbt([128, 2048], BF16, AA0 + (104 + 4 * i) * KB, f"xnA{i}") for i in range(3)]
        junkA = [sbt([128, 2048], BF16, AA0 + (116 + 4 * i) * KB, f"junkA{i}") for i in range(4)]
        E("sync", lambda e: e.dma_start(out=gbcA.h[:, :], in_=gbc_d[0]), w=(gbcA.whole,), key="gbc")
        def hT_rd(kc, a0, a1):
            return hT.rng(kc * 2048 + a0, kc * 2048 + a1)

        def hT_wr(t, g):
            return tuple(hT.rng(kc * 2048 + t * 128, kc * 2048 + (t + 1) * 128) for kc in range(g * 8, g * 8 + 8))

        cA_ss = stcol(16)
        cA_rs = stcol(16)
        brsA = {}

        def A1(t):
            s_ = t % 4
            src = x_prev[t * 128:(t + 1) * 128, :] if t < 8 else x_own[(t - 8) * 128:(t - 7) * 128, :]
            E("sync", (lambda s__, src_: (lambda e: e.dma_start(out=xa[s__].h[:, :], in_=src_)))(s_, src),
              w=(xa[s_].whole,), key=f"xa{s_}")
            brsA[t] = nt_stats(xa[s_].h[:, :], xa[s_].whole, junkA[t % 4], 2048, cA_ss + t, cA_rs + t)

        usedA = {}

        def A2a(t):
            s_ = t % 4
            usedA[t] = nt_apply_a(xa[s_].h[:, :], xa[s_].whole, brsA[t], gbcA, xnA[t % 3], 2048, cA_rs + t)

        def A2b(t):
            nt_apply_b(usedA[t], (lambda g: hT.h[:, g * 8:(g + 1) * 8, t * 128:(t + 1) * 128]),
                       (lambda g: hT_wr(t, g)), ["act", "dve"])

        A1(0)
        A1(1)
        A1(2)
        issue_weights(NSLOT, after=(xa[0].whole, xa[1].whole, xa[2].whole))
        A2a(0)
        for t in range(16):
            if t + 1 < 16:
                A2a(t + 1)
            if t + 3 < 16:
                A1(t + 3)
            A2b(t)

        ckpt('A')
        zz = sbt([128, 4, 2048], F32, AA0 + 64 * KB, "zz")
        zq = sbt([128, 6, 1024], F32, AA0 + 64 * KB, "zq")
        sqz = sbt([128, 4, 2048], BF16, AA0 + 96 * KB, "sqz")
        sqq = sbt([128, 6, 1024], BF16, AA0 + 96 * KB, "sqq")
        t_k = sbt([128, 2048], BF16, AA0 + 112 * KB, "t_k")
        kvn = sbt([128, 4, 2048], BF16, AA0 + 116 * KB, "kvn")
        qln = sbt([128, 6, 1024], BF16, AA0 + 132 * KB, "qln")
        u_bf = sbt([128, 8, 1152], BF16, AA0 + 148 * KB, "u_bf")
        kr2 = sbt([128, 2048], BF16, AA0 + 166 * KB, "kr2")
        sg = [sbt([128, 1152], F32, AA0 + 64 * KB + i * 4608, f"sg{i}") for i in range(2)]

        CS = sbt([128, 2048], F32, AA0 + 170 * KB, "CS")

        def emit_rope():
            posi = sbt([128, 2048], I32, AA0 + 132 * KB, "posi")
            posf = sbt([128, 2048], F32, AA0 + 140 * KB, "posf")
            tmpa = sbt([128, 2048], F32, AA0 + 148 * KB, "tmpa")
            E("sync", lambda e: e.dma_start(out=posi.h[:, :], in_=pos_bc), w=(posi.whole,), key="pos")
            E("dve", lambda e: e.tensor_copy(out=posf.h[:, :], in_=posi.h[:, :]), r=(posi.whole,), w=(posf.whole,))
            E("dve", lambda e: e.tensor_scalar(out=CS.h[:, :], in0=posf.h[:, :], scalar1=cvc(IFQ), scalar2=cvc(PHS),
                                               op0=ALU.mult, op1=ALU.add), r=(posf.whole, cv.whole), w=(CS.whole,))
            E("dve", lambda e: e.tensor_scalar(out=tmpa.h[:, :], in0=CS.h[:, :], scalar1=1.0 / (2 * PI), scalar2=None,
                                               op0=ALU.mult), r=(CS.whole,), w=(tmpa.whole,))
            E("dve", lambda e: e.tensor_copy(out=posi.h[:, :], in_=tmpa.h[:, :]), r=(tmpa.whole,), w=(posi.whole,))
            E("dve", lambda e: e.tensor_copy(out=posf.h[:, :], in_=posi.h[:, :]), r=(posi.whole,), w=(posf.whole,))
            C1 = 6.28125
            C2 = 2 * PI - C1
            E("dve", lambda e: e.scalar_tensor_tensor(out=CS.h[:, :], in0=posf.h[:, :], scalar=-C1, in1=CS.h[:, :],
                                                      op0=ALU.mult, op1=ALU.add), r=(posf.whole, CS.whole), w=(CS.whole,))
            E("dve", lambda e: e.scalar_tensor_tensor(out=CS.h[:, :], in0=posf.h[:, :], scalar=-C2, in1=CS.h[:, :],
                                                      op0=ALU.mult, op1=ALU.add), r=(posf.whole, CS.whole), w=(CS.whole,))
            E("dve", lambda e: e.tensor_single_scalar(out=tmpa.h[:, :], in_=CS.h[:, :], scalar=PI, op=ALU.is_gt),
              r=(CS.whole,), w=(tmpa.whole,))
            E("dve", lambda e: e.scalar_tensor_tensor(out=CS.h[:, :], in0=tmpa.h[:, :], scalar=-2 * PI, in1=CS.h[:, :],
                                                      op0=ALU.mult, op1=ALU.add), r=(tmpa.whole, CS.whole), w=(CS.whole,))
            E("dve", lambda e: e.tensor_scalar(out=CS.h[:, :], in0=CS.h[:, :], scalar1=-PI, scalar2=PI,
                                               op0=ALU.max, op1=ALU.min), r=(CS.whole,), w=(CS.whole,))
            E("act", lambda e: e.activation(out=CS.h[:, :], in_=CS.h[:, :], func=AF.Sin), r=(CS.whole,), w=(CS.whole,))
            E("dve", lambda e: e.tensor_scalar(out=CS.h[:, :], in0=CS.h[:, :], scalar1=cvc(SGN), scalar2=None,
                                               op0=ALU.mult), r=(CS.whole, cv.whole), w=(CS.whole,))


        for b in range(4):
            if b == 1:
                emit_rope()
            ws = next_w()
            banks = [nbank() for _ in range(4)]
            for kc in range(16):
                for n in range(4):
                    mm(psb[banks[n]][:, :], ws.h[:, kc * 128:(kc + 1) * 128], hT.h[:, kc, n * 512:(n + 1) * 512],
                       kc == 0, kc == 15, (ws.whole, hT_rd(kc, n * 512, (n + 1) * 512)), (psbuf[banks[n]],))
            for n in range(4):
                if os.environ.get("KDBG") == "noevac":
                    break
                bk = banks[n]
                E("dve", (lambda bk_, b_, n_: (lambda e: e.tensor_scalar(
                    out=zz.h[:, b_, n_ * 512:(n_ + 1) * 512], in0=psb[bk_][:, :], scalar1=cvc(GKV + b_), scalar2=None,
                    op0=ALU.mult)))(bk, b, n), r=(psbuf[bk], cv.whole), w=(zz.sub(b),))
                if os.environ.get("KDBG") == "noact":
                    continue
                E("act", (lambda bk_, b_, n_: (lambda e: e.activation(
                    out=sqz.h[:, b_, n_ * 512:(n_ + 1) * 512], in_=psb[bk_][:, :], func=AF.Square)))(bk, b, n),
                  r=(psbuf[bk],), w=(sqz.sub(b),) + ((psbuf[bk],) if os.environ.get("KDBG") == "serial" else ()))
        ckpt('B1')
        for n in range(4):
            bk = nbank()
            for b in range(4):
                mm(psb[bk][:, :], ones_b, sqz.h[:, b, n * 512:(n + 1) * 512], b == 0, b == 3,
                   (cb.whole, sqz.sub(b)), (psbuf[bk],))
            rstd_psum_inplace(bk, 512, 1.0 / KV_LORA)
            for b in range(4):
                E("dve", (lambda bk_, b_, n_: (lambda e: e.tensor_tensor(
                    out=kvn.h[:, b_, n_ * 512:(n_ + 1) * 512], in0=zz.h[:, b_, n_ * 512:(n_ + 1) * 512],
                    in1=psb[bk_][:, :], op=ALU.mult)))(bk, b, n), r=(zz.sub(b), psbuf[bk]), w=(kvn.sub(b),))
        ckpt('B2')
        ws = next_w()
        banks = [nbank() for _ in range(4)]
        for kc in range(16):
            for n in range(4):
                mm(psb[banks[n]][:, :], ws.h[:, kc * 128:(kc + 1) * 128], hT.h[:, kc, n * 512:(n + 1) * 512],
                   kc == 0, kc == 15, (ws.whole, hT_rd(kc, n * 512, (n + 1) * 512)), (psbuf[banks[n]],))
        for n in range(4):
            bk = banks[n]
            E("dve", (lambda bk_, n_: (lambda e: e.tensor_tensor(
                out=t_k.h[:, n_ * 512:(n_ + 1) * 512], in0=psb[bk_][:, :], in1=CS.h[:, n_ * 512:(n_ + 1) * 512],
                op=ALU.mult)))(bk, n), r=(psbuf[bk], CS.whole), w=(t_k.rng(n * 512, (n + 1) * 512),))
            bk2 = nbank()
            mm(psb[bk2][:, :], f2, t_k.h[:, n * 512:(n + 1) * 512], True, True,
               (cb.whole, t_k.rng(n * 512, (n + 1) * 512)), (psbuf[bk2],))
            E("act", (lambda bk_, n_: (lambda e: e.copy(out=kr2.h[:, n_ * 512:(n_ + 1) * 512], in_=psb[bk_][:, :])))(bk2, n),
              r=(psbuf[bk2],), w=(kr2.rng(n * 512, (n + 1) * 512),))
        ckpt('B3')
        for b in range(6):
            ws = next_w()
            banks = [nbank() for _ in range(2)]
            for kc in range(16):
                for n in range(2):
                    mm(psb[banks[n]][:, :], ws.h[:, kc * 128:(kc + 1) * 128],
                       hT.h[:, kc, 1024 + n * 512:1024 + (n + 1) * 512],
                       kc == 0, kc == 15, (ws.whole, hT_rd(kc, 1024 + n * 512, 1024 + (n + 1) * 512)), (psbuf[banks[n]],))
            for n in range(2):
                bk = banks[n]
                E("dve", (lambda bk_, b_, n_: (lambda e: e.tensor_scalar(
                    out=zq.h[:, b_, n_ * 512:(n_ + 1) * 512], in0=psb[bk_][:, :], scalar1=cvc(GQ + b_), scalar2=None,
                    op0=ALU.mult)))(bk, b, n), r=(psbuf[bk], cv.whole), w=(zq.sub(b),))
                E("act", (lambda bk_, b_, n_: (lambda e: e.activation(
                    out=sqq.h[:, b_, n_ * 512:(n_ + 1) * 512], in_=psb[bk_][:, :], func=AF.Square)))(bk, b, n),
                  r=(psbuf[bk],), w=(sqq.sub(b),))
        for n in range(2):
            bk = nbank()
            for b in range(6):
                mm(psb[bk][:, :], ones_b, sqq.h[:, b, n * 512:(n + 1) * 512], b == 0, b == 5,
                   (cb.whole, sqq.sub(b)), (psbuf[bk],))
            rstd_psum_inplace(bk, 512, 1.0 / Q_LORA)
            for b in range(6):
                E("dve", (lambda bk_, b_, n_: (lambda e: e.tensor_tensor(
                    out=qln.h[:, b_, n_ * 512:(n_ + 1) * 512], in0=zq.h[:, b_, n_ * 512:(n_ + 1) * 512],
                    in1=psb[bk_][:, :], op=ALU.mult)))(bk, b, n), r=(zq.sub(b), psbuf[bk]), w=(qln.sub(b),))
        ckpt('B4')
        tokr = [(896, 1024), (1024, 1536), (1536, 2048)]
        for c in range(8):
            sgt = sg[c % 2]
            ws = next_w()
            banks = [nbank() for _ in range(3)]
            for kc in range(16):
                for n, (a0, a1) in enumerate(tokr):
                    mm(psb[banks[n]][:, 0:a1 - a0], ws.h[:, kc * 128:(kc + 1) * 128], hT.h[:, kc, a0:a1],
                       kc == 0, kc == 15, (ws.whole, hT_rd(kc, a0, a1)), (psbuf[banks[n]],))
            for n, (a0, a1) in enumerate(tokr):
                E("act", (lambda bk_, a0_, a1_, sg_: (lambda e: e.activation(
                    out=sg_.h[:, a0_ - 896:a1_ - 896], in_=psb[bk_][:, 0:a1_ - a0_], func=AF.Sigmoid)))(banks[n], a0, a1, sgt),
                  r=(psbuf[banks[n]],), w=(sgt.whole,))
            ws = next_w()
            banks = [nbank() for _ in range(3)]
            for kc in range(16):
                for n, (a0, a1) in enumerate(tokr):
                    mm(psb[banks[n]][:, 0:a1 - a0], ws.h[:, kc * 128:(kc + 1) * 128], hT.h[:, kc, a0:a1],
                       kc == 0, kc == 15, (ws.whole, hT_rd(kc, a0, a1)), (psbuf[banks[n]],))
            for n, (a0, a1) in enumerate(tokr):
                E("dve", (lambda bk_, a0_, a1_, sg_, c_: (lambda e: e.tensor_tensor(
                    out=u_bf.h[:, c_, a0_ - 896:a1_ - 896], in0=psb[bk_][:, 0:a1_ - a0_], in1=sg_.h[:, a0_ - 896:a1_ - 896],
                    op=ALU.mult)))(banks[n], a0, a1, sgt, c), r=(psbuf[banks[n]], sgt.whole), w=(u_bf.sub(c),))

        ckpt('B')
        yv = sbt([128, 8, 1024], F32, AA0 + 0, "yv")
        Dr = [sbt([128, 31, 128], BF16, AA0 + (32 + 8 * i) * KB, f"Dr{i}") for i in range(2)]
        sqs = [sbt([128, 512], BF16, AA0 + 48 * KB + i * 1024, f"sqs{i}") for i in range(4)]
        bc0 = sbt([128, 1024], F32, AA0 + 52 * KB, "bc0")
        bc1 = sbt([128, 1024], F32, AA0 + 56 * KB, "bc1")
        mixT = sbt([128, 16, 1024], BF16, AA0 + 64 * KB, "mixT")
        mixT_attn = P.buf("sb", mixT.off + 8 * 2048, mixT.off + 16 * 2048, "mixT_attn")
        bS1 = [nbank(), nbank()]
        bS2 = [nbank(), nbank()]
        nsq = [0]
        for c in range(8):
            dr = Dr[c % 2]
            for k in range(CONV_K):
                E("dve", (lambda dr_, k_, c_: (lambda e: e.tensor_scalar(
                    out=dr_.h[:, k_, :], in0=ident, scalar1=cvc(CW + c_ * 31 + k_), scalar2=None, op0=ALU.mult)))(dr, k, c),
                  r=(cb.whole, cv.whole), w=(dr.sub(k),))
            for n in range(2):
                bk = nbank()
                while bk in bS1 or bk in bS2:
                    bk = nbank()
                for k in range(CONV_K):
                    mm(psb[bk][:, :], dr.h[:, k, :], u_bf.h[:, c, 98 + k + n * 512:98 + k + (n + 1) * 512],
                       k == 0, k == CONV_K - 1, (dr.sub(k), u_bf.sub(c)), (psbuf[bk],))
                yb = yv.rng(c * 1024 + n * 512, c * 1024 + (n + 1) * 512)
                E("act", (lambda bk_, c_, n_: (lambda e: e.activation(
                    out=yv.h[:, c_, n_ * 512:(n_ + 1) * 512], in_=psb[bk_][:, :], func=AF.Identity,
                    bias=cvc(CB + c_))))(bk, c, n), r=(psbuf[bk], cv.whole), w=(yb,))
                s1 = sqs[nsq[0] % 4]
                nsq[0] += 1
                s2 = sqs[nsq[0] % 4]
                nsq[0] += 1
                E("act", (lambda bk_, c_, s_: (lambda e: e.activation(
                    out=s_.h[:, :], in_=psb[bk_][:, :], func=AF.Identity, bias=cvc(CB + c_))))(bk, c, s1),
                  r=(psbuf[bk], cv.whole), w=(s1.whole,))
                E("act", (lambda bk_, c_, s_: (lambda e: e.activation(
                    out=s_.h[:, :], in_=psb[bk_][:, :], func=AF.Square, bias=cvc(CB + c_))))(bk, c, s2),
                  r=(psbuf[bk], cv.whole), w=(s2.whole,))
                mm(psb[bS1[n]][:, :], ones_b, s1.h[:, :], c == 0, c == 7, (cb.whole, s1.whole), (psbuf[bS1[n]],))
                mm(psb[bS2[n]][:, :], ones_b, s2.h[:, :], c == 0, c == 7, (cb.whole, s2.whole), (psbuf[bS2[n]],))
        for n in range(2):
            sl = slice(n * 512, (n + 1) * 512)
            b0 = bc0.rng(n * 512, (n + 1) * 512)
            b1 = bc1.rng(n * 512, (n + 1) * 512)
            E("dve", (lambda n_, sl_: (lambda e: e.tensor_scalar(out=bc0.h[:, sl_], in0=psb[bS1[n_]][:, :],
                                                                scalar1=1.0 / CONV_CH, scalar2=None, op0=ALU.mult)))(n, sl),
              r=(psbuf[bS1[n]],), w=(b0,))
            E("dve", (lambda sl_: (lambda e: e.tensor_tensor(out=bc1.h[:, sl_], in0=bc0.h[:, sl_], in1=bc0.h[:, sl_],
                                                            op=ALU.mult)))(sl), r=(b0,), w=(b1,))
            E("dve", (lambda n_, sl_: (lambda e: e.scalar_tensor_tensor(out=bc1.h[:, sl_], in0=psb[bS2[n_]][:, :],
                                                                       scalar=1.0 / CONV_CH, in1=bc1.h[:, sl_],
                                                                       op0=ALU.mult, op1=ALU.subtract)))(n, sl),
              r=(psbuf[bS2[n]], b1), w=(b1,))
            E("dve", (lambda sl_: (lambda e: e.tensor_scalar(out=bc1.h[:, sl_], in0=bc1.h[:, sl_], scalar1=EPS,
                                                            scalar2=None, op0=ALU.add)))(sl), r=(b1,), w=(b1,))
            E("act", (lambda sl_: (lambda e: e.activation(out=bc1.h[:, sl_], in_=bc1.h[:, sl_], func=AF.Sqrt)))(sl),
              r=(b1,), w=(b1,))
            E("dve", (lambda sl_: (lambda e: e.reciprocal(out=bc1.h[:, sl_], in_=bc1.h[:, sl_])))(sl), r=(b1,), w=(b1,))
        bS3 = [bS1[0], bS1[1]]
        for c in range(8):
            for n in range(2):
                sl = slice(n * 512, (n + 1) * 512)
                yb = yv.rng(c * 1024 + n * 512, c * 1024 + (n + 1) * 512)
                b0 = bc0.rng(n * 512, (n + 1) * 512)
                b1 = bc1.rng(n * 512, (n + 1) * 512)
                E("dve", (lambda c_, sl_: (lambda e: e.tensor_tensor(out=yv.h[:, c_, sl_], in0=yv.h[:, c_, sl_],
                                                                    in1=bc0.h[:, sl_], op=ALU.subtract)))(c, sl),
                  r=(yb, b0), w=(yb,))
                E("dve", (lambda c_, sl_: (lambda e: e.tensor_tensor(out=yv.h[:, c_, sl_], in0=yv.h[:, c_, sl_],
                                                                    in1=bc1.h[:, sl_], op=ALU.mult)))(c, sl),
                  r=(yb, b1), w=(yb,))
                E("act", (lambda c_, sl_: (lambda e: e.activation(out=yv.h[:, c_, sl_], in_=yv.h[:, c_, sl_], func=AF.Silu,
                                                                 scale=cvc(LG + c_), bias=cvc(LB + c_))))(c, sl),
                  r=(yb, cv.whole), w=(yb,))
                s2 = sqs[nsq[0] % 4]
                nsq[0] += 1
                E("act", (lambda c_, sl_, s_: (lambda e: e.activation(out=s_.h[:, :], in_=yv.h[:, c_, sl_],
                                                                     func=AF.Square)))(c, sl, s2), r=(yb,), w=(s2.whole,))
                mm(psb[bS3[n]][:, :], ones_b, s2.h[:, :], c == 0, c == 7, (cb.whole, s2.whole), (psbuf[bS3[n]],))
        for n in range(2):
            rstd_psum_inplace(bS3[n], 512, 1.0 / CONV_CH)
        for c in range(8):
            for n in range(2):
                sl = slice(n * 512, (n + 1) * 512)
                yb = yv.rng(c * 1024 + n * 512, c * 1024 + (n + 1) * 512)
                E("dve", (lambda c_, n_, sl_: (lambda e: e.scalar_tensor_tensor(
                    out=mixT.h[:, c_, sl_], in0=yv.h[:, c_, sl_], scalar=cvc(G2 + c_), in1=psb[bS3[n_]][:, :],
                    op0=ALU.mult, op1=ALU.mult)))(c, n, sl), r=(yb, cv.whole, psbuf[bS3[n]]), w=(mixT.sub(c),))

        ckpt('E')
        knT = sbt([128, 8, 2048], BF16, AA0 + 0, "knT")
        qnT = sbt([128, 8, 1024], BF16, AA0 + 32 * KB, "qnT")
        tq = sbt([128, 8, 1024], BF16, AA0 + 48 * KB, "tq")
        VA = sbt([128, 8, 8 * 130], BF16, AA0 + 96 * KB, "VA")
        VB = sbt([128, 8, 8 * 130], BF16, AA0 + 148 * KB, "VB")
        for h in range(N_HEADS):
            ws = next_w()
            banks = [nbank() for _ in range(2)]
            for kc in range(6):
                for n in range(2):
                    mm(psb[banks[n]][:, :], ws.h[:, kc * 128:(kc + 1) * 128], qln.h[:, kc, n * 512:(n + 1) * 512],
                       kc == 0, kc == 5, (ws.whole, qln.sub(kc)), (psbuf[banks[n]],))
            for n in range(2):
                E("act", (lambda bk_, h_, n_: (lambda e: e.copy(out=qnT.h[:, h_, n_ * 512:(n_ + 1) * 512],
                                                               in_=psb[bk_][:, :])))(banks[n], h, n),
                  r=(psbuf[banks[n]],), w=(qnT.sub(h),))
            ws = next_w()
            banks = [nbank() for _ in range(2)]
            for kc in range(6):
                for n in range(2):
                    mm(psb[banks[n]][:, :], ws.h[:, kc * 128:(kc + 1) * 128], qln.h[:, kc, n * 512:(n + 1) * 512],
                       kc == 0, kc == 5, (ws.whole, qln.sub(kc)), (psbuf[banks[n]],))
            for n in range(2):
                E("dve", (lambda bk_, h_, n_: (lambda e: e.tensor_tensor(
                    out=tq.h[:, h_, n_ * 512:(n_ + 1) * 512], in0=psb[bk_][:, :],
                    in1=CS.h[:, 1024 + n_ * 512:1024 + (n_ + 1) * 512], op=ALU.mult)))(banks[n], h, n),
                  r=(psbuf[banks[n]], CS.whole), w=(tq.sub(h),))
        for h in range(N_HEADS):
            ws = next_w()
            banks = [nbank() for _ in range(4)]
            for kc in range(4):
                for n in range(4):
                    mm(psb[banks[n]][:, :], ws.h[:, kc * 128:(kc + 1) * 128], kvn.h[:, kc, n * 512:(n + 1) * 512],
                       kc == 0, kc == 3, (ws.whole, kvn.sub(kc)), (psbuf[banks[n]],))
            for n in range(4):
                if n % 2 == 0:
                    E("act", (lambda bk_, h_, n_: (lambda e: e.copy(out=knT.h[:, h_, n_ * 512:(n_ + 1) * 512],
                                                                   in_=psb[bk_][:, :])))(banks[n], h, n),
                      r=(psbuf[banks[n]],), w=(knT.sub(h),))
                else:
                    E("dve", (lambda bk_, h_, n_: (lambda e: e.tensor_copy(out=knT.h[:, h_, n_ * 512:(n_ + 1) * 512],
                                                                          in_=psb[bk_][:, :])))(banks[n], h, n),
                      r=(psbuf[banks[n]],), w=(knT.sub(h),))
        wv = [next_w(), next_w()]
        E("dve", lambda e: e.tensor_scalar(out=VA.h[:, :, :].rearrange("p t (h d) -> p (t h) d", d=130)[:, :, 128:129],
                                           in0=onesf.h[:, 0:64].rearrange("p (a b) -> p a b", b=1),
                                           scalar1=cvc(PFL), scalar2=None, op0=ALU.mult),
          r=(onesf.whole, cv.whole), w=(VA.whole,))
        E("dve", lambda e: e.memset(VB.h[:, :, :].rearrange("p t (h d) -> p (t h) d", d=130)[:, :, 128:129], 1.0),
          w=(VB.whole,))
        for t in range(16):
            Vt = VA if t < 8 else VB
            tt = t % 8
            banks = [nbank() for _ in range(2)]
            for kc in range(4):
                for hf in range(2):
                    mm(psb[banks[hf]][:, :], kvn.h[:, kc, t * 128:(t + 1) * 128],
                       wv[kc // 2].h[:, (kc % 2) * 1024 + hf * 512:(kc % 2) * 1024 + (hf + 1) * 512],
                       kc == 0, kc == 3, (kvn.sub(kc), wv[kc // 2].whole), (psbuf[banks[hf]],))
            for hf in range(2):
                dst = Vt.h[:, tt, :].rearrange("p (h d) -> p h d", d=130)[:, hf * 4:(hf + 1) * 4, 0:128]
                srcp = psb[banks[hf]][:, :].rearrange("p (h d) -> p h d", d=128)
                if t < 8:
                    E("dve", (lambda d_, s_: (lambda e: e.tensor_scalar(out=d_, in0=s_, scalar1=cvc(PFL), scalar2=None,
                                                                       op0=ALU.mult)))(dst, srcp),
                      r=(psbuf[banks[hf]], cv.whole), w=(Vt.sub(tt),))
                else:
                    E("act", (lambda d_, s_: (lambda e: e.copy(out=d_, in_=s_)))(dst, srcp),
                      r=(psbuf[banks[hf]],), w=(Vt.sub(tt),))

        ckpt('C')
        attn = sbt([128, 8, 1024], F32, AA0 + 116 * KB, "attn")
        PT = [sbt([128, 512], BF16, AA0 + 170 * KB + i * 1024, f"PT{i}") for i in range(4)]
        rc = sbt([128, 64], F32, AA0 + 174 * KB, "rc")
        nrc = [0]
        ST_B = [4, 5, 6, 7]
        ACC_B = [0, 1, 2, 3]
        items = []
        for h in range(N_HEADS):
            for qb in range(2):
                for kc in range(8 + 4 * qb + 4):
                    items.append((h, qb, kc))
        LOOK = 3

        def d_qk(n):
            h, qb, kc = items[n]
            q0 = max(kc - 8 - 4 * qb, 0) * 128
            bk = ST_B[n % 4]
            mm(psb[bk][:, q0:512], knT.h[:, h, kc * 128:(kc + 1) * 128],
               qnT.h[:, h, qb * 512 + q0:qb * 512 + 512], True, False,
               (knT.sub(h), qnT.sub(h)), (psbuf[bk],))
            mm(psb[bk][:, q0:512], kr2.h[:, kc * 128:(kc + 1) * 128],
               tq.h[:, h, qb * 512 + q0:qb * 512 + 512], False, True,
               (kr2.whole, tq.sub(h)), (psbuf[bk],))

        def d_rest(n):
            h, qb, kc = items[n]
            j = kc - 8 - 4 * qb
            i0 = max(j, 0)
            q0 = i0 * 128
            bk = ST_B[n % 4]
            pt = PT[n % 4]
            Vt = VA if kc < 8 else VB
            E("act", lambda e: e.activation(out=pt.h[:, q0:512], in_=psb[bk][:, q0:512], func=AF.Exp, scale=SCALE),
              r=(psbuf[bk],), w=(pt.whole,))
            if j >= 0:
                E("dve", lambda e: e.tensor_tensor(out=pt.h[:, q0:q0 + 128], in0=pt.h[:, q0:q0 + 128], in1=tri,
                                                   op=ALU.mult), r=(pt.whole, cb.whole), w=(pt.whole,))
            for i in range(i0, 4):
                last = 8 + 4 * qb + i
                ab = ACC_B[i]
                mm(psb[ab][:, 0:129], pt.h[:, i * 128:(i + 1) * 128], Vt.h[:, kc % 8, h * 130:h * 130 + 129],
                   kc == 0, kc == last, (pt.whole, Vt.sub(kc % 8)), (psbuf[ab],))
                if kc == last:
                    col = nrc[0] % 64
                    nrc[0] += 1
                    rb = rc.rng(col, col + 1)
                    E("dve", (lambda ab_, col_: (lambda e: e.reciprocal(out=rc.h[:, col_:col_ + 1],
                                                                       in_=psb[ab_][:, 128:129])))(ab, col),
                      r=(psbuf[ab],), w=(rb,))
                    E("dve", (lambda ab_, col_, i_: (lambda e: e.tensor_scalar(
                        out=attn.h[:, qb * 4 + i_, h * 128:(h + 1) * 128], in0=psb[ab_][:, 0:128],
                        scalar1=rc.h[:, col_:col_ + 1], scalar2=None, op0=ALU.mult)))(ab, col, i),
                      r=(psbuf[ab], rb), w=(attn.rng((qb * 4 + i) * 1024 + h * 128, (qb * 4 + i) * 1024 + (h + 1) * 128),))

        for n in range(len(items) + LOOK):
            if n < len(items):
                d_qk(n)
            if n >= LOOK:
                d_rest(n - LOOK)

        ckpt('D')
        gbcD = sbt([128, 1024], F32, AA0 + 148 * KB, "gbcD")
        junkD = [sbt([128, 1024], BF16, AA0 + (152 + 2 * i) * KB, f"junkD{i}") for i in range(2)]
        xnD = [sbt([128, 1024], BF16, AA0 + (156 + 2 * i) * KB, f"xnD{i}") for i in range(2)]
        E("sync", lambda e: e.dma_start(out=gbcD.h[:, :], in_=gbc_d[4][:, 0:1024]), w=(gbcD.whole,), key="gbc")
        cD_ss = stcol(8)
        cD_rs = stcol(8)
        brsD = {}
        usedD = {}

        def mixT_attn_tile(i):
            return tuple(P.buf("sb", mixT.off + kc * 2048 + i * 256, mixT.off + kc * 2048 + (i + 1) * 256)
                         for kc in range(8, 16))

        def Dn_s(i):
            brsD[i] = nt_stats(attn.h[:, i, :], attn.sub(i), junkD[i % 2], 1024, cD_ss + i, cD_rs + i)

        def Dn_a(i):
            usedD[i] = nt_apply_a(attn.h[:, i, :], attn.sub(i), brsD[i], gbcD, xnD[i % 2], 1024, cD_rs + i,
                                  banks=[4 + i % 4])
            nt_apply_b(usedD[i], (lambda g: mixT.h[:, 8:16, i * 128:(i + 1) * 128]), mixT_attn_tile(i), ["dve"])

        ckpt('Dn')
        w_out = sbt([128, 16, 2048], BF16, AA0 + 0, "w_out")
        for kc in (0, 1, 8, 12, 2, 3, 9, 13, 4, 5, 10, 14, 6, 7, 11, 15):
            E("pool", (lambda kc_: (lambda e: e.dma_start(out=w_out.h[:, kc_, :],
                                                          in_=w_out_p[:, kc_ * 2048:(kc_ + 1) * 2048])))(kc),
              w=(w_out.sub(kc),), key=f"wout{kc % 4}")
        xF = [sbt([128, 2048], F32, AA0 + (96 + 8 * i) * KB, f"xF{i}") for i in range(2)]
        gpost = sbt([128, 2048], F32, AA0 + 112 * KB, "gpost")
        gpre = sbt([128, 2048], F32, AA0 + 120 * KB, "gpre")
        x1t = sbt([128, 2048], F32, AA0 + 128 * KB, "x1t")
        xnF = sbt([128, 2048], BF16, AA0 + 136 * KB, "xnF")
        hfT = sbt([128, 16, 1024], BF16, AA0 + 140 * KB, "hfT")
        junkF = sbt([128, 2048], BF16, AA0 + 172 * KB, "junkF")
        cF_p = stcol(32)
        cF_ss = stcol(8)
        cF_rs = stcol(8)
        cF_ss2 = stcol(8)
        cF_rs2 = stcol(8)
        brsF = {}
        brsF2 = {}

        KC_ORDER = [0, 1, 2, 3, 4, 5, 6, 8, 9, 10, 12, 13, 14, 7, 11, 15]

        def F1a(i):
            s_ = i % 2
            E("sync", lambda e: e.dma_start(out=xF[s_].h[:, :], in_=x_own[i * 128:(i + 1) * 128, :]),
              w=(xF[s_].whole,), key=f"xF{s_}")
            for cbk in (2, 3, 0, 1):
                bk = (i % 2) * 4 + cbk
                for kc in KC_ORDER:
                    mb = mixT.sub(kc) if kc < 8 else P.buf("sb", mixT.off + kc * 2048 + i * 256,
                                                           mixT.off + kc * 2048 + (i + 1) * 256)
                    mm(psb[bk][:, :], mixT.h[:, kc, i * 128:(i + 1) * 128], w_out.h[:, kc, cbk * 512:(cbk + 1) * 512],
                       kc == KC_ORDER[0], kc == KC_ORDER[-1], (mb, w_out.sub(kc)), (psbuf[bk],))

        def F1b(i):
            for cbk in (2, 3, 0, 1):
                bk = (i % 2) * 4 + cbk
                pc = cF_p + i * 4 + cbk
                E("act", (lambda bk_, pc_, c_: (lambda e: e.activation(
                    out=junkF.h[:, c_ * 512:(c_ + 1) * 512], in_=psb[bk_][:, :], func=AF.Square,
                    accum_out=st.h[:, pc_:pc_ + 1])))(bk, pc, cbk),
                  r=(psbuf[bk],), w=(junkF.rng(cbk * 512, (cbk + 1) * 512), st.rng(pc, pc + 1)))
            E("dve", lambda e: e.tensor_reduce(out=st.h[:, cF_ss + i:cF_ss + i + 1],
                                               in_=st.h[:, cF_p + 4 * i:cF_p + 4 * i + 4],
                                               axis=mybir.AxisListType.X, op=ALU.add),
              r=(st.rng(cF_p + 4 * i, cF_p + 4 * i + 4),), w=(st.rng(cF_ss + i, cF_ss + i + 1),))
            brsF[i] = rstd_from_ss(cF_ss + i, cF_rs + i, 1, 1.0 / D_MODEL)

        def F2(i):
            s_ = i % 2
            for cbk in range(4):
                bk = (i % 2) * 4 + cbk
                sl = slice(cbk * 512, (cbk + 1) * 512)
                E("dve", (lambda bk_, sl_: (lambda e: e.scalar_tensor_tensor(
                    out=x1t.h[:, sl_], in0=psb[bk_][:, :], scalar=st.h[:, cF_rs + i:cF_rs + i + 1], in1=gpost.h[:, sl_],
                    op0=ALU.mult, op1=ALU.mult)))(bk, sl),
                  r=(psbuf[bk], brsF[i], gpost.whole), w=(x1t.rng(cbk * 512, (cbk + 1) * 512),))
            E("dve", lambda e: e.tensor_tensor(out=x1t.h[:, :], in0=x1t.h[:, :], in1=xF[s_].h[:, :], op=ALU.add),
              r=(x1t.whole, xF[s_].whole), w=(x1t.whole,))
            E("sync", lambda e: e.dma_start(out=x1_d[i * 128:(i + 1) * 128, :], in_=x1t.h[:, :]),
              r=(x1t.whole,), w=(), key="x1w")
            brsF2[i] = nt_stats(x1t.h[:, :], x1t.whole, xnF, 2048, cF_ss2 + i, cF_rs2 + i)

        def F3(i):
            nt_apply(x1t.h[:, :], x1t.whole, brsF2[i], gpre, xnF, 2048, cF_rs2 + i,
                     (lambda g: hfT.h[:, g * 8:(g + 1) * 8, i * 128:(i + 1) * 128]),
                     (hfT.whole,), ["act", "dve"], banks=[(i % 2) * 4, (i % 2) * 4 + 1])

        Dn_s(0)
        Dn_s(1)
        Dn_a(0)
        Dn_s(2)
        Dn_a(1)
        F1a(0)
        for i in range(2, 8):
            if i + 1 < 8:
                Dn_s(i + 1)
            Dn_a(i)
        E("sync", lambda e: e.dma_start(out=gpost.h[:, :], in_=gbc_d[1]), w=(gpost.whole,), key="gbc2")
        E("sync", lambda e: e.dma_start(out=gpre.h[:, :], in_=gbc_d[2]), w=(gpre.whole,), key="gbc2")
        F1b(0)
        for i in range(8):
            if i + 1 < 8:
                F1a(i + 1)
            F2(i)
            F3(i)
            if i + 1 < 8:
                F1b(i + 1)
        ckpt('F')
        actT = sbt([128, NFB, 1024], BF16, AA0 + 0, "actT")
        sgf = [sbt([128, 512], F32, AA0 + 172 * KB + i * 2048, f"sgf{i}") for i in range(2)]
        nsg = [0]
        for f in range(NFB):
            wg = next_w()
            wu = next_w()
            bg = [nbank(), nbank()]
            bu = [nbank(), nbank()]
            for kc in range(16):
                for n in range(2):
                    mm(psb[bg[n]][:, :], wg.h[:, kc * 128:(kc + 1) * 128], hfT.h[:, kc, n * 512:(n + 1) * 512],
                       kc == 0, kc == 15, (wg.whole, hfT.whole), (psbuf[bg[n]],))
            for kc in range(16):
                for n in range(2):
                    mm(psb[bu[n]][:, :], wu.h[:, kc * 128:(kc + 1) * 128], hfT.h[:, kc, n * 512:(n + 1) * 512],
                       kc == 0, kc == 15, (wu.whole, hfT.whole), (psbuf[bu[n]],))
            for n in range(2):
                sgt = sgf[nsg[0] % 2]
                nsg[0] += 1
                E("act", (lambda bk_, s_: (lambda e: e.activation(out=s_.h[:, :], in_=psb[bk_][:, :], func=AF.Silu)))(bg[n], sgt),
                  r=(psbuf[bg[n]],), w=(sgt.whole,))
                E("dve", (lambda bk_, s_, f_, n_: (lambda e: e.tensor_tensor(
                    out=actT.h[:, f_, n_ * 512:(n_ + 1) * 512], in0=psb[bk_][:, :], in1=s_.h[:, :], op=ALU.mult)))(bu[n], sgt, f, n),
                  r=(psbuf[bu[n]], sgt.whole), w=(actT.sub(f),))

        ckpt('G1')
        ff = sbt([128, 8, 2048], F32, AA0 + 88 * KB, "ff")
        xr = [sbt([128, 2048], F32, AA0 + 8 * i * KB, f"xr{i}") for i in range(8)]
        gffn = sbt([128, 2048], F32, AA0 + 168 * KB, "gffn")
        junkG = [sbt([128, 512], BF16, AA0 + 176 * KB + i * 1024, f"junkG{i}") for i in range(2)]
        E("sync", lambda e: e.dma_start(out=gffn.h[:, :], in_=gbc_d[3]), w=(gffn.whole,), key="gbc3")
        cG_p = stcol(32)
        cG_ss = stcol(8)
        cG_rs = stcol(8)
        for cbk in range(4):
            for fg in range(11):
                ws = next_w()
                for i in range(8):
                    for fb in range(4):
                        fidx = fg * 4 + fb
                        mm(psb[i][:, :], actT.h[:, fidx, i * 128:(i + 1) * 128], ws.h[:, fb * 512:(fb + 1) * 512],
                           fg == 0 and fb == 0, fg == 10 and fb == 3, (actT.sub(fidx), ws.whole), (psbuf[i],))
            for i in range(8):
                sl = slice(cbk * 512, (cbk + 1) * 512)
                pc = cG_p + i * 4 + cbk
                fb_ = ff.rng(i * 2048 + cbk * 512, i * 2048 + (cbk + 1) * 512)
                E("dve", (lambda i_, sl_: (lambda e: e.tensor_tensor(out=ff.h[:, i_, sl_], in0=psb[i_][:, :],
                                                                     in1=gffn.h[:, sl_], op=ALU.mult)))(i, sl),
                  r=(psbuf[i], gffn.whole), w=(fb_,))
                E("act", (lambda i_, pc_: (lambda e: e.activation(out=junkG[i_ % 2].h[:, :], in_=psb[i_][:, :], func=AF.Square,
                                                                 accum_out=st.h[:, pc_:pc_ + 1])))(i, pc),
                  r=(psbuf[i],), w=(junkG[i % 2].whole, st.rng(pc, pc + 1)))
        E("dve", lambda e: e.tensor_reduce(out=st.h[:, cG_ss:cG_ss + 8],
                                           in_=st.h[:, cG_p:cG_p + 32].rearrange("p (a b) -> p a b", b=4),
                                           axis=mybir.AxisListType.X, op=ALU.add),
          r=(st.rng(cG_p, cG_p + 32),), w=(st.rng(cG_ss, cG_ss + 8),))
        brsG = rstd_from_ss(cG_ss, cG_rs, 8, 1.0 / D_MODEL)
        for i in range(8):
            E("sync", (lambda i_: (lambda e: e.dma_start(out=xr[i_].h[:, :], in_=x1_d[i_ * 128:(i_ + 1) * 128, :])))(i),
              w=(xr[i].whole,), key=f"xr{i % 2}")
            P.q["sync"][-1].waits.append(("dma", "x1w", P.dma_cnt["x1w"]))
        for i in range(8):
            E("dve", (lambda i_: (lambda e: e.scalar_tensor_tensor(
                out=ff.h[:, i_, :], in0=ff.h[:, i_, :], scalar=st.h[:, cG_rs + i_:cG_rs + i_ + 1], in1=xr[i_].h[:, :],
                op0=ALU.mult, op1=ALU.add)))(i), r=(ff.sub(i), brsG, xr[i].whole), w=(ff.sub(i),))
            E("sync", (lambda i_: (lambda e: e.dma_start(out=out_d[i_ * 128:(i_ + 1) * 128, :], in_=ff.h[:, i_, :])))(i),
              r=(ff.sub(i),), w=(), key="outw")
        fin = E("sync", None)
        fin.waits.append(("dma", "outw", P.dma_cnt["outw"]))
        assert wcur[0] == len(wpieces), (wcur[0], len(wpieces))


    except _Stop:
        pass

    fin_all = P.op("sync", None)
    for k_, v_ in P.dma_cnt.items():
        fin_all.waits.append(("dma", k_, v_))

    for e_ in ENGS:
        cnt = 0
        for ins in P.q[e_]:
            if ins.signal and not ins.is_dma:
                cnt += 1
                ins.value = cnt
    keys = sorted(P.dma_cnt.keys())
    sem_ctx = {}
    sems_eng = {e_: nc.alloc_semaphore(f"s_{e_}") for e_ in ENGS}
    sems_key = {k: nc.alloc_semaphore(f"d_{k}") for k in keys}

    def replay(ename, eng):
        waited = {}
        for ins in P.q[ename]:
            for w in ins.waits:
                if w[0] == "eng":
                    p = w[1]
                    sem, val = sems_eng[p.eng], p.value
                else:
                    sem, val = sems_key[w[1]], w[2]
                k = id(sem)
                if waited.get(k, 0) < val:
                    eng.wait_ge(sem, val)
                    waited[k] = val
            if ins.fn is None:
                continue
            bi = ins.fn(eng)
            if ins.is_dma:
                bi.then_inc(sems_key[ins.key], 16)
            elif ins.signal:
                bi.then_inc(sems_eng[ename], 1)

    with nc.Block() as block:
        @block.sync
        def _(e):
            replay("sync", e)

        @block.scalar
        def _(e):
            replay("act", e)

        @block.vector
        def _(e):
            replay("dve", e)

        @block.gpsimd
        def _(e):
            replay("pool", e)

        @block.tensor
        def _(e):
            replay("pe", e)

    stats = {e_: len(P.q[e_]) for e_ in ENGS}
    stats["sig"] = {e_: sum(1 for i in P.q[e_] if i.signal and not i.is_dma) for e_ in ENGS}
    return nc, stats


def _blocks_k(w, ncols_per_block):
    K, N = w.shape
    kc = K // 128
    nb = N // ncols_per_block
    a = w.reshape(kc, 128, nb, ncols_per_block).transpose(2, 1, 0, 3)
    return np.ascontiguousarray(a.reshape(nb, 128, kc * ncols_per_block))


def prepare_inputs(x, positions, pre_mix_norm, w_in, q_norm, w_uq, kv_norm, w_ukv, conv_w, conv_b, conv_ln_g,
                   conv_ln_b, conv_out_norm, attn_out_norm, w_out, post_mix_norm, pre_ffn_norm, w_gate, w_up,
                   w_down, post_ffn_norm):
    f = np.float32
    x = np.asarray(x, f)
    positions = np.asarray(positions, np.int32)
    w_in = np.asarray(w_in, f)[0]
    w_uq = np.asarray(w_uq, f)[0]
    w_ukv = np.asarray(w_ukv, f)[0]
    w_out = np.asarray(w_out, f)[0]
    w_gate = np.asarray(w_gate, f)[0]
    w_up = np.asarray(w_up, f)[0]
    w_down = np.asarray(w_down, f)[0]

    c1 = 2 * CONV_CH
    c2 = c1 + Q_LORA
    c3 = c2 + KV_LORA
    cols = []
    cols += list(range(c2, c3))
    cols += list(range(c3, c3 + 64)) + list(range(c3 + 32, c3 + 64)) + list(range(c3, c3 + 32))
    cols += list(range(c1, c2))
    for c in range(8):
        cols += list(range(CONV_CH + c * 128, CONV_CH + (c + 1) * 128))
        cols += list(range(c * 128, (c + 1) * 128))
    w_in_p = _blocks_k(w_in[:, cols], 128)
    cols = []
    for h in range(N_HEADS):
        b0 = h * 192
        cols += list(range(b0, b0 + 128))
        cols += list(range(b0 + 128, b0 + 192)) + list(range(b0 + 160, b0 + 192)) + list(range(b0 + 128, b0 + 160))
    w_uq_p = _blocks_k(w_uq[:, cols], 128)
    kcols = []
    vcols = []
    for h in range(N_HEADS):
        kcols += list(range(h * 256, h * 256 + 128))
        vcols += list(range(h * 256 + 128, h * 256 + 256))
    w_uk_p = _blocks_k(w_ukv[:, kcols], 128)
    wv = w_ukv[:, vcols]
    w_uv_p = np.ascontiguousarray(wv.reshape(2, 2, 128, 1024).transpose(0, 2, 1, 3).reshape(2, 128, 2048))
    w_out_p = np.ascontiguousarray(w_out.reshape(16, 128, 2048).transpose(1, 0, 2).reshape(128, 16 * 2048))
    g_p = _blocks_k(w_gate, 128)
    u_p = _blocks_k(w_up, 128)
    w_gu_p = np.ascontiguousarray(np.stack([g_p, u_p], axis=1).reshape(88, 128, 2048))
    wd = w_down.reshape(11, 4, 128, 4, 512)
    w_dn_p = np.ascontiguousarray(wd.transpose(3, 0, 2, 1, 4).reshape(44, 128, 2048))

    c_bf = np.zeros((128, 512), np.float32)
    c_bf[:, 0:128] = np.eye(128)
    c_bf[:, 128:256] = 1.0
    e64 = np.eye(64)
    c_bf[:, 256:384] = np.block([[e64, e64], [e64, e64]])
    kk = np.arange(128)[:, None]
    qq = np.arange(128)[None, :]
    c_bf[:, 384:512] = (qq >= kk).astype(np.float32)
    c_bf = c_bf.astype(ml_dtypes.bfloat16)

    cvec = np.zeros((128, NCV), f)
    cw = np.asarray(conv_w, f)[0]
    for c in range(8):
        cvec[:, CW + c * 31:CW + (c + 1) * 31] = cw[:, c * 128:(c + 1) * 128].T
    cvec[:, CB:CB + 8] = np.asarray(conv_b, f)[0].reshape(8, 128).T
    cvec[:, LG:LG + 8] = np.asarray(conv_ln_g, f)[0].reshape(8, 128).T
    cvec[:, LB:LB + 8] = np.asarray(conv_ln_b, f)[0].reshape(8, 128).T
    cvec[:, G2:G2 + 8] = np.asarray(conv_out_norm, f)[0].reshape(8, 128).T
    cvec[:, GQ:GQ + 6] = np.asarray(q_norm, f)[0].reshape(6, 128).T
    cvec[:, GKV:GKV + 4] = np.asarray(kv_norm, f)[0].reshape(4, 128).T
    inv_freq = (np.float32(10000.0) ** (-np.arange(0, 64, 2, dtype=np.float32) / np.float32(64))).astype(f)
    cvec[:, IFQ] = np.tile(inv_freq, 4)
    cvec[0:64, PHS] = np.float32(np.pi / 2)
    cvec[:, SGN] = 1.0
    cvec[64:96, SGN] = -1.0

    gbc = np.zeros((5, 128, 2048), f)
    gbc[0] = np.asarray(pre_mix_norm, f)[0][None, :]
    gbc[1] = np.asarray(post_mix_norm, f)[0][None, :]
    gbc[2] = np.asarray(pre_ffn_norm, f)[0][None, :]
    gbc[3] = np.asarray(post_ffn_norm, f)[0][None, :]
    gbc[4, :, 0:1024] = np.asarray(attn_out_norm, f)[0][None, :]

    shared = dict(c_bf=c_bf, gbc=gbc, w_in_p=w_in_p, w_uq_p=w_uq_p, w_uk_p=w_uk_p, w_uv_p=w_uv_p,
                  w_out_p=w_out_p, w_gu_p=w_gu_p, w_dn_p=w_dn_p)
    in_maps = []
    for core in range(8):
        b, half = core // 2, core % 2
        m = dict(shared)
        m["x_own"] = np.ascontiguousarray(x[b, half * TOWN:(half + 1) * TOWN])
        cvc = cvec.copy()
        pos = np.zeros((2048,), np.int32)
        if half == 1:
            m["x_prev"] = np.ascontiguousarray(x[b, 0:TOWN])
            pos[:] = positions[b, 0:2048]
            cvc[:, PFL] = 1.0
        else:
            m["x_prev"] = np.zeros((TOWN, D_MODEL), f)
            pos[1024:] = positions[b, 0:1024]
            cvc[:, PFL] = 0.0
        m["cvec"] = cvc
        m["pos_bc"] = np.ascontiguousarray(np.broadcast_to(pos[None, :], (128, 2048)))
        in_maps.append(m)
    return in_maps


_CACHE = {}


def kernel(**inputs):
    if "nc" not in _CACHE:
        _CACHE["nc"], _CACHE["stats"] = build_program()
    nc = _CACHE["nc"]
    in_maps = prepare_inputs(**inputs)
    res = run_bass_kernel_spmd(nc, in_maps, core_ids=list(range(8)))
    out = np.zeros((BATCH, SEQ, D_MODEL), np.float32)
    for core in range(8):
        b, half = core // 2, core % 2
        out[b, half * TOWN:(half + 1) * TOWN] = res.results[core]["out"]
    return out
```

```python
import os
import numpy as np
import ml_dtypes
import concourse.bass as bass
import concourse.mybir as mybir
from concourse.bass_utils import run_bass_kernel_spmd

F32 = mybir.dt.float32
BF16 = mybir.dt.bfloat16
I32 = mybir.dt.int32
AF = mybir.ActivationFunctionType
ALU = mybir.AluOpType
PI = float(np.pi)

D_MODEL = 2048
SEQ = 2048
BATCH = 4
TOWN = 1024
CONV_CH = 1024
CONV_K = 31
N_HEADS = 8
Q_LORA = 768
KV_LORA = 512
D_FF = 5632
NFB = D_FF // 128
EPS = 1e-6
SCALE = 192 ** -0.5

CW = 0
CB = 248
LG = 256
LB = 264
G2 = 272
GQ = 280
GKV = 286
IFQ = 290
PHS = 291
SGN = 292
PFL = 293
NCV = 320

KB = 1024


class Buf:
    __slots__ = ("space", "lo", "hi", "name")

    def __init__(self, space, lo, hi, name=""):
        self.space, self.lo, self.hi, self.name = space, lo, hi, name


class Ins:
    __slots__ = ("eng", "fn", "is_dma", "key", "signal", "value", "waits", "dma_val", "idx")

    def __init__(self, eng, fn, is_dma, key):
        self.eng, self.fn, self.is_dma, self.key = eng, fn, is_dma, key
        self.idx = -1
        self.signal = False
        self.value = None
        self.waits = []
        self.dma_val = None


ENGS = ["sync", "act", "dve", "pool", "pe"]


class Prog:
    def __init__(self):
        self.q = {e: [] for e in ENGS}
        self.wr = {"sb": [], "ps": []}
        self.rd = {"sb": [], "ps": []}
        self.dma_cnt = {}

    def buf(self, space, lo, hi, name=""):
        return Buf(space, lo, hi, name)

    def op(self, eng, fn, reads=(), writes=(), dma_key=None):
        ins = Ins(eng, fn, dma_key is not None, dma_key)
        ins.idx = len(self.q[eng])
        need_eng = {}
        need_dma = {}

        def add(p):
            if p is ins:
                return
            if p.is_dma:
                need_dma[p.key] = 1
                return
            if p.eng == eng and not ins.is_dma and eng == "pe":
                return
            cur = need_eng.get(p.eng)
            if cur is None or cur.idx < p.idx:
                need_eng[p.eng] = p

        for b in reads:
            for w in self.wr[b.space]:
                if w[0] < b.hi and b.lo < w[1]:
                    add(w[2])
            if b.space == "ps" and eng != "pe":
                lo = b.lo // 2048 * 2048
                hi = -(-b.hi // 2048) * 2048
                for r in self.rd["ps"]:
                    if r[0] < hi and lo < r[1] and r[3].eng != eng and r[3].eng != "pe":
                        add(r[3])
        for b in writes:
            for w in self.wr[b.space]:
                if w[0] < b.hi and b.lo < w[1]:
                    add(w[2])
            for r in self.rd[b.space]:
                if r[0] < b.hi and b.lo < r[1]:
                    add(r[3])
        for p in need_eng.values():
            p.signal = True
            ins.waits.append(("eng", p))
        for k in need_dma:
            ins.waits.append(("dma", k, self.dma_cnt[k]))
        if dma_key is not None:
            self.dma_cnt[dma_key] = self.dma_cnt.get(dma_key, 0) + 16
            ins.dma_val = self.dma_cnt[dma_key]
        for b in writes:
            sp = b.space
            self.wr[sp] = [w for w in self.wr[sp] if not (b.lo <= w[0] and w[1] <= b.hi)]
            self.wr[sp].append([b.lo, b.hi, ins])
            self.rd[sp] = [r for r in self.rd[sp] if not (b.lo <= r[0] and r[1] <= b.hi)]
        rk = ("dma", dma_key) if ins.is_dma else eng
        for b in reads:
            lst = self.rd[b.space]
            for r in lst:
                if r[0] == b.lo and r[1] == b.hi and r[2] == rk:
                    r[3] = ins
                    break
            else:
                lst.append([b.lo, b.hi, rk, ins])
        self.q[eng].append(ins)
        return ins


def build_program(stop_after=None, tensors=None):
    nc = bass.Bass("TRN2", target_bir_lowering=False)
    P = Prog()

    def din(name, shape, dt):
        return nc.dram_tensor(name, list(shape), dt, kind="ExternalInput").ap()

    x_own = din("x_own", [TOWN, D_MODEL], F32)
    x_prev = din("x_prev", [TOWN, D_MODEL], F32)
    pos_bc = din("pos_bc", [128, 2048], I32)
    c_bf = din("c_bf", [128, 512], BF16)
    cvec_d = din("cvec", [128, NCV], F32)
    gbc_d = din("gbc", [5, 128, 2048], F32)
    w_in_p = din("w_in_p", [27, 128, 2048], F32)
    w_uq_p = din("w_uq_p", [16, 128, 768], F32)
    w_uk_p = din("w_uk_p", [8, 128, 512], F32)
    w_uv_p = din("w_uv_p", [2, 128, 2048], F32)
    w_out_p = din("w_out_p", [128, 16 * 2048], F32)
    w_gu_p = din("w_gu_p", [88, 128, 2048], F32)
    w_dn_p = din("w_dn_p", [44, 128, 2048], F32)
    out_d = nc.dram_tensor("out", [TOWN, D_MODEL], F32, kind="ExternalOutput").ap()
    x1_d = nc.dram_tensor("x1_scratch", [TOWN, D_MODEL], F32).ap()
    dbg = {}

    base = (nc.sbuf_base + 63) // 64 * 64
    CONST0 = base
    WR0 = CONST0 + 4 * KB
    AA0 = WR0 + 24 * KB
    assert AA0 + 178 * KB <= nc.sbuf_top, (AA0 + 178 * KB, nc.sbuf_top)

    def dsz(dt):
        return 4 if dt in (F32, I32) else 2

    class T:
        def __init__(self, name, shape, dt, off):
            self.h = nc.alloc_sbuf_tensor_at(name, list(shape), dt, offset=off)
            self.off = off
            self.shape = shape
            self.dt = dt
            self.nbytes = int(np.prod(shape[1:])) * dsz(dt)
            self.whole = P.buf("sb", off, off + self.nbytes, name)
            self._subs = {}
            if tensors is not None:
                tensors[name] = self

        def sub(self, i, n=None):
            key = (i, n)
            if key not in self._subs:
                slab = self.nbytes // self.shape[1]
                cnt = 1 if n is None else n
                self._subs[key] = P.buf("sb", self.off + i * slab, self.off + (i + cnt) * slab)
            return self._subs[key]

        def rng(self, lo_el, hi_el):
            key = ("r", lo_el, hi_el)
            if key not in self._subs:
                self._subs[key] = P.buf("sb", self.off + lo_el * dsz(self.dt), self.off + hi_el * dsz(self.dt))
            return self._subs[key]

    _names = [0]

    def sbt(shape, dt, off, name=None):
        _names[0] += 1
        return T(name or f"t{_names[0]}", shape, dt, off)

    cb = sbt([128, 512], BF16, CONST0, "cb")
    ident = cb.h[:, 0:128]
    ones_b = cb.h[:, 128:256]
    f2 = cb.h[:, 256:384]
    tri = cb.h[:, 384:512]
    cv = sbt([128, NCV], F32, CONST0 + 1024, "cv")
    st = sbt([128, 320], F32, CONST0 + 1024 + NCV * 4, "st")
    onesf = sbt([128, 64], F32, CONST0 + 1024 + NCV * 4 + 1280, "onesf")
    assert CONST0 + 1024 + NCV * 4 + 1280 + 256 <= WR0

    def cvc(col):
        return cv.h[:, col:col + 1]

    _stn = [0]

    def stcol(n=1):
        c = _stn[0]
        _stn[0] += n
        assert _stn[0] <= 320
        return c

    NSLOT = 6
    wslots = [sbt([128, 2048], BF16, WR0 + i * 4 * KB, f"ws{i}") for i in range(NSLOT)]
    wpieces = []
    wstate = {"issued": 0}

    psb = [nc.alloc_psum_tensor(f"psb{i}", [128, 512], F32) for i in range(8)]
    psbuf = [P.buf("ps", i * 2048, (i + 1) * 2048, f"ps{i}") for i in range(8)]
    psbf = [psb[i][:, :].bitcast(BF16) for i in range(8)]
    _bank = [0]

    def nbank():
        b = _bank[0] % 8
        _bank[0] += 1
        return b

    _psub = {}

    def psub(bank, lo, hi):
        k = (bank, lo, hi)
        if k not in _psub:
            _psub[k] = P.buf("ps", bank * 2048 + lo * 4, bank * 2048 + hi * 4)
        return _psub[k]

    def E(eng, fn, r=(), w=(), key=None):
        return P.op(eng, fn, r, w, key)

    def mm(out, lhsT, rhs, start, stop, r, w):
        return E("pe", lambda e: e.matmul(out, lhsT=lhsT, rhs=rhs, start=start, stop=stop), r, w)

    def issue_weights(upto, after=()):
        while wstate["issued"] < min(upto, len(wpieces)):
            i = wstate["issued"]
            src, n = wpieces[i]
            slot = wslots[i % NSLOT]
            E("pool", (lambda s, n_, sl: (lambda e: e.dma_start(out=sl.h[:, 0:n_], in_=s)))(src, n, slot),
              r=after, w=(slot.whole,), key=f"ws{i % NSLOT}")
            wstate["issued"] += 1

    wcur = [0]

    def next_w(prefetch=4):
        i = wcur[0]
        wcur[0] += 1
        issue_weights(i + 1 + prefetch)
        return wslots[i % NSLOT]

    for j in range(27):
        wpieces.append((w_in_p[j], 2048))
    for j in range(16):
        wpieces.append((w_uq_p[j], 768))
    for j in range(8):
        wpieces.append((w_uk_p[j], 512))
    for j in range(2):
        wpieces.append((w_uv_p[j], 2048))
    for j in range(88):
        wpieces.append((w_gu_p[j], 2048))
    for j in range(44):
        wpieces.append((w_dn_p[j], 2048))

    class _Stop(Exception):
        pass

    def ckpt(name):
        if stop_after == name:
            raise _Stop()

    try:
        E("sync", lambda e: e.dma_start(out=cb.h[:, :], in_=c_bf), w=(cb.whole,), key="const")
        E("sync", lambda e: e.dma_start(out=cv.h[:, :], in_=cvec_d), w=(cv.whole,), key="const")
        E("dve", lambda e: e.memset(st.h[:, :], 0.0), w=(st.whole,))
        E("dve", lambda e: e.memset(onesf.h[:, :], 1.0), w=(onesf.whole,))

        def rstd_from_ss(c_ss, c_out, n, inv_n):
            c_ms = stcol(n)
            c_sq = stcol(n)
            bs = st.rng(c_ss, c_ss + n)
            bm = st.rng(c_ms, c_ms + n)
            bq = st.rng(c_sq, c_sq + n)
            bo = st.rng(c_out, c_out + n)
            E("dve", lambda e: e.tensor_scalar(out=st.h[:, c_ms:c_ms + n], in0=st.h[:, c_ss:c_ss + n], scalar1=inv_n,
                                               scalar2=EPS, op0=ALU.mult, op1=ALU.add), r=(bs,), w=(bm,))
            E("act", lambda e: e.activation(out=st.h[:, c_sq:c_sq + n], in_=st.h[:, c_ms:c_ms + n], func=AF.Sqrt),
              r=(bm,), w=(bq,))
            E("dve", lambda e: e.reciprocal(out=st.h[:, c_out:c_out + n], in_=st.h[:, c_sq:c_sq + n]), r=(bq,), w=(bo,))
            return bo

        def rstd_psum_inplace(bank, n, inv_n):
            b = psbuf[bank]
            ap = psb[bank][:, 0:n]
            E("dve", lambda e: e.tensor_scalar(out=ap, in0=ap, scalar1=inv_n, scalar2=EPS, op0=ALU.mult, op1=ALU.add),
              r=(b,), w=(b,))
            E("act", lambda e: e.activation(out=ap, in_=ap, func=AF.Sqrt), r=(b,), w=(b,))
            E("dve", lambda e: e.reciprocal(out=ap, in_=ap), r=(b,), w=(b,))

        def nt_stats(src_ap, src_buf, junk_t, width, c_ss, c_rs):
            bs = st.rng(c_ss, c_ss + 1)
            E("act", lambda e: e.activation(out=junk_t.h[:, 0:width], in_=src_ap, func=AF.Square,
                                            accum_out=st.h[:, c_ss:c_ss + 1]), r=(src_buf,), w=(junk_t.whole, bs))
            return rstd_from_ss(c_ss, c_rs, 1, 1.0 / width)

        def nt_apply_a(src_ap, src_buf, brs, gb_t, xn_t, width, c_rs, banks=None):
            nchunk = width // 128
            E("dve", lambda e: e.scalar_tensor_tensor(out=xn_t.h[:, 0:width], in0=src_ap,
                                                      scalar=st.h[:, c_rs:c_rs + 1], in1=gb_t.h[:, 0:width],
                                                      op0=ALU.mult, op1=ALU.mult),
              r=(src_buf, brs, gb_t.whole), w=(xn_t.whole,))
            used = []
            for g in range(nchunk // 8):
                bk = nbank() if banks is None else banks[g]
                used.append(bk)
                for c8 in range(8):
                    c = g * 8 + c8
                    E("pe", (lambda bk_, c8_, c_: (lambda e: e.transpose(psbf[bk_][:, c8_ * 128:(c8_ + 1) * 128],
                                                                       xn_t.h[:, c_ * 128:(c_ + 1) * 128], ident)))(bk, c8, c),
                      r=(xn_t.whole, cb.whole), w=(psbuf[bk],))
            return used

        def nt_apply_b(used, dst_fn, dst_bufs, evac_engs):
            for g, bk in enumerate(used):
                eng = evac_engs[g % len(evac_engs)]
                src = psbf[bk][:, 0:1024].rearrange("p (a b) -> p a b", a=8)
                dst = dst_fn(g)
                wb = dst_bufs(g) if callable(dst_bufs) else dst_bufs
                if eng == "act":
                    E("act", (lambda d_, s_: (lambda e: e.copy(out=d_, in_=s_)))(dst, src), r=(psbuf[bk],), w=wb)
                else:
                    E("dve", (lambda d_, s_: (lambda e: e.tensor_copy(out=d_, in_=s_)))(dst, src), r=(psbuf[bk],), w=wb)

        def nt_apply(src_ap, src_buf, brs, gb_t, xn_t, width, c_rs, dst_fn, dst_bufs, evac_engs, banks=None):
            used = nt_apply_a(src_ap, src_buf, brs, gb_t, xn_t, width, c_rs, banks)
            nt_apply_b(used, dst_fn, dst_bufs, evac_engs)

        hT = sbt([128, 16, 2048], BF16, AA0 + 0, "hT")
        xa = [sbt([128, 2048], F32, AA0 + (64 + 8 * i) * KB, f"xa{i}") for i in range(4)]
        gbcA = sbt([128, 2048], F32, AA0 + 96 * KB, "gbcA")
        xnA = [sbt([128, 2048], BF16, AA0 + (104 + 4 * i) * KB, f"xnA{i}") for i in range(3)]
        junkA = [sbt([128, 2048], BF16, AA0 + (116 + 4 * i) * KB, f"junkA{i}") for i in range(4)]
        E("sync", lambda e: e.dma_start(out=gbcA.h[:, :], in_=gbc_d[0]), w=(gbcA.whole,), key="gbc")
        def hT_rd(kc, a0, a1):
            return hT.rng(kc * 2048 + a0, kc * 2048 + a1)

        def hT_wr(t, g):
            return tuple(hT.rng(kc * 2048 + t * 128, kc * 2048 + (t + 1) * 128) for kc in range(g * 8, g * 8 + 8))

        cA_ss = stcol(16)
        cA_rs = stcol(16)
        brsA = {}

        def A1(t):
            s_ = t % 4
            src = x_prev[t * 128:(t + 1) * 128, :] if t < 8 else x_own[(t - 8) * 128:(t - 7) * 128, :]
            E("sync", (lambda s__, src_: (lambda e: e.dma_start(out=xa[s__].h[:, :], in_=src_)))(s_, src),
              w=(xa[s_].whole,), key=f"xa{s_}")
            brsA[t] = nt_stats(xa[s_].h[:, :], xa[s_].whole, junkA[t % 4], 2048, cA_ss + t, cA_rs + t)

        usedA = {}

        def A2a(t):
            s_ = t % 4
            usedA[t] = nt_apply_a(xa[s_].h[:, :], xa[s_].whole, brsA[t], gbcA, xnA[t % 3], 2048, cA_rs + t)

        def A2b(t):
            nt_apply_b(usedA[t], (lambda g: hT.h[:, g * 8:(g + 1) * 8, t * 128:(t + 1) * 128]),
                       (lambda g: hT_wr(t, g)), ["act", "dve"])

        A1(0)
        A1(1)
        A1(2)
        issue_weights(NSLOT, after=(xa[0].whole, xa[1].whole, xa[2].whole))
        A2a(0)
        for t in range(16):
            if t + 1 < 16:
                A2a(t + 1)
            if t + 3 < 16:
                A1(t + 3)
            A2b(t)

        ckpt('A')
        zz = sbt([128, 4, 2048], F32, AA0 + 64 * KB, "zz")
        zq = sbt([128, 6, 1024], F32, AA0 + 64 * KB, "zq")
        sqz = sbt([128, 4, 2048], BF16, AA0 + 96 * KB, "sqz")
        sqq = sbt([128, 6, 1024], BF16, AA0 + 96 * KB, "sqq")
        t_k = sbt([128, 2048], BF16, AA0 + 112 * KB, "t_k")
        kvn = sbt([128, 4, 2048], BF16, AA0 + 116 * KB, "kvn")
        qln = sbt([128, 6, 1024], BF16, AA0 + 132 * KB, "qln")
        u_bf = sbt([128, 8, 1152], BF16, AA0 + 148 * KB, "u_bf")
        kr2 = sbt([128, 2048], BF16, AA0 + 166 * KB, "kr2")
        sg = [sbt([128, 1152], F32, AA0 + 64 * KB + i * 4608, f"sg{i}") for i in range(2)]

        CS = sbt([128, 2048], F32, AA0 + 170 * KB, "CS")

        def emit_rope():
            posi = sbt([128, 2048], I32, AA0 + 132 * KB, "posi")
            posf = sbt([128, 2048], F32, AA0 + 140 * KB, "posf")
            tmpa = sbt([128, 2048], F32, AA0 + 148 * KB, "tmpa")
            E("sync", lambda e: e.dma_start(out=posi.h[:, :], in_=pos_bc), w=(posi.whole,), key="pos")
            E("dve", lambda e: e.tensor_copy(out=posf.h[:, :], in_=posi.h[:, :]), r=(posi.whole,), w=(posf.whole,))
            E("dve", lambda e: e.tensor_scalar(out=CS.h[:, :], in0=posf.h[:, :], scalar1=cvc(IFQ), scalar2=cvc(PHS),
                                               op0=ALU.mult, op1=ALU.add), r=(posf.whole, cv.whole), w=(CS.whole,))
            E("dve", lambda e: e.tensor_scalar(out=tmpa.h[:, :], in0=CS.h[:, :], scalar1=1.0 / (2 * PI), scalar2=None,
                                               op0=ALU.mult), r=(CS.whole,), w=(tmpa.whole,))
            E("dve", lambda e: e.tensor_copy(out=posi.h[:, :], in_=tmpa.h[:, :]), r=(tmpa.whole,), w=(posi.whole,))
            E("dve", lambda e: e.tensor_copy(out=posf.h[:, :], in_=posi.h[:, :]), r=(posi.whole,), w=(posf.whole,))
            C1 = 6.28125
            C2 = 2 * PI - C1
            E("dve", lambda e: e.scalar_tensor_tensor(out=CS.h[:, :], in0=posf.h[:, :], scalar=-C1, in1=CS.h[:, :],
                                                      op0=ALU.mult, op1=ALU.add), r=(posf.whole, CS.whole), w=(CS.whole,))
            E("dve", lambda e: e.scalar_tensor_tensor(out=CS.h[:, :], in0=posf.h[:, :], scalar=-C2, in1=CS.h[:, :],
                                                      op0=ALU.mult, op1=ALU.add), r=(posf.whole, CS.whole), w=(CS.whole,))
            E("dve", lambda e: e.tensor_single_scalar(out=tmpa.h[:, :], in_=CS.h[:, :], scalar=PI, op=ALU.is_gt),
              r=(CS.whole,), w=(tmpa.whole,))
            E("dve", lambda e: e.scalar_tensor_tensor(out=CS.h[:, :], in0=tmpa.h[:, :], scalar=-2 * PI, in1=CS.h[:, :],
                                                      op0=ALU.mult, op1=ALU.add), r=(tmpa.whole, CS.whole), w=(CS.whole,))
            E("dve", lambda e: e.tensor_scalar(out=CS.h[:, :], in0=CS.h[:, :], scalar1=-PI, scalar2=PI,
                                               op0=ALU.max, op1=ALU.min), r=(CS.whole,), w=(CS.whole,))
            E("act", lambda e: e.activation(out=CS.h[:, :], in_=CS.h[:, :], func=AF.Sin), r=(CS.whole,), w=(CS.whole,))
            E("dve", lambda e: e.tensor_scalar(out=CS.h[:, :], in0=CS.h[:, :], scalar1=cvc(SGN), scalar2=None,
                                               op0=ALU.mult), r=(CS.whole, cv.whole), w=(CS.whole,))


        for b in range(4):
            if b == 1:
                emit_rope()
            ws = next_w()
            banks = [nbank() for _ in range(4)]
            for kc in range(16):
                for n in range(4):
                    mm(psb[banks[n]][:, :], ws.h[:, kc * 128:(kc + 1) * 128], hT.h[:, kc, n * 512:(n + 1) * 512],
                       kc == 0, kc == 15, (ws.whole, hT_rd(kc, n * 512, (n + 1) * 512)), (psbuf[banks[n]],))
            for n in range(4):
                if os.environ.get("KDBG") == "noevac":
                    break
                bk = banks[n]
                E("dve", (lambda bk_, b_, n_: (lambda e: e.tensor_scalar(
                    out=zz.h[:, b_, n_ * 512:(n_ + 1) * 512], in0=psb[bk_][:, :], scalar1=cvc(GKV + b_), scalar2=None,
                    op0=ALU.mult)))(bk, b, n), r=(psbuf[bk], cv.whole), w=(zz.sub(b),))
                if os.environ.get("KDBG") == "noact":
                    continue
                E("act", (lambda bk_, b_, n_: (lambda e: e.activation(
                    out=sqz.h[:, b_, n_ * 512:(n_ + 1) * 512], in_=psb[bk_][:, :], func=AF.Square)))(bk, b, n),
                  r=(psbuf[bk],), w=(sqz.sub(b),) + ((psbuf[bk],) if os.environ.get("KDBG") == "serial" else ()))
        ckpt('B1')
        for n in range(4):
            bk = nbank()
            for b in range(4):
                mm(psb[bk][:, :], ones_b, sqz.h[:, b, n * 512:(n + 1) * 512], b == 0, b == 3,
                   (cb.whole, sqz.sub(b)), (psbuf[bk],))
            rstd_psum_inplace(bk, 512, 1.0 / KV_LORA)
            for b in range(4):
                E("dve", (lambda bk_, b_, n_: (lambda e: e.tensor_tensor(
                    out=kvn.h[:, b_, n_ * 512:(n_ + 1) * 512], in0=zz.h[:, b_, n_ * 512:(n_ + 1) * 512],
                    in1=psb[bk_][:, :], op=ALU.mult)))(bk, b, n), r=(zz.sub(b), psbuf[bk]), w=(kvn.sub(b),))
        ckpt('B2')
        ws = next_w()
        banks = [nbank() for _ in range(4)]
        for kc in range(16):
            for n in range(4):
                mm(psb[banks[n]][:, :], ws.h[:, kc * 128:(kc + 1) * 128], hT.h[:, kc, n * 512:(n + 1) * 512],
                   kc == 0, kc == 15, (ws.whole, hT_rd(kc, n * 512, (n + 1) * 512)), (psbuf[banks[n]],))
        for n in range(4):
            bk = banks[n]
            E("dve", (lambda bk_, n_: (lambda e: e.tensor_tensor(
                out=t_k.h[:, n_ * 512:(n_ + 1) * 512], in0=psb[bk_][:, :], in1=CS.h[:, n_ * 512:(n_ + 1) * 512],
                op=ALU.mult)))(bk, n), r=(psbuf[bk], CS.whole), w=(t_k.rng(n * 512, (n + 1) * 512),))
            bk2 = nbank()
            mm(psb[bk2][:, :], f2, t_k.h[:, n * 512:(n + 1) * 512], True, True,
               (cb.whole, t_k.rng(n * 512, (n + 1) * 512)), (psbuf[bk2],))
            E("act", (lambda bk_, n_: (lambda e: e.copy(out=kr2.h[:, n_ * 512:(n_ + 1) * 512], in_=psb[bk_][:, :])))(bk2, n),
              r=(psbuf[bk2],), w=(kr2.rng(n * 512, (n + 1) * 512),))
        ckpt('B3')
        for b in range(6):
            ws = next_w()
            banks = [nbank() for _ in range(2)]
            for kc in range(16):
                for n in range(2):
                    mm(psb[banks[n]][:, :], ws.h[:, kc * 128:(kc + 1) * 128],
                       hT.h[:, kc, 1024 + n * 512:1024 + (n + 1) * 512],
                       kc == 0, kc == 15, (ws.whole, hT_rd(kc, 1024 + n * 512, 1024 + (n + 1) * 512)), (psbuf[banks[n]],))
            for n in range(2):
                bk = banks[n]
                E("dve", (lambda bk_, b_, n_: (lambda e: e.tensor_scalar(
                    out=zq.h[:, b_, n_ * 512:(n_ + 1) * 512], in0=psb[bk_][:, :], scalar1=cvc(GQ + b_), scalar2=None,
                    op0=ALU.mult)))(bk, b, n), r=(psbuf[bk], cv.whole), w=(zq.sub(b),))
                E("act", (lambda bk_, b_, n_: (lambda e: e.activation(
                    out=sqq.h[:, b_, n_ * 512:(n_ + 1) * 512], in_=psb[bk_][:, :], func=AF.Square)))(bk, b, n),
                  r=(psbuf[bk],), w=(sqq.sub(b),))
        for n in range(2):
            bk = nbank()
            for b in range(6):
                mm(psb[bk][:, :], ones_b, sqq.h[:, b, n * 512:(n + 1) * 512], b == 0, b == 5,
                   (cb.whole, sqq.sub(b)), (psbuf[bk],))
            rstd_psum_inplace(bk, 512, 1.0 / Q_LORA)
            for b in range(6):
                E("dve", (lambda bk_, b_, n_: (lambda e: e.tensor_tensor(
                    out=qln.h[:, b_, n_ * 512:(n_ + 1) * 512], in0=zq.h[:, b_, n_ * 512:(n_ + 1) * 512],
                    in1=psb[bk_][:, :], op=ALU.mult)))(bk, b, n), r=(zq.sub(b), psbuf[bk]), w=(qln.sub(b),))
        ckpt('B4')
        tokr = [(896, 1024), (1024, 1536), (1536, 2048)]
        for c in range(8):
            sgt = sg[c % 2]
            ws = next_w()
            banks = [nbank() for _ in range(3)]
            for kc in range(16):
                for n, (a0, a1) in enumerate(tokr):
                    mm(psb[banks[n]][:, 0:a1 - a0], ws.h[:, kc * 128:(kc + 1) * 128], hT.h[:, kc, a0:a1],
                       kc == 0, kc == 15, (ws.whole, hT_rd(kc, a0, a1)), (psbuf[banks[n]],))
            for n, (a0, a1) in enumerate(tokr):
                E("act", (lambda bk_, a0_, a1_, sg_: (lambda e: e.activation(
                    out=sg_.h[:, a0_ - 896:a1_ - 896], in_=psb[bk_][:, 0:a1_ - a0_], func=AF.Sigmoid)))(banks[n], a0, a1, sgt),
                  r=(psbuf[banks[n]],), w=(sgt.whole,))
            ws = next_w()
            banks = [nbank() for _ in range(3)]
            for kc in range(16):
                for n, (a0, a1) in enumerate(tokr):
                    mm(psb[banks[n]][:, 0:a1 - a0], ws.h[:, kc * 128:(kc + 1) * 128], hT.h[:, kc, a0:a1],
                       kc == 0, kc == 15, (ws.whole, hT_rd(kc, a0, a1)), (psbuf[banks[n]],))
            for n, (a0, a1) in enumerate(tokr):
                E("dve", (lambda bk_, a0_, a1_, sg_, c_: (lambda e: e.tensor_tensor(
                    out=u_bf.h[:, c_, a0_ - 896:a1_ - 896], in0=psb[bk_][:, 0:a1_ - a0_], in1=sg_.h[:, a0_ - 896:a1_ - 896],
                    op=ALU.mult)))(banks[n], a0, a1, sgt, c), r=(psbuf[banks[n]], sgt.whole), w=(u_bf.sub(c),))

        ckpt('B')
        yv = sbt([128, 8, 1024], F32, AA0 + 0, "yv")
        Dr = [sbt([128, 31, 128], BF16, AA0 + (32 + 8 * i) * KB, f"Dr{i}") for i in range(2)]
        sqs = [sbt([128, 512], BF16, AA0 + 48 * KB + i * 1024, f"sqs{i}") for i in range(4)]
        bc0 = sbt([128, 1024], F32, AA0 + 52 * KB, "bc0")
        bc1 = sbt([128, 1024], F32, AA0 + 56 * KB, "bc1")
        mixT = sbt([128, 16, 1024], BF16, AA0 + 64 * KB, "mixT")
        mixT_attn = P.buf("sb", mixT.off + 8 * 2048, mixT.off + 16 * 2048, "mixT_attn")
        bS1 = [nbank(), nbank()]
        bS2 = [nbank(), nbank()]
        nsq = [0]
        pend_stats = []
        def build_D(c):
            dr_ = Dr[c % 2]
            E("dve", lambda e: e.tensor_tensor(
                out=dr_.h[:, :, :],
                in0=ident.unsqueeze(1).to_broadcast([128, CONV_K, 128]),
                in1=cv.h[:, CW + c * 31:CW + (c + 1) * 31].unsqueeze(2).to_broadcast([128, CONV_K, 128]),
                op=ALU.mult), r=(cb.whole, cv.whole), w=(dr_.whole,))

        build_D(0)
        for c in range(8):
            dr = Dr[c % 2]
            if c + 1 < 8:
                build_D(c + 1)
            for n in range(2):
                bk = nbank()
                while bk in bS1 or bk in bS2:
                    bk = nbank()
                for k in range(CONV_K):
                    mm(psb[bk][:, :], dr.h[:, k, :], u_bf.h[:, c, 98 + k + n * 512:98 + k + (n + 1) * 512],
                       k == 0, k == CONV_K - 1, (dr.sub(k), u_bf.sub(c)), (psbuf[bk],))
                yb = yv.rng(c * 1024 + n * 512, c * 1024 + (n + 1) * 512)
                E("act", (lambda bk_, c_, n_: (lambda e: e.activation(
                    out=yv.h[:, c_, n_ * 512:(n_ + 1) * 512], in_=psb[bk_][:, :], func=AF.Identity,
                    bias=cvc(CB + c_))))(bk, c, n), r=(psbuf[bk], cv.whole), w=(yb,))
                s1 = sqs[nsq[0] % 4]
                nsq[0] += 1
                s2 = sqs[nsq[0] % 4]
                nsq[0] += 1
                E("act", (lambda bk_, c_, s_: (lambda e: e.activation(
                    out=s_.h[:, :], in_=psb[bk_][:, :], func=AF.Square, bias=cvc(CB + c_))))(bk, c, s2),
                  r=(psbuf[bk], cv.whole), w=(s2.whole,))
                E("dve", (lambda c_, n_, s_: (lambda e: e.tensor_copy(
                    out=s_.h[:, :], in_=yv.h[:, c_, n_ * 512:(n_ + 1) * 512])))(c, n, s1), r=(yb,), w=(s1.whole,))
                if pend_stats:
                    pn, ps1, ps2, pc = pend_stats.pop()
                    mm(psb[bS1[pn]][:, :], ones_b, ps1.h[:, :], pc == 0, pc == 7, (cb.whole, ps1.whole), (psbuf[bS1[pn]],))
                    mm(psb[bS2[pn]][:, :], ones_b, ps2.h[:, :], pc == 0, pc == 7, (cb.whole, ps2.whole), (psbuf[bS2[pn]],))
                pend_stats.append((n, s1, s2, c))
        pn, ps1, ps2, pc = pend_stats.pop()
        mm(psb[bS1[pn]][:, :], ones_b, ps1.h[:, :], pc == 0, pc == 7, (cb.whole, ps1.whole), (psbuf[bS1[pn]],))
        mm(psb[bS2[pn]][:, :], ones_b, ps2.h[:, :], pc == 0, pc == 7, (cb.whole, ps2.whole), (psbuf[bS2[pn]],))
        for n in range(2):
            sl = slice(n * 512, (n + 1) * 512)
            b0 = bc0.rng(n * 512, (n + 1) * 512)
            b1 = bc1.rng(n * 512, (n + 1) * 512)
            E("dve", (lambda n_, sl_: (lambda e: e.tensor_scalar(out=bc0.h[:, sl_], in0=psb[bS1[n_]][:, :],
                                                                scalar1=1.0 / CONV_CH, scalar2=None, op0=ALU.mult)))(n, sl),
              r=(psbuf[bS1[n]],), w=(b0,))
            E("dve", (lambda sl_: (lambda e: e.tensor_tensor(out=bc1.h[:, sl_], in0=bc0.h[:, sl_], in1=bc0.h[:, sl_],
                                                            op=ALU.mult)))(sl), r=(b0,), w=(b1,))
            E("dve", (lambda n_, sl_: (lambda e: e.scalar_tensor_tensor(out=bc1.h[:, sl_], in0=psb[bS2[n_]][:, :],
                                                                       scalar=1.0 / CONV_CH, in1=bc1.h[:, sl_],
                                                                       op0=ALU.mult, op1=ALU.subtract)))(n, sl),
              r=(psbuf[bS2[n]], b1), w=(b1,))
            E("dve", (lambda sl_: (lambda e: e.tensor_scalar(out=bc1.h[:, sl_], in0=bc1.h[:, sl_], scalar1=EPS,
                                                            scalar2=None, op0=ALU.add)))(sl), r=(b1,), w=(b1,))
            E("act", (lambda sl_: (lambda e: e.activation(out=bc1.h[:, sl_], in_=bc1.h[:, sl_], func=AF.Sqrt)))(sl),
              r=(b1,), w=(b1,))
            E("dve", (lambda sl_: (lambda e: e.reciprocal(out=bc1.h[:, sl_], in_=bc1.h[:, sl_])))(sl), r=(b1,), w=(b1,))
        bS3 = [bS1[0], bS1[1]]
        for c in range(8):
            for n in range(2):
                sl = slice(n * 512, (n + 1) * 512)
                yb = yv.rng(c * 1024 + n * 512, c * 1024 + (n + 1) * 512)
                b0 = bc0.rng(n * 512, (n + 1) * 512)
                b1 = bc1.rng(n * 512, (n + 1) * 512)
                E("dve", (lambda c_, sl_: (lambda e: e.tensor_tensor(out=yv.h[:, c_, sl_], in0=yv.h[:, c_, sl_],
                                                                    in1=bc0.h[:, sl_], op=ALU.subtract)))(c, sl),
                  r=(yb, b0), w=(yb,))
                E("dve", (lambda c_, sl_: (lambda e: e.tensor_tensor(out=yv.h[:, c_, sl_], in0=yv.h[:, c_, sl_],
                                                                    in1=bc1.h[:, sl_], op=ALU.mult)))(c, sl),
                  r=(yb, b1), w=(yb,))
                E("act", (lambda c_, sl_: (lambda e: e.activation(out=yv.h[:, c_, sl_], in_=yv.h[:, c_, sl_], func=AF.Silu,
                                                                 scale=cvc(LG + c_), bias=cvc(LB + c_))))(c, sl),
                  r=(yb, cv.whole), w=(yb,))
                s2 = sqs[nsq[0] % 4]
                nsq[0] += 1
                E("act", (lambda c_, sl_, s_: (lambda e: e.activation(out=s_.h[:, :], in_=yv.h[:, c_, sl_],
                                                                     func=AF.Square)))(c, sl, s2), r=(yb,), w=(s2.whole,))
                mm(psb[bS3[n]][:, :], ones_b, s2.h[:, :], c == 0, c == 7, (cb.whole, s2.whole), (psbuf[bS3[n]],))
        for n in range(2):
            rstd_psum_inplace(bS3[n], 512, 1.0 / CONV_CH)
        for c in range(8):
            for n in range(2):
                sl = slice(n * 512, (n + 1) * 512)
                yb = yv.rng(c * 1024 + n * 512, c * 1024 + (n + 1) * 512)
                E("dve", (lambda c_, n_, sl_: (lambda e: e.scalar_tensor_tensor(
                    out=mixT.h[:, c_, sl_], in0=yv.h[:, c_, sl_], scalar=cvc(G2 + c_), in1=psb[bS3[n_]][:, :],
                    op0=ALU.mult, op1=ALU.mult)))(c, n, sl), r=(yb, cv.whole, psbuf[bS3[n]]), w=(mixT.sub(c),))

        ckpt('E')
        knT = sbt([128, 8, 2048], BF16, AA0 + 0, "knT")
        qnT = sbt([128, 8, 1024], BF16, AA0 + 32 * KB, "qnT")
        tq = sbt([128, 8, 1024], BF16, AA0 + 48 * KB, "tq")
        VA = sbt([128, 8, 8 * 130], BF16, AA0 + 96 * KB, "VA")
        VB = sbt([128, 8, 8 * 130], BF16, AA0 + 148 * KB, "VB")
        for h in range(N_HEADS):
            ws = next_w()
            banks = [nbank() for _ in range(2)]
            for kc in range(6):
                for n in range(2):
                    mm(psb[banks[n]][:, :], ws.h[:, kc * 128:(kc + 1) * 128], qln.h[:, kc, n * 512:(n + 1) * 512],
                       kc == 0, kc == 5, (ws.whole, qln.sub(kc)), (psbuf[banks[n]],))
            for n in range(2):
                E("act", (lambda bk_, h_, n_: (lambda e: e.copy(out=qnT.h[:, h_, n_ * 512:(n_ + 1) * 512],
                                                               in_=psb[bk_][:, :])))(banks[n], h, n),
                  r=(psbuf[banks[n]],), w=(qnT.sub(h),))
            ws = next_w()
            banks = [nbank() for _ in range(2)]
            for kc in range(6):
                for n in range(2):
                    mm(psb[banks[n]][:, :], ws.h[:, kc * 128:(kc + 1) * 128], qln.h[:, kc, n * 512:(n + 1) * 512],
                       kc == 0, kc == 5, (ws.whole, qln.sub(kc)), (psbuf[banks[n]],))
            for n in range(2):
                E("dve", (lambda bk_, h_, n_: (lambda e: e.tensor_tensor(
                    out=tq.h[:, h_, n_ * 512:(n_ + 1) * 512], in0=psb[bk_][:, :],
                    in1=CS.h[:, 1024 + n_ * 512:1024 + (n_ + 1) * 512], op=ALU.mult)))(banks[n], h, n),
                  r=(psbuf[banks[n]], CS.whole), w=(tq.sub(h),))
        for h in range(N_HEADS):
            ws = next_w()
            banks = [nbank() for _ in range(4)]
            for kc in range(4):
                for n in range(4):
                    mm(psb[banks[n]][:, :], ws.h[:, kc * 128:(kc + 1) * 128], kvn.h[:, kc, n * 512:(n + 1) * 512],
                       kc == 0, kc == 3, (ws.whole, kvn.sub(kc)), (psbuf[banks[n]],))
            for n in range(4):
                if n % 2 == 0:
                    E("act", (lambda bk_, h_, n_: (lambda e: e.copy(out=knT.h[:, h_, n_ * 512:(n_ + 1) * 512],
                                                                   in_=psb[bk_][:, :])))(banks[n], h, n),
                      r=(psbuf[banks[n]],), w=(knT.sub(h),))
                else:
                    E("dve", (lambda bk_, h_, n_: (lambda e: e.tensor_copy(out=knT.h[:, h_, n_ * 512:(n_ + 1) * 512],
                                                                          in_=psb[bk_][:, :])))(banks[n], h, n),
                      r=(psbuf[banks[n]],), w=(knT.sub(h),))
        wv = [next_w(), next_w()]
        E("dve", lambda e: e.tensor_scalar(out=VA.h[:, :, :].rearrange("p t (h d) -> p (t h) d", d=130)[:, :, 128:129],
                                           in0=onesf.h[:, 0:64].rearrange("p (a b) -> p a b", b=1),
                                           scalar1=cvc(PFL), scalar2=None, op0=ALU.mult),
          r=(onesf.whole, cv.whole), w=(VA.whole,))
        E("dve", lambda e: e.memset(VB.h[:, :, :].rearrange("p t (h d) -> p (t h) d", d=130)[:, :, 128:129], 1.0),
          w=(VB.whole,))
        for t in range(16):
            Vt = VA if t < 8 else VB
            tt = t % 8
            banks = [nbank() for _ in range(2)]
            for kc in range(4):
                for hf in range(2):
                    mm(psb[banks[hf]][:, :], kvn.h[:, kc, t * 128:(t + 1) * 128],
                       wv[kc // 2].h[:, (kc % 2) * 1024 + hf * 512:(kc % 2) * 1024 + (hf + 1) * 512],
                       kc == 0, kc == 3, (kvn.sub(kc), wv[kc // 2].whole), (psbuf[banks[hf]],))
            for hf in range(2):
                dst = Vt.h[:, tt, :].rearrange("p (h d) -> p h d", d=130)[:, hf * 4:(hf + 1) * 4, 0:128]
                srcp = psb[banks[hf]][:, :].rearrange("p (h d) -> p h d", d=128)
                if t < 8:
                    E("dve", (lambda d_, s_: (lambda e: e.tensor_scalar(out=d_, in0=s_, scalar1=cvc(PFL), scalar2=None,
                                                                       op0=ALU.mult)))(dst, srcp),
                      r=(psbuf[banks[hf]], cv.whole), w=(Vt.sub(tt),))
                else:
                    E("act", (lambda d_, s_: (lambda e: e.copy(out=d_, in_=s_)))(dst, srcp),
                      r=(psbuf[banks[hf]],), w=(Vt.sub(tt),))

        ckpt('C')
        attn = sbt([128, 8, 1024], F32, AA0 + 116 * KB, "attn")
        PT = [sbt([128, 512], BF16, AA0 + 170 * KB + i * 1024, f"PT{i}") for i in range(4)]
        rc = sbt([128, 64], F32, AA0 + 174 * KB, "rc")
        nrc = [0]
        ST_B = [4, 5, 6, 7]
        ACC_B = [0, 1, 2, 3]
        items = []
        for h in range(N_HEADS):
            for qb in range(2):
                for kc in range(8 + 4 * qb + 4):
                    items.append((h, qb, kc))
        LOOK = 3

        def d_qk(n):
            h, qb, kc = items[n]
            q0 = max(kc - 8 - 4 * qb, 0) * 128
            bk = ST_B[n % 4]
            mm(psb[bk][:, q0:512], knT.h[:, h, kc * 128:(kc + 1) * 128],
               qnT.h[:, h, qb * 512 + q0:qb * 512 + 512], True, False,
               (knT.sub(h), qnT.sub(h)), (psbuf[bk],))
            mm(psb[bk][:, q0:512], kr2.h[:, kc * 128:(kc + 1) * 128],
               tq.h[:, h, qb * 512 + q0:qb * 512 + 512], False, True,
               (kr2.whole, tq.sub(h)), (psbuf[bk],))

        def d_rest(n):
            h, qb, kc = items[n]
            j = kc - 8 - 4 * qb
            i0 = max(j, 0)
            q0 = i0 * 128
            bk = ST_B[n % 4]
            pt = PT[n % 4]
            Vt = VA if kc < 8 else VB
            E("act", lambda e: e.activation(out=pt.h[:, q0:512], in_=psb[bk][:, q0:512], func=AF.Exp, scale=SCALE),
              r=(psbuf[bk],), w=(pt.whole,))
            if j >= 0:
                E("dve", lambda e: e.tensor_tensor(out=pt.h[:, q0:q0 + 128], in0=pt.h[:, q0:q0 + 128], in1=tri,
                                                   op=ALU.mult), r=(pt.whole, cb.whole), w=(pt.whole,))
            for i in range(i0, 4):
                last = 8 + 4 * qb + i
                ab = ACC_B[i]
                mm(psb[ab][:, 0:129], pt.h[:, i * 128:(i + 1) * 128], Vt.h[:, kc % 8, h * 130:h * 130 + 129],
                   kc == 0, kc == last, (pt.whole, Vt.sub(kc % 8)), (psbuf[ab],))
                if kc == last:
                    col = nrc[0] % 64
                    nrc[0] += 1
                    rb = rc.rng(col, col + 1)
                    E("dve", (lambda ab_, col_: (lambda e: e.reciprocal(out=rc.h[:, col_:col_ + 1],
                                                                       in_=psb[ab_][:, 128:129])))(ab, col),
                      r=(psbuf[ab],), w=(rb,))
                    E("dve", (lambda ab_, col_, i_: (lambda e: e.tensor_scalar(
                        out=attn.h[:, qb * 4 + i_, h * 128:(h + 1) * 128], in0=psb[ab_][:, 0:128],
                        scalar1=rc.h[:, col_:col_ + 1], scalar2=None, op0=ALU.mult)))(ab, col, i),
                      r=(psbuf[ab], rb), w=(attn.rng((qb * 4 + i) * 1024 + h * 128, (qb * 4 + i) * 1024 + (h + 1) * 128),))

        for n in range(len(items) + LOOK):
            if n < len(items):
                d_qk(n)
            if n >= LOOK:
                d_rest(n - LOOK)

        ckpt('D')
        gbcD = sbt([128, 1024], F32, AA0 + 148 * KB, "gbcD")
        junkD = [sbt([128, 1024], BF16, AA0 + (152 + 2 * i) * KB, f"junkD{i}") for i in range(2)]
        xnD = [sbt([128, 1024], BF16, AA0 + (156 + 2 * i) * KB, f"xnD{i}") for i in range(2)]
        E("sync", lambda e: e.dma_start(out=gbcD.h[:, :], in_=gbc_d[4][:, 0:1024]), w=(gbcD.whole,), key="gbc")
        cD_ss = stcol(8)
        cD_rs = stcol(8)
        brsD = {}
        usedD = {}

        def mixT_attn_tile(i):
            return tuple(P.buf("sb", mixT.off + kc * 2048 + i * 256, mixT.off + kc * 2048 + (i + 1) * 256)
                         for kc in range(8, 16))

        def Dn_s(i):
            brsD[i] = nt_stats(attn.h[:, i, :], attn.sub(i), junkD[i % 2], 1024, cD_ss + i, cD_rs + i)

        def Dn_a(i):
            usedD[i] = nt_apply_a(attn.h[:, i, :], attn.sub(i), brsD[i], gbcD, xnD[i % 2], 1024, cD_rs + i,
                                  banks=[4 + i % 4])
            nt_apply_b(usedD[i], (lambda g: mixT.h[:, 8:16, i * 128:(i + 1) * 128]), mixT_attn_tile(i), ["dve"])

        ckpt('Dn')
        w_out = sbt([128, 16, 2048], BF16, AA0 + 0, "w_out")
        for kc in (0, 1, 8, 12, 2, 3, 9, 13, 4, 5, 10, 14, 6, 7, 11, 15):
            E("pool", (lambda kc_: (lambda e: e.dma_start(out=w_out.h[:, kc_, :],
                                                          in_=w_out_p[:, kc_ * 2048:(kc_ + 1) * 2048])))(kc),
              w=(w_out.sub(kc),), key=f"wout{kc % 4}")
        xF = [sbt([128, 2048], F32, AA0 + (96 + 8 * i) * KB, f"xF{i}") for i in range(2)]
        gpost = sbt([128, 2048], F32, AA0 + 112 * KB, "gpost")
        gpre = sbt([128, 2048], F32, AA0 + 120 * KB, "gpre")
        x1t = sbt([128, 2048], F32, AA0 + 128 * KB, "x1t")
        xnF = sbt([128, 2048], BF16, AA0 + 136 * KB, "xnF")
        hfT = sbt([128, 16, 1024], BF16, AA0 + 140 * KB, "hfT")
        junkF = sbt([128, 2048], BF16, AA0 + 172 * KB, "junkF")
        cF_p = stcol(32)
        cF_ss = stcol(8)
        cF_rs = stcol(8)
        cF_ss2 = stcol(8)
        cF_rs2 = stcol(8)
        brsF = {}
        brsF2 = {}

        KC_ORDER = [0, 1, 2, 3, 4, 5, 6, 8, 9, 10, 12, 13, 14, 7, 11, 15]

        def F1a(i):
            s_ = i % 2
            E("sync", lambda e: e.dma_start(out=xF[s_].h[:, :], in_=x_own[i * 128:(i + 1) * 128, :]),
              w=(xF[s_].whole,), key=f"xF{s_}")
            for cbk in (2, 3, 0, 1):
                bk = (i % 2) * 4 + cbk
                for kc in KC_ORDER:
                    mb = mixT.sub(kc) if kc < 8 else P.buf("sb", mixT.off + kc * 2048 + i * 256,
                                                           mixT.off + kc * 2048 + (i + 1) * 256)
                    mm(psb[bk][:, :], mixT.h[:, kc, i * 128:(i + 1) * 128], w_out.h[:, kc, cbk * 512:(cbk + 1) * 512],
                       kc == KC_ORDER[0], kc == KC_ORDER[-1], (mb, w_out.sub(kc)), (psbuf[bk],))

        def F1b(i):
            for cbk in (2, 3, 0, 1):
                bk = (i % 2) * 4 + cbk
                pc = cF_p + i * 4 + cbk
                E("act", (lambda bk_, pc_, c_: (lambda e: e.activation(
                    out=junkF.h[:, c_ * 512:(c_ + 1) * 512], in_=psb[bk_][:, :], func=AF.Square,
                    accum_out=st.h[:, pc_:pc_ + 1])))(bk, pc, cbk),
                  r=(psbuf[bk],), w=(junkF.rng(cbk * 512, (cbk + 1) * 512), st.rng(pc, pc + 1)))
            E("dve", lambda e: e.tensor_reduce(out=st.h[:, cF_ss + i:cF_ss + i + 1],
                                               in_=st.h[:, cF_p + 4 * i:cF_p + 4 * i + 4],
                                               axis=mybir.AxisListType.X, op=ALU.add),
              r=(st.rng(cF_p + 4 * i, cF_p + 4 * i + 4),), w=(st.rng(cF_ss + i, cF_ss + i + 1),))
            brsF[i] = rstd_from_ss(cF_ss + i, cF_rs + i, 1, 1.0 / D_MODEL)

        def F2(i):
            s_ = i % 2
            for cbk in range(4):
                bk = (i % 2) * 4 + cbk
                sl = slice(cbk * 512, (cbk + 1) * 512)
                E("dve", (lambda bk_, sl_: (lambda e: e.scalar_tensor_tensor(
                    out=x1t.h[:, sl_], in0=psb[bk_][:, :], scalar=st.h[:, cF_rs + i:cF_rs + i + 1], in1=gpost.h[:, sl_],
                    op0=ALU.mult, op1=ALU.mult)))(bk, sl),
                  r=(psbuf[bk], brsF[i], gpost.whole), w=(x1t.rng(cbk * 512, (cbk + 1) * 512),))
            E("dve", lambda e: e.tensor_tensor(out=x1t.h[:, :], in0=x1t.h[:, :], in1=xF[s_].h[:, :], op=ALU.add),
              r=(x1t.whole, xF[s_].whole), w=(x1t.whole,))
            E("sync", lambda e: e.dma_start(out=x1_d[i * 128:(i + 1) * 128, :], in_=x1t.h[:, :]),
              r=(x1t.whole,), w=(), key="x1w")
            brsF2[i] = nt_stats(x1t.h[:, :], x1t.whole, xnF, 2048, cF_ss2 + i, cF_rs2 + i)

        def F3(i):
            nt_apply(x1t.h[:, :], x1t.whole, brsF2[i], gpre, xnF, 2048, cF_rs2 + i,
                     (lambda g: hfT.h[:, g * 8:(g + 1) * 8, i * 128:(i + 1) * 128]),
                     (hfT.whole,), ["act", "dve"], banks=[(i % 2) * 4, (i % 2) * 4 + 1])

        Dn_s(0)
        Dn_s(1)
        Dn_a(0)
        Dn_s(2)
        Dn_a(1)
        F1a(0)
        for i in range(2, 8):
            if i + 1 < 8:
                Dn_s(i + 1)
            Dn_a(i)
        E("sync", lambda e: e.dma_start(out=gpost.h[:, :], in_=gbc_d[1]), w=(gpost.whole,), key="gbc2")
        E("sync", lambda e: e.dma_start(out=gpre.h[:, :], in_=gbc_d[2]), w=(gpre.whole,), key="gbc2")
        F1b(0)
        for i in range(8):
            if i + 1 < 8:
                F1a(i + 1)
            F2(i)
            F3(i)
            if i + 1 < 8:
                F1b(i + 1)
        ckpt('F')
        actT = sbt([128, NFB, 1024], BF16, AA0 + 0, "actT")
        sgf = [sbt([128, 512], F32, AA0 + 172 * KB + i * 2048, f"sgf{i}") for i in range(2)]
        nsg = [0]
        for f in range(NFB):
            wg = next_w()
            wu = next_w()
            bg = [nbank(), nbank()]
            bu = [nbank(), nbank()]
            for kc in range(16):
                for n in range(2):
                    mm(psb[bg[n]][:, :], wg.h[:, kc * 128:(kc + 1) * 128], hfT.h[:, kc, n * 512:(n + 1) * 512],
                       kc == 0, kc == 15, (wg.whole, hfT.whole), (psbuf[bg[n]],))
            for kc in range(16):
                for n in range(2):
                    mm(psb[bu[n]][:, :], wu.h[:, kc * 128:(kc + 1) * 128], hfT.h[:, kc, n * 512:(n + 1) * 512],
                       kc == 0, kc == 15, (wu.whole, hfT.whole), (psbuf[bu[n]],))
            for n in range(2):
                sgt = sgf[nsg[0] % 2]
                nsg[0] += 1
                E("act", (lambda bk_, s_: (lambda e: e.activation(out=s_.h[:, :], in_=psb[bk_][:, :], func=AF.Silu)))(bg[n], sgt),
                  r=(psbuf[bg[n]],), w=(sgt.whole,))
                E("dve", (lambda bk_, s_, f_, n_: (lambda e: e.tensor_tensor(
                    out=actT.h[:, f_, n_ * 512:(n_ + 1) * 512], in0=psb[bk_][:, :], in1=s_.h[:, :], op=ALU.mult)))(bu[n], sgt, f, n),
                  r=(psbuf[bu[n]], sgt.whole), w=(actT.sub(f),))

        ckpt('G1')
        ff = sbt([128, 8, 2048], F32, AA0 + 88 * KB, "ff")
        xr = [sbt([128, 2048], F32, AA0 + 8 * i * KB, f"xr{i}") for i in range(8)]
        gffn = sbt([128, 2048], F32, AA0 + 168 * KB, "gffn")
        junkG = [sbt([128, 512], BF16, AA0 + 176 * KB + i * 1024, f"junkG{i}") for i in range(2)]
        E("sync", lambda e: e.dma_start(out=gffn.h[:, :], in_=gbc_d[3]), w=(gffn.whole,), key="gbc3")
        cG_p = stcol(32)
        cG_ss = stcol(8)
        cG_rs = stcol(8)
        for cbk in range(4):
            for fg in range(11):
                ws = next_w()
                for i in range(8):
                    for fb in range(4):
                        fidx = fg * 4 + fb
                        mm(psb[i][:, :], actT.h[:, fidx, i * 128:(i + 1) * 128], ws.h[:, fb * 512:(fb + 1) * 512],
                           fg == 0 and fb == 0, fg == 10 and fb == 3, (actT.sub(fidx), ws.whole), (psbuf[i],))
            for i in range(8):
                sl = slice(cbk * 512, (cbk + 1) * 512)
                pc = cG_p + i * 4 + cbk
                fb_ = ff.rng(i * 2048 + cbk * 512, i * 2048 + (cbk + 1) * 512)
                E("dve", (lambda i_, sl_: (lambda e: e.tensor_tensor(out=ff.h[:, i_, sl_], in0=psb[i_][:, :],
                                                                     in1=gffn.h[:, sl_], op=ALU.mult)))(i, sl),
                  r=(psbuf[i], gffn.whole), w=(fb_,))
                E("act", (lambda i_, pc_: (lambda e: e.activation(out=junkG[i_ % 2].h[:, :], in_=psb[i_][:, :], func=AF.Square,
                                                                 accum_out=st.h[:, pc_:pc_ + 1])))(i, pc),
                  r=(psbuf[i],), w=(junkG[i % 2].whole, st.rng(pc, pc + 1)))
        E("dve", lambda e: e.tensor_reduce(out=st.h[:, cG_ss:cG_ss + 8],
                                           in_=st.h[:, cG_p:cG_p + 32].rearrange("p (a b) -> p a b", b=4),
                                           axis=mybir.AxisListType.X, op=ALU.add),
          r=(st.rng(cG_p, cG_p + 32),), w=(st.rng(cG_ss, cG_ss + 8),))
        brsG = rstd_from_ss(cG_ss, cG_rs, 8, 1.0 / D_MODEL)
        for i in range(8):
            E("sync", (lambda i_: (lambda e: e.dma_start(out=xr[i_].h[:, :], in_=x1_d[i_ * 128:(i_ + 1) * 128, :])))(i),
              w=(xr[i].whole,), key=f"xr{i % 2}")
            P.q["sync"][-1].waits.append(("dma", "x1w", P.dma_cnt["x1w"]))
        for i in range(8):
            E("dve", (lambda i_: (lambda e: e.scalar_tensor_tensor(
                out=ff.h[:, i_, :], in0=ff.h[:, i_, :], scalar=st.h[:, cG_rs + i_:cG_rs + i_ + 1], in1=xr[i_].h[:, :],
                op0=ALU.mult, op1=ALU.add)))(i), r=(ff.sub(i), brsG, xr[i].whole), w=(ff.sub(i),))
            E("sync", (lambda i_: (lambda e: e.dma_start(out=out_d[i_ * 128:(i_ + 1) * 128, :], in_=ff.h[:, i_, :])))(i),
              r=(ff.sub(i),), w=(), key="outw")
        fin = E("sync", None)
        fin.waits.append(("dma", "outw", P.dma_cnt["outw"]))
        assert wcur[0] == len(wpieces), (wcur[0], len(wpieces))


    except _Stop:
        pass

    fin_all = P.op("sync", None)
    for k_, v_ in P.dma_cnt.items():
        fin_all.waits.append(("dma", k_, v_))

    for e_ in ENGS:
        cnt = 0
        for ins in P.q[e_]:
            if ins.signal and not ins.is_dma:
                cnt += 1
                ins.value = cnt
    keys = sorted(P.dma_cnt.keys())
    sem_ctx = {}
    sems_eng = {e_: nc.alloc_semaphore(f"s_{e_}") for e_ in ENGS}
    sems_key = {k: nc.alloc_semaphore(f"d_{k}") for k in keys}

    def replay(ename, eng):
        waited = {}
        for ins in P.q[ename]:
            for w in ins.waits:
                if w[0] == "eng":
                    p = w[1]
                    sem, val = sems_eng[p.eng], p.value
                else:
                    sem, val = sems_key[w[1]], w[2]
                k = id(sem)
                if waited.get(k, 0) < val:
                    eng.wait_ge(sem, val)
                    waited[k] = val
            if ins.fn is None:
                continue
            bi = ins.fn(eng)
            if ins.is_dma:
                bi.then_inc(sems_key[ins.key], 16)
            elif ins.signal:
                bi.then_inc(sems_eng[ename], 1)

    with nc.Block() as block:
        @block.sync
        def _(e):
            replay("sync", e)

        @block.scalar
        def _(e):
            replay("act", e)

        @block.vector
        def _(e):
            replay("dve", e)

        @block.gpsimd
        def _(e):
            replay("pool", e)

        @block.tensor
        def _(e):
            replay("pe", e)

    stats = {e_: len(P.q[e_]) for e_ in ENGS}
    stats["sig"] = {e_: sum(1 for i in P.q[e_] if i.signal and not i.is_dma) for e_ in ENGS}
    return nc, stats


def _blocks_k(w, ncols_per_block):
    K, N = w.shape
    kc = K // 128
    nb = N // ncols_per_block
    a = w.reshape(kc, 128, nb, ncols_per_block).transpose(2, 1, 0, 3)
    return np.ascontiguousarray(a.reshape(nb, 128, kc * ncols_per_block))


def prepare_inputs(x, positions, pre_mix_norm, w_in, q_norm, w_uq, kv_norm, w_ukv, conv_w, conv_b, conv_ln_g,
                   conv_ln_b, conv_out_norm, attn_out_norm, w_out, post_mix_norm, pre_ffn_norm, w_gate, w_up,
                   w_down, post_ffn_norm):
    f = np.float32
    x = np.asarray(x, f)
    positions = np.asarray(positions, np.int32)
    w_in = np.asarray(w_in, f)[0]
    w_uq = np.asarray(w_uq, f)[0]
    w_ukv = np.asarray(w_ukv, f)[0]
    w_out = np.asarray(w_out, f)[0]
    w_gate = np.asarray(w_gate, f)[0]
    w_up = np.asarray(w_up, f)[0]
    w_down = np.asarray(w_down, f)[0]

    c1 = 2 * CONV_CH
    c2 = c1 + Q_LORA
    c3 = c2 + KV_LORA
    cols = []
    cols += list(range(c2, c3))
    cols += list(range(c3, c3 + 64)) + list(range(c3 + 32, c3 + 64)) + list(range(c3, c3 + 32))
    cols += list(range(c1, c2))
    for c in range(8):
        cols += list(range(CONV_CH + c * 128, CONV_CH + (c + 1) * 128))
        cols += list(range(c * 128, (c + 1) * 128))
    w_in_p = _blocks_k(w_in[:, cols], 128)
    cols = []
    for h in range(N_HEADS):
        b0 = h * 192
        cols += list(range(b0, b0 + 128))
        cols += list(range(b0 + 128, b0 + 192)) + list(range(b0 + 160, b0 + 192)) + list(range(b0 + 128, b0 + 160))
    w_uq_p = _blocks_k(w_uq[:, cols], 128)
    kcols = []
    vcols = []
    for h in range(N_HEADS):
        kcols += list(range(h * 256, h * 256 + 128))
        vcols += list(range(h * 256 + 128, h * 256 + 256))
    w_uk_p = _blocks_k(w_ukv[:, kcols], 128)
    wv = w_ukv[:, vcols]
    w_uv_p = np.ascontiguousarray(wv.reshape(2, 2, 128, 1024).transpose(0, 2, 1, 3).reshape(2, 128, 2048))
    w_out_p = np.ascontiguousarray(w_out.reshape(16, 128, 2048).transpose(1, 0, 2).reshape(128, 16 * 2048))
    g_p = _blocks_k(w_gate, 128)
    u_p = _blocks_k(w_up, 128)
    w_gu_p = np.ascontiguousarray(np.stack([g_p, u_p], axis=1).reshape(88, 128, 2048))
    wd = w_down.reshape(11, 4, 128, 4, 512)
    w_dn_p = np.ascontiguousarray(wd.transpose(3, 0, 2, 1, 4).reshape(44, 128, 2048))

    c_bf = np.zeros((128, 512), np.float32)
    c_bf[:, 0:128] = np.eye(128)
    c_bf[:, 128:256] = 1.0
    e64 = np.eye(64)
    c_bf[:, 256:384] = np.block([[e64, e64], [e64, e64]])
    kk = np.arange(128)[:, None]
    qq = np.arange(128)[None, :]
    c_bf[:, 384:512] = (qq >= kk).astype(np.float32)
    c_bf = c_bf.astype(ml_dtypes.bfloat16)

    cvec = np.zeros((128, NCV), f)
    cw = np.asarray(conv_w, f)[0]
    for c in range(8):
        cvec[:, CW + c * 31:CW + (c + 1) * 31] = cw[:, c * 128:(c + 1) * 128].T
    cvec[:, CB:CB + 8] = np.asarray(conv_b, f)[0].reshape(8, 128).T
    cvec[:, LG:LG + 8] = np.asarray(conv_ln_g, f)[0].reshape(8, 128).T
    cvec[:, LB:LB + 8] = np.asarray(conv_ln_b, f)[0].reshape(8, 128).T
    cvec[:, G2:G2 + 8] = np.asarray(conv_out_norm, f)[0].reshape(8, 128).T
    cvec[:, GQ:GQ + 6] = np.asarray(q_norm, f)[0].reshape(6, 128).T
    cvec[:, GKV:GKV + 4] = np.asarray(kv_norm, f)[0].reshape(4, 128).T
    inv_freq = (np.float32(10000.0) ** (-np.arange(0, 64, 2, dtype=np.float32) / np.float32(64))).astype(f)
    cvec[:, IFQ] = np.tile(inv_freq, 4)
    cvec[0:64, PHS] = np.float32(np.pi / 2)
    cvec[:, SGN] = 1.0
    cvec[64:96, SGN] = -1.0

    gbc = np.zeros((5, 128, 2048), f)
    gbc[0] = np.asarray(pre_mix_norm, f)[0][None, :]
    gbc[1] = np.asarray(post_mix_norm, f)[0][None, :]
    gbc[2] = np.asarray(pre_ffn_norm, f)[0][None, :]
    gbc[3] = np.asarray(post_ffn_norm, f)[0][None, :]
    gbc[4, :, 0:1024] = np.asarray(attn_out_norm, f)[0][None, :]

    shared = dict(c_bf=c_bf, gbc=gbc, w_in_p=w_in_p, w_uq_p=w_uq_p, w_uk_p=w_uk_p, w_uv_p=w_uv_p,
                  w_out_p=w_out_p, w_gu_p=w_gu_p, w_dn_p=w_dn_p)
    in_maps = []
    for core in range(8):
        b, half = core // 2, core % 2
        m = dict(shared)
        m["x_own"] = np.ascontiguousarray(x[b, half * TOWN:(half + 1) * TOWN])
        cvc = cvec.copy()
        pos = np.zeros((2048,), np.int32)
        if half == 1:
            m["x_prev"] = np.ascontiguousarray(x[b, 0:TOWN])
            pos[:] = positions[b, 0:2048]
            cvc[:, PFL] = 1.0
        else:
            m["x_prev"] = np.zeros((TOWN, D_MODEL), f)
            pos[1024:] = positions[b, 0:1024]
            cvc[:, PFL] = 0.0
        m["cvec"] = cvc
        m["pos_bc"] = np.ascontiguousarray(np.broadcast_to(pos[None, :], (128, 2048)))
        in_maps.append(m)
    return in_maps


_CACHE = {}


def kernel(**inputs):
    if "nc" not in _CACHE:
        _CACHE["nc"], _CACHE["stats"] = build_program()
    nc = _CACHE["nc"]
    in_maps = prepare_inputs(**inputs)
    res = run_bass_kernel_spmd(nc, in_maps, core_ids=list(range(8)))
    out = np.zeros((BATCH, SEQ, D_MODEL), np.float32)
    for core in range(8):
        b, half = core // 2, core % 2
        out[b, half * TOWN:(half + 1) * TOWN] = res.results[core]["out"]
    return out
```

```python
import os
import numpy as np
import ml_dtypes
import concourse.bass as bass
import concourse.mybir as mybir
from concourse.bass_utils import run_bass_kernel_spmd

F32 = mybir.dt.float32
BF16 = mybir.dt.bfloat16
I32 = mybir.dt.int32
AF = mybir.ActivationFunctionType
ALU = mybir.AluOpType
PI = float(np.pi)

D_MODEL = 2048
SEQ = 2048
BATCH = 4
TOWN = 1024
CONV_CH = 1024
CONV_K = 31
N_HEADS = 8
Q_LORA = 768
KV_LORA = 512
D_FF = 5632
NFB = D_FF // 128
EPS = 1e-6
SCALE = 192 ** -0.5

CW = 0
CB = 248
LG = 256
LB = 264
G2 = 272
GQ = 280
GKV = 286
IFQ = 290
PHS = 291
SGN = 292
PFL = 293
NCV = 320

KB = 1024


class Buf:
    __slots__ = ("space", "lo", "hi", "name")

    def __init__(self, space, lo, hi, name=""):
        self.space, self.lo, self.hi, self.name = space, lo, hi, name


class Ins:
    __slots__ = ("eng", "fn", "is_dma", "key", "signal", "value", "waits", "dma_val", "idx")

    def __init__(self, eng, fn, is_dma, key):
        self.eng, self.fn, self.is_dma, self.key = eng, fn, is_dma, key
        self.idx = -1
        self.signal = False
        self.value = None
        self.waits = []
        self.dma_val = None


ENGS = ["sync", "act", "dve", "pool", "pe"]


class Prog:
    def __init__(self):
        self.q = {e: [] for e in ENGS}
        self.wr = {"sb": [], "ps": []}
        self.rd = {"sb": [], "ps": []}
        self.dma_cnt = {}

    def buf(self, space, lo, hi, name=""):
        return Buf(space, lo, hi, name)

    def op(self, eng, fn, reads=(), writes=(), dma_key=None):
        ins = Ins(eng, fn, dma_key is not None, dma_key)
        ins.idx = len(self.q[eng])
        need_eng = {}
        need_dma = {}

        def add(p):
            if p is ins:
                return
            if p.is_dma:
                need_dma[p.key] = 1
                return
            if p.eng == eng and not ins.is_dma and eng == "pe":
                return
            cur = need_eng.get(p.eng)
            if cur is None or cur.idx < p.idx:
                need_eng[p.eng] = p

        for b in reads:
            for w in self.wr[b.space]:
                if w[0] < b.hi and b.lo < w[1]:
                    add(w[2])
            if b.space == "ps" and eng != "pe":
                lo = b.lo // 2048 * 2048
                hi = -(-b.hi // 2048) * 2048
                for r in self.rd["ps"]:
                    if r[0] < hi and lo < r[1] and r[3].eng != eng and r[3].eng != "pe":
                        add(r[3])
        for b in writes:
            for w in self.wr[b.space]:
                if w[0] < b.hi and b.lo < w[1]:
                    add(w[2])
            for r in self.rd[b.space]:
                if r[0] < b.hi and b.lo < r[1]:
                    add(r[3])
        for p in need_eng.values():
            p.signal = True
            ins.waits.append(("eng", p))
        for k in need_dma:
            ins.waits.append(("dma", k, self.dma_cnt[k]))
        if dma_key is not None:
            self.dma_cnt[dma_key] = self.dma_cnt.get(dma_key, 0) + 16
            ins.dma_val = self.dma_cnt[dma_key]
        for b in writes:
            sp = b.space
            self.wr[sp] = [w for w in self.wr[sp] if not (b.lo <= w[0] and w[1] <= b.hi)]
            self.wr[sp].append([b.lo, b.hi, ins])
            self.rd[sp] = [r for r in self.rd[sp] if not (b.lo <= r[0] and r[1] <= b.hi)]
        rk = ("dma", dma_key) if ins.is_dma else eng
        for b in reads:
            lst = self.rd[b.space]
            for r in lst:
                if r[0] == b.lo and r[1] == b.hi and r[2] == rk:
                    r[3] = ins
                    break
            else:
                lst.append([b.lo, b.hi, rk, ins])
        self.q[eng].append(ins)
        return ins


def build_program(stop_after=None, tensors=None):
    nc = bass.Bass("TRN2", target_bir_lowering=False)
    P = Prog()

    def din(name, shape, dt):
        return nc.dram_tensor(name, list(shape), dt, kind="ExternalInput").ap()

    x_own = din("x_own", [TOWN, D_MODEL], F32)
    x_prev = din("x_prev", [TOWN, D_MODEL], F32)
    pos_bc = din("pos_bc", [128, 2048], I32)
    c_bf = din("c_bf", [128, 512], BF16)
    cvec_d = din("cvec", [128, NCV], F32)
    gbc_d = din("gbc", [5, 128, 2048], F32)
    w_in_p = din("w_in_p", [27, 128, 2048], F32)
    w_uq_p = din("w_uq_p", [16, 128, 768], F32)
    w_uk_p = din("w_uk_p", [8, 128, 512], F32)
    w_uv_p = din("w_uv_p", [2, 128, 2048], F32)
    w_out_p = din("w_out_p", [128, 16 * 2048], F32)
    w_gu_p = din("w_gu_p", [88, 128, 2048], F32)
    w_dn_p = din("w_dn_p", [44, 128, 2048], F32)
    out_d = nc.dram_tensor("out", [TOWN, D_MODEL], F32, kind="ExternalOutput").ap()
    x1_d = nc.dram_tensor("x1_scratch", [TOWN, D_MODEL], F32).ap()
    dbg = {}

    base = (nc.sbuf_base + 63) // 64 * 64
    CONST0 = base
    WR0 = CONST0 + 4 * KB
    AA0 = WR0 + 24 * KB
    assert AA0 + 178 * KB <= nc.sbuf_top, (AA0 + 178 * KB, nc.sbuf_top)

    def dsz(dt):
        return 4 if dt in (F32, I32) else 2

    class T:
        def __init__(self, name, shape, dt, off):
            self.h = nc.alloc_sbuf_tensor_at(name, list(shape), dt, offset=off)
            self.off = off
            self.shape = shape
            self.dt = dt
            self.nbytes = int(np.prod(shape[1:])) * dsz(dt)
            self.whole = P.buf("sb", off, off + self.nbytes, name)
            self._subs = {}
            if tensors is not None:
                tensors[name] = self

        def sub(self, i, n=None):
            key = (i, n)
            if key not in self._subs:
                slab = self.nbytes // self.shape[1]
                cnt = 1 if n is None else n
                self._subs[key] = P.buf("sb", self.off + i * slab, self.off + (i + cnt) * slab)
            return self._subs[key]

        def rng(self, lo_el, hi_el):
            key = ("r", lo_el, hi_el)
            if key not in self._subs:
                self._subs[key] = P.buf("sb", self.off + lo_el * dsz(self.dt), self.off + hi_el * dsz(self.dt))
            return self._subs[key]

    _names = [0]

    def sbt(shape, dt, off, name=None):
        _names[0] += 1
        return T(name or f"t{_names[0]}", shape, dt, off)

    cb = sbt([128, 512], BF16, CONST0, "cb")
    ident = cb.h[:, 0:128]
    ones_b = cb.h[:, 128:256]
    f2 = cb.h[:, 256:384]
    tri = cb.h[:, 384:512]
    cv = sbt([128, NCV], F32, CONST0 + 1024, "cv")
    st = sbt([128, 320], F32, CONST0 + 1024 + NCV * 4, "st")
    onesf = sbt([128, 64], F32, CONST0 + 1024 + NCV * 4 + 1280, "onesf")
    assert CONST0 + 1024 + NCV * 4 + 1280 + 256 <= WR0

    def cvc(col):
        return cv.h[:, col:col + 1]

    _stn = [0]

    def stcol(n=1):
        c = _stn[0]
        _stn[0] += n
        assert _stn[0] <= 320
        return c

    NSLOT = 6
    wslots = [sbt([128, 2048], BF16, WR0 + i * 4 * KB, f"ws{i}") for i in range(NSLOT)]
    wpieces = []
    wstate = {"issued": 0}

    psb = [nc.alloc_psum_tensor(f"psb{i}", [128, 512], F32) for i in range(8)]
    psbuf = [P.buf("ps", i * 2048, (i + 1) * 2048, f"ps{i}") for i in range(8)]
    psbf = [psb[i][:, :].bitcast(BF16) for i in range(8)]
    _bank = [0]

    def nbank():
        b = _bank[0] % 8
        _bank[0] += 1
        return b

    _psub = {}

    def psub(bank, lo, hi):
        k = (bank, lo, hi)
        if k not in _psub:
            _psub[k] = P.buf("ps", bank * 2048 + lo * 4, bank * 2048 + hi * 4)
        return _psub[k]

    def E(eng, fn, r=(), w=(), key=None):
        return P.op(eng, fn, r, w, key)

    def mm(out, lhsT, rhs, start, stop, r, w):
        return E("pe", lambda e: e.matmul(out, lhsT=lhsT, rhs=rhs, start=start, stop=stop), r, w)

    def issue_weights(upto, after=()):
        while wstate["issued"] < min(upto, len(wpieces)):
            i = wstate["issued"]
            src, n = wpieces[i]
            slot = wslots[i % NSLOT]
            E("pool", (lambda s, n_, sl: (lambda e: e.dma_start(out=sl.h[:, 0:n_], in_=s)))(src, n, slot),
              r=after, w=(slot.whole,), key=f"ws{i % NSLOT}")
            wstate["issued"] += 1

    wcur = [0]

    def next_w(prefetch=4):
        i = wcur[0]
        wcur[0] += 1
        issue_weights(i + 1 + prefetch)
        return wslots[i % NSLOT]

    for j in range(27):
        wpieces.append((w_in_p[j], 2048))
    for j in range(16):
        wpieces.append((w_uq_p[j], 768))
    for j in range(8):
        wpieces.append((w_uk_p[j], 512))
    for j in range(2):
        wpieces.append((w_uv_p[j], 2048))
    for j in range(88):
        wpieces.append((w_gu_p[j], 2048))
    for j in range(44):
        wpieces.append((w_dn_p[j], 2048))

    class _Stop(Exception):
        pass

    def ckpt(name):
        if stop_after == name:
            raise _Stop()

    try:
        E("sync", lambda e: e.dma_start(out=cb.h[:, :], in_=c_bf), w=(cb.whole,), key="const")
        E("sync", lambda e: e.dma_start(out=cv.h[:, :], in_=cvec_d), w=(cv.whole,), key="const")
        E("dve", lambda e: e.memset(st.h[:, :], 0.0), w=(st.whole,))
        E("dve", lambda e: e.memset(onesf.h[:, :], 1.0), w=(onesf.whole,))

        def rstd_from_ss(c_ss, c_out, n, inv_n):
            c_ms = stcol(n)
            c_sq = stcol(n)
            bs = st.rng(c_ss, c_ss + n)
            bm = st.rng(c_ms, c_ms + n)
            bq = st.rng(c_sq, c_sq + n)
            bo = st.rng(c_out, c_out + n)
            E("dve", lambda e: e.tensor_scalar(out=st.h[:, c_ms:c_ms + n], in0=st.h[:, c_ss:c_ss + n], scalar1=inv_n,
                                               scalar2=EPS, op0=ALU.mult, op1=ALU.add), r=(bs,), w=(bm,))
            E("act", lambda e: e.activation(out=st.h[:, c_sq:c_sq + n], in_=st.h[:, c_ms:c_ms + n], func=AF.Sqrt),
              r=(bm,), w=(bq,))
            E("dve", lambda e: e.reciprocal(out=st.h[:, c_out:c_out + n], in_=st.h[:, c_sq:c_sq + n]), r=(bq,), w=(bo,))
            return bo

        def rstd_psum_inplace(bank, n, inv_n):
            b = psbuf[bank]
            ap = psb[bank][:, 0:n]
            E("dve", lambda e: e.tensor_scalar(out=ap, in0=ap, scalar1=inv_n, scalar2=EPS, op0=ALU.mult, op1=ALU.add),
              r=(b,), w=(b,))
            E("act", lambda e: e.activation(out=ap, in_=ap, func=AF.Sqrt), r=(b,), w=(b,))
            E("dve", lambda e: e.reciprocal(out=ap, in_=ap), r=(b,), w=(b,))

        def nt_stats(src_ap, src_buf, junk_t, width, c_ss, c_rs):
            bs = st.rng(c_ss, c_ss + 1)
            E("act", lambda e: e.activation(out=junk_t.h[:, 0:width], in_=src_ap, func=AF.Square,
                                            accum_out=st.h[:, c_ss:c_ss + 1]), r=(src_buf,), w=(junk_t.whole, bs))
            return rstd_from_ss(c_ss, c_rs, 1, 1.0 / width)

        def nt_apply_a(src_ap, src_buf, brs, gb_t, xn_t, width, c_rs, banks=None):
            nchunk = width // 128
            E("dve", lambda e: e.scalar_tensor_tensor(out=xn_t.h[:, 0:width], in0=src_ap,
                                                      scalar=st.h[:, c_rs:c_rs + 1], in1=gb_t.h[:, 0:width],
                                                      op0=ALU.mult, op1=ALU.mult),
              r=(src_buf, brs, gb_t.whole), w=(xn_t.whole,))
            used = []
            for g in range(nchunk // 8):
                bk = nbank() if banks is None else banks[g]
                used.append(bk)
                for c8 in range(8):
                    c = g * 8 + c8
                    E("pe", (lambda bk_, c8_, c_: (lambda e: e.transpose(psbf[bk_][:, c8_ * 128:(c8_ + 1) * 128],
                                                                       xn_t.h[:, c_ * 128:(c_ + 1) * 128], ident)))(bk, c8, c),
                      r=(xn_t.whole, cb.whole), w=(psbuf[bk],))
            return used

        def nt_apply_b(used, dst_fn, dst_bufs, evac_engs):
            for g, bk in enumerate(used):
                eng = evac_engs[g % len(evac_engs)]
                src = psbf[bk][:, 0:1024].rearrange("p (a b) -> p a b", a=8)
                dst = dst_fn(g)
                wb = dst_bufs(g) if callable(dst_bufs) else dst_bufs
                if eng == "act":
                    E("act", (lambda d_, s_: (lambda e: e.copy(out=d_, in_=s_)))(dst, src), r=(psbuf[bk],), w=wb)
                else:
                    E("dve", (lambda d_, s_: (lambda e: e.tensor_copy(out=d_, in_=s_)))(dst, src), r=(psbuf[bk],), w=wb)

        def nt_apply(src_ap, src_buf, brs, gb_t, xn_t, width, c_rs, dst_fn, dst_bufs, evac_engs, banks=None):
            used = nt_apply_a(src_ap, src_buf, brs, gb_t, xn_t, width, c_rs, banks)
            nt_apply_b(used, dst_fn, dst_bufs, evac_engs)

        hT = sbt([128, 16, 2048], BF16, AA0 + 0, "hT")
        xa = [sbt([128, 2048], F32, AA0 + (64 + 8 * i) * KB, f"xa{i}") for i in range(4)]
        gbcA = sbt([128, 2048], F32, AA0 + 96 * KB, "gbcA")
        xnA = [sbt([128, 2048], BF16, AA0 + (104 + 4 * i) * KB, f"xnA{i}") for i in range(3)]
        junkA = [sbt([128, 2048], BF16, AA0 + (116 + 4 * i) * KB, f"junkA{i}") for i in range(4)]
        E("sync", lambda e: e.dma_start(out=gbcA.h[:, :], in_=gbc_d[0]), w=(gbcA.whole,), key="gbc")
        def hT_rd(kc, a0, a1):
            return hT.rng(kc * 2048 + a0, kc * 2048 + a1)

        def hT_wr(t, g):
            return tuple(hT.rng(kc * 2048 + t * 128, kc * 2048 + (t + 1) * 128) for kc in range(g * 8, g * 8 + 8))

        cA_ss = stcol(16)
        cA_rs = stcol(16)
        brsA = {}

        def A1(t):
            s_ = t % 4
            src = x_prev[t * 128:(t + 1) * 128, :] if t < 8 else x_own[(t - 8) * 128:(t - 7) * 128, :]
            E("sync", (lambda s__, src_: (lambda e: e.dma_start(out=xa[s__].h[:, :], in_=src_)))(s_, src),
              w=(xa[s_].whole,), key=f"xa{s_}")
            brsA[t] = nt_stats(xa[s_].h[:, :], xa[s_].whole, junkA[t % 4], 2048, cA_ss + t, cA_rs + t)

        usedA = {}

        def A2a(t):
            s_ = t % 4
            usedA[t] = nt_apply_a(xa[s_].h[:, :], xa[s_].whole, brsA[t], gbcA, xnA[t % 3], 2048, cA_rs + t)

        def A2b(t):
            nt_apply_b(usedA[t], (lambda g: hT.h[:, g * 8:(g + 1) * 8, t * 128:(t + 1) * 128]),
                       (lambda g: hT_wr(t, g)), ["act", "dve"])

        A1(0)
        A1(1)
        A1(2)
        issue_weights(NSLOT, after=(xa[0].whole, xa[1].whole, xa[2].whole))
        A2a(0)
        for t in range(16):
            if t + 1 < 16:
                A2a(t + 1)
            if t + 3 < 16:
                A1(t + 3)
            A2b(t)

        ckpt('A')
        zz = sbt([128, 4, 2048], F32, AA0 + 64 * KB, "zz")
        zq = sbt([128, 6, 1024], F32, AA0 + 64 * KB, "zq")
        sqz = sbt([128, 4, 2048], BF16, AA0 + 96 * KB, "sqz")
        sqq = sbt([128, 6, 1024], BF16, AA0 + 96 * KB, "sqq")
        t_k = sbt([128, 2048], BF16, AA0 + 112 * KB, "t_k")
        kvn = sbt([128, 4, 2048], BF16, AA0 + 116 * KB, "kvn")
        qln = sbt([128, 6, 1024], BF16, AA0 + 132 * KB, "qln")
        u_bf = sbt([128, 8, 1152], BF16, AA0 + 148 * KB, "u_bf")
        kr2 = sbt([128, 2048], BF16, AA0 + 166 * KB, "kr2")
        sg = [sbt([128, 1152], F32, AA0 + 64 * KB + i * 4608, f"sg{i}") for i in range(2)]

        CS = sbt([128, 2048], F32, AA0 + 170 * KB, "CS")

        def emit_rope():
            posi = sbt([128, 2048], I32, AA0 + 132 * KB, "posi")
            posf = sbt([128, 2048], F32, AA0 + 140 * KB, "posf")
            tmpa = sbt([128, 2048], F32, AA0 + 148 * KB, "tmpa")
            E("sync", lambda e: e.dma_start(out=posi.h[:, :], in_=pos_bc), w=(posi.whole,), key="pos")
            E("dve", lambda e: e.tensor_copy(out=posf.h[:, :], in_=posi.h[:, :]), r=(posi.whole,), w=(posf.whole,))
            E("dve", lambda e: e.tensor_scalar(out=CS.h[:, :], in0=posf.h[:, :], scalar1=cvc(IFQ), scalar2=cvc(PHS),
                                               op0=ALU.mult, op1=ALU.add), r=(posf.whole, cv.whole), w=(CS.whole,))
            E("dve", lambda e: e.tensor_scalar(out=tmpa.h[:, :], in0=CS.h[:, :], scalar1=1.0 / (2 * PI), scalar2=None,
                                               op0=ALU.mult), r=(CS.whole,), w=(tmpa.whole,))
            E("dve", lambda e: e.tensor_copy(out=posi.h[:, :], in_=tmpa.h[:, :]), r=(tmpa.whole,), w=(posi.whole,))
            E("dve", lambda e: e.tensor_copy(out=posf.h[:, :], in_=posi.h[:, :]), r=(posi.whole,), w=(posf.whole,))
            C1 = 6.28125
            C2 = 2 * PI - C1
            E("dve", lambda e: e.scalar_tensor_tensor(out=CS.h[:, :], in0=posf.h[:, :], scalar=-C1, in1=CS.h[:, :],
                                                      op0=ALU.mult, op1=ALU.add), r=(posf.whole, CS.whole), w=(CS.whole,))
            E("dve", lambda e: e.scalar_tensor_tensor(out=CS.h[:, :], in0=posf.h[:, :], scalar=-C2, in1=CS.h[:, :],
                                                      op0=ALU.mult, op1=ALU.add), r=(posf.whole, CS.whole), w=(CS.whole,))
            E("dve", lambda e: e.tensor_single_scalar(out=tmpa.h[:, :], in_=CS.h[:, :], scalar=PI, op=ALU.is_gt),
              r=(CS.whole,), w=(tmpa.whole,))
            E("dve", lambda e: e.scalar_tensor_tensor(out=CS.h[:, :], in0=tmpa.h[:, :], scalar=-2 * PI, in1=CS.h[:, :],
                                                      op0=ALU.mult, op1=ALU.add), r=(tmpa.whole, CS.whole), w=(CS.whole,))
            E("dve", lambda e: e.tensor_scalar(out=CS.h[:, :], in0=CS.h[:, :], scalar1=-PI, scalar2=PI,
                                               op0=ALU.max, op1=ALU.min), r=(CS.whole,), w=(CS.whole,))
            E("act", lambda e: e.activation(out=CS.h[:, :], in_=CS.h[:, :], func=AF.Sin), r=(CS.whole,), w=(CS.whole,))
            E("dve", lambda e: e.tensor_scalar(out=CS.h[:, :], in0=CS.h[:, :], scalar1=cvc(SGN), scalar2=None,
                                               op0=ALU.mult), r=(CS.whole, cv.whole), w=(CS.whole,))


        for b in range(4):
            if b == 1:
                emit_rope()
            ws = next_w()
            banks = [nbank() for _ in range(4)]
            for kc in range(16):
                for n in range(4):
                    mm(psb[banks[n]][:, :], ws.h[:, kc * 128:(kc + 1) * 128], hT.h[:, kc, n * 512:(n + 1) * 512],
                       kc == 0, kc == 15, (ws.whole, hT_rd(kc, n * 512, (n + 1) * 512)), (psbuf[banks[n]],))
            for n in range(4):
                if os.environ.get("KDBG") == "noevac":
                    break
                bk = banks[n]
                E("dve", (lambda bk_, b_, n_: (lambda e: e.tensor_scalar(
                    out=zz.h[:, b_, n_ * 512:(n_ + 1) * 512], in0=psb[bk_][:, :], scalar1=cvc(GKV + b_), scalar2=None,
                    op0=ALU.mult)))(bk, b, n), r=(psbuf[bk], cv.whole), w=(zz.sub(b),))
                if os.environ.get("KDBG") == "noact":
                    continue
                E("act", (lambda bk_, b_, n_: (lambda e: e.activation(
                    out=sqz.h[:, b_, n_ * 512:(n_ + 1) * 512], in_=psb[bk_][:, :], func=AF.Square)))(bk, b, n),
                  r=(psbuf[bk],), w=(sqz.sub(b),) + ((psbuf[bk],) if os.environ.get("KDBG") == "serial" else ()))
        ckpt('B1')
        for n in range(4):
            bk = nbank()
            for b in range(4):
                mm(psb[bk][:, :], ones_b, sqz.h[:, b, n * 512:(n + 1) * 512], b == 0, b == 3,
                   (cb.whole, sqz.sub(b)), (psbuf[bk],))
            rstd_psum_inplace(bk, 512, 1.0 / KV_LORA)
            for b in range(4):
                E("dve", (lambda bk_, b_, n_: (lambda e: e.tensor_tensor(
                    out=kvn.h[:, b_, n_ * 512:(n_ + 1) * 512], in0=zz.h[:, b_, n_ * 512:(n_ + 1) * 512],
                    in1=psb[bk_][:, :], op=ALU.mult)))(bk, b, n), r=(zz.sub(b), psbuf[bk]), w=(kvn.sub(b),))
        ckpt('B2')
        ws = next_w()
        banks = [nbank() for _ in range(4)]
        for kc in range(16):
            for n in range(4):
                mm(psb[banks[n]][:, :], ws.h[:, kc * 128:(kc + 1) * 128], hT.h[:, kc, n * 512:(n + 1) * 512],
                   kc == 0, kc == 15, (ws.whole, hT_rd(kc, n * 512, (n + 1) * 512)), (psbuf[banks[n]],))
        for n in range(4):
            bk = banks[n]
            E("dve", (lambda bk_, n_: (lambda e: e.tensor_tensor(
                out=t_k.h[:, n_ * 512:(n_ + 1) * 512], in0=psb[bk_][:, :], in1=CS.h[:, n_ * 512:(n_ + 1) * 512],
                op=ALU.mult)))(bk, n), r=(psbuf[bk], CS.whole), w=(t_k.rng(n * 512, (n + 1) * 512),))
            bk2 = nbank()
            mm(psb[bk2][:, :], f2, t_k.h[:, n * 512:(n + 1) * 512], True, True,
               (cb.whole, t_k.rng(n * 512, (n + 1) * 512)), (psbuf[bk2],))
            E("act", (lambda bk_, n_: (lambda e: e.copy(out=kr2.h[:, n_ * 512:(n_ + 1) * 512], in_=psb[bk_][:, :])))(bk2, n),
              r=(psbuf[bk2],), w=(kr2.rng(n * 512, (n + 1) * 512),))
        ckpt('B3')
        for b in range(6):
            ws = next_w()
            banks = [nbank() for _ in range(2)]
            for kc in range(16):
                for n in range(2):
                    mm(psb[banks[n]][:, :], ws.h[:, kc * 128:(kc + 1) * 128],
                       hT.h[:, kc, 1024 + n * 512:1024 + (n + 1) * 512],
                       kc == 0, kc == 15, (ws.whole, hT_rd(kc, 1024 + n * 512, 1024 + (n + 1) * 512)), (psbuf[banks[n]],))
            for n in range(2):
                bk = banks[n]
                E("dve", (lambda bk_, b_, n_: (lambda e: e.tensor_scalar(
                    out=zq.h[:, b_, n_ * 512:(n_ + 1) * 512], in0=psb[bk_][:, :], scalar1=cvc(GQ + b_), scalar2=None,
                    op0=ALU.mult)))(bk, b, n), r=(psbuf[bk], cv.whole), w=(zq.sub(b),))
                E("act", (lambda bk_, b_, n_: (lambda e: e.activation(
                    out=sqq.h[:, b_, n_ * 512:(n_ + 1) * 512], in_=psb[bk_][:, :], func=AF.Square)))(bk, b, n),
                  r=(psbuf[bk],), w=(sqq.sub(b),))
        for n in range(2):
            bk = nbank()
            for b in range(6):
                mm(psb[bk][:, :], ones_b, sqq.h[:, b, n * 512:(n + 1) * 512], b == 0, b == 5,
                   (cb.whole, sqq.sub(b)), (psbuf[bk],))
            rstd_psum_inplace(bk, 512, 1.0 / Q_LORA)
            for b in range(6):
                E("dve", (lambda bk_, b_, n_: (lambda e: e.tensor_tensor(
                    out=qln.h[:, b_, n_ * 512:(n_ + 1) * 512], in0=zq.h[:, b_, n_ * 512:(n_ + 1) * 512],
                    in1=psb[bk_][:, :], op=ALU.mult)))(bk, b, n), r=(zq.sub(b), psbuf[bk]), w=(qln.sub(b),))
        ckpt('B4')
        tokr = [(896, 1024), (1024, 1536), (1536, 2048)]
        for c in range(8):
            sgt = sg[c % 2]
            ws = next_w()
            banks = [nbank() for _ in range(3)]
            for kc in range(16):
                for n, (a0, a1) in enumerate(tokr):
                    mm(psb[banks[n]][:, 0:a1 - a0], ws.h[:, kc * 128:(kc + 1) * 128], hT.h[:, kc, a0:a1],
                       kc == 0, kc == 15, (ws.whole, hT_rd(kc, a0, a1)), (psbuf[banks[n]],))
            for n, (a0, a1) in enumerate(tokr):
                E("act", (lambda bk_, a0_, a1_, sg_: (lambda e: e.activation(
                    out=sg_.h[:, a0_ - 896:a1_ - 896], in_=psb[bk_][:, 0:a1_ - a0_], func=AF.Sigmoid)))(banks[n], a0, a1, sgt),
                  r=(psbuf[banks[n]],), w=(sgt.whole,))
            ws = next_w()
            banks = [nbank() for _ in range(3)]
            for kc in range(16):
                for n, (a0, a1) in enumerate(tokr):
                    mm(psb[banks[n]][:, 0:a1 - a0], ws.h[:, kc * 128:(kc + 1) * 128], hT.h[:, kc, a0:a1],
                       kc == 0, kc == 15, (ws.whole, hT_rd(kc, a0, a1)), (psbuf[banks[n]],))
            for n, (a0, a1) in enumerate(tokr):
                E("dve", (lambda bk_, a0_, a1_, sg_, c_: (lambda e: e.tensor_tensor(
                    out=u_bf.h[:, c_, a0_ - 896:a1_ - 896], in0=psb[bk_][:, 0:a1_ - a0_], in1=sg_.h[:, a0_ - 896:a1_ - 896],
                    op=ALU.mult)))(banks[n], a0, a1, sgt, c), r=(psbuf[banks[n]], sgt.whole), w=(u_bf.sub(c),))

        ckpt('B')
        yv = sbt([128, 8, 1024], F32, AA0 + 0, "yv")
        Dr = [sbt([128, 31, 128], BF16, AA0 + (32 + 8 * i) * KB, f"Dr{i}") for i in range(2)]
        sqs = [sbt([128, 512], BF16, AA0 + 48 * KB + i * 1024, f"sqs{i}") for i in range(4)]
        bc0 = sbt([128, 1024], F32, AA0 + 52 * KB, "bc0")
        bc1 = sbt([128, 1024], F32, AA0 + 56 * KB, "bc1")
        mixT = sbt([128, 16, 1024], BF16, AA0 + 64 * KB, "mixT")
        mixT_attn = P.buf("sb", mixT.off + 8 * 2048, mixT.off + 16 * 2048, "mixT_attn")
        bS1 = [nbank(), nbank()]
        bS2 = [nbank(), nbank()]
        nsq = [0]
        pend_stats = []
        def build_D(c):
            dr_ = Dr[c % 2]
            E("dve", lambda e: e.tensor_tensor(
                out=dr_.h[:, :, :],
                in0=ident.unsqueeze(1).to_broadcast([128, CONV_K, 128]),
                in1=cv.h[:, CW + c * 31:CW + (c + 1) * 31].unsqueeze(2).to_broadcast([128, CONV_K, 128]),
                op=ALU.mult), r=(cb.whole, cv.whole), w=(dr_.whole,))

        build_D(0)
        for c in range(8):
            dr = Dr[c % 2]
            if c + 1 < 8:
                build_D(c + 1)
            for n in range(2):
                bk = nbank()
                while bk in bS1 or bk in bS2:
                    bk = nbank()
                for k in range(CONV_K):
                    mm(psb[bk][:, :], dr.h[:, k, :], u_bf.h[:, c, 98 + k + n * 512:98 + k + (n + 1) * 512],
                       k == 0, k == CONV_K - 1, (dr.sub(k), u_bf.sub(c)), (psbuf[bk],))
                yb = yv.rng(c * 1024 + n * 512, c * 1024 + (n + 1) * 512)
                E("act", (lambda bk_, c_, n_: (lambda e: e.activation(
                    out=yv.h[:, c_, n_ * 512:(n_ + 1) * 512], in_=psb[bk_][:, :], func=AF.Identity,
                    bias=cvc(CB + c_))))(bk, c, n), r=(psbuf[bk], cv.whole), w=(yb,))
                s1 = sqs[nsq[0] % 4]
                nsq[0] += 1
                s2 = sqs[nsq[0] % 4]
                nsq[0] += 1
                E("act", (lambda bk_, c_, s_: (lambda e: e.activation(
                    out=s_.h[:, :], in_=psb[bk_][:, :], func=AF.Square, bias=cvc(CB + c_))))(bk, c, s2),
                  r=(psbuf[bk], cv.whole), w=(s2.whole,))
                E("dve", (lambda c_, n_, s_: (lambda e: e.tensor_copy(
                    out=s_.h[:, :], in_=yv.h[:, c_, n_ * 512:(n_ + 1) * 512])))(c, n, s1), r=(yb,), w=(s1.whole,))
                if pend_stats:
                    pn, ps1, ps2, pc = pend_stats.pop()
                    mm(psb[bS1[pn]][:, :], ones_b, ps1.h[:, :], pc == 0, pc == 7, (cb.whole, ps1.whole), (psbuf[bS1[pn]],))
                    mm(psb[bS2[pn]][:, :], ones_b, ps2.h[:, :], pc == 0, pc == 7, (cb.whole, ps2.whole), (psbuf[bS2[pn]],))
                pend_stats.append((n, s1, s2, c))
        pn, ps1, ps2, pc = pend_stats.pop()
        mm(psb[bS1[pn]][:, :], ones_b, ps1.h[:, :], pc == 0, pc == 7, (cb.whole, ps1.whole), (psbuf[bS1[pn]],))
        mm(psb[bS2[pn]][:, :], ones_b, ps2.h[:, :], pc == 0, pc == 7, (cb.whole, ps2.whole), (psbuf[bS2[pn]],))
        for n in range(2):
            sl = slice(n * 512, (n + 1) * 512)
            b0 = bc0.rng(n * 512, (n + 1) * 512)
            b1 = bc1.rng(n * 512, (n + 1) * 512)
            E("dve", (lambda n_, sl_: (lambda e: e.tensor_scalar(out=bc0.h[:, sl_], in0=psb[bS1[n_]][:, :],
                                                                scalar1=1.0 / CONV_CH, scalar2=None, op0=ALU.mult)))(n, sl),
              r=(psbuf[bS1[n]],), w=(b0,))
            E("dve", (lambda sl_: (lambda e: e.tensor_tensor(out=bc1.h[:, sl_], in0=bc0.h[:, sl_], in1=bc0.h[:, sl_],
                                                            op=ALU.mult)))(sl), r=(b0,), w=(b1,))
            E("dve", (lambda n_, sl_: (lambda e: e.scalar_tensor_tensor(out=bc1.h[:, sl_], in0=psb[bS2[n_]][:, :],
                                                                       scalar=1.0 / CONV_CH, in1=bc1.h[:, sl_],
                                                                       op0=ALU.mult, op1=ALU.subtract)))(n, sl),
              r=(psbuf[bS2[n]], b1), w=(b1,))
            E("dve", (lambda sl_: (lambda e: e.tensor_scalar(out=bc1.h[:, sl_], in0=bc1.h[:, sl_], scalar1=EPS,
                                                            scalar2=None, op0=ALU.add)))(sl), r=(b1,), w=(b1,))
            E("act", (lambda sl_: (lambda e: e.activation(out=bc1.h[:, sl_], in_=bc1.h[:, sl_], func=AF.Sqrt)))(sl),
              r=(b1,), w=(b1,))
            E("dve", (lambda sl_: (lambda e: e.reciprocal(out=bc1.h[:, sl_], in_=bc1.h[:, sl_])))(sl), r=(b1,), w=(b1,))
        bS3 = [bS1[0], bS1[1]]
        for c in range(8):
            for n in range(2):
                sl = slice(n * 512, (n + 1) * 512)
                yb = yv.rng(c * 1024 + n * 512, c * 1024 + (n + 1) * 512)
                b0 = bc0.rng(n * 512, (n + 1) * 512)
                b1 = bc1.rng(n * 512, (n + 1) * 512)
                E("dve", (lambda c_, sl_: (lambda e: e.tensor_tensor(out=yv.h[:, c_, sl_], in0=yv.h[:, c_, sl_],
                                                                    in1=bc0.h[:, sl_], op=ALU.subtract)))(c, sl),
                  r=(yb, b0), w=(yb,))
                E("dve", (lambda c_, sl_: (lambda e: e.tensor_tensor(out=yv.h[:, c_, sl_], in0=yv.h[:, c_, sl_],
                                                                    in1=bc1.h[:, sl_], op=ALU.mult)))(c, sl),
                  r=(yb, b1), w=(yb,))
                E("act", (lambda c_, sl_: (lambda e: e.activation(out=yv.h[:, c_, sl_], in_=yv.h[:, c_, sl_], func=AF.Silu,
                                                                 scale=cvc(LG + c_), bias=cvc(LB + c_))))(c, sl),
                  r=(yb, cv.whole), w=(yb,))
                s2 = sqs[nsq[0] % 4]
                nsq[0] += 1
                E("act", (lambda c_, sl_, s_: (lambda e: e.activation(out=s_.h[:, :], in_=yv.h[:, c_, sl_],
                                                                     func=AF.Square)))(c, sl, s2), r=(yb,), w=(s2.whole,))
                mm(psb[bS3[n]][:, :], ones_b, s2.h[:, :], c == 0, c == 7, (cb.whole, s2.whole), (psbuf[bS3[n]],))
        for n in range(2):
            rstd_psum_inplace(bS3[n], 512, 1.0 / CONV_CH)
        for c in range(8):
            for n in range(2):
                sl = slice(n * 512, (n + 1) * 512)
                yb = yv.rng(c * 1024 + n * 512, c * 1024 + (n + 1) * 512)
                E("dve", (lambda c_, n_, sl_: (lambda e: e.scalar_tensor_tensor(
                    out=mixT.h[:, c_, sl_], in0=yv.h[:, c_, sl_], scalar=cvc(G2 + c_), in1=psb[bS3[n_]][:, :],
                    op0=ALU.mult, op1=ALU.mult)))(c, n, sl), r=(yb, cv.whole, psbuf[bS3[n]]), w=(mixT.sub(c),))

        ckpt('E')
        knT = sbt([128, 8, 2048], BF16, AA0 + 0, "knT")
        qnT = sbt([128, 8, 1024], BF16, AA0 + 32 * KB, "qnT")
        tq = sbt([128, 8, 1024], BF16, AA0 + 48 * KB, "tq")
        VA = sbt([128, 8, 8 * 130], BF16, AA0 + 96 * KB, "VA")
        VB = sbt([128, 8, 8 * 130], BF16, AA0 + 148 * KB, "VB")
        for h in range(N_HEADS):
            ws = next_w()
            banks = [nbank() for _ in range(2)]
            for kc in range(6):
                for n in range(2):
                    mm(psb[banks[n]][:, :], ws.h[:, kc * 128:(kc + 1) * 128], qln.h[:, kc, n * 512:(n + 1) * 512],
                       kc == 0, kc == 5, (ws.whole, qln.sub(kc)), (psbuf[banks[n]],))
            for n in range(2):
                E("act", (lambda bk_, h_, n_: (lambda e: e.copy(out=qnT.h[:, h_, n_ * 512:(n_ + 1) * 512],
                                                               in_=psb[bk_][:, :])))(banks[n], h, n),
                  r=(psbuf[banks[n]],), w=(qnT.sub(h),))
            ws = next_w()
            banks = [nbank() for _ in range(2)]
            for kc in range(6):
                for n in range(2):
                    mm(psb[banks[n]][:, :], ws.h[:, kc * 128:(kc + 1) * 128], qln.h[:, kc, n * 512:(n + 1) * 512],
                       kc == 0, kc == 5, (ws.whole, qln.sub(kc)), (psbuf[banks[n]],))
            for n in range(2):
                E("dve", (lambda bk_, h_, n_: (lambda e: e.tensor_tensor(
                    out=tq.h[:, h_, n_ * 512:(n_ + 1) * 512], in0=psb[bk_][:, :],
                    in1=CS.h[:, 1024 + n_ * 512:1024 + (n_ + 1) * 512], op=ALU.mult)))(banks[n], h, n),
                  r=(psbuf[banks[n]], CS.whole), w=(tq.sub(h),))
        for h in range(N_HEADS):
            ws = next_w()
            banks = [nbank() for _ in range(4)]
            for kc in range(4):
                for n in range(4):
                    mm(psb[banks[n]][:, :], ws.h[:, kc * 128:(kc + 1) * 128], kvn.h[:, kc, n * 512:(n + 1) * 512],
                       kc == 0, kc == 3, (ws.whole, kvn.sub(kc)), (psbuf[banks[n]],))
            for n in range(4):
                if n % 2 == 0:
                    E("act", (lambda bk_, h_, n_: (lambda e: e.copy(out=knT.h[:, h_, n_ * 512:(n_ + 1) * 512],
                                                                   in_=psb[bk_][:, :])))(banks[n], h, n),
                      r=(psbuf[banks[n]],), w=(knT.sub(h),))
                else:
                    E("dve", (lambda bk_, h_, n_: (lambda e: e.tensor_copy(out=knT.h[:, h_, n_ * 512:(n_ + 1) * 512],
                                                                          in_=psb[bk_][:, :])))(banks[n], h, n),
                      r=(psbuf[banks[n]],), w=(knT.sub(h),))
        wv = [next_w(), next_w()]
        E("dve", lambda e: e.tensor_scalar(out=VA.h[:, :, :].rearrange("p t (h d) -> p (t h) d", d=130)[:, :, 128:129],
                                           in0=onesf.h[:, 0:64].rearrange("p (a b) -> p a b", b=1),
                                           scalar1=cvc(PFL), scalar2=None, op0=ALU.mult),
          r=(onesf.whole, cv.whole), w=(VA.whole,))
        E("dve", lambda e: e.memset(VB.h[:, :, :].rearrange("p t (h d) -> p (t h) d", d=130)[:, :, 128:129], 1.0),
          w=(VB.whole,))
        for t in range(16):
            Vt = VA if t < 8 else VB
            tt = t % 8
            banks = [nbank() for _ in range(2)]
            for kc in range(4):
                for hf in range(2):
                    mm(psb[banks[hf]][:, :], kvn.h[:, kc, t * 128:(t + 1) * 128],
                       wv[kc // 2].h[:, (kc % 2) * 1024 + hf * 512:(kc % 2) * 1024 + (hf + 1) * 512],
                       kc == 0, kc == 3, (kvn.sub(kc), wv[kc // 2].whole), (psbuf[banks[hf]],))
            for hf in range(2):
                dst = Vt.h[:, tt, :].rearrange("p (h d) -> p h d", d=130)[:, hf * 4:(hf + 1) * 4, 0:128]
                srcp = psb[banks[hf]][:, :].rearrange("p (h d) -> p h d", d=128)
                if t < 8:
                    E("dve", (lambda d_, s_: (lambda e: e.tensor_scalar(out=d_, in0=s_, scalar1=cvc(PFL), scalar2=None,
                                                                       op0=ALU.mult)))(dst, srcp),
                      r=(psbuf[banks[hf]], cv.whole), w=(Vt.sub(tt),))
                else:
                    E("act", (lambda d_, s_: (lambda e: e.copy(out=d_, in_=s_)))(dst, srcp),
                      r=(psbuf[banks[hf]],), w=(Vt.sub(tt),))

        ckpt('C')
        attn = sbt([128, 8, 1024], F32, AA0 + 116 * KB, "attn")
        PT = [sbt([128, 512], BF16, AA0 + 170 * KB + i * 1024, f"PT{i}") for i in range(4)]
        rc = sbt([128, 64], F32, AA0 + 174 * KB, "rc")
        nrc = [0]
        ST_B = [4, 5, 6, 7]
        ACC_B = [0, 1, 2, 3]
        items = []
        for h in range(N_HEADS):
            for qb in range(2):
                for kc in range(8 + 4 * qb + 4):
                    items.append((h, qb, kc))
        LOOK = 3

        def d_qk(n):
            h, qb, kc = items[n]
            q0 = max(kc - 8 - 4 * qb, 0) * 128
            bk = ST_B[n % 4]
            mm(psb[bk][:, q0:512], knT.h[:, h, kc * 128:(kc + 1) * 128],
               qnT.h[:, h, qb * 512 + q0:qb * 512 + 512], True, False,
               (knT.sub(h), qnT.sub(h)), (psbuf[bk],))
            mm(psb[bk][:, q0:512], kr2.h[:, kc * 128:(kc + 1) * 128],
               tq.h[:, h, qb * 512 + q0:qb * 512 + 512], False, True,
               (kr2.whole, tq.sub(h)), (psbuf[bk],))

        def d_rest(n):
            h, qb, kc = items[n]
            j = kc - 8 - 4 * qb
            i0 = max(j, 0)
            q0 = i0 * 128
            bk = ST_B[n % 4]
            pt = PT[n % 4]
            Vt = VA if kc < 8 else VB
            E("act", lambda e: e.activation(out=pt.h[:, q0:512], in_=psb[bk][:, q0:512], func=AF.Exp, scale=SCALE),
              r=(psbuf[bk],), w=(pt.whole,))
            if j >= 0:
                E("dve", lambda e: e.tensor_tensor(out=pt.h[:, q0:q0 + 128], in0=pt.h[:, q0:q0 + 128], in1=tri,
                                                   op=ALU.mult), r=(pt.whole, cb.whole), w=(pt.whole,))
            for i in range(i0, 4):
                last = 8 + 4 * qb + i
                ab = ACC_B[i]
                mm(psb[ab][:, 0:129], pt.h[:, i * 128:(i + 1) * 128], Vt.h[:, kc % 8, h * 130:h * 130 + 129],
                   kc == 0, kc == last, (pt.whole, Vt.sub(kc % 8)), (psbuf[ab],))
                if kc == last:
                    col = nrc[0] % 64
                    nrc[0] += 1
                    rb = rc.rng(col, col + 1)
                    E("dve", (lambda ab_, col_: (lambda e: e.reciprocal(out=rc.h[:, col_:col_ + 1],
                                                                       in_=psb[ab_][:, 128:129])))(ab, col),
                      r=(psbuf[ab],), w=(rb,))
                    E("dve", (lambda ab_, col_, i_: (lambda e: e.tensor_scalar(
                        out=attn.h[:, qb * 4 + i_, h * 128:(h + 1) * 128], in0=psb[ab_][:, 0:128],
                        scalar1=rc.h[:, col_:col_ + 1], scalar2=None, op0=ALU.mult)))(ab, col, i),
                      r=(psbuf[ab], rb), w=(attn.rng((qb * 4 + i) * 1024 + h * 128, (qb * 4 + i) * 1024 + (h + 1) * 128),))

        for n in range(len(items) + LOOK):
            if n < len(items):
                d_qk(n)
            if n >= LOOK:
                d_rest(n - LOOK)

        ckpt('D')
        gbcD = sbt([128, 1024], F32, AA0 + 148 * KB, "gbcD")
        junkD = [sbt([128, 1024], BF16, AA0 + (152 + 2 * i) * KB, f"junkD{i}") for i in range(2)]
        xnD = [sbt([128, 1024], BF16, AA0 + (156 + 2 * i) * KB, f"xnD{i}") for i in range(8)]
        E("sync", lambda e: e.dma_start(out=gbcD.h[:, :], in_=gbc_d[4][:, 0:1024]), w=(gbcD.whole,), key="gbc")
        cD_ss = stcol(8)
        cD_rs = stcol(8)
        brsD = {}
        usedD = {}

        def mixT_attn_tile(i):
            return tuple(P.buf("sb", mixT.off + kc * 2048 + i * 256, mixT.off + kc * 2048 + (i + 1) * 256)
                         for kc in range(8, 16))

        def Dn_s(i):
            brsD[i] = nt_stats(attn.h[:, i, :], attn.sub(i), junkD[i % 2], 1024, cD_ss + i, cD_rs + i)

        def Dn_aa(i):
            usedD[i] = nt_apply_a(attn.h[:, i, :], attn.sub(i), brsD[i], gbcD, xnD[i], 1024, cD_rs + i,
                                  banks=[4 + i % 4])

        def Dn_ab(i):
            nt_apply_b(usedD[i], (lambda g: mixT.h[:, 8:16, i * 128:(i + 1) * 128]), mixT_attn_tile(i), ["dve"])

        def Dn_a(i):
            Dn_aa(i)
            Dn_ab(i)

        ckpt('Dn')
        w_out = sbt([128, 16, 2048], BF16, AA0 + 0, "w_out")
        for kc in (0, 1, 8, 12, 2, 3, 9, 13, 4, 5, 10, 14, 6, 7, 11, 15):
            E("pool", (lambda kc_: (lambda e: e.dma_start(out=w_out.h[:, kc_, :],
                                                          in_=w_out_p[:, kc_ * 2048:(kc_ + 1) * 2048])))(kc),
              w=(w_out.sub(kc),), key=f"wout{kc % 4}")
        xF = [sbt([128, 2048], F32, AA0 + (96 + 8 * i) * KB, f"xF{i}") for i in range(2)]
        gpost = sbt([128, 2048], F32, AA0 + 112 * KB, "gpost")
        gpre = sbt([128, 2048], F32, AA0 + 120 * KB, "gpre")
        x1t = sbt([128, 2048], F32, AA0 + 128 * KB, "x1t")
        xnF = sbt([128, 2048], BF16, AA0 + 136 * KB, "xnF")
        hfT = sbt([128, 16, 1024], BF16, AA0 + 140 * KB, "hfT")
        junkF = sbt([128, 2048], BF16, AA0 + 172 * KB, "junkF")
        cF_p = stcol(32)
        cF_ss = stcol(8)
        cF_rs = stcol(8)
        cF_ss2 = stcol(8)
        cF_rs2 = stcol(8)
        brsF = {}
        brsF2 = {}

        KC_ORDER = [0, 1, 2, 3, 4, 5, 6, 8, 9, 10, 12, 13, 14, 7, 11, 15]

        def F1a(i):
            s_ = i % 2
            E("sync", lambda e: e.dma_start(out=xF[s_].h[:, :], in_=x_own[i * 128:(i + 1) * 128, :]),
              w=(xF[s_].whole,), key=f"xF{s_}")
            for cbk in (2, 3, 0, 1):
                bk = (i % 2) * 4 + cbk
                for kc in KC_ORDER:
                    mb = mixT.sub(kc) if kc < 8 else P.buf("sb", mixT.off + kc * 2048 + i * 256,
                                                           mixT.off + kc * 2048 + (i + 1) * 256)
                    mm(psb[bk][:, :], mixT.h[:, kc, i * 128:(i + 1) * 128], w_out.h[:, kc, cbk * 512:(cbk + 1) * 512],
                       kc == KC_ORDER[0], kc == KC_ORDER[-1], (mb, w_out.sub(kc)), (psbuf[bk],))

        def F1b(i):
            for cbk in (2, 3, 0, 1):
                bk = (i % 2) * 4 + cbk
                pc = cF_p + i * 4 + cbk
                E("act", (lambda bk_, pc_, c_: (lambda e: e.activation(
                    out=junkF.h[:, c_ * 512:(c_ + 1) * 512], in_=psb[bk_][:, :], func=AF.Square,
                    accum_out=st.h[:, pc_:pc_ + 1])))(bk, pc, cbk),
                  r=(psbuf[bk],), w=(junkF.rng(cbk * 512, (cbk + 1) * 512), st.rng(pc, pc + 1)))
            E("dve", lambda e: e.tensor_reduce(out=st.h[:, cF_ss + i:cF_ss + i + 1],
                                               in_=st.h[:, cF_p + 4 * i:cF_p + 4 * i + 4],
                                               axis=mybir.AxisListType.X, op=ALU.add),
              r=(st.rng(cF_p + 4 * i, cF_p + 4 * i + 4),), w=(st.rng(cF_ss + i, cF_ss + i + 1),))
            brsF[i] = rstd_from_ss(cF_ss + i, cF_rs + i, 1, 1.0 / D_MODEL)

        def F2(i):
            s_ = i % 2
            for cbk in range(4):
                bk = (i % 2) * 4 + cbk
                sl = slice(cbk * 512, (cbk + 1) * 512)
                E("dve", (lambda bk_, sl_: (lambda e: e.scalar_tensor_tensor(
                    out=x1t.h[:, sl_], in0=psb[bk_][:, :], scalar=st.h[:, cF_rs + i:cF_rs + i + 1], in1=gpost.h[:, sl_],
                    op0=ALU.mult, op1=ALU.mult)))(bk, sl),
                  r=(psbuf[bk], brsF[i], gpost.whole), w=(x1t.rng(cbk * 512, (cbk + 1) * 512),))
            E("dve", lambda e: e.tensor_tensor(out=x1t.h[:, :], in0=x1t.h[:, :], in1=xF[s_].h[:, :], op=ALU.add),
              r=(x1t.whole, xF[s_].whole), w=(x1t.whole,))
            E("sync", lambda e: e.dma_start(out=x1_d[i * 128:(i + 1) * 128, :], in_=x1t.h[:, :]),
              r=(x1t.whole,), w=(), key="x1w")
            brsF2[i] = nt_stats(x1t.h[:, :], x1t.whole, xnF, 2048, cF_ss2 + i, cF_rs2 + i)

        def F3(i):
            nt_apply(x1t.h[:, :], x1t.whole, brsF2[i], gpre, xnF, 2048, cF_rs2 + i,
                     (lambda g: hfT.h[:, g * 8:(g + 1) * 8, i * 128:(i + 1) * 128]),
                     (hfT.whole,), ["act", "dve"], banks=[(i % 2) * 4, (i % 2) * 4 + 1])

        Dn_s(0)
        Dn_s(1)
        Dn_a(0)
        Dn_s(2)
        Dn_a(1)
        F1a(0)
        for i in range(2, 6):
            Dn_s(i + 1)
            Dn_aa(i)
        Dn_s(7)
        for i in range(2, 6):
            Dn_ab(i)
        for i in range(6, 8):
            Dn_aa(i)
        for i in range(6, 8):
            Dn_ab(i)
        E("sync", lambda e: e.dma_start(out=gpost.h[:, :], in_=gbc_d[1]), w=(gpost.whole,), key="gbc2")
        E("sync", lambda e: e.dma_start(out=gpre.h[:, :], in_=gbc_d[2]), w=(gpre.whole,), key="gbc2")
        F1b(0)
        for i in range(8):
            if i + 1 < 8:
                F1a(i + 1)
            F2(i)
            F3(i)
            if i + 1 < 8:
                F1b(i + 1)
        ckpt('F')
        actT = sbt([128, NFB, 1024], BF16, AA0 + 0, "actT")
        sgf = [sbt([128, 512], F32, AA0 + 172 * KB + i * 2048, f"sgf{i}") for i in range(2)]
        nsg = [0]
        for f in range(NFB):
            wg = next_w()
            wu = next_w()
            bg = [nbank(), nbank()]
            bu = [nbank(), nbank()]
            for kc in range(16):
                for n in range(2):
                    mm(psb[bg[n]][:, :], wg.h[:, kc * 128:(kc + 1) * 128], hfT.h[:, kc, n * 512:(n + 1) * 512],
                       kc == 0, kc == 15, (wg.whole, hfT.whole), (psbuf[bg[n]],))
            for kc in range(16):
                for n in range(2):
                    mm(psb[bu[n]][:, :], wu.h[:, kc * 128:(kc + 1) * 128], hfT.h[:, kc, n * 512:(n + 1) * 512],
                       kc == 0, kc == 15, (wu.whole, hfT.whole), (psbuf[bu[n]],))
            for n in range(2):
                sgt = sgf[nsg[0] % 2]
                nsg[0] += 1
                E("act", (lambda bk_, s_: (lambda e: e.activation(out=s_.h[:, :], in_=psb[bk_][:, :], func=AF.Silu)))(bg[n], sgt),
                  r=(psbuf[bg[n]],), w=(sgt.whole,))
                E("dve", (lambda bk_, s_, f_, n_: (lambda e: e.tensor_tensor(
                    out=actT.h[:, f_, n_ * 512:(n_ + 1) * 512], in0=psb[bk_][:, :], in1=s_.h[:, :], op=ALU.mult)))(bu[n], sgt, f, n),
                  r=(psbuf[bu[n]], sgt.whole), w=(actT.sub(f),))

        ckpt('G1')
        ff = sbt([128, 8, 2048], F32, AA0 + 88 * KB, "ff")
        xr = [sbt([128, 2048], F32, AA0 + 8 * i * KB, f"xr{i}") for i in range(8)]
        gffn = sbt([128, 2048], F32, AA0 + 168 * KB, "gffn")
        junkG = [sbt([128, 512], BF16, AA0 + 176 * KB + i * 1024, f"junkG{i}") for i in range(2)]
        E("sync", lambda e: e.dma_start(out=gffn.h[:, :], in_=gbc_d[3]), w=(gffn.whole,), key="gbc3")
        cG_p = stcol(32)
        cG_ss = stcol(8)
        cG_rs = stcol(8)
        for cbk in range(4):
            for fg in range(11):
                ws = next_w()
                for i in range(8):
                    for fb in range(4):
                        fidx = fg * 4 + fb
                        mm(psb[i][:, :], actT.h[:, fidx, i * 128:(i + 1) * 128], ws.h[:, fb * 512:(fb + 1) * 512],
                           fg == 0 and fb == 0, fg == 10 and fb == 3, (actT.sub(fidx), ws.whole), (psbuf[i],))
            for i in range(8):
                sl = slice(cbk * 512, (cbk + 1) * 512)
                pc = cG_p + i * 4 + cbk
                fb_ = ff.rng(i * 2048 + cbk * 512, i * 2048 + (cbk + 1) * 512)
                E("dve", (lambda i_, sl_: (lambda e: e.tensor_tensor(out=ff.h[:, i_, sl_], in0=psb[i_][:, :],
                                                                     in1=gffn.h[:, sl_], op=ALU.mult)))(i, sl),
                  r=(psbuf[i], gffn.whole), w=(fb_,))
                E("act", (lambda i_, pc_: (lambda e: e.activation(out=junkG[i_ % 2].h[:, :], in_=psb[i_][:, :], func=AF.Square,
                                                                 accum_out=st.h[:, pc_:pc_ + 1])))(i, pc),
                  r=(psbuf[i],), w=(junkG[i % 2].whole, st.rng(pc, pc + 1)))
        E("dve", lambda e: e.tensor_reduce(out=st.h[:, cG_ss:cG_ss + 8],
                                           in_=st.h[:, cG_p:cG_p + 32].rearrange("p (a b) -> p a b", b=4),
                                           axis=mybir.AxisListType.X, op=ALU.add),
          r=(st.rng(cG_p, cG_p + 32),), w=(st.rng(cG_ss, cG_ss + 8),))
        brsG = rstd_from_ss(cG_ss, cG_rs, 8, 1.0 / D_MODEL)
        for i in range(8):
            E("sync", (lambda i_: (lambda e: e.dma_start(out=xr[i_].h[:, :], in_=x1_d[i_ * 128:(i_ + 1) * 128, :])))(i),
              w=(xr[i].whole,), key=f"xr{i % 2}")
            P.q["sync"][-1].waits.append(("dma", "x1w", P.dma_cnt["x1w"]))
        for i in range(8):
            E("dve", (lambda i_: (lambda e: e.scalar_tensor_tensor(
                out=ff.h[:, i_, :], in0=ff.h[:, i_, :], scalar=st.h[:, cG_rs + i_:cG_rs + i_ + 1], in1=xr[i_].h[:, :],
                op0=ALU.mult, op1=ALU.add)))(i), r=(ff.sub(i), brsG, xr[i].whole), w=(ff.sub(i),))
            E("sync", (lambda i_: (lambda e: e.dma_start(out=out_d[i_ * 128:(i_ + 1) * 128, :], in_=ff.h[:, i_, :])))(i),
              r=(ff.sub(i),), w=(), key="outw")
        fin = E("sync", None)
        fin.waits.append(("dma", "outw", P.dma_cnt["outw"]))
        assert wcur[0] == len(wpieces), (wcur[0], len(wpieces))


    except _Stop:
        pass

    fin_all = P.op("sync", None)
    for k_, v_ in P.dma_cnt.items():
        fin_all.waits.append(("dma", k_, v_))

    for e_ in ENGS:
        cnt = 0
        for ins in P.q[e_]:
            if ins.signal and not ins.is_dma:
                cnt += 1
                ins.value = cnt
    keys = sorted(P.dma_cnt.keys())
    sem_ctx = {}
    sems_eng = {e_: nc.alloc_semaphore(f"s_{e_}") for e_ in ENGS}
    sems_key = {k: nc.alloc_semaphore(f"d_{k}") for k in keys}

    def replay(ename, eng):
        waited = {}
        for ins in P.q[ename]:
            for w in ins.waits:
                if w[0] == "eng":
                    p = w[1]
                    sem, val = sems_eng[p.eng], p.value
                else:
                    sem, val = sems_key[w[1]], w[2]
                k = id(sem)
                if waited.get(k, 0) < val:
                    eng.wait_ge(sem, val)
                    waited[k] = val
            if ins.fn is None:
                continue
            bi = ins.fn(eng)
            if ins.is_dma:
                bi.then_inc(sems_key[ins.key], 16)
            elif ins.signal:
                bi.then_inc(sems_eng[ename], 1)

    with nc.Block() as block:
        @block.sync
        def _(e):
            replay("sync", e)

        @block.scalar
        def _(e):
            replay("act", e)

        @block.vector
        def _(e):
            replay("dve", e)

        @block.gpsimd
        def _(e):
            replay("pool", e)

        @block.tensor
        def _(e):
            replay("pe", e)

    stats = {e_: len(P.q[e_]) for e_ in ENGS}
    stats["sig"] = {e_: sum(1 for i in P.q[e_] if i.signal and not i.is_dma) for e_ in ENGS}
    return nc, stats


def _blocks_k(w, ncols_per_block):
    K, N = w.shape
    kc = K // 128
    nb = N // ncols_per_block
    a = w.reshape(kc, 128, nb, ncols_per_block).transpose(2, 1, 0, 3)
    return np.ascontiguousarray(a.reshape(nb, 128, kc * ncols_per_block))


def prepare_inputs(x, positions, pre_mix_norm, w_in, q_norm, w_uq, kv_norm, w_ukv, conv_w, conv_b, conv_ln_g,
                   conv_ln_b, conv_out_norm, attn_out_norm, w_out, post_mix_norm, pre_ffn_norm, w_gate, w_up,
                   w_down, post_ffn_norm):
    f = np.float32
    x = np.asarray(x, f)
    positions = np.asarray(positions, np.int32)
    w_in = np.asarray(w_in, f)[0]
    w_uq = np.asarray(w_uq, f)[0]
    w_ukv = np.asarray(w_ukv, f)[0]
    w_out = np.asarray(w_out, f)[0]
    w_gate = np.asarray(w_gate, f)[0]
    w_up = np.asarray(w_up, f)[0]
    w_down = np.asarray(w_down, f)[0]

    c1 = 2 * CONV_CH
    c2 = c1 + Q_LORA
    c3 = c2 + KV_LORA
    cols = []
    cols += list(range(c2, c3))
    cols += list(range(c3, c3 + 64)) + list(range(c3 + 32, c3 + 64)) + list(range(c3, c3 + 32))
    cols += list(range(c1, c2))
    for c in range(8):
        cols += list(range(CONV_CH + c * 128, CONV_CH + (c + 1) * 128))
        cols += list(range(c * 128, (c + 1) * 128))
    w_in_p = _blocks_k(w_in[:, cols], 128)
    cols = []
    for h in range(N_HEADS):
        b0 = h * 192
        cols += list(range(b0, b0 + 128))
        cols += list(range(b0 + 128, b0 + 192)) + list(range(b0 + 160, b0 + 192)) + list(range(b0 + 128, b0 + 160))
    w_uq_p = _blocks_k(w_uq[:, cols], 128)
    kcols = []
    vcols = []
    for h in range(N_HEADS):
        kcols += list(range(h * 256, h * 256 + 128))
        vcols += list(range(h * 256 + 128, h * 256 + 256))
    w_uk_p = _blocks_k(w_ukv[:, kcols], 128)
    wv = w_ukv[:, vcols]
    w_uv_p = np.ascontiguousarray(wv.reshape(2, 2, 128, 1024).transpose(0, 2, 1, 3).reshape(2, 128, 2048))
    w_out_p = np.ascontiguousarray(w_out.reshape(16, 128, 2048).transpose(1, 0, 2).reshape(128, 16 * 2048))
    g_p = _blocks_k(w_gate, 128)
    u_p = _blocks_k(w_up, 128)
    w_gu_p = np.ascontiguousarray(np.stack([g_p, u_p], axis=1).reshape(88, 128, 2048))
    wd = w_down.reshape(11, 4, 128, 4, 512)
    w_dn_p = np.ascontiguousarray(wd.transpose(3, 0, 2, 1, 4).reshape(44, 128, 2048))

    c_bf = np.zeros((128, 512), np.float32)
    c_bf[:, 0:128] = np.eye(128)
    c_bf[:, 128:256] = 1.0
    e64 = np.eye(64)
    c_bf[:, 256:384] = np.block([[e64, e64], [e64, e64]])
    kk = np.arange(128)[:, None]
    qq = np.arange(128)[None, :]
    c_bf[:, 384:512] = (qq >= kk).astype(np.float32)
    c_bf = c_bf.astype(ml_dtypes.bfloat16)

    cvec = np.zeros((128, NCV), f)
    cw = np.asarray(conv_w, f)[0]
    for c in range(8):
        cvec[:, CW + c * 31:CW + (c + 1) * 31] = cw[:, c * 128:(c + 1) * 128].T
    cvec[:, CB:CB + 8] = np.asarray(conv_b, f)[0].reshape(8, 128).T
    cvec[:, LG:LG + 8] = np.asarray(conv_ln_g, f)[0].reshape(8, 128).T
    cvec[:, LB:LB + 8] = np.asarray(conv_ln_b, f)[0].reshape(8, 128).T
    cvec[:, G2:G2 + 8] = np.asarray(conv_out_norm, f)[0].reshape(8, 128).T
    cvec[:, GQ:GQ + 6] = np.asarray(q_norm, f)[0].reshape(6, 128).T
    cvec[:, GKV:GKV + 4] = np.asarray(kv_norm, f)[0].reshape(4, 128).T
    inv_freq = (np.float32(10000.0) ** (-np.arange(0, 64, 2, dtype=np.float32) / np.float32(64))).astype(f)
    cvec[:, IFQ] = np.tile(inv_freq, 4)
    cvec[0:64, PHS] = np.float32(np.pi / 2)
    cvec[:, SGN] = 1.0
    cvec[64:96, SGN] = -1.0

    gbc = np.zeros((5, 128, 2048), f)
    gbc[0] = np.asarray(pre_mix_norm, f)[0][None, :]
    gbc[1] = np.asarray(post_mix_norm, f)[0][None, :]
    gbc[2] = np.asarray(pre_ffn_norm, f)[0][None, :]
    gbc[3] = np.asarray(post_ffn_norm, f)[0][None, :]
    gbc[4, :, 0:1024] = np.asarray(attn_out_norm, f)[0][None, :]

    shared = dict(c_bf=c_bf, gbc=gbc, w_in_p=w_in_p, w_uq_p=w_uq_p, w_uk_p=w_uk_p, w_uv_p=w_uv_p,
                  w_out_p=w_out_p, w_gu_p=w_gu_p, w_dn_p=w_dn_p)
    in_maps = []
    for core in range(8):
        b, half = core // 2, core % 2
        m = dict(shared)
        m["x_own"] = np.ascontiguousarray(x[b, half * TOWN:(half + 1) * TOWN])
        cvc = cvec.copy()
        pos = np.zeros((2048,), np.int32)
        if half == 1:
            m["x_prev"] = np.ascontiguousarray(x[b, 0:TOWN])
            pos[:] = positions[b, 0:2048]
            cvc[:, PFL] = 1.0
        else:
            m["x_prev"] = np.zeros((TOWN, D_MODEL), f)
            pos[1024:] = positions[b, 0:1024]
            cvc[:, PFL] = 0.0
        m["cvec"] = cvc
        m["pos_bc"] = np.ascontiguousarray(np.broadcast_to(pos[None, :], (128, 2048)))
        in_maps.append(m)
    return in_maps


_CACHE = {}


def kernel(**inputs):
    if "nc" not in _CACHE:
        _CACHE["nc"], _CACHE["stats"] = build_program()
    nc = _CACHE["nc"]
    in_maps = prepare_inputs(**inputs)
    res = run_bass_kernel_spmd(nc, in_maps, core_ids=list(range(8)))
    out = np.zeros((BATCH, SEQ, D_MODEL), np.float32)
    for core in range(8):
        b, half = core // 2, core % 2
        out[b, half * TOWN:(half + 1) * TOWN] = res.results[core]["out"]
    return out
```

```python
import os
import numpy as np
import ml_dtypes
import concourse.bass as bass
import concourse.mybir as mybir
from concourse.bass_utils import run_bass_kernel_spmd

F32 = mybir.dt.float32
BF16 = mybir.dt.bfloat16
I32 = mybir.dt.int32
AF = mybir.ActivationFunctionType
ALU = mybir.AluOpType
PI = float(np.pi)

D_MODEL = 2048
SEQ = 2048
BATCH = 4
TOWN = 1024
CONV_CH = 1024
CONV_K = 31
N_HEADS = 8
Q_LORA = 768
KV_LORA = 512
D_FF = 5632
NFB = D_FF // 128
EPS = 1e-6
SCALE = 192 ** -0.5

CW = 0
CB = 248
LG = 256
LB = 264
G2 = 272
GQ = 280
GKV = 286
IFQ = 290
PHS = 291
SGN = 292
PFL = 293
NCV = 320

KB = 1024


class Buf:
    __slots__ = ("space", "lo", "hi", "name")

    def __init__(self, space, lo, hi, name=""):
        self.space, self.lo, self.hi, self.name = space, lo, hi, name


class Ins:
    __slots__ = ("eng", "fn", "is_dma", "key", "signal", "value", "waits", "dma_val", "idx")

    def __init__(self, eng, fn, is_dma, key):
        self.eng, self.fn, self.is_dma, self.key = eng, fn, is_dma, key
        self.idx = -1
        self.signal = False
        self.value = None
        self.waits = []
        self.dma_val = None


ENGS = ["sync", "act", "dve", "pool", "pe"]


class Prog:
    def __init__(self):
        self.q = {e: [] for e in ENGS}
        self.wr = {"sb": [], "ps": []}
        self.rd = {"sb": [], "ps": []}
        self.dma_cnt = {}

    def buf(self, space, lo, hi, name=""):
        return Buf(space, lo, hi, name)

    def op(self, eng, fn, reads=(), writes=(), dma_key=None):
        ins = Ins(eng, fn, dma_key is not None, dma_key)
        ins.idx = len(self.q[eng])
        need_eng = {}
        need_dma = {}

        def add(p):
            if p is ins:
                return
            if p.is_dma:
                need_dma[p.key] = 1
                return
            if p.eng == eng and not ins.is_dma and eng == "pe":
                return
            cur = need_eng.get(p.eng)
            if cur is None or cur.idx < p.idx:
                need_eng[p.eng] = p

        for b in reads:
            for w in self.wr[b.space]:
                if w[0] < b.hi and b.lo < w[1]:
                    add(w[2])
            if b.space == "ps" and eng != "pe":
                lo = b.lo // 2048 * 2048
                hi = -(-b.hi // 2048) * 2048
                for r in self.rd["ps"]:
                    if r[0] < hi and lo < r[1] and r[3].eng != eng and r[3].eng != "pe":
                        add(r[3])
        for b in writes:
            for w in self.wr[b.space]:
                if w[0] < b.hi and b.lo < w[1]:
                    add(w[2])
            for r in self.rd[b.space]:
                if r[0] < b.hi and b.lo < r[1]:
                    add(r[3])
        for p in need_eng.values():
            p.signal = True
            ins.waits.append(("eng", p))
        for k in need_dma:
            ins.waits.append(("dma", k, self.dma_cnt[k]))
        if dma_key is not None:
            self.dma_cnt[dma_key] = self.dma_cnt.get(dma_key, 0) + 16
            ins.dma_val = self.dma_cnt[dma_key]
        for b in writes:
            sp = b.space
            self.wr[sp] = [w for w in self.wr[sp] if not (b.lo <= w[0] and w[1] <= b.hi)]
            self.wr[sp].append([b.lo, b.hi, ins])
            self.rd[sp] = [r for r in self.rd[sp] if not (b.lo <= r[0] and r[1] <= b.hi)]
        rk = ("dma", dma_key) if ins.is_dma else eng
        for b in reads:
            lst = self.rd[b.space]
            for r in lst:
                if r[0] == b.lo and r[1] == b.hi and r[2] == rk:
                    r[3] = ins
                    break
            else:
                lst.append([b.lo, b.hi, rk, ins])
        self.q[eng].append(ins)
        return ins


def build_program(stop_after=None, tensors=None):
    nc = bass.Bass("TRN2", target_bir_lowering=False)
    P = Prog()

    def din(name, shape, dt):
        return nc.dram_tensor(name, list(shape), dt, kind="ExternalInput").ap()

    x_own = din("x_own", [TOWN, D_MODEL], F32)
    x_prev = din("x_prev", [TOWN, D_MODEL], F32)
    pos_bc = din("pos_bc", [128, 2048], I32)
    c_bf = din("c_bf", [128, 512], BF16)
    cvec_d = din("cvec", [128, NCV], F32)
    gbc_d = din("gbc", [5, 128, 2048], F32)
    w_in_p = din("w_in_p", [27, 128, 2048], F32)
    w_uq_p = din("w_uq_p", [16, 128, 768], F32)
    w_uk_p = din("w_uk_p", [8, 128, 512], F32)
    w_uv_p = din("w_uv_p", [2, 128, 2048], F32)
    w_out_p = din("w_out_p", [128, 16 * 2048], F32)
    w_gu_p = din("w_gu_p", [88, 128, 2048], F32)
    w_dn_p = din("w_dn_p", [44, 128, 2048], F32)
    out_d = nc.dram_tensor("out", [TOWN, D_MODEL], F32, kind="ExternalOutput").ap()
    x1_d = nc.dram_tensor("x1_scratch", [TOWN, D_MODEL], F32).ap()
    dbg = {}

    base = (nc.sbuf_base + 63) // 64 * 64
    CONST0 = base
    WR0 = CONST0 + 4 * KB
    AA0 = WR0 + 24 * KB
    assert AA0 + 178 * KB <= nc.sbuf_top, (AA0 + 178 * KB, nc.sbuf_top)

    def dsz(dt):
        return 4 if dt in (F32, I32) else 2

    class T:
        def __init__(self, name, shape, dt, off):
            self.h = nc.alloc_sbuf_tensor_at(name, list(shape), dt, offset=off)
            self.off = off
            self.shape = shape
            self.dt = dt
            self.nbytes = int(np.prod(shape[1:])) * dsz(dt)
            self.whole = P.buf("sb", off, off + self.nbytes, name)
            self._subs = {}
            if tensors is not None:
                tensors[name] = self

        def sub(self, i, n=None):
            key = (i, n)
            if key not in self._subs:
                slab = self.nbytes // self.shape[1]
                cnt = 1 if n is None else n
                self._subs[key] = P.buf("sb", self.off + i * slab, self.off + (i + cnt) * slab)
            return self._subs[key]

        def rng(self, lo_el, hi_el):
            key = ("r", lo_el, hi_el)
            if key not in self._subs:
                self._subs[key] = P.buf("sb", self.off + lo_el * dsz(self.dt), self.off + hi_el * dsz(self.dt))
            return self._subs[key]

    _names = [0]

    def sbt(shape, dt, off, name=None):
        _names[0] += 1
        return T(name or f"t{_names[0]}", shape, dt, off)

    cb = sbt([128, 512], BF16, CONST0, "cb")
    ident = cb.h[:, 0:128]
    ones_b = cb.h[:, 128:256]
    f2 = cb.h[:, 256:384]
    tri = cb.h[:, 384:512]
    cv = sbt([128, NCV], F32, CONST0 + 1024, "cv")
    st = sbt([128, 320], F32, CONST0 + 1024 + NCV * 4, "st")
    onesf = sbt([128, 64], F32, CONST0 + 1024 + NCV * 4 + 1280, "onesf")
    assert CONST0 + 1024 + NCV * 4 + 1280 + 256 <= WR0

    def cvc(col):
        return cv.h[:, col:col + 1]

    _stn = [0]

    def stcol(n=1):
        c = _stn[0]
        _stn[0] += n
        assert _stn[0] <= 320
        return c

    NSLOT = 6
    wslots = [sbt([128, 2048], BF16, WR0 + i * 4 * KB, f"ws{i}") for i in range(NSLOT)]
    wpieces = []
    wstate = {"issued": 0}

    psb = [nc.alloc_psum_tensor(f"psb{i}", [128, 512], F32) for i in range(8)]
    psbuf = [P.buf("ps", i * 2048, (i + 1) * 2048, f"ps{i}") for i in range(8)]
    psbf = [psb[i][:, :].bitcast(BF16) for i in range(8)]
    _bank = [0]

    def nbank():
        b = _bank[0] % 8
        _bank[0] += 1
        return b

    _psub = {}

    def psub(bank, lo, hi):
        k = (bank, lo, hi)
        if k not in _psub:
            _psub[k] = P.buf("ps", bank * 2048 + lo * 4, bank * 2048 + hi * 4)
        return _psub[k]

    def E(eng, fn, r=(), w=(), key=None):
        return P.op(eng, fn, r, w, key)

    def mm(out, lhsT, rhs, start, stop, r, w):
        return E("pe", lambda e: e.matmul(out, lhsT=lhsT, rhs=rhs, start=start, stop=stop), r, w)

    def issue_weights(upto, after=()):
        while wstate["issued"] < min(upto, len(wpieces)):
            i = wstate["issued"]
            src, n = wpieces[i]
            slot = wslots[i % NSLOT]
            E("pool", (lambda s, n_, sl: (lambda e: e.dma_start(out=sl.h[:, 0:n_], in_=s)))(src, n, slot),
              r=after, w=(slot.whole,), key=f"ws{i % NSLOT}")
            wstate["issued"] += 1

    wcur = [0]

    def next_w(prefetch=4):
        i = wcur[0]
        wcur[0] += 1
        issue_weights(i + 1 + prefetch)
        return wslots[i % NSLOT]

    for j in range(27):
        wpieces.append((w_in_p[j], 2048))
    for j in range(16):
        wpieces.append((w_uq_p[j], 768))
    for j in range(8):
        wpieces.append((w_uk_p[j], 512))
    for j in range(2):
        wpieces.append((w_uv_p[j], 2048))
    for j in range(88):
        wpieces.append((w_gu_p[j], 2048))
    for j in range(44):
        wpieces.append((w_dn_p[j], 2048))

    class _Stop(Exception):
        pass

    def ckpt(name):
        if stop_after == name:
            raise _Stop()

    try:
        E("sync", lambda e: e.dma_start(out=cb.h[:, :], in_=c_bf), w=(cb.whole,), key="const")
        E("sync", lambda e: e.dma_start(out=cv.h[:, :], in_=cvec_d), w=(cv.whole,), key="const")
        E("dve", lambda e: e.memset(st.h[:, :], 0.0), w=(st.whole,))
        E("dve", lambda e: e.memset(onesf.h[:, :], 1.0), w=(onesf.whole,))

        def rstd_from_ss(c_ss, c_out, n, inv_n):
            c_ms = stcol(n)
            c_sq = stcol(n)
            bs = st.rng(c_ss, c_ss + n)
            bm = st.rng(c_ms, c_ms + n)
            bq = st.rng(c_sq, c_sq + n)
            bo = st.rng(c_out, c_out + n)
            E("dve", lambda e: e.tensor_scalar(out=st.h[:, c_ms:c_ms + n], in0=st.h[:, c_ss:c_ss + n], scalar1=inv_n,
                                               scalar2=EPS, op0=ALU.mult, op1=ALU.add), r=(bs,), w=(bm,))
            E("act", lambda e: e.activation(out=st.h[:, c_sq:c_sq + n], in_=st.h[:, c_ms:c_ms + n], func=AF.Sqrt),
              r=(bm,), w=(bq,))
            E("dve", lambda e: e.reciprocal(out=st.h[:, c_out:c_out + n], in_=st.h[:, c_sq:c_sq + n]), r=(bq,), w=(bo,))
            return bo

        def rstd_psum_inplace(bank, n, inv_n):
            b = psbuf[bank]
            ap = psb[bank][:, 0:n]
            E("dve", lambda e: e.tensor_scalar(out=ap, in0=ap, scalar1=inv_n, scalar2=EPS, op0=ALU.mult, op1=ALU.add),
              r=(b,), w=(b,))
            E("act", lambda e: e.activation(out=ap, in_=ap, func=AF.Sqrt), r=(b,), w=(b,))
            E("dve", lambda e: e.reciprocal(out=ap, in_=ap), r=(b,), w=(b,))

        def nt_stats(src_ap, src_buf, junk_t, width, c_ss, c_rs):
            bs = st.rng(c_ss, c_ss + 1)
            E("act", lambda e: e.activation(out=junk_t.h[:, 0:width], in_=src_ap, func=AF.Square,
                                            accum_out=st.h[:, c_ss:c_ss + 1]), r=(src_buf,), w=(junk_t.whole, bs))
            return rstd_from_ss(c_ss, c_rs, 1, 1.0 / width)

        def nt_apply_a(src_ap, src_buf, brs, gb_t, xn_t, width, c_rs, banks=None):
            nchunk = width // 128
            E("dve", lambda e: e.scalar_tensor_tensor(out=xn_t.h[:, 0:width], in0=src_ap,
                                                      scalar=st.h[:, c_rs:c_rs + 1], in1=gb_t.h[:, 0:width],
                                                      op0=ALU.mult, op1=ALU.mult),
              r=(src_buf, brs, gb_t.whole), w=(xn_t.whole,))
            used = []
            for g in range(nchunk // 8):
                bk = nbank() if banks is None else banks[g]
                used.append(bk)
                for c8 in range(8):
                    c = g * 8 + c8
                    E("pe", (lambda bk_, c8_, c_: (lambda e: e.transpose(psbf[bk_][:, c8_ * 128:(c8_ + 1) * 128],
                                                                       xn_t.h[:, c_ * 128:(c_ + 1) * 128], ident)))(bk, c8, c),
                      r=(xn_t.whole, cb.whole), w=(psbuf[bk],))
            return used

        def nt_apply_b(used, dst_fn, dst_bufs, evac_engs):
            for g, bk in enumerate(used):
                eng = evac_engs[g % len(evac_engs)]
                src = psbf[bk][:, 0:1024].rearrange("p (a b) -> p a b", a=8)
                dst = dst_fn(g)
                wb = dst_bufs(g) if callable(dst_bufs) else dst_bufs
                if eng == "act":
                    E("act", (lambda d_, s_: (lambda e: e.copy(out=d_, in_=s_)))(dst, src), r=(psbuf[bk],), w=wb)
                else:
                    E("dve", (lambda d_, s_: (lambda e: e.tensor_copy(out=d_, in_=s_)))(dst, src), r=(psbuf[bk],), w=wb)

        def nt_apply(src_ap, src_buf, brs, gb_t, xn_t, width, c_rs, dst_fn, dst_bufs, evac_engs, banks=None):
            used = nt_apply_a(src_ap, src_buf, brs, gb_t, xn_t, width, c_rs, banks)
            nt_apply_b(used, dst_fn, dst_bufs, evac_engs)

        hT = sbt([128, 16, 2048], BF16, AA0 + 0, "hT")
        xa = [sbt([128, 2048], F32, AA0 + (64 + 8 * i) * KB, f"xa{i}") for i in range(4)]
        gbcA = sbt([128, 2048], F32, AA0 + 96 * KB, "gbcA")
        xnA = [sbt([128, 2048], BF16, AA0 + (104 + 4 * i) * KB, f"xnA{i}") for i in range(3)]
        junkA = [sbt([128, 2048], BF16, AA0 + (116 + 4 * i) * KB, f"junkA{i}") for i in range(4)]
        E("sync", lambda e: e.dma_start(out=gbcA.h[:, :], in_=gbc_d[0]), w=(gbcA.whole,), key="gbc")
        def hT_rd(kc, a0, a1):
            return hT.rng(kc * 2048 + a0, kc * 2048 + a1)

        def hT_wr(t, g):
            return tuple(hT.rng(kc * 2048 + t * 128, kc * 2048 + (t + 1) * 128) for kc in range(g * 8, g * 8 + 8))

        cA_ss = stcol(16)
        cA_rs = stcol(16)
        brsA = {}

        def A1(t):
            s_ = t % 4
            src = x_prev[t * 128:(t + 1) * 128, :] if t < 8 else x_own[(t - 8) * 128:(t - 7) * 128, :]
            E("sync", (lambda s__, src_: (lambda e: e.dma_start(out=xa[s__].h[:, :], in_=src_)))(s_, src),
              w=(xa[s_].whole,), key=f"xa{s_}")
            brsA[t] = nt_stats(xa[s_].h[:, :], xa[s_].whole, junkA[t % 4], 2048, cA_ss + t, cA_rs + t)

        usedA = {}

        def A2a(t):
            s_ = t % 4
            usedA[t] = nt_apply_a(xa[s_].h[:, :], xa[s_].whole, brsA[t], gbcA, xnA[t % 3], 2048, cA_rs + t)

        def A2b(t):
            nt_apply_b(usedA[t], (lambda g: hT.h[:, g * 8:(g + 1) * 8, t * 128:(t + 1) * 128]),
                       (lambda g: hT_wr(t, g)), ["act", "dve"])

        A1(0)
        A1(1)
        A1(2)
        issue_weights(NSLOT, after=(xa[0].whole, xa[1].whole, xa[2].whole))
        A2a(0)
        for t in range(16):
            if t + 1 < 16:
                A2a(t + 1)
            if t + 3 < 16:
                A1(t + 3)
            A2b(t)

        ckpt('A')
        zz = sbt([128, 4, 2048], F32, AA0 + 64 * KB, "zz")
        zq = sbt([128, 6, 1024], F32, AA0 + 64 * KB, "zq")
        sqz = sbt([128, 4, 2048], BF16, AA0 + 96 * KB, "sqz")
        sqq = sbt([128, 6, 1024], BF16, AA0 + 96 * KB, "sqq")
        t_k = sbt([128, 2048], BF16, AA0 + 112 * KB, "t_k")
        kvn = sbt([128, 4, 2048], BF16, AA0 + 116 * KB, "kvn")
        qln = sbt([128, 6, 1024], BF16, AA0 + 132 * KB, "qln")
        u_bf = sbt([128, 8, 1152], BF16, AA0 + 148 * KB, "u_bf")
        kr2 = sbt([128, 2048], BF16, AA0 + 166 * KB, "kr2")
        sg = [sbt([128, 1152], F32, AA0 + 64 * KB + i * 4608, f"sg{i}") for i in range(2)]

        CS = sbt([128, 2048], F32, AA0 + 170 * KB, "CS")

        def emit_rope():
            posi = sbt([128, 2048], I32, AA0 + 132 * KB, "posi")
            posf = sbt([128, 2048], F32, AA0 + 140 * KB, "posf")
            tmpa = sbt([128, 2048], F32, AA0 + 148 * KB, "tmpa")
            E("sync", lambda e: e.dma_start(out=posi.h[:, :], in_=pos_bc), w=(posi.whole,), key="pos")
            E("dve", lambda e: e.tensor_copy(out=posf.h[:, :], in_=posi.h[:, :]), r=(posi.whole,), w=(posf.whole,))
            E("dve", lambda e: e.tensor_scalar(out=CS.h[:, :], in0=posf.h[:, :], scalar1=cvc(IFQ), scalar2=cvc(PHS),
                                               op0=ALU.mult, op1=ALU.add), r=(posf.whole, cv.whole), w=(CS.whole,))
            E("dve", lambda e: e.tensor_scalar(out=tmpa.h[:, :], in0=CS.h[:, :], scalar1=1.0 / (2 * PI), scalar2=None,
                                               op0=ALU.mult), r=(CS.whole,), w=(tmpa.whole,))
            E("dve", lambda e: e.tensor_copy(out=posi.h[:, :], in_=tmpa.h[:, :]), r=(tmpa.whole,), w=(posi.whole,))
            E("dve", lambda e: e.tensor_copy(out=posf.h[:, :], in_=posi.h[:, :]), r=(posi.whole,), w=(posf.whole,))
            C1 = 6.28125
            C2 = 2 * PI - C1
            E("dve", lambda e: e.scalar_tensor_tensor(out=CS.h[:, :], in0=posf.h[:, :], scalar=-C1, in1=CS.h[:, :],
                                                      op0=ALU.mult, op1=ALU.add), r=(posf.whole, CS.whole), w=(CS.whole,))
            E("dve", lambda e: e.scalar_tensor_tensor(out=CS.h[:, :], in0=posf.h[:, :], scalar=-C2, in1=CS.h[:, :],
                                                      op0=ALU.mult, op1=ALU.add), r=(posf.whole, CS.whole), w=(CS.whole,))
            E("dve", lambda e: e.tensor_single_scalar(out=tmpa.h[:, :], in_=CS.h[:, :], scalar=PI, op=ALU.is_gt),
              r=(CS.whole,), w=(tmpa.whole,))
            E("dve", lambda e: e.scalar_tensor_tensor(out=CS.h[:, :], in0=tmpa.h[:, :], scalar=-2 * PI, in1=CS.h[:, :],
                                                      op0=ALU.mult, op1=ALU.add), r=(tmpa.whole, CS.whole), w=(CS.whole,))
            E("dve", lambda e: e.tensor_scalar(out=CS.h[:, :], in0=CS.h[:, :], scalar1=-PI, scalar2=PI,
                                               op0=ALU.max, op1=ALU.min), r=(CS.whole,), w=(CS.whole,))
            E("act", lambda e: e.activation(out=CS.h[:, :], in_=CS.h[:, :], func=AF.Sin), r=(CS.whole,), w=(CS.whole,))
            E("dve", lambda e: e.tensor_scalar(out=CS.h[:, :], in0=CS.h[:, :], scalar1=cvc(SGN), scalar2=None,
                                               op0=ALU.mult), r=(CS.whole, cv.whole), w=(CS.whole,))


        for b in range(4):
            if b == 1:
                emit_rope()
            ws = next_w()
            banks = [nbank() for _ in range(4)]
            for kc in range(16):
                for n in range(4):
                    mm(psb[banks[n]][:, :], ws.h[:, kc * 128:(kc + 1) * 128], hT.h[:, kc, n * 512:(n + 1) * 512],
                       kc == 0, kc == 15, (ws.whole, hT_rd(kc, n * 512, (n + 1) * 512)), (psbuf[banks[n]],))
            for n in range(4):
                if os.environ.get("KDBG") == "noevac":
                    break
                bk = banks[n]
                E("dve", (lambda bk_, b_, n_: (lambda e: e.tensor_scalar(
                    out=zz.h[:, b_, n_ * 512:(n_ + 1) * 512], in0=psb[bk_][:, :], scalar1=cvc(GKV + b_), scalar2=None,
                    op0=ALU.mult)))(bk, b, n), r=(psbuf[bk], cv.whole), w=(zz.sub(b),))
                if os.environ.get("KDBG") == "noact":
                    continue
                E("act", (lambda bk_, b_, n_: (lambda e: e.activation(
                    out=sqz.h[:, b_, n_ * 512:(n_ + 1) * 512], in_=psb[bk_][:, :], func=AF.Square)))(bk, b, n),
                  r=(psbuf[bk],), w=(sqz.sub(b),) + ((psbuf[bk],) if os.environ.get("KDBG") == "serial" else ()))
        ckpt('B1')
        for n in range(4):
            bk = nbank()
            for b in range(4):
                mm(psb[bk][:, :], ones_b, sqz.h[:, b, n * 512:(n + 1) * 512], b == 0, b == 3,
                   (cb.whole, sqz.sub(b)), (psbuf[bk],))
            rstd_psum_inplace(bk, 512, 1.0 / KV_LORA)
            for b in range(4):
                E("dve", (lambda bk_, b_, n_: (lambda e: e.tensor_tensor(
                    out=kvn.h[:, b_, n_ * 512:(n_ + 1) * 512], in0=zz.h[:, b_, n_ * 512:(n_ + 1) * 512],
                    in1=psb[bk_][:, :], op=ALU.mult)))(bk, b, n), r=(zz.sub(b), psbuf[bk]), w=(kvn.sub(b),))
        ckpt('B2')
        ws = next_w()
        banks = [nbank() for _ in range(4)]
        for kc in range(16):
            for n in range(4):
                mm(psb[banks[n]][:, :], ws.h[:, kc * 128:(kc + 1) * 128], hT.h[:, kc, n * 512:(n + 1) * 512],
                   kc == 0, kc == 15, (ws.whole, hT_rd(kc, n * 512, (n + 1) * 512)), (psbuf[banks[n]],))
        for n in range(4):
            bk = banks[n]
            E("dve", (lambda bk_, n_: (lambda e: e.tensor_tensor(
                out=t_k.h[:, n_ * 512:(n_ + 1) * 512], in0=psb[bk_][:, :], in1=CS.h[:, n_ * 512:(n_ + 1) * 512],
                op=ALU.mult)))(bk, n), r=(psbuf[bk], CS.whole), w=(t_k.rng(n * 512, (n + 1) * 512),))
            bk2 = nbank()
            mm(psb[bk2][:, :], f2, t_k.h[:, n * 512:(n + 1) * 512], True, True,
               (cb.whole, t_k.rng(n * 512, (n + 1) * 512)), (psbuf[bk2],))
            E("act", (lambda bk_, n_: (lambda e: e.copy(out=kr2.h[:, n_ * 512:(n_ + 1) * 512], in_=psb[bk_][:, :])))(bk2, n),
              r=(psbuf[bk2],), w=(kr2.rng(n * 512, (n + 1) * 512),))
        ckpt('B3')
        for b in range(6):
            ws = next_w()
            banks = [nbank() for _ in range(2)]
            for kc in range(16):
                for n in range(2):
                    mm(psb[banks[n]][:, :], ws.h[:, kc * 128:(kc + 1) * 128],
                       hT.h[:, kc, 1024 + n * 512:1024 + (n + 1) * 512],
                       kc == 0, kc == 15, (ws.whole, hT_rd(kc, 1024 + n * 512, 1024 + (n + 1) * 512)), (psbuf[banks[n]],))
            for n in range(2):
                bk = banks[n]
                E("dve", (lambda bk_, b_, n_: (lambda e: e.tensor_scalar(
                    out=zq.h[:, b_, n_ * 512:(n_ + 1) * 512], in0=psb[bk_][:, :], scalar1=cvc(GQ + b_), scalar2=None,
                    op0=ALU.mult)))(bk, b, n), r=(psbuf[bk], cv.whole), w=(zq.sub(b),))
                E("act", (lambda bk_, b_, n_: (lambda e: e.activation(
                    out=sqq.h[:, b_, n_ * 512:(n_ + 1) * 512], in_=psb[bk_][:, :], func=AF.Square)))(bk, b, n),
                  r=(psbuf[bk],), w=(sqq.sub(b),))
        for n in range(2):
            bk = nbank()
            for b in range(6):
                mm(psb[bk][:, :], ones_b, sqq.h[:, b, n * 512:(n + 1) * 512], b == 0, b == 5,
                   (cb.whole, sqq.sub(b)), (psbuf[bk],))
            rstd_psum_inplace(bk, 512, 1.0 / Q_LORA)
            for b in range(6):
                E("dve", (lambda bk_, b_, n_: (lambda e: e.tensor_tensor(
                    out=qln.h[:, b_, n_ * 512:(n_ + 1) * 512], in0=zq.h[:, b_, n_ * 512:(n_ + 1) * 512],
                    in1=psb[bk_][:, :], op=ALU.mult)))(bk, b, n), r=(zq.sub(b), psbuf[bk]), w=(qln.sub(b),))
        ckpt('B4')
        tokr = [(896, 1024), (1024, 1536), (1536, 2048)]
        for c in range(8):
            sgt = sg[c % 2]
            ws = next_w()
            banks = [nbank() for _ in range(3)]
            for kc in range(16):
                for n, (a0, a1) in enumerate(tokr):
                    mm(psb[banks[n]][:, 0:a1 - a0], ws.h[:, kc * 128:(kc + 1) * 128], hT.h[:, kc, a0:a1],
                       kc == 0, kc == 15, (ws.whole, hT_rd(kc, a0, a1)), (psbuf[banks[n]],))
            for n, (a0, a1) in enumerate(tokr):
                E("act", (lambda bk_, a0_, a1_, sg_: (lambda e: e.activation(
                    out=sg_.h[:, a0_ - 896:a1_ - 896], in_=psb[bk_][:, 0:a1_ - a0_], func=AF.Sigmoid)))(banks[n], a0, a1, sgt),
                  r=(psbuf[banks[n]],), w=(sgt.whole,))
            ws = next_w()
            banks = [nbank() for _ in range(3)]
            for kc in range(16):
                for n, (a0, a1) in enumerate(tokr):
                    mm(psb[banks[n]][:, 0:a1 - a0], ws.h[:, kc * 128:(kc + 1) * 128], hT.h[:, kc, a0:a1],
                       kc == 0, kc == 15, (ws.whole, hT_rd(kc, a0, a1)), (psbuf[banks[n]],))
            for n, (a0, a1) in enumerate(tokr):
                E("dve", (lambda bk_, a0_, a1_, sg_, c_: (lambda e: e.tensor_tensor(
                    out=u_bf.h[:, c_, a0_ - 896:a1_ - 896], in0=psb[bk_][:, 0:a1_ - a0_], in1=sg_.h[:, a0_ - 896:a1_ - 896],
                    op=ALU.mult)))(banks[n], a0, a1, sgt, c), r=(psbuf[banks[n]], sgt.whole), w=(u_bf.sub(c),))

        ckpt('B')
        yv = sbt([128, 8, 1024], F32, AA0 + 0, "yv")
        Dr = [sbt([128, 31, 128], BF16, AA0 + (32 + 8 * i) * KB, f"Dr{i}") for i in range(2)]
        sqs = [sbt([128, 512], BF16, AA0 + 48 * KB + i * 1024, f"sqs{i}") for i in range(4)]
        bc0 = sbt([128, 1024], F32, AA0 + 52 * KB, "bc0")
        bc1 = sbt([128, 1024], F32, AA0 + 56 * KB, "bc1")
        mixT = sbt([128, 16, 1024], BF16, AA0 + 64 * KB, "mixT")
        mixT_attn = P.buf("sb", mixT.off + 8 * 2048, mixT.off + 16 * 2048, "mixT_attn")
        bS1 = [nbank(), nbank()]
        bS2 = [nbank(), nbank()]
        nsq = [0]
        pend_stats = []
        def build_D(c):
            dr_ = Dr[c % 2]
            E("dve", lambda e: e.tensor_tensor(
                out=dr_.h[:, :, :],
                in0=ident.unsqueeze(1).to_broadcast([128, CONV_K, 128]),
                in1=cv.h[:, CW + c * 31:CW + (c + 1) * 31].unsqueeze(2).to_broadcast([128, CONV_K, 128]),
                op=ALU.mult), r=(cb.whole, cv.whole), w=(dr_.whole,))

        build_D(0)
        for c in range(8):
            dr = Dr[c % 2]
            if c + 1 < 8:
                build_D(c + 1)
            for n in range(2):
                bk = nbank()
                while bk in bS1 or bk in bS2:
                    bk = nbank()
                for k in range(CONV_K):
                    mm(psb[bk][:, :], dr.h[:, k, :], u_bf.h[:, c, 98 + k + n * 512:98 + k + (n + 1) * 512],
                       k == 0, k == CONV_K - 1, (dr.sub(k), u_bf.sub(c)), (psbuf[bk],))
                yb = yv.rng(c * 1024 + n * 512, c * 1024 + (n + 1) * 512)
                E("act", (lambda bk_, c_, n_: (lambda e: e.activation(
                    out=yv.h[:, c_, n_ * 512:(n_ + 1) * 512], in_=psb[bk_][:, :], func=AF.Identity,
                    bias=cvc(CB + c_))))(bk, c, n), r=(psbuf[bk], cv.whole), w=(yb,))
                s1 = sqs[nsq[0] % 4]
                nsq[0] += 1
                s2 = sqs[nsq[0] % 4]
                nsq[0] += 1
                E("act", (lambda bk_, c_, s_: (lambda e: e.activation(
                    out=s_.h[:, :], in_=psb[bk_][:, :], func=AF.Square, bias=cvc(CB + c_))))(bk, c, s2),
                  r=(psbuf[bk], cv.whole), w=(s2.whole,))
                E("dve", (lambda c_, n_, s_: (lambda e: e.tensor_copy(
                    out=s_.h[:, :], in_=yv.h[:, c_, n_ * 512:(n_ + 1) * 512])))(c, n, s1), r=(yb,), w=(s1.whole,))
                if pend_stats:
                    pn, ps1, ps2, pc = pend_stats.pop()
                    mm(psb[bS1[pn]][:, :], ones_b, ps1.h[:, :], pc == 0, pc == 7, (cb.whole, ps1.whole), (psbuf[bS1[pn]],))
                    mm(psb[bS2[pn]][:, :], ones_b, ps2.h[:, :], pc == 0, pc == 7, (cb.whole, ps2.whole), (psbuf[bS2[pn]],))
                pend_stats.append((n, s1, s2, c))
        pn, ps1, ps2, pc = pend_stats.pop()
        mm(psb[bS1[pn]][:, :], ones_b, ps1.h[:, :], pc == 0, pc == 7, (cb.whole, ps1.whole), (psbuf[bS1[pn]],))
        mm(psb[bS2[pn]][:, :], ones_b, ps2.h[:, :], pc == 0, pc == 7, (cb.whole, ps2.whole), (psbuf[bS2[pn]],))
        for n in range(2):
            sl = slice(n * 512, (n + 1) * 512)
            b0 = bc0.rng(n * 512, (n + 1) * 512)
            b1 = bc1.rng(n * 512, (n + 1) * 512)
            E("dve", (lambda n_, sl_: (lambda e: e.tensor_scalar(out=bc0.h[:, sl_], in0=psb[bS1[n_]][:, :],
                                                                scalar1=1.0 / CONV_CH, scalar2=None, op0=ALU.mult)))(n, sl),
              r=(psbuf[bS1[n]],), w=(b0,))
            E("dve", (lambda sl_: (lambda e: e.tensor_tensor(out=bc1.h[:, sl_], in0=bc0.h[:, sl_], in1=bc0.h[:, sl_],
                                                            op=ALU.mult)))(sl), r=(b0,), w=(b1,))
            E("dve", (lambda n_, sl_: (lambda e: e.scalar_tensor_tensor(out=bc1.h[:, sl_], in0=psb[bS2[n_]][:, :],
                                                                       scalar=1.0 / CONV_CH, in1=bc1.h[:, sl_],
                                                                       op0=ALU.mult, op1=ALU.subtract)))(n, sl),
              r=(psbuf[bS2[n]], b1), w=(b1,))
            E("dve", (lambda sl_: (lambda e: e.tensor_scalar(out=bc1.h[:, sl_], in0=bc1.h[:, sl_], scalar1=EPS,
                                                            scalar2=None, op0=ALU.add)))(sl), r=(b1,), w=(b1,))
            E("act", (lambda sl_: (lambda e: e.activation(out=bc1.h[:, sl_], in_=bc1.h[:, sl_], func=AF.Sqrt)))(sl),
              r=(b1,), w=(b1,))
            E("dve", (lambda sl_: (lambda e: e.reciprocal(out=bc1.h[:, sl_], in_=bc1.h[:, sl_])))(sl), r=(b1,), w=(b1,))
        bS3 = [bS1[0], bS1[1]]
        for c in range(8):
            for n in range(2):
                sl = slice(n * 512, (n + 1) * 512)
                yb = yv.rng(c * 1024 + n * 512, c * 1024 + (n + 1) * 512)
                b0 = bc0.rng(n * 512, (n + 1) * 512)
                b1 = bc1.rng(n * 512, (n + 1) * 512)
                E("dve", (lambda c_, sl_: (lambda e: e.tensor_tensor(out=yv.h[:, c_, sl_], in0=yv.h[:, c_, sl_],
                                                                    in1=bc0.h[:, sl_], op=ALU.subtract)))(c, sl),
                  r=(yb, b0), w=(yb,))
                E("dve", (lambda c_, sl_: (lambda e: e.tensor_tensor(out=yv.h[:, c_, sl_], in0=yv.h[:, c_, sl_],
                                                                    in1=bc1.h[:, sl_], op=ALU.mult)))(c, sl),
                  r=(yb, b1), w=(yb,))
                E("act", (lambda c_, sl_: (lambda e: e.activation(out=yv.h[:, c_, sl_], in_=yv.h[:, c_, sl_], func=AF.Silu,
                                                                 scale=cvc(LG + c_), bias=cvc(LB + c_))))(c, sl),
                  r=(yb, cv.whole), w=(yb,))
                s2 = sqs[nsq[0] % 4]
                nsq[0] += 1
                E("act", (lambda c_, sl_, s_: (lambda e: e.activation(out=s_.h[:, :], in_=yv.h[:, c_, sl_],
                                                                     func=AF.Square)))(c, sl, s2), r=(yb,), w=(s2.whole,))
                mm(psb[bS3[n]][:, :], ones_b, s2.h[:, :], c == 0, c == 7, (cb.whole, s2.whole), (psbuf[bS3[n]],))
        for n in range(2):
            rstd_psum_inplace(bS3[n], 512, 1.0 / CONV_CH)
        for c in range(8):
            for n in range(2):
                sl = slice(n * 512, (n + 1) * 512)
                yb = yv.rng(c * 1024 + n * 512, c * 1024 + (n + 1) * 512)
                E("dve", (lambda c_, n_, sl_: (lambda e: e.scalar_tensor_tensor(
                    out=mixT.h[:, c_, sl_], in0=yv.h[:, c_, sl_], scalar=cvc(G2 + c_), in1=psb[bS3[n_]][:, :],
                    op0=ALU.mult, op1=ALU.mult)))(c, n, sl), r=(yb, cv.whole, psbuf[bS3[n]]), w=(mixT.sub(c),))

        ckpt('E')
        knT = sbt([128, 8, 2048], BF16, AA0 + 0, "knT")
        qnT = sbt([128, 8, 1024], BF16, AA0 + 32 * KB, "qnT")
        tq = sbt([128, 8, 1024], BF16, AA0 + 48 * KB, "tq")
        VA = sbt([128, 8, 8 * 130], BF16, AA0 + 96 * KB, "VA")
        VB = sbt([128, 8, 8 * 130], BF16, AA0 + 148 * KB, "VB")
        for h in range(N_HEADS):
            ws = next_w()
            banks = [nbank() for _ in range(2)]
            for kc in range(6):
                for n in range(2):
                    mm(psb[banks[n]][:, :], ws.h[:, kc * 128:(kc + 1) * 128], qln.h[:, kc, n * 512:(n + 1) * 512],
                       kc == 0, kc == 5, (ws.whole, qln.sub(kc)), (psbuf[banks[n]],))
            for n in range(2):
                E("act", (lambda bk_, h_, n_: (lambda e: e.copy(out=qnT.h[:, h_, n_ * 512:(n_ + 1) * 512],
                                                               in_=psb[bk_][:, :])))(banks[n], h, n),
                  r=(psbuf[banks[n]],), w=(qnT.sub(h),))
            ws = next_w()
            banks = [nbank() for _ in range(2)]
            for kc in range(6):
                for n in range(2):
                    mm(psb[banks[n]][:, :], ws.h[:, kc * 128:(kc + 1) * 128], qln.h[:, kc, n * 512:(n + 1) * 512],
                       kc == 0, kc == 5, (ws.whole, qln.sub(kc)), (psbuf[banks[n]],))
            for n in range(2):
                E("dve", (lambda bk_, h_, n_: (lambda e: e.tensor_tensor(
                    out=tq.h[:, h_, n_ * 512:(n_ + 1) * 512], in0=psb[bk_][:, :],
                    in1=CS.h[:, 1024 + n_ * 512:1024 + (n_ + 1) * 512], op=ALU.mult)))(banks[n], h, n),
                  r=(psbuf[banks[n]], CS.whole), w=(tq.sub(h),))
        for h in range(N_HEADS):
            ws = next_w()
            banks = [nbank() for _ in range(4)]
            for kc in range(4):
                for n in range(4):
                    mm(psb[banks[n]][:, :], ws.h[:, kc * 128:(kc + 1) * 128], kvn.h[:, kc, n * 512:(n + 1) * 512],
                       kc == 0, kc == 3, (ws.whole, kvn.sub(kc)), (psbuf[banks[n]],))
            for n in range(4):
                if n % 2 == 0:
                    E("act", (lambda bk_, h_, n_: (lambda e: e.copy(out=knT.h[:, h_, n_ * 512:(n_ + 1) * 512],
                                                                   in_=psb[bk_][:, :])))(banks[n], h, n),
                      r=(psbuf[banks[n]],), w=(knT.sub(h),))
                else:
                    E("dve", (lambda bk_, h_, n_: (lambda e: e.tensor_copy(out=knT.h[:, h_, n_ * 512:(n_ + 1) * 512],
                                                                          in_=psb[bk_][:, :])))(banks[n], h, n),
                      r=(psbuf[banks[n]],), w=(knT.sub(h),))
        wv = [next_w(), next_w()]
        E("dve", lambda e: e.tensor_scalar(out=VA.h[:, :, :].rearrange("p t (h d) -> p (t h) d", d=130)[:, :, 128:129],
                                           in0=onesf.h[:, 0:64].rearrange("p (a b) -> p a b", b=1),
                                           scalar1=cvc(PFL), scalar2=None, op0=ALU.mult),
          r=(onesf.whole, cv.whole), w=(VA.whole,))
        E("dve", lambda e: e.memset(VB.h[:, :, :].rearrange("p t (h d) -> p (t h) d", d=130)[:, :, 128:129], 1.0),
          w=(VB.whole,))
        for t in range(16):
            Vt = VA if t < 8 else VB
            tt = t % 8
            banks = [nbank() for _ in range(2)]
            for kc in range(4):
                for hf in range(2):
                    mm(psb[banks[hf]][:, :], kvn.h[:, kc, t * 128:(t + 1) * 128],
                       wv[kc // 2].h[:, (kc % 2) * 1024 + hf * 512:(kc % 2) * 1024 + (hf + 1) * 512],
                       kc == 0, kc == 3, (kvn.sub(kc), wv[kc // 2].whole), (psbuf[banks[hf]],))
            for hf in range(2):
                dst = Vt.h[:, tt, :].rearrange("p (h d) -> p h d", d=130)[:, hf * 4:(hf + 1) * 4, 0:128]
                srcp = psb[banks[hf]][:, :].rearrange("p (h d) -> p h d", d=128)
                if t < 8:
                    E("dve", (lambda d_, s_: (lambda e: e.tensor_scalar(out=d_, in0=s_, scalar1=cvc(PFL), scalar2=None,
                                                                       op0=ALU.mult)))(dst, srcp),
                      r=(psbuf[banks[hf]], cv.whole), w=(Vt.sub(tt),))
                else:
                    E("act", (lambda d_, s_: (lambda e: e.copy(out=d_, in_=s_)))(dst, srcp),
                      r=(psbuf[banks[hf]],), w=(Vt.sub(tt),))

        ckpt('C')
        attn = sbt([128, 8, 1024], F32, AA0 + 116 * KB, "attn")
        PT = [sbt([128, 512], BF16, AA0 + 170 * KB + i * 1024, f"PT{i}") for i in range(4)]
        rc = sbt([128, 64], F32, AA0 + 174 * KB, "rc")
        nrc = [0]
        ST_B = [4, 5, 6, 7]
        ACC_B = [0, 1, 2, 3]
        items = []
        for h in range(N_HEADS):
            for qb in range(2):
                for kc in range(8 + 4 * qb + 4):
                    items.append((h, qb, kc))
        LOOK = 3

        def d_qk(n):
            h, qb, kc = items[n]
            q0 = max(kc - 8 - 4 * qb, 0) * 128
            bk = ST_B[n % 4]
            mm(psb[bk][:, q0:512], knT.h[:, h, kc * 128:(kc + 1) * 128],
               qnT.h[:, h, qb * 512 + q0:qb * 512 + 512], True, False,
               (knT.sub(h), qnT.sub(h)), (psbuf[bk],))
            mm(psb[bk][:, q0:512], kr2.h[:, kc * 128:(kc + 1) * 128],
               tq.h[:, h, qb * 512 + q0:qb * 512 + 512], False, True,
               (kr2.whole, tq.sub(h)), (psbuf[bk],))

        def d_rest(n):
            h, qb, kc = items[n]
            j = kc - 8 - 4 * qb
            i0 = max(j, 0)
            q0 = i0 * 128
            bk = ST_B[n % 4]
            pt = PT[n % 4]
            Vt = VA if kc < 8 else VB
            E("act", lambda e: e.activation(out=pt.h[:, q0:512], in_=psb[bk][:, q0:512], func=AF.Exp, scale=SCALE),
              r=(psbuf[bk],), w=(pt.whole,))
            if j >= 0:
                E("dve", lambda e: e.tensor_tensor(out=pt.h[:, q0:q0 + 128], in0=pt.h[:, q0:q0 + 128], in1=tri,
                                                   op=ALU.mult), r=(pt.whole, cb.whole), w=(pt.whole,))
            for i in range(i0, 4):
                last = 8 + 4 * qb + i
                ab = ACC_B[i]
                mm(psb[ab][:, 0:129], pt.h[:, i * 128:(i + 1) * 128], Vt.h[:, kc % 8, h * 130:h * 130 + 129],
                   kc == 0, kc == last, (pt.whole, Vt.sub(kc % 8)), (psbuf[ab],))
                if kc == last:
                    col = nrc[0] % 64
                    nrc[0] += 1
                    rb = rc.rng(col, col + 1)
                    E("dve", (lambda ab_, col_: (lambda e: e.reciprocal(out=rc.h[:, col_:col_ + 1],
                                                                       in_=psb[ab_][:, 128:129])))(ab, col),
                      r=(psbuf[ab],), w=(rb,))
                    E("dve", (lambda ab_, col_, i_: (lambda e: e.tensor_scalar(
                        out=attn.h[:, qb * 4 + i_, h * 128:(h + 1) * 128], in0=psb[ab_][:, 0:128],
                        scalar1=rc.h[:, col_:col_ + 1], scalar2=None, op0=ALU.mult)))(ab, col, i),
                      r=(psbuf[ab], rb), w=(attn.rng((qb * 4 + i) * 1024 + h * 128, (qb * 4 + i) * 1024 + (h + 1) * 128),))

        for n in range(len(items) + LOOK):
            if n < len(items):
                d_qk(n)
            if n >= LOOK:
                d_rest(n - LOOK)

        ckpt('D')
        gbcD = sbt([128, 1024], F32, AA0 + 148 * KB, "gbcD")
        junkD = [sbt([128, 1024], BF16, AA0 + (152 + 2 * i) * KB, f"junkD{i}") for i in range(2)]
        xnD = [sbt([128, 1024], BF16, AA0 + (156 + 2 * i) * KB, f"xnD{i}") for i in range(8)]
        E("sync", lambda e: e.dma_start(out=gbcD.h[:, :], in_=gbc_d[4][:, 0:1024]), w=(gbcD.whole,), key="gbc")
        cD_ss = stcol(8)
        cD_rs = stcol(8)
        brsD = {}
        usedD = {}

        def mixT_attn_tile(i):
            return tuple(P.buf("sb", mixT.off + kc * 2048 + i * 256, mixT.off + kc * 2048 + (i + 1) * 256)
                         for kc in range(8, 16))

        def Dn_s(i):
            brsD[i] = nt_stats(attn.h[:, i, :], attn.sub(i), junkD[i % 2], 1024, cD_ss + i, cD_rs + i)

        def Dn_aa(i):
            usedD[i] = nt_apply_a(attn.h[:, i, :], attn.sub(i), brsD[i], gbcD, xnD[i], 1024, cD_rs + i,
                                  banks=[4 + i % 4])

        def Dn_ab(i):
            nt_apply_b(usedD[i], (lambda g: mixT.h[:, 8:16, i * 128:(i + 1) * 128]), mixT_attn_tile(i), ["dve"])

        def Dn_a(i):
            Dn_aa(i)
            Dn_ab(i)

        ckpt('Dn')
        w_out = sbt([128, 16, 2048], BF16, AA0 + 0, "w_out")
        for kc in (0, 1, 8, 12, 2, 3, 9, 13, 4, 5, 10, 14, 6, 7, 11, 15):
            E("pool", (lambda kc_: (lambda e: e.dma_start(out=w_out.h[:, kc_, :],
                                                          in_=w_out_p[:, kc_ * 2048:(kc_ + 1) * 2048])))(kc),
              w=(w_out.sub(kc),), key=f"wout{kc % 4}")
        xF = [sbt([128, 2048], F32, AA0 + (96 + 8 * i) * KB, f"xF{i}") for i in range(2)]
        gpost = sbt([128, 2048], F32, AA0 + 112 * KB, "gpost")
        gpre = sbt([128, 2048], F32, AA0 + 120 * KB, "gpre")
        x1t = sbt([128, 2048], F32, AA0 + 128 * KB, "x1t")
        xnF = sbt([128, 2048], BF16, AA0 + 136 * KB, "xnF")
        hfT = sbt([128, 16, 1024], BF16, AA0 + 140 * KB, "hfT")
        junkF = sbt([128, 2048], BF16, AA0 + 172 * KB, "junkF")
        cF_p = stcol(32)
        cF_ss = stcol(8)
        cF_rs = stcol(8)
        cF_ss2 = stcol(8)
        cF_rs2 = stcol(8)
        brsF = {}
        brsF2 = {}

        KC_ORDER = [0, 1, 2, 3, 4, 5, 6, 8, 9, 10, 12, 13, 14, 7, 11, 15]

        def F1a(i):
            s_ = i % 2
            E("sync", lambda e: e.dma_start(out=xF[s_].h[:, :], in_=x_own[i * 128:(i + 1) * 128, :]),
              w=(xF[s_].whole,), key=f"xF{s_}")
            for cbk in (2, 3, 0, 1):
                bk = (i % 2) * 4 + cbk
                for kc in KC_ORDER:
                    mb = mixT.sub(kc) if kc < 8 else P.buf("sb", mixT.off + kc * 2048 + i * 256,
                                                           mixT.off + kc * 2048 + (i + 1) * 256)
                    mm(psb[bk][:, :], mixT.h[:, kc, i * 128:(i + 1) * 128], w_out.h[:, kc, cbk * 512:(cbk + 1) * 512],
                       kc == KC_ORDER[0], kc == KC_ORDER[-1], (mb, w_out.sub(kc)), (psbuf[bk],))

        def F1b(i):
            for cbk in (2, 3, 0, 1):
                bk = (i % 2) * 4 + cbk
                pc = cF_p + i * 4 + cbk
                E("act", (lambda bk_, pc_, c_: (lambda e: e.activation(
                    out=junkF.h[:, c_ * 512:(c_ + 1) * 512], in_=psb[bk_][:, :], func=AF.Square,
                    accum_out=st.h[:, pc_:pc_ + 1])))(bk, pc, cbk),
                  r=(psbuf[bk],), w=(junkF.rng(cbk * 512, (cbk + 1) * 512), st.rng(pc, pc + 1)))
            E("dve", lambda e: e.tensor_reduce(out=st.h[:, cF_ss + i:cF_ss + i + 1],
                                               in_=st.h[:, cF_p + 4 * i:cF_p + 4 * i + 4],
                                               axis=mybir.AxisListType.X, op=ALU.add),
              r=(st.rng(cF_p + 4 * i, cF_p + 4 * i + 4),), w=(st.rng(cF_ss + i, cF_ss + i + 1),))
            brsF[i] = rstd_from_ss(cF_ss + i, cF_rs + i, 1, 1.0 / D_MODEL)

        def F2(i):
            s_ = i % 2
            for cbk in range(4):
                bk = (i % 2) * 4 + cbk
                sl = slice(cbk * 512, (cbk + 1) * 512)
                E("dve", (lambda bk_, sl_: (lambda e: e.scalar_tensor_tensor(
                    out=x1t.h[:, sl_], in0=psb[bk_][:, :], scalar=st.h[:, cF_rs + i:cF_rs + i + 1], in1=gpost.h[:, sl_],
                    op0=ALU.mult, op1=ALU.mult)))(bk, sl),
                  r=(psbuf[bk], brsF[i], gpost.whole), w=(x1t.rng(cbk * 512, (cbk + 1) * 512),))
            E("dve", lambda e: e.tensor_tensor(out=x1t.h[:, :], in0=x1t.h[:, :], in1=xF[s_].h[:, :], op=ALU.add),
              r=(x1t.whole, xF[s_].whole), w=(x1t.whole,))
            E("sync", lambda e: e.dma_start(out=x1_d[i * 128:(i + 1) * 128, :], in_=x1t.h[:, :]),
              r=(x1t.whole,), w=(), key="x1w")
            brsF2[i] = nt_stats(x1t.h[:, :], x1t.whole, xnF, 2048, cF_ss2 + i, cF_rs2 + i)

        def F3(i):
            nt_apply(x1t.h[:, :], x1t.whole, brsF2[i], gpre, xnF, 2048, cF_rs2 + i,
                     (lambda g: hfT.h[:, g * 8:(g + 1) * 8, i * 128:(i + 1) * 128]),
                     (lambda g: tuple(hfT.rng(kc * 1024 + i * 128, kc * 1024 + (i + 1) * 128) for kc in range(g * 8, g * 8 + 8))),
                     ["act", "dve"], banks=[(i % 2) * 4, (i % 2) * 4 + 1])

        actT = sbt([128, NFB, 1024], BF16, AA0 + 0, "actT")
        sgf = [sbt([128, 512], F32, AA0 + 172 * KB + i * 2048, f"sgf{i}") for i in range(2)]
        nsg = [0]
        g1_early = {}

        def g1_mm(w_t, bk, n):
            for kc in range(16):
                mm(psb[bk][:, :], w_t.h[:, kc * 128:(kc + 1) * 128], hfT.h[:, kc, n * 512:(n + 1) * 512],
                   kc == 0, kc == 15, (w_t.whole, hfT.rng(kc * 1024 + n * 512, kc * 1024 + (n + 1) * 512)), (psbuf[bk],))

        def g1_early_emit():
            bmap = {0: ([0, 4], [2, 5]), 1: ([1, 6], [3, 7])}
            for f in range(2):
                wg = next_w(prefetch=2)
                wu = next_w(prefetch=2)
                gb, ub = bmap[f]
                g1_early[f] = (wg, wu, gb, ub)
                g1_mm(wg, gb[0], 0)
                g1_mm(wu, ub[0], 0)

        Dn_s(0)
        Dn_s(1)
        Dn_a(0)
        Dn_s(2)
        Dn_a(1)
        F1a(0)
        for i in range(2, 6):
            Dn_s(i + 1)
            Dn_aa(i)
        Dn_s(7)
        for i in range(2, 6):
            Dn_ab(i)
        for i in range(6, 8):
            Dn_aa(i)
        for i in range(6, 8):
            Dn_ab(i)
        E("sync", lambda e: e.dma_start(out=gpost.h[:, :], in_=gbc_d[1]), w=(gpost.whole,), key="gbc2")
        E("sync", lambda e: e.dma_start(out=gpre.h[:, :], in_=gbc_d[2]), w=(gpre.whole,), key="gbc2")
        F1b(0)
        for i in range(8):
            if i + 1 < 8:
                F1a(i + 1)
            if i == 7:
                g1_early_emit()
            F2(i)
            F3(i)
            if i + 1 < 8:
                F1b(i + 1)
        ckpt('F')
        for f in range(NFB):
            if f < 2:
                wg, wu, gb, ub = g1_early[f]
            else:
                wg = next_w()
                wu = next_w()
                gb = [nbank(), nbank()]
                ub = [nbank(), nbank()]
                g1_mm(wg, gb[0], 0)
                g1_mm(wg, gb[1], 1)
                g1_mm(wu, ub[0], 0)
                g1_mm(wu, ub[1], 1)
            if f < 2:
                g1_mm(wg, gb[1], 1)
                g1_mm(wu, ub[1], 1)
            for n in range(2):
                sgt = sgf[nsg[0] % 2]
                nsg[0] += 1
                E("act", (lambda bk_, s_: (lambda e: e.activation(out=s_.h[:, :], in_=psb[bk_][:, :], func=AF.Silu)))(gb[n], sgt),
                  r=(psbuf[gb[n]],), w=(sgt.whole,))
                E("dve", (lambda bk_, s_, f_, n_: (lambda e: e.tensor_tensor(
                    out=actT.h[:, f_, n_ * 512:(n_ + 1) * 512], in0=psb[bk_][:, :], in1=s_.h[:, :], op=ALU.mult)))(ub[n], sgt, f, n),
                  r=(psbuf[ub[n]], sgt.whole), w=(actT.sub(f),))

        ckpt('G1')
        ff = sbt([128, 8, 2048], F32, AA0 + 88 * KB, "ff")
        xr = [sbt([128, 2048], F32, AA0 + 8 * i * KB, f"xr{i}") for i in range(8)]
        gffn = sbt([128, 2048], F32, AA0 + 168 * KB, "gffn")
        junkG = [sbt([128, 512], BF16, AA0 + 176 * KB + i * 1024, f"junkG{i}") for i in range(2)]
        E("sync", lambda e: e.dma_start(out=gffn.h[:, :], in_=gbc_d[3]), w=(gffn.whole,), key="gbc3")
        cG_p = stcol(32)
        cG_ss = stcol(8)
        cG_rs = stcol(8)
        for cbk in range(4):
            for fg in range(11):
                ws = next_w()
                for i in range(8):
                    for fb in range(4):
                        fidx = fg * 4 + fb
                        mm(psb[i][:, :], actT.h[:, fidx, i * 128:(i + 1) * 128], ws.h[:, fb * 512:(fb + 1) * 512],
                           fg == 0 and fb == 0, fg == 10 and fb == 3, (actT.sub(fidx), ws.whole), (psbuf[i],))
            for i in range(8):
                sl = slice(cbk * 512, (cbk + 1) * 512)
                pc = cG_p + i * 4 + cbk
                fb_ = ff.rng(i * 2048 + cbk * 512, i * 2048 + (cbk + 1) * 512)
                E("dve", (lambda i_, sl_: (lambda e: e.tensor_tensor(out=ff.h[:, i_, sl_], in0=psb[i_][:, :],
                                                                     in1=gffn.h[:, sl_], op=ALU.mult)))(i, sl),
                  r=(psbuf[i], gffn.whole), w=(fb_,))
                E("act", (lambda i_, pc_: (lambda e: e.activation(out=junkG[i_ % 2].h[:, :], in_=psb[i_][:, :], func=AF.Square,
                                                                 accum_out=st.h[:, pc_:pc_ + 1])))(i, pc),
                  r=(psbuf[i],), w=(junkG[i % 2].whole, st.rng(pc, pc + 1)))
        E("dve", lambda e: e.tensor_reduce(out=st.h[:, cG_ss:cG_ss + 8],
                                           in_=st.h[:, cG_p:cG_p + 32].rearrange("p (a b) -> p a b", b=4),
                                           axis=mybir.AxisListType.X, op=ALU.add),
          r=(st.rng(cG_p, cG_p + 32),), w=(st.rng(cG_ss, cG_ss + 8),))
        brsG = rstd_from_ss(cG_ss, cG_rs, 8, 1.0 / D_MODEL)
        for i in range(8):
            E("sync", (lambda i_: (lambda e: e.dma_start(out=xr[i_].h[:, :], in_=x1_d[i_ * 128:(i_ + 1) * 128, :])))(i),
              w=(xr[i].whole,), key=f"xr{i % 2}")
            P.q["sync"][-1].waits.append(("dma", "x1w", P.dma_cnt["x1w"]))
        for i in range(8):
            E("dve", (lambda i_: (lambda e: e.scalar_tensor_tensor(
                out=ff.h[:, i_, :], in0=ff.h[:, i_, :], scalar=st.h[:, cG_rs + i_:cG_rs + i_ + 1], in1=xr[i_].h[:, :],
                op0=ALU.mult, op1=ALU.add)))(i), r=(ff.sub(i), brsG, xr[i].whole), w=(ff.sub(i),))
            E("sync", (lambda i_: (lambda e: e.dma_start(out=out_d[i_ * 128:(i_ + 1) * 128, :], in_=ff.h[:, i_, :])))(i),
              r=(ff.sub(i),), w=(), key="outw")
        fin = E("sync", None)
        fin.waits.append(("dma", "outw", P.dma_cnt["outw"]))
        assert wcur[0] == len(wpieces), (wcur[0], len(wpieces))


    except _Stop:
        pass

    fin_all = P.op("sync", None)
    for k_, v_ in P.dma_cnt.items():
        fin_all.waits.append(("dma", k_, v_))

    for e_ in ENGS:
        cnt = 0
        for ins in P.q[e_]:
            if ins.signal and not ins.is_dma:
                cnt += 1
                ins.value = cnt
    keys = sorted(P.dma_cnt.keys())
    sem_ctx = {}
    sems_eng = {e_: nc.alloc_semaphore(f"s_{e_}") for e_ in ENGS}
    sems_key = {k: nc.alloc_semaphore(f"d_{k}") for k in keys}

    def replay(ename, eng):
        waited = {}
        for ins in P.q[ename]:
            for w in ins.waits:
                if w[0] == "eng":
                    p = w[1]
                    sem, val = sems_eng[p.eng], p.value
                else:
                    sem, val = sems_key[w[1]], w[2]
                k = id(sem)
                if waited.get(k, 0) < val:
                    eng.wait_ge(sem, val)
                    waited[k] = val
            if ins.fn is None:
                continue
            bi = ins.fn(eng)
            if ins.is_dma:
                bi.then_inc(sems_key[ins.key], 16)
            elif ins.signal:
                bi.then_inc(sems_eng[ename], 1)

    with nc.Block() as block:
        @block.sync
        def _(e):
            replay("sync", e)

        @block.scalar
        def _(e):
            replay("act", e)

        @block.vector
        def _(e):
            replay("dve", e)

        @block.gpsimd
        def _(e):
            replay("pool", e)

        @block.tensor
        def _(e):
            replay("pe", e)

    stats = {e_: len(P.q[e_]) for e_ in ENGS}
    stats["sig"] = {e_: sum(1 for i in P.q[e_] if i.signal and not i.is_dma) for e_ in ENGS}
    return nc, stats


def _blocks_k(w, ncols_per_block):
    K, N = w.shape
    kc = K // 128
    nb = N // ncols_per_block
    a = w.reshape(kc, 128, nb, ncols_per_block).transpose(2, 1, 0, 3)
    return np.ascontiguousarray(a.reshape(nb, 128, kc * ncols_per_block))


def prepare_inputs(x, positions, pre_mix_norm, w_in, q_norm, w_uq, kv_norm, w_ukv, conv_w, conv_b, conv_ln_g,
                   conv_ln_b, conv_out_norm, attn_out_norm, w_out, post_mix_norm, pre_ffn_norm, w_gate, w_up,
                   w_down, post_ffn_norm):
    f = np.float32
    x = np.asarray(x, f)
    positions = np.asarray(positions, np.int32)
    w_in = np.asarray(w_in, f)[0]
    w_uq = np.asarray(w_uq, f)[0]
    w_ukv = np.asarray(w_ukv, f)[0]
    w_out = np.asarray(w_out, f)[0]
    w_gate = np.asarray(w_gate, f)[0]
    w_up = np.asarray(w_up, f)[0]
    w_down = np.asarray(w_down, f)[0]

    c1 = 2 * CONV_CH
    c2 = c1 + Q_LORA
    c3 = c2 + KV_LORA
    cols = []
    cols += list(range(c2, c3))
    cols += list(range(c3, c3 + 64)) + list(range(c3 + 32, c3 + 64)) + list(range(c3, c3 + 32))
    cols += list(range(c1, c2))
    for c in range(8):
        cols += list(range(CONV_CH + c * 128, CONV_CH + (c + 1) * 128))
        cols += list(range(c * 128, (c + 1) * 128))
    w_in_p = _blocks_k(w_in[:, cols], 128)
    cols = []
    for h in range(N_HEADS):
        b0 = h * 192
        cols += list(range(b0, b0 + 128))
        cols += list(range(b0 + 128, b0 + 192)) + list(range(b0 + 160, b0 + 192)) + list(range(b0 + 128, b0 + 160))
    w_uq_p = _blocks_k(w_uq[:, cols], 128)
    kcols = []
    vcols = []
    for h in range(N_HEADS):
        kcols += list(range(h * 256, h * 256 + 128))
        vcols += list(range(h * 256 + 128, h * 256 + 256))
    w_uk_p = _blocks_k(w_ukv[:, kcols], 128)
    wv = w_ukv[:, vcols]
    w_uv_p = np.ascontiguousarray(wv.reshape(2, 2, 128, 1024).transpose(0, 2, 1, 3).reshape(2, 128, 2048))
    w_out_p = np.ascontiguousarray(w_out.reshape(16, 128, 2048).transpose(1, 0, 2).reshape(128, 16 * 2048))
    g_p = _blocks_k(w_gate, 128)
    u_p = _blocks_k(w_up, 128)
    w_gu_p = np.ascontiguousarray(np.stack([g_p, u_p], axis=1).reshape(88, 128, 2048))
    wd = w_down.reshape(11, 4, 128, 4, 512)
    w_dn_p = np.ascontiguousarray(wd.transpose(3, 0, 2, 1, 4).reshape(44, 128, 2048))

    c_bf = np.zeros((128, 512), np.float32)
    c_bf[:, 0:128] = np.eye(128)
    c_bf[:, 128:256] = 1.0
    e64 = np.eye(64)
    c_bf[:, 256:384] = np.block([[e64, e64], [e64, e64]])
    kk = np.arange(128)[:, None]
    qq = np.arange(128)[None, :]
    c_bf[:, 384:512] = (qq >= kk).astype(np.float32)
    c_bf = c_bf.astype(ml_dtypes.bfloat16)

    cvec = np.zeros((128, NCV), f)
    cw = np.asarray(conv_w, f)[0]
    for c in range(8):
        cvec[:, CW + c * 31:CW + (c + 1) * 31] = cw[:, c * 128:(c + 1) * 128].T
    cvec[:, CB:CB + 8] = np.asarray(conv_b, f)[0].reshape(8, 128).T
    cvec[:, LG:LG + 8] = np.asarray(conv_ln_g, f)[0].reshape(8, 128).T
    cvec[:, LB:LB + 8] = np.asarray(conv_ln_b, f)[0].reshape(8, 128).T
    cvec[:, G2:G2 + 8] = np.asarray(conv_out_norm, f)[0].reshape(8, 128).T
    cvec[:, GQ:GQ + 6] = np.asarray(q_norm, f)[0].reshape(6, 128).T
    cvec[:, GKV:GKV + 4] = np.asarray(kv_norm, f)[0].reshape(4, 128).T
    inv_freq = (np.float32(10000.0) ** (-np.arange(0, 64, 2, dtype=np.float32) / np.float32(64))).astype(f)
    cvec[:, IFQ] = np.tile(inv_freq, 4)
    cvec[0:64, PHS] = np.float32(np.pi / 2)
    cvec[:, SGN] = 1.0
    cvec[64:96, SGN] = -1.0

    gbc = np.zeros((5, 128, 2048), f)
    gbc[0] = np.asarray(pre_mix_norm, f)[0][None, :]
    gbc[1] = np.asarray(post_mix_norm, f)[0][None, :]
    gbc[2] = np.asarray(pre_ffn_norm, f)[0][None, :]
    gbc[3] = np.asarray(post_ffn_norm, f)[0][None, :]
    gbc[4, :, 0:1024] = np.asarray(attn_out_norm, f)[0][None, :]

    shared = dict(c_bf=c_bf, gbc=gbc, w_in_p=w_in_p, w_uq_p=w_uq_p, w_uk_p=w_uk_p, w_uv_p=w_uv_p,
                  w_out_p=w_out_p, w_gu_p=w_gu_p, w_dn_p=w_dn_p)
    in_maps = []
    for core in range(8):
        b, half = core // 2, core % 2
        m = dict(shared)
        m["x_own"] = np.ascontiguousarray(x[b, half * TOWN:(half + 1) * TOWN])
        cvc = cvec.copy()
        pos = np.zeros((2048,), np.int32)
        if half == 1:
            m["x_prev"] = np.ascontiguousarray(x[b, 0:TOWN])
            pos[:] = positions[b, 0:2048]
            cvc[:, PFL] = 1.0
        else:
            m["x_prev"] = np.zeros((TOWN, D_MODEL), f)
            pos[1024:] = positions[b, 0:1024]
            cvc[:, PFL] = 0.0
        m["cvec"] = cvc
        m["pos_bc"] = np.ascontiguousarray(np.broadcast_to(pos[None, :], (128, 2048)))
        in_maps.append(m)
    return in_maps


_CACHE = {}


def kernel(**inputs):
    if "nc" not in _CACHE:
        _CACHE["nc"], _CACHE["stats"] = build_program()
    nc = _CACHE["nc"]
    in_maps = prepare_inputs(**inputs)
    res = run_bass_kernel_spmd(nc, in_maps, core_ids=list(range(8)))
    out = np.zeros((BATCH, SEQ, D_MODEL), np.float32)
    for core in range(8):
        b, half = core // 2, core % 2
        out[b, half * TOWN:(half + 1) * TOWN] = res.results[core]["out"]
    return out
```

```python
import os
import numpy as np
import ml_dtypes
import concourse.bass as bass
import concourse.mybir as mybir
from concourse.bass_utils import run_bass_kernel_spmd

F32 = mybir.dt.float32
BF16 = mybir.dt.bfloat16
I32 = mybir.dt.int32
AF = mybir.ActivationFunctionType
ALU = mybir.AluOpType
PI = float(np.pi)

D_MODEL = 2048
SEQ = 2048
BATCH = 4
TOWN = 1024
CONV_CH = 1024
CONV_K = 31
N_HEADS = 8
Q_LORA = 768
KV_LORA = 512
D_FF = 5632
NFB = D_FF // 128
EPS = 1e-6
SCALE = 192 ** -0.5

CW = 0
CB = 248
LG = 256
LB = 264
G2 = 272
GQ = 280
GKV = 286
IFQ = 290
PHS = 291
SGN = 292
PFL = 293
NCV = 320

KB = 1024


class Buf:
    __slots__ = ("space", "lo", "hi", "name")

    def __init__(self, space, lo, hi, name=""):
        self.space, self.lo, self.hi, self.name = space, lo, hi, name


class Ins:
    __slots__ = ("eng", "fn", "is_dma", "key", "signal", "value", "waits", "dma_val", "idx")

    def __init__(self, eng, fn, is_dma, key):
        self.eng, self.fn, self.is_dma, self.key = eng, fn, is_dma, key
        self.idx = -1
        self.signal = False
        self.value = None
        self.waits = []
        self.dma_val = None


ENGS = ["sync", "act", "dve", "pool", "pe"]


class Prog:
    def __init__(self):
        self.q = {e: [] for e in ENGS}
        self.wr = {"sb": [], "ps": []}
        self.rd = {"sb": [], "ps": []}
        self.dma_cnt = {}

    def buf(self, space, lo, hi, name=""):
        return Buf(space, lo, hi, name)

    def op(self, eng, fn, reads=(), writes=(), dma_key=None):
        ins = Ins(eng, fn, dma_key is not None, dma_key)
        ins.idx = len(self.q[eng])
        need_eng = {}
        need_dma = {}

        def add(p):
            if p is ins:
                return
            if p.is_dma:
                need_dma[p.key] = 1
                return
            if p.eng == eng and not ins.is_dma and eng == "pe":
                return
            cur = need_eng.get(p.eng)
            if cur is None or cur.idx < p.idx:
                need_eng[p.eng] = p

        for b in reads:
            for w in self.wr[b.space]:
                if w[0] < b.hi and b.lo < w[1]:
                    add(w[2])
            if b.space == "ps" and eng != "pe":
                lo = b.lo // 2048 * 2048
                hi = -(-b.hi // 2048) * 2048
                for r in self.rd["ps"]:
                    if r[0] < hi and lo < r[1] and r[3].eng != eng and r[3].eng != "pe":
                        add(r[3])
        for b in writes:
            for w in self.wr[b.space]:
                if w[0] < b.hi and b.lo < w[1]:
                    add(w[2])
            for r in self.rd[b.space]:
                if r[0] < b.hi and b.lo < r[1]:
                    add(r[3])
        for p in need_eng.values():
            p.signal = True
            ins.waits.append(("eng", p))
        for k in need_dma:
            ins.waits.append(("dma", k, self.dma_cnt[k]))
        if dma_key is not None:
            self.dma_cnt[dma_key] = self.dma_cnt.get(dma_key, 0) + 16
            ins.dma_val = self.dma_cnt[dma_key]
        for b in writes:
            sp = b.space
            self.wr[sp] = [w for w in self.wr[sp] if not (b.lo <= w[0] and w[1] <= b.hi)]
            self.wr[sp].append([b.lo, b.hi, ins])
            self.rd[sp] = [r for r in self.rd[sp] if not (b.lo <= r[0] and r[1] <= b.hi)]
        rk = ("dma", dma_key) if ins.is_dma else eng
        for b in reads:
            lst = self.rd[b.space]
            for r in lst:
                if r[0] == b.lo and r[1] == b.hi and r[2] == rk:
                    r[3] = ins
                    break
            else:
                lst.append([b.lo, b.hi, rk, ins])
        self.q[eng].append(ins)
        return ins


def build_program(stop_after=None, tensors=None):
    nc = bass.Bass("TRN2", target_bir_lowering=False)
    P = Prog()

    def din(name, shape, dt):
        return nc.dram_tensor(name, list(shape), dt, kind="ExternalInput").ap()

    x_own = din("x_own", [TOWN, D_MODEL], F32)
    x_prev = din("x_prev", [TOWN, D_MODEL], F32)
    pos_bc = din("pos_bc", [128, 2048], I32)
    c_bf = din("c_bf", [128, 512], BF16)
    cvec_d = din("cvec", [128, NCV], F32)
    gbc_d = din("gbc", [5, 128, 2048], F32)
    w_in_p = din("w_in_p", [27, 128, 2048], F32)
    w_uq_p = din("w_uq_p", [16, 128, 768], F32)
    w_uk_p = din("w_uk_p", [8, 128, 512], F32)
    w_uv_p = din("w_uv_p", [2, 128, 2048], F32)
    w_out_p = din("w_out_p", [128, 16 * 2048], F32)
    w_gu_p = din("w_gu_p", [88, 128, 2048], F32)
    w_dn_p = din("w_dn_p", [44, 128, 2048], F32)
    out_d = nc.dram_tensor("out", [TOWN, D_MODEL], F32, kind="ExternalOutput").ap()
    x1_d = nc.dram_tensor("x1_scratch", [TOWN, D_MODEL], F32).ap()
    dbg = {}

    base = (nc.sbuf_base + 63) // 64 * 64
    CONST0 = base
    WR0 = CONST0 + 4 * KB
    AA0 = WR0 + 24 * KB
    assert AA0 + 178 * KB <= nc.sbuf_top, (AA0 + 178 * KB, nc.sbuf_top)

    def dsz(dt):
        return 4 if dt in (F32, I32) else 2

    class T:
        def __init__(self, name, shape, dt, off):
            self.h = nc.alloc_sbuf_tensor_at(name, list(shape), dt, offset=off)
            self.off = off
            self.shape = shape
            self.dt = dt
            self.nbytes = int(np.prod(shape[1:])) * dsz(dt)
            self.whole = P.buf("sb", off, off + self.nbytes, name)
            self._subs = {}
            if tensors is not None:
                tensors[name] = self

        def sub(self, i, n=None):
            key = (i, n)
            if key not in self._subs:
                slab = self.nbytes // self.shape[1]
                cnt = 1 if n is None else n
                self._subs[key] = P.buf("sb", self.off + i * slab, self.off + (i + cnt) * slab)
            return self._subs[key]

        def rng(self, lo_el, hi_el):
            key = ("r", lo_el, hi_el)
            if key not in self._subs:
                self._subs[key] = P.buf("sb", self.off + lo_el * dsz(self.dt), self.off + hi_el * dsz(self.dt))
            return self._subs[key]

    _names = [0]

    def sbt(shape, dt, off, name=None):
        _names[0] += 1
        return T(name or f"t{_names[0]}", shape, dt, off)

    cb = sbt([128, 512], BF16, CONST0, "cb")
    ident = cb.h[:, 0:128]
    ones_b = cb.h[:, 128:256]
    f2 = cb.h[:, 256:384]
    tri = cb.h[:, 384:512]
    cv = sbt([128, NCV], F32, CONST0 + 1024, "cv")
    st = sbt([128, 320], F32, CONST0 + 1024 + NCV * 4, "st")
    onesf = sbt([128, 64], F32, CONST0 + 1024 + NCV * 4 + 1280, "onesf")
    assert CONST0 + 1024 + NCV * 4 + 1280 + 256 <= WR0

    def cvc(col):
        return cv.h[:, col:col + 1]

    _stn = [0]

    def stcol(n=1):
        c = _stn[0]
        _stn[0] += n
        assert _stn[0] <= 320
        return c

    NSLOT = 6
    wslots = [sbt([128, 2048], BF16, WR0 + i * 4 * KB, f"ws{i}") for i in range(NSLOT)]
    wpieces = []
    wstate = {"issued": 0}

    psb = [nc.alloc_psum_tensor(f"psb{i}", [128, 512], F32) for i in range(8)]
    psbuf = [P.buf("ps", i * 2048, (i + 1) * 2048, f"ps{i}") for i in range(8)]
    psbf = [psb[i][:, :].bitcast(BF16) for i in range(8)]
    _bank = [0]

    def nbank():
        b = _bank[0] % 8
        _bank[0] += 1
        return b

    _psub = {}

    def psub(bank, lo, hi):
        k = (bank, lo, hi)
        if k not in _psub:
            _psub[k] = P.buf("ps", bank * 2048 + lo * 4, bank * 2048 + hi * 4)
        return _psub[k]

    def E(eng, fn, r=(), w=(), key=None):
        return P.op(eng, fn, r, w, key)

    def mm(out, lhsT, rhs, start, stop, r, w):
        return E("pe", lambda e: e.matmul(out, lhsT=lhsT, rhs=rhs, start=start, stop=stop), r, w)

    def issue_weights(upto, after=()):
        while wstate["issued"] < min(upto, len(wpieces)):
            i = wstate["issued"]
            src, n = wpieces[i]
            slot = wslots[i % NSLOT]
            E("pool", (lambda s, n_, sl: (lambda e: e.dma_start(out=sl.h[:, 0:n_], in_=s)))(src, n, slot),
              r=after, w=(slot.whole,), key=f"ws{i % NSLOT}")
            wstate["issued"] += 1

    wcur = [0]

    def next_w(prefetch=4):
        i = wcur[0]
        wcur[0] += 1
        issue_weights(i + 1 + prefetch)
        return wslots[i % NSLOT]

    for j in range(27):
        wpieces.append((w_in_p[j], 2048))
    for j in range(16):
        wpieces.append((w_uq_p[j], 768))
    for j in range(8):
        wpieces.append((w_uk_p[j], 512))
    for j in range(2):
        wpieces.append((w_uv_p[j], 2048))
    for j in range(88):
        wpieces.append((w_gu_p[j], 2048))
    for j in range(44):
        wpieces.append((w_dn_p[j], 2048))

    class _Stop(Exception):
        pass

    def ckpt(name):
        if stop_after == name:
            raise _Stop()

    try:
        E("sync", lambda e: e.dma_start(out=cb.h[:, :], in_=c_bf), w=(cb.whole,), key="const")
        E("sync", lambda e: e.dma_start(out=cv.h[:, :], in_=cvec_d), w=(cv.whole,), key="const")
        E("dve", lambda e: e.memset(st.h[:, :], 0.0), w=(st.whole,))
        E("dve", lambda e: e.memset(onesf.h[:, :], 1.0), w=(onesf.whole,))

        def rstd_from_ss(c_ss, c_out, n, inv_n):
            c_ms = stcol(n)
            c_sq = stcol(n)
            bs = st.rng(c_ss, c_ss + n)
            bm = st.rng(c_ms, c_ms + n)
            bq = st.rng(c_sq, c_sq + n)
            bo = st.rng(c_out, c_out + n)
            E("dve", lambda e: e.tensor_scalar(out=st.h[:, c_ms:c_ms + n], in0=st.h[:, c_ss:c_ss + n], scalar1=inv_n,
                                               scalar2=EPS, op0=ALU.mult, op1=ALU.add), r=(bs,), w=(bm,))
            E("act", lambda e: e.activation(out=st.h[:, c_sq:c_sq + n], in_=st.h[:, c_ms:c_ms + n], func=AF.Sqrt),
              r=(bm,), w=(bq,))
            E("dve", lambda e: e.reciprocal(out=st.h[:, c_out:c_out + n], in_=st.h[:, c_sq:c_sq + n]), r=(bq,), w=(bo,))
            return bo

        def rstd_psum_inplace(bank, n, inv_n):
            b = psbuf[bank]
            ap = psb[bank][:, 0:n]
            E("dve", lambda e: e.tensor_scalar(out=ap, in0=ap, scalar1=inv_n, scalar2=EPS, op0=ALU.mult, op1=ALU.add),
              r=(b,), w=(b,))
            E("act", lambda e: e.activation(out=ap, in_=ap, func=AF.Sqrt), r=(b,), w=(b,))
            E("dve", lambda e: e.reciprocal(out=ap, in_=ap), r=(b,), w=(b,))

        def nt_stats(src_ap, src_buf, junk_t, width, c_ss, c_rs):
            bs = st.rng(c_ss, c_ss + 1)
            E("act", lambda e: e.activation(out=junk_t.h[:, 0:width], in_=src_ap, func=AF.Square,
                                            accum_out=st.h[:, c_ss:c_ss + 1]), r=(src_buf,), w=(junk_t.whole, bs))
            return rstd_from_ss(c_ss, c_rs, 1, 1.0 / width)

        def nt_apply_a(src_ap, src_buf, brs, gb_t, xn_t, width, c_rs, banks=None):
            nchunk = width // 128
            E("dve", lambda e: e.scalar_tensor_tensor(out=xn_t.h[:, 0:width], in0=src_ap,
                                                      scalar=st.h[:, c_rs:c_rs + 1], in1=gb_t.h[:, 0:width],
                                                      op0=ALU.mult, op1=ALU.mult),
              r=(src_buf, brs, gb_t.whole), w=(xn_t.whole,))
            used = []
            for g in range(nchunk // 8):
                bk = nbank() if banks is None else banks[g]
                used.append(bk)
                for c8 in range(8):
                    c = g * 8 + c8
                    E("pe", (lambda bk_, c8_, c_: (lambda e: e.transpose(psbf[bk_][:, c8_ * 128:(c8_ + 1) * 128],
                                                                       xn_t.h[:, c_ * 128:(c_ + 1) * 128], ident)))(bk, c8, c),
                      r=(xn_t.whole, cb.whole), w=(psbuf[bk],))
            return used

        def nt_apply_b(used, dst_fn, dst_bufs, evac_engs):
            for g, bk in enumerate(used):
                eng = evac_engs[g % len(evac_engs)]
                src = psbf[bk][:, 0:1024].rearrange("p (a b) -> p a b", a=8)
                dst = dst_fn(g)
                wb = dst_bufs(g) if callable(dst_bufs) else dst_bufs
                if eng == "act":
                    E("act", (lambda d_, s_: (lambda e: e.copy(out=d_, in_=s_)))(dst, src), r=(psbuf[bk],), w=wb)
                else:
                    E("dve", (lambda d_, s_: (lambda e: e.tensor_copy(out=d_, in_=s_)))(dst, src), r=(psbuf[bk],), w=wb)

        def nt_apply(src_ap, src_buf, brs, gb_t, xn_t, width, c_rs, dst_fn, dst_bufs, evac_engs, banks=None):
            used = nt_apply_a(src_ap, src_buf, brs, gb_t, xn_t, width, c_rs, banks)
            nt_apply_b(used, dst_fn, dst_bufs, evac_engs)

        hT = sbt([128, 16, 2048], BF16, AA0 + 0, "hT")
        xa = [sbt([128, 2048], F32, AA0 + (64 + 8 * i) * KB, f"xa{i}") for i in range(4)]
        gbcA = sbt([128, 2048], F32, AA0 + 96 * KB, "gbcA")
        xnA = [sbt([128, 2048], BF16, AA0 + (104 + 4 * i) * KB, f"xnA{i}") for i in range(3)]
        junkA = [sbt([128, 2048], BF16, AA0 + (116 + 4 * i) * KB, f"junkA{i}") for i in range(4)]
        E("sync", lambda e: e.dma_start(out=gbcA.h[:, :], in_=gbc_d[0]), w=(gbcA.whole,), key="gbc")
        def hT_rd(kc, a0, a1):
            return hT.rng(kc * 2048 + a0, kc * 2048 + a1)

        def hT_wr(t, g):
            return tuple(hT.rng(kc * 2048 + t * 128, kc * 2048 + (t + 1) * 128) for kc in range(g * 8, g * 8 + 8))

        cA_ss = stcol(16)
        cA_rs = stcol(16)
        brsA = {}

        def A1(t):
            s_ = t % 4
            src = x_prev[t * 128:(t + 1) * 128, :] if t < 8 else x_own[(t - 8) * 128:(t - 7) * 128, :]
            E("sync", (lambda s__, src_: (lambda e: e.dma_start(out=xa[s__].h[:, :], in_=src_)))(s_, src),
              w=(xa[s_].whole,), key=f"xa{s_}")
            brsA[t] = nt_stats(xa[s_].h[:, :], xa[s_].whole, junkA[t % 4], 2048, cA_ss + t, cA_rs + t)

        usedA = {}

        def A2a(t):
            s_ = t % 4
            usedA[t] = nt_apply_a(xa[s_].h[:, :], xa[s_].whole, brsA[t], gbcA, xnA[t % 3], 2048, cA_rs + t)

        def A2b(t):
            nt_apply_b(usedA[t], (lambda g: hT.h[:, g * 8:(g + 1) * 8, t * 128:(t + 1) * 128]),
                       (lambda g: hT_wr(t, g)), ["act", "dve"])

        A1(0)
        A1(1)
        A1(2)
        issue_weights(NSLOT, after=(xa[0].whole, xa[1].whole, xa[2].whole))
        A2a(0)
        for t in range(16):
            if t + 1 < 16:
                A2a(t + 1)
            if t + 3 < 16:
                A1(t + 3)
            A2b(t)

        ckpt('A')
        zz = sbt([128, 4, 2048], F32, AA0 + 64 * KB, "zz")
        zq = sbt([128, 6, 1024], F32, AA0 + 64 * KB, "zq")
        sqz = sbt([128, 4, 2048], BF16, AA0 + 96 * KB, "sqz")
        sqq = sbt([128, 6, 1024], BF16, AA0 + 96 * KB, "sqq")
        t_k = sbt([128, 2048], BF16, AA0 + 112 * KB, "t_k")
        kvn = sbt([128, 4, 2048], BF16, AA0 + 116 * KB, "kvn")
        qln = sbt([128, 6, 1024], BF16, AA0 + 132 * KB, "qln")
        u_bf = sbt([128, 8, 1152], BF16, AA0 + 148 * KB, "u_bf")
        kr2 = sbt([128, 2048], BF16, AA0 + 166 * KB, "kr2")
        sg = [sbt([128, 1152], F32, AA0 + 64 * KB + i * 4608, f"sg{i}") for i in range(2)]

        CS = sbt([128, 2048], F32, AA0 + 170 * KB, "CS")

        def emit_rope():
            posi = sbt([128, 2048], I32, AA0 + 132 * KB, "posi")
            posf = sbt([128, 2048], F32, AA0 + 140 * KB, "posf")
            tmpa = sbt([128, 2048], F32, AA0 + 148 * KB, "tmpa")
            E("sync", lambda e: e.dma_start(out=posi.h[:, :], in_=pos_bc), w=(posi.whole,), key="pos")
            E("dve", lambda e: e.tensor_copy(out=posf.h[:, :], in_=posi.h[:, :]), r=(posi.whole,), w=(posf.whole,))
            E("dve", lambda e: e.tensor_scalar(out=CS.h[:, :], in0=posf.h[:, :], scalar1=cvc(IFQ), scalar2=cvc(PHS),
                                               op0=ALU.mult, op1=ALU.add), r=(posf.whole, cv.whole), w=(CS.whole,))
            E("dve", lambda e: e.tensor_scalar(out=tmpa.h[:, :], in0=CS.h[:, :], scalar1=1.0 / (2 * PI), scalar2=None,
                                               op0=ALU.mult), r=(CS.whole,), w=(tmpa.whole,))
            E("dve", lambda e: e.tensor_copy(out=posi.h[:, :], in_=tmpa.h[:, :]), r=(tmpa.whole,), w=(posi.whole,))
            E("dve", lambda e: e.tensor_copy(out=posf.h[:, :], in_=posi.h[:, :]), r=(posi.whole,), w=(posf.whole,))
            C1 = 6.28125
            C2 = 2 * PI - C1
            E("dve", lambda e: e.scalar_tensor_tensor(out=CS.h[:, :], in0=posf.h[:, :], scalar=-C1, in1=CS.h[:, :],
                                                      op0=ALU.mult, op1=ALU.add), r=(posf.whole, CS.whole), w=(CS.whole,))
            E("dve", lambda e: e.scalar_tensor_tensor(out=CS.h[:, :], in0=posf.h[:, :], scalar=-C2, in1=CS.h[:, :],
                                                      op0=ALU.mult, op1=ALU.add), r=(posf.whole, CS.whole), w=(CS.whole,))
            E("dve", lambda e: e.tensor_single_scalar(out=tmpa.h[:, :], in_=CS.h[:, :], scalar=PI, op=ALU.is_gt),
              r=(CS.whole,), w=(tmpa.whole,))
            E("dve", lambda e: e.scalar_tensor_tensor(out=CS.h[:, :], in0=tmpa.h[:, :], scalar=-2 * PI, in1=CS.h[:, :],
                                                      op0=ALU.mult, op1=ALU.add), r=(tmpa.whole, CS.whole), w=(CS.whole,))
            E("dve", lambda e: e.tensor_scalar(out=CS.h[:, :], in0=CS.h[:, :], scalar1=-PI, scalar2=PI,
                                               op0=ALU.max, op1=ALU.min), r=(CS.whole,), w=(CS.whole,))
            E("act", lambda e: e.activation(out=CS.h[:, :], in_=CS.h[:, :], func=AF.Sin), r=(CS.whole,), w=(CS.whole,))
            E("dve", lambda e: e.tensor_scalar(out=CS.h[:, :], in0=CS.h[:, :], scalar1=cvc(SGN), scalar2=None,
                                               op0=ALU.mult), r=(CS.whole, cv.whole), w=(CS.whole,))


        for b in range(4):
            if b == 1:
                emit_rope()
            ws = next_w()
            banks = [nbank() for _ in range(4)]
            for kc in range(16):
                for n in range(4):
                    mm(psb[banks[n]][:, :], ws.h[:, kc * 128:(kc + 1) * 128], hT.h[:, kc, n * 512:(n + 1) * 512],
                       kc == 0, kc == 15, (ws.whole, hT_rd(kc, n * 512, (n + 1) * 512)), (psbuf[banks[n]],))
            for n in range(4):
                if os.environ.get("KDBG") == "noevac":
                    break
                bk = banks[n]
                E("dve", (lambda bk_, b_, n_: (lambda e: e.tensor_scalar(
                    out=zz.h[:, b_, n_ * 512:(n_ + 1) * 512], in0=psb[bk_][:, :], scalar1=cvc(GKV + b_), scalar2=None,
                    op0=ALU.mult)))(bk, b, n), r=(psbuf[bk], cv.whole), w=(zz.sub(b),))
                if os.environ.get("KDBG") == "noact":
                    continue
                E("act", (lambda bk_, b_, n_: (lambda e: e.activation(
                    out=sqz.h[:, b_, n_ * 512:(n_ + 1) * 512], in_=psb[bk_][:, :], func=AF.Square)))(bk, b, n),
                  r=(psbuf[bk],), w=(sqz.sub(b),) + ((psbuf[bk],) if os.environ.get("KDBG") == "serial" else ()))
        ckpt('B1')
        for n in range(4):
            bk = nbank()
            for b in range(4):
                mm(psb[bk][:, :], ones_b, sqz.h[:, b, n * 512:(n + 1) * 512], b == 0, b == 3,
                   (cb.whole, sqz.sub(b)), (psbuf[bk],))
            rstd_psum_inplace(bk, 512, 1.0 / KV_LORA)
            for b in range(4):
                E("dve", (lambda bk_, b_, n_: (lambda e: e.tensor_tensor(
                    out=kvn.h[:, b_, n_ * 512:(n_ + 1) * 512], in0=zz.h[:, b_, n_ * 512:(n_ + 1) * 512],
                    in1=psb[bk_][:, :], op=ALU.mult)))(bk, b, n), r=(zz.sub(b), psbuf[bk]), w=(kvn.sub(b),))
        ckpt('B2')
        ws = next_w()
        banks = [nbank() for _ in range(4)]
        for kc in range(16):
            for n in range(4):
                mm(psb[banks[n]][:, :], ws.h[:, kc * 128:(kc + 1) * 128], hT.h[:, kc, n * 512:(n + 1) * 512],
                   kc == 0, kc == 15, (ws.whole, hT_rd(kc, n * 512, (n + 1) * 512)), (psbuf[banks[n]],))
        for n in range(4):
            bk = banks[n]
            E("dve", (lambda bk_, n_: (lambda e: e.tensor_tensor(
                out=t_k.h[:, n_ * 512:(n_ + 1) * 512], in0=psb[bk_][:, :], in1=CS.h[:, n_ * 512:(n_ + 1) * 512],
                op=ALU.mult)))(bk, n), r=(psbuf[bk], CS.whole), w=(t_k.rng(n * 512, (n + 1) * 512),))
            bk2 = nbank()
            mm(psb[bk2][:, :], f2, t_k.h[:, n * 512:(n + 1) * 512], True, True,
               (cb.whole, t_k.rng(n * 512, (n + 1) * 512)), (psbuf[bk2],))
            E("act", (lambda bk_, n_: (lambda e: e.copy(out=kr2.h[:, n_ * 512:(n_ + 1) * 512], in_=psb[bk_][:, :])))(bk2, n),
              r=(psbuf[bk2],), w=(kr2.rng(n * 512, (n + 1) * 512),))
        ckpt('B3')
        for b in range(6):
            ws = next_w()
            banks = [nbank() for _ in range(2)]
            for kc in range(16):
                for n in range(2):
                    mm(psb[banks[n]][:, :], ws.h[:, kc * 128:(kc + 1) * 128],
                       hT.h[:, kc, 1024 + n * 512:1024 + (n + 1) * 512],
                       kc == 0, kc == 15, (ws.whole, hT_rd(kc, 1024 + n * 512, 1024 + (n + 1) * 512)), (psbuf[banks[n]],))
            for n in range(2):
                bk = banks[n]
                E("dve", (lambda bk_, b_, n_: (lambda e: e.tensor_scalar(
                    out=zq.h[:, b_, n_ * 512:(n_ + 1) * 512], in0=psb[bk_][:, :], scalar1=cvc(GQ + b_), scalar2=None,
                    op0=ALU.mult)))(bk, b, n), r=(psbuf[bk], cv.whole), w=(zq.sub(b),))
                E("act", (lambda bk_, b_, n_: (lambda e: e.activation(
                    out=sqq.h[:, b_, n_ * 512:(n_ + 1) * 512], in_=psb[bk_][:, :], func=AF.Square)))(bk, b, n),
                  r=(psbuf[bk],), w=(sqq.sub(b),))
        for n in range(2):
            bk = nbank()
            for b in range(6):
                mm(psb[bk][:, :], ones_b, sqq.h[:, b, n * 512:(n + 1) * 512], b == 0, b == 5,
                   (cb.whole, sqq.sub(b)), (psbuf[bk],))
            rstd_psum_inplace(bk, 512, 1.0 / Q_LORA)
            for b in range(6):
                E("dve", (lambda bk_, b_, n_: (lambda e: e.tensor_tensor(
                    out=qln.h[:, b_, n_ * 512:(n_ + 1) * 512], in0=zq.h[:, b_, n_ * 512:(n_ + 1) * 512],
                    in1=psb[bk_][:, :], op=ALU.mult)))(bk, b, n), r=(zq.sub(b), psbuf[bk]), w=(qln.sub(b),))
        ckpt('B4')
        tokr = [(896, 1024), (1024, 1536), (1536, 2048)]
        for c in range(8):
            sgt = sg[c % 2]
            ws = next_w()
            banks = [nbank() for _ in range(3)]
            for kc in range(16):
                for n, (a0, a1) in enumerate(tokr):
                    mm(psb[banks[n]][:, 0:a1 - a0], ws.h[:, kc * 128:(kc + 1) * 128], hT.h[:, kc, a0:a1],
                       kc == 0, kc == 15, (ws.whole, hT_rd(kc, a0, a1)), (psbuf[banks[n]],))
            for n, (a0, a1) in enumerate(tokr):
                E("act", (lambda bk_, a0_, a1_, sg_: (lambda e: e.activation(
                    out=sg_.h[:, a0_ - 896:a1_ - 896], in_=psb[bk_][:, 0:a1_ - a0_], func=AF.Sigmoid)))(banks[n], a0, a1, sgt),
                  r=(psbuf[banks[n]],), w=(sgt.whole,))
            ws = next_w()
            banks = [nbank() for _ in range(3)]
            for kc in range(16):
                for n, (a0, a1) in enumerate(tokr):
                    mm(psb[banks[n]][:, 0:a1 - a0], ws.h[:, kc * 128:(kc + 1) * 128], hT.h[:, kc, a0:a1],
                       kc == 0, kc == 15, (ws.whole, hT_rd(kc, a0, a1)), (psbuf[banks[n]],))
            for n, (a0, a1) in enumerate(tokr):
                E("dve", (lambda bk_, a0_, a1_, sg_, c_: (lambda e: e.tensor_tensor(
                    out=u_bf.h[:, c_, a0_ - 896:a1_ - 896], in0=psb[bk_][:, 0:a1_ - a0_], in1=sg_.h[:, a0_ - 896:a1_ - 896],
                    op=ALU.mult)))(banks[n], a0, a1, sgt, c), r=(psbuf[banks[n]], sgt.whole), w=(u_bf.sub(c),))

        ckpt('B')
        yv = sbt([128, 8, 1024], F32, AA0 + 0, "yv")
        Dr = [sbt([128, 31, 128], BF16, AA0 + (32 + 8 * i) * KB, f"Dr{i}") for i in range(2)]
        sqs = [sbt([128, 512], BF16, AA0 + 48 * KB + i * 1024, f"sqs{i}") for i in range(4)]
        bc0 = sbt([128, 1024], F32, AA0 + 52 * KB, "bc0")
        bc1 = sbt([128, 1024], F32, AA0 + 56 * KB, "bc1")
        mixT = sbt([128, 16, 1024], BF16, AA0 + 64 * KB, "mixT")
        mixT_attn = P.buf("sb", mixT.off + 8 * 2048, mixT.off + 16 * 2048, "mixT_attn")
        bS1 = [nbank(), nbank()]
        bS2 = [nbank(), nbank()]
        nsq = [0]
        pend_stats = []
        def build_D(c):
            dr_ = Dr[c % 2]
            E("dve", lambda e: e.tensor_tensor(
                out=dr_.h[:, :, :],
                in0=ident.unsqueeze(1).to_broadcast([128, CONV_K, 128]),
                in1=cv.h[:, CW + c * 31:CW + (c + 1) * 31].unsqueeze(2).to_broadcast([128, CONV_K, 128]),
                op=ALU.mult), r=(cb.whole, cv.whole), w=(dr_.whole,))

        build_D(0)
        for c in range(8):
            dr = Dr[c % 2]
            if c + 1 < 8:
                build_D(c + 1)
            for n in range(2):
                bk = nbank()
                while bk in bS1 or bk in bS2:
                    bk = nbank()
                for k in range(CONV_K):
                    mm(psb[bk][:, :], dr.h[:, k, :], u_bf.h[:, c, 98 + k + n * 512:98 + k + (n + 1) * 512],
                       k == 0, k == CONV_K - 1, (dr.sub(k), u_bf.sub(c)), (psbuf[bk],))
                yb = yv.rng(c * 1024 + n * 512, c * 1024 + (n + 1) * 512)
                E("act", (lambda bk_, c_, n_: (lambda e: e.activation(
                    out=yv.h[:, c_, n_ * 512:(n_ + 1) * 512], in_=psb[bk_][:, :], func=AF.Identity,
                    bias=cvc(CB + c_))))(bk, c, n), r=(psbuf[bk], cv.whole), w=(yb,))
                s1 = sqs[nsq[0] % 4]
                nsq[0] += 1
                s2 = sqs[nsq[0] % 4]
                nsq[0] += 1
                E("act", (lambda bk_, c_, s_: (lambda e: e.activation(
                    out=s_.h[:, :], in_=psb[bk_][:, :], func=AF.Square, bias=cvc(CB + c_))))(bk, c, s2),
                  r=(psbuf[bk], cv.whole), w=(s2.whole,))
                E("dve", (lambda c_, n_, s_: (lambda e: e.tensor_copy(
                    out=s_.h[:, :], in_=yv.h[:, c_, n_ * 512:(n_ + 1) * 512])))(c, n, s1), r=(yb,), w=(s1.whole,))
                if pend_stats:
                    pn, ps1, ps2, pc = pend_stats.pop()
                    mm(psb[bS1[pn]][:, :], ones_b, ps1.h[:, :], pc == 0, pc == 7, (cb.whole, ps1.whole), (psbuf[bS1[pn]],))
                    mm(psb[bS2[pn]][:, :], ones_b, ps2.h[:, :], pc == 0, pc == 7, (cb.whole, ps2.whole), (psbuf[bS2[pn]],))
                pend_stats.append((n, s1, s2, c))
        pn, ps1, ps2, pc = pend_stats.pop()
        mm(psb[bS1[pn]][:, :], ones_b, ps1.h[:, :], pc == 0, pc == 7, (cb.whole, ps1.whole), (psbuf[bS1[pn]],))
        mm(psb[bS2[pn]][:, :], ones_b, ps2.h[:, :], pc == 0, pc == 7, (cb.whole, ps2.whole), (psbuf[bS2[pn]],))
        for n in range(2):
            sl = slice(n * 512, (n + 1) * 512)
            b0 = bc0.rng(n * 512, (n + 1) * 512)
            b1 = bc1.rng(n * 512, (n + 1) * 512)
            E("dve", (lambda n_, sl_: (lambda e: e.tensor_scalar(out=bc0.h[:, sl_], in0=psb[bS1[n_]][:, :],
                                                                scalar1=1.0 / CONV_CH, scalar2=None, op0=ALU.mult)))(n, sl),
              r=(psbuf[bS1[n]],), w=(b0,))
            E("dve", (lambda sl_: (lambda e: e.tensor_tensor(out=bc1.h[:, sl_], in0=bc0.h[:, sl_], in1=bc0.h[:, sl_],
                                                            op=ALU.mult)))(sl), r=(b0,), w=(b1,))
            E("dve", (lambda n_, sl_: (lambda e: e.scalar_tensor_tensor(out=bc1.h[:, sl_], in0=psb[bS2[n_]][:, :],
                                                                       scalar=1.0 / CONV_CH, in1=bc1.h[:, sl_],
                                                                       op0=ALU.mult, op1=ALU.subtract)))(n, sl),
              r=(psbuf[bS2[n]], b1), w=(b1,))
            E("dve", (lambda sl_: (lambda e: e.tensor_scalar(out=bc1.h[:, sl_], in0=bc1.h[:, sl_], scalar1=EPS,
                                                            scalar2=None, op0=ALU.add)))(sl), r=(b1,), w=(b1,))
            E("act", (lambda sl_: (lambda e: e.activation(out=bc1.h[:, sl_], in_=bc1.h[:, sl_], func=AF.Sqrt)))(sl),
              r=(b1,), w=(b1,))
            E("dve", (lambda sl_: (lambda e: e.reciprocal(out=bc1.h[:, sl_], in_=bc1.h[:, sl_])))(sl), r=(b1,), w=(b1,))
        qnT = sbt([128, 8, 1024], BF16, AA0 + 32 * KB, "qnT")
        for h in range(N_HEADS):
            ws = next_w()
            banks = []
            while len(banks) < 2:
                bk = nbank()
                if bk not in bS1 and bk not in bS2:
                    banks.append(bk)
            for kc in range(6):
                for n in range(2):
                    mm(psb[banks[n]][:, :], ws.h[:, kc * 128:(kc + 1) * 128], qln.h[:, kc, n * 512:(n + 1) * 512],
                       kc == 0, kc == 5, (ws.whole, qln.sub(kc)), (psbuf[banks[n]],))
            for n in range(2):
                E("act", (lambda bk_, h_, n_: (lambda e: e.copy(out=qnT.h[:, h_, n_ * 512:(n_ + 1) * 512],
                                                               in_=psb[bk_][:, :])))(banks[n], h, n),
                  r=(psbuf[banks[n]],), w=(qnT.sub(h),))
        bS3 = [bS1[0], bS1[1]]
        for c in range(8):
            for n in range(2):
                sl = slice(n * 512, (n + 1) * 512)
                yb = yv.rng(c * 1024 + n * 512, c * 1024 + (n + 1) * 512)
                b0 = bc0.rng(n * 512, (n + 1) * 512)
                b1 = bc1.rng(n * 512, (n + 1) * 512)
                E("dve", (lambda c_, sl_: (lambda e: e.tensor_tensor(out=yv.h[:, c_, sl_], in0=yv.h[:, c_, sl_],
                                                                    in1=bc0.h[:, sl_], op=ALU.subtract)))(c, sl),
                  r=(yb, b0), w=(yb,))
                E("dve", (lambda c_, sl_: (lambda e: e.tensor_tensor(out=yv.h[:, c_, sl_], in0=yv.h[:, c_, sl_],
                                                                    in1=bc1.h[:, sl_], op=ALU.mult)))(c, sl),
                  r=(yb, b1), w=(yb,))
                E("act", (lambda c_, sl_: (lambda e: e.activation(out=yv.h[:, c_, sl_], in_=yv.h[:, c_, sl_], func=AF.Silu,
                                                                 scale=cvc(LG + c_), bias=cvc(LB + c_))))(c, sl),
                  r=(yb, cv.whole), w=(yb,))
                s2 = sqs[nsq[0] % 4]
                nsq[0] += 1
                E("act", (lambda c_, sl_, s_: (lambda e: e.activation(out=s_.h[:, :], in_=yv.h[:, c_, sl_],
                                                                     func=AF.Square)))(c, sl, s2), r=(yb,), w=(s2.whole,))
                mm(psb[bS3[n]][:, :], ones_b, s2.h[:, :], c == 0, c == 7, (cb.whole, s2.whole), (psbuf[bS3[n]],))
        for n in range(2):
            rstd_psum_inplace(bS3[n], 512, 1.0 / CONV_CH)
        for c in range(8):
            for n in range(2):
                sl = slice(n * 512, (n + 1) * 512)
                yb = yv.rng(c * 1024 + n * 512, c * 1024 + (n + 1) * 512)
                E("dve", (lambda c_, n_, sl_: (lambda e: e.scalar_tensor_tensor(
                    out=mixT.h[:, c_, sl_], in0=yv.h[:, c_, sl_], scalar=cvc(G2 + c_), in1=psb[bS3[n_]][:, :],
                    op0=ALU.mult, op1=ALU.mult)))(c, n, sl), r=(yb, cv.whole, psbuf[bS3[n]]), w=(mixT.sub(c),))

        ckpt('E')
        knT = sbt([128, 8, 2048], BF16, AA0 + 0, "knT")
        tq = sbt([128, 8, 1024], BF16, AA0 + 48 * KB, "tq")
        VA = sbt([128, 8, 8 * 130], BF16, AA0 + 96 * KB, "VA")
        VB = sbt([128, 8, 8 * 130], BF16, AA0 + 148 * KB, "VB")
        for h in range(N_HEADS):
            ws = next_w()
            banks = [nbank() for _ in range(2)]
            for kc in range(6):
                for n in range(2):
                    mm(psb[banks[n]][:, :], ws.h[:, kc * 128:(kc + 1) * 128], qln.h[:, kc, n * 512:(n + 1) * 512],
                       kc == 0, kc == 5, (ws.whole, qln.sub(kc)), (psbuf[banks[n]],))
            for n in range(2):
                E("dve", (lambda bk_, h_, n_: (lambda e: e.tensor_tensor(
                    out=tq.h[:, h_, n_ * 512:(n_ + 1) * 512], in0=psb[bk_][:, :],
                    in1=CS.h[:, 1024 + n_ * 512:1024 + (n_ + 1) * 512], op=ALU.mult)))(banks[n], h, n),
                  r=(psbuf[banks[n]], CS.whole), w=(tq.sub(h),))
        for h in range(N_HEADS):
            ws = next_w()
            banks = [nbank() for _ in range(4)]
            for kc in range(4):
                for n in range(4):
                    mm(psb[banks[n]][:, :], ws.h[:, kc * 128:(kc + 1) * 128], kvn.h[:, kc, n * 512:(n + 1) * 512],
                       kc == 0, kc == 3, (ws.whole, kvn.sub(kc)), (psbuf[banks[n]],))
            for n in range(4):
                if n % 2 == 0:
                    E("act", (lambda bk_, h_, n_: (lambda e: e.copy(out=knT.h[:, h_, n_ * 512:(n_ + 1) * 512],
                                                                   in_=psb[bk_][:, :])))(banks[n], h, n),
                      r=(psbuf[banks[n]],), w=(knT.sub(h),))
                else:
                    E("dve", (lambda bk_, h_, n_: (lambda e: e.tensor_copy(out=knT.h[:, h_, n_ * 512:(n_ + 1) * 512],
                                                                          in_=psb[bk_][:, :])))(banks[n], h, n),
                      r=(psbuf[banks[n]],), w=(knT.sub(h),))
        wv = [next_w(), next_w()]
        E("dve", lambda e: e.tensor_scalar(out=VA.h[:, :, :].rearrange("p t (h d) -> p (t h) d", d=130)[:, :, 128:129],
                                           in0=onesf.h[:, 0:64].rearrange("p (a b) -> p a b", b=1),
                                           scalar1=cvc(PFL), scalar2=None, op0=ALU.mult),
          r=(onesf.whole, cv.whole), w=(VA.whole,))
        E("dve", lambda e: e.memset(VB.h[:, :, :].rearrange("p t (h d) -> p (t h) d", d=130)[:, :, 128:129], 1.0),
          w=(VB.whole,))
        for t in range(16):
            Vt = VA if t < 8 else VB
            tt = t % 8
            banks = [nbank() for _ in range(2)]
            for kc in range(4):
                for hf in range(2):
                    mm(psb[banks[hf]][:, :], kvn.h[:, kc, t * 128:(t + 1) * 128],
                       wv[kc // 2].h[:, (kc % 2) * 1024 + hf * 512:(kc % 2) * 1024 + (hf + 1) * 512],
                       kc == 0, kc == 3, (kvn.sub(kc), wv[kc // 2].whole), (psbuf[banks[hf]],))
            for hf in range(2):
                dst = Vt.h[:, tt, :].rearrange("p (h d) -> p h d", d=130)[:, hf * 4:(hf + 1) * 4, 0:128]
                srcp = psb[banks[hf]][:, :].rearrange("p (h d) -> p h d", d=128)
                if t < 8:
                    E("dve", (lambda d_, s_: (lambda e: e.tensor_scalar(out=d_, in0=s_, scalar1=cvc(PFL), scalar2=None,
                                                                       op0=ALU.mult)))(dst, srcp),
                      r=(psbuf[banks[hf]], cv.whole), w=(Vt.sub(tt),))
                else:
                    E("act", (lambda d_, s_: (lambda e: e.copy(out=d_, in_=s_)))(dst, srcp),
                      r=(psbuf[banks[hf]],), w=(Vt.sub(tt),))

        ckpt('C')
        attn = sbt([128, 8, 1024], F32, AA0 + 116 * KB, "attn")
        PT = [sbt([128, 512], BF16, AA0 + 170 * KB + i * 1024, f"PT{i}") for i in range(4)]
        rc = sbt([128, 64], F32, AA0 + 174 * KB, "rc")
        nrc = [0]
        ST_B = [4, 5, 6, 7]
        ACC_B = [0, 1, 2, 3]
        items = []
        for h in range(N_HEADS):
            for qb in range(2):
                for kc in range(8 + 4 * qb + 4):
                    items.append((h, qb, kc))
        LOOK = 3

        def d_qk(n):
            h, qb, kc = items[n]
            q0 = max(kc - 8 - 4 * qb, 0) * 128
            bk = ST_B[n % 4]
            mm(psb[bk][:, q0:512], knT.h[:, h, kc * 128:(kc + 1) * 128],
               qnT.h[:, h, qb * 512 + q0:qb * 512 + 512], True, False,
               (knT.sub(h), qnT.sub(h)), (psbuf[bk],))
            mm(psb[bk][:, q0:512], kr2.h[:, kc * 128:(kc + 1) * 128],
               tq.h[:, h, qb * 512 + q0:qb * 512 + 512], False, True,
               (kr2.whole, tq.sub(h)), (psbuf[bk],))

        def d_rest(n):
            h, qb, kc = items[n]
            j = kc - 8 - 4 * qb
            i0 = max(j, 0)
            q0 = i0 * 128
            bk = ST_B[n % 4]
            pt = PT[n % 4]
            Vt = VA if kc < 8 else VB
            E("act", lambda e: e.activation(out=pt.h[:, q0:512], in_=psb[bk][:, q0:512], func=AF.Exp, scale=SCALE),
              r=(psbuf[bk],), w=(pt.whole,))
            if j >= 0:
                E("dve", lambda e: e.tensor_tensor(out=pt.h[:, q0:q0 + 128], in0=pt.h[:, q0:q0 + 128], in1=tri,
                                                   op=ALU.mult), r=(pt.whole, cb.whole), w=(pt.whole,))
            for i in range(i0, 4):
                last = 8 + 4 * qb + i
                ab = ACC_B[i]
                mm(psb[ab][:, 0:129], pt.h[:, i * 128:(i + 1) * 128], Vt.h[:, kc % 8, h * 130:h * 130 + 129],
                   kc == 0, kc == last, (pt.whole, Vt.sub(kc % 8)), (psbuf[ab],))
                if kc == last:
                    col = nrc[0] % 64
                    nrc[0] += 1
                    rb = rc.rng(col, col + 1)
                    E("dve", (lambda ab_, col_: (lambda e: e.reciprocal(out=rc.h[:, col_:col_ + 1],
                                                                       in_=psb[ab_][:, 128:129])))(ab, col),
                      r=(psbuf[ab],), w=(rb,))
                    E("dve", (lambda ab_, col_, i_: (lambda e: e.tensor_scalar(
                        out=attn.h[:, qb * 4 + i_, h * 128:(h + 1) * 128], in0=psb[ab_][:, 0:128],
                        scalar1=rc.h[:, col_:col_ + 1], scalar2=None, op0=ALU.mult)))(ab, col, i),
                      r=(psbuf[ab], rb), w=(attn.rng((qb * 4 + i) * 1024 + h * 128, (qb * 4 + i) * 1024 + (h + 1) * 128),))

        for n in range(len(items) + LOOK):
            if n < len(items):
                d_qk(n)
            if n >= LOOK:
                d_rest(n - LOOK)

        ckpt('D')
        gbcD = sbt([128, 1024], F32, AA0 + 148 * KB, "gbcD")
        junkD = [sbt([128, 1024], BF16, AA0 + (152 + 2 * i) * KB, f"junkD{i}") for i in range(2)]
        xnD = [sbt([128, 1024], BF16, AA0 + (156 + 2 * i) * KB, f"xnD{i}") for i in range(8)]
        E("sync", lambda e: e.dma_start(out=gbcD.h[:, :], in_=gbc_d[4][:, 0:1024]), w=(gbcD.whole,), key="gbc")
        cD_ss = stcol(8)
        cD_rs = stcol(8)
        brsD = {}
        usedD = {}

        def mixT_attn_tile(i):
            return tuple(P.buf("sb", mixT.off + kc * 2048 + i * 256, mixT.off + kc * 2048 + (i + 1) * 256)
                         for kc in range(8, 16))

        def Dn_s(i):
            brsD[i] = nt_stats(attn.h[:, i, :], attn.sub(i), junkD[i % 2], 1024, cD_ss + i, cD_rs + i)

        def Dn_aa(i):
            usedD[i] = nt_apply_a(attn.h[:, i, :], attn.sub(i), brsD[i], gbcD, xnD[i], 1024, cD_rs + i,
                                  banks=[4 + i % 4])

        def Dn_ab(i):
            nt_apply_b(usedD[i], (lambda g: mixT.h[:, 8:16, i * 128:(i + 1) * 128]), mixT_attn_tile(i), ["dve"])

        def Dn_a(i):
            Dn_aa(i)
            Dn_ab(i)

        ckpt('Dn')
        w_out = sbt([128, 16, 2048], BF16, AA0 + 0, "w_out")
        for kc in (0, 1, 8, 12, 2, 3, 9, 13, 4, 5, 10, 14, 6, 7, 11, 15):
            E("pool", (lambda kc_: (lambda e: e.dma_start(out=w_out.h[:, kc_, :],
                                                          in_=w_out_p[:, kc_ * 2048:(kc_ + 1) * 2048])))(kc),
              w=(w_out.sub(kc),), key=f"wout{kc % 4}")
        xF = [sbt([128, 2048], F32, AA0 + (96 + 8 * i) * KB, f"xF{i}") for i in range(2)]
        gpost = sbt([128, 2048], F32, AA0 + 112 * KB, "gpost")
        gpre = sbt([128, 2048], F32, AA0 + 120 * KB, "gpre")
        x1t = sbt([128, 2048], F32, AA0 + 128 * KB, "x1t")
        xnF = sbt([128, 2048], BF16, AA0 + 136 * KB, "xnF")
        hfT = sbt([128, 16, 1024], BF16, AA0 + 140 * KB, "hfT")
        junkF = sbt([128, 2048], BF16, AA0 + 172 * KB, "junkF")
        cF_p = stcol(32)
        cF_ss = stcol(8)
        cF_rs = stcol(8)
        cF_ss2 = stcol(8)
        cF_rs2 = stcol(8)
        brsF = {}
        brsF2 = {}

        KC_ORDER = [0, 1, 2, 3, 4, 5, 6, 8, 9, 10, 12, 13, 14, 7, 11, 15]

        def F1a(i):
            s_ = i % 2
            E("sync", lambda e: e.dma_start(out=xF[s_].h[:, :], in_=x_own[i * 128:(i + 1) * 128, :]),
              w=(xF[s_].whole,), key=f"xF{s_}")
            for cbk in (2, 3, 0, 1):
                bk = (i % 2) * 4 + cbk
                for kc in KC_ORDER:
                    mb = mixT.sub(kc) if kc < 8 else P.buf("sb", mixT.off + kc * 2048 + i * 256,
                                                           mixT.off + kc * 2048 + (i + 1) * 256)
                    mm(psb[bk][:, :], mixT.h[:, kc, i * 128:(i + 1) * 128], w_out.h[:, kc, cbk * 512:(cbk + 1) * 512],
                       kc == KC_ORDER[0], kc == KC_ORDER[-1], (mb, w_out.sub(kc)), (psbuf[bk],))

        def F1b(i):
            for cbk in (2, 3, 0, 1):
                bk = (i % 2) * 4 + cbk
                pc = cF_p + i * 4 + cbk
                E("act", (lambda bk_, pc_, c_: (lambda e: e.activation(
                    out=junkF.h[:, c_ * 512:(c_ + 1) * 512], in_=psb[bk_][:, :], func=AF.Square,
                    accum_out=st.h[:, pc_:pc_ + 1])))(bk, pc, cbk),
                  r=(psbuf[bk],), w=(junkF.rng(cbk * 512, (cbk + 1) * 512), st.rng(pc, pc + 1)))
            E("dve", lambda e: e.tensor_reduce(out=st.h[:, cF_ss + i:cF_ss + i + 1],
                                               in_=st.h[:, cF_p + 4 * i:cF_p + 4 * i + 4],
                                               axis=mybir.AxisListType.X, op=ALU.add),
              r=(st.rng(cF_p + 4 * i, cF_p + 4 * i + 4),), w=(st.rng(cF_ss + i, cF_ss + i + 1),))
            brsF[i] = rstd_from_ss(cF_ss + i, cF_rs + i, 1, 1.0 / D_MODEL)

        def F2(i):
            s_ = i % 2
            for cbk in range(4):
                bk = (i % 2) * 4 + cbk
                sl = slice(cbk * 512, (cbk + 1) * 512)
                E("dve", (lambda bk_, sl_: (lambda e: e.scalar_tensor_tensor(
                    out=x1t.h[:, sl_], in0=psb[bk_][:, :], scalar=st.h[:, cF_rs + i:cF_rs + i + 1], in1=gpost.h[:, sl_],
                    op0=ALU.mult, op1=ALU.mult)))(bk, sl),
                  r=(psbuf[bk], brsF[i], gpost.whole), w=(x1t.rng(cbk * 512, (cbk + 1) * 512),))
            E("dve", lambda e: e.tensor_tensor(out=x1t.h[:, :], in0=x1t.h[:, :], in1=xF[s_].h[:, :], op=ALU.add),
              r=(x1t.whole, xF[s_].whole), w=(x1t.whole,))
            E("sync", lambda e: e.dma_start(out=x1_d[i * 128:(i + 1) * 128, :], in_=x1t.h[:, :]),
              r=(x1t.whole,), w=(), key="x1w")
            brsF2[i] = nt_stats(x1t.h[:, :], x1t.whole, xnF, 2048, cF_ss2 + i, cF_rs2 + i)

        def F3(i):
            nt_apply(x1t.h[:, :], x1t.whole, brsF2[i], gpre, xnF, 2048, cF_rs2 + i,
                     (lambda g: hfT.h[:, g * 8:(g + 1) * 8, i * 128:(i + 1) * 128]),
                     (lambda g: tuple(hfT.rng(kc * 1024 + i * 128, kc * 1024 + (i + 1) * 128) for kc in range(g * 8, g * 8 + 8))),
                     ["act", "dve"], banks=[(i % 2) * 4, (i % 2) * 4 + 1])

        actT = sbt([128, NFB, 1024], BF16, AA0 + 0, "actT")
        sgf = [sbt([128, 512], F32, AA0 + 172 * KB + i * 2048, f"sgf{i}") for i in range(2)]
        nsg = [0]
        g1_early = {}

        def g1_mm(w_t, bk, n):
            for kc in range(16):
                mm(psb[bk][:, :], w_t.h[:, kc * 128:(kc + 1) * 128], hfT.h[:, kc, n * 512:(n + 1) * 512],
                   kc == 0, kc == 15, (w_t.whole, hfT.rng(kc * 1024 + n * 512, kc * 1024 + (n + 1) * 512)), (psbuf[bk],))

        def g1_early_emit():
            bmap = {0: ([0, 4], [2, 5]), 1: ([1, 6], [3, 7])}
            for f in range(2):
                wg = next_w(prefetch=2)
                wu = next_w(prefetch=2)
                gb, ub = bmap[f]
                g1_early[f] = (wg, wu, gb, ub)
                g1_mm(wg, gb[0], 0)
                g1_mm(wu, ub[0], 0)

        Dn_s(0)
        Dn_s(1)
        Dn_a(0)
        Dn_s(2)
        Dn_a(1)
        F1a(0)
        for i in range(2, 6):
            Dn_s(i + 1)
            Dn_aa(i)
        Dn_s(7)
        for i in range(2, 6):
            Dn_ab(i)
        for i in range(6, 8):
            Dn_aa(i)
        for i in range(6, 8):
            Dn_ab(i)
        E("sync", lambda e: e.dma_start(out=gpost.h[:, :], in_=gbc_d[1]), w=(gpost.whole,), key="gbc2")
        E("sync", lambda e: e.dma_start(out=gpre.h[:, :], in_=gbc_d[2]), w=(gpre.whole,), key="gbc2")
        F1b(0)
        for i in range(8):
            if i + 1 < 8:
                F1a(i + 1)
            if i == 7:
                g1_early_emit()
            F2(i)
            F3(i)
            if i + 1 < 8:
                F1b(i + 1)
        ckpt('F')
        for f in range(NFB):
            if f < 2:
                wg, wu, gb, ub = g1_early[f]
            else:
                wg = next_w()
                wu = next_w()
                gb = [nbank(), nbank()]
                ub = [nbank(), nbank()]
                g1_mm(wg, gb[0], 0)
                g1_mm(wg, gb[1], 1)
                g1_mm(wu, ub[0], 0)
                g1_mm(wu, ub[1], 1)
            if f < 2:
                g1_mm(wg, gb[1], 1)
                g1_mm(wu, ub[1], 1)
            for n in range(2):
                sgt = sgf[nsg[0] % 2]
                nsg[0] += 1
                E("act", (lambda bk_, s_: (lambda e: e.activation(out=s_.h[:, :], in_=psb[bk_][:, :], func=AF.Silu)))(gb[n], sgt),
                  r=(psbuf[gb[n]],), w=(sgt.whole,))
                E("dve", (lambda bk_, s_, f_, n_: (lambda e: e.tensor_tensor(
                    out=actT.h[:, f_, n_ * 512:(n_ + 1) * 512], in0=psb[bk_][:, :], in1=s_.h[:, :], op=ALU.mult)))(ub[n], sgt, f, n),
                  r=(psbuf[ub[n]], sgt.whole), w=(actT.sub(f),))

        ckpt('G1')
        ff = sbt([128, 8, 2048], F32, AA0 + 88 * KB, "ff")
        xr = [sbt([128, 2048], F32, AA0 + 8 * i * KB, f"xr{i}") for i in range(8)]
        gffn = sbt([128, 2048], F32, AA0 + 168 * KB, "gffn")
        junkG = [sbt([128, 512], BF16, AA0 + 176 * KB + i * 1024, f"junkG{i}") for i in range(2)]
        E("sync", lambda e: e.dma_start(out=gffn.h[:, :], in_=gbc_d[3]), w=(gffn.whole,), key="gbc3")
        cG_p = stcol(32)
        cG_ss = stcol(8)
        cG_rs = stcol(8)
        for cbk in range(4):
            for fg in range(11):
                ws = next_w()
                for i in range(8):
                    for fb in range(4):
                        fidx = fg * 4 + fb
                        mm(psb[i][:, :], actT.h[:, fidx, i * 128:(i + 1) * 128], ws.h[:, fb * 512:(fb + 1) * 512],
                           fg == 0 and fb == 0, fg == 10 and fb == 3, (actT.sub(fidx), ws.whole), (psbuf[i],))
            for i in range(8):
                sl = slice(cbk * 512, (cbk + 1) * 512)
                pc = cG_p + i * 4 + cbk
                fb_ = ff.rng(i * 2048 + cbk * 512, i * 2048 + (cbk + 1) * 512)
                E("dve", (lambda i_, sl_: (lambda e: e.tensor_tensor(out=ff.h[:, i_, sl_], in0=psb[i_][:, :],
                                                                     in1=gffn.h[:, sl_], op=ALU.mult)))(i, sl),
                  r=(psbuf[i], gffn.whole), w=(fb_,))
                E("act", (lambda i_, pc_: (lambda e: e.activation(out=junkG[i_ % 2].h[:, :], in_=psb[i_][:, :], func=AF.Square,
                                                                 accum_out=st.h[:, pc_:pc_ + 1])))(i, pc),
                  r=(psbuf[i],), w=(junkG[i % 2].whole, st.rng(pc, pc + 1)))
        E("dve", lambda e: e.tensor_reduce(out=st.h[:, cG_ss:cG_ss + 8],
                                           in_=st.h[:, cG_p:cG_p + 32].rearrange("p (a b) -> p a b", b=4),
                                           axis=mybir.AxisListType.X, op=ALU.add),
          r=(st.rng(cG_p, cG_p + 32),), w=(st.rng(cG_ss, cG_ss + 8),))
        brsG = rstd_from_ss(cG_ss, cG_rs, 8, 1.0 / D_MODEL)
        for i in range(8):
            E("sync", (lambda i_: (lambda e: e.dma_start(out=xr[i_].h[:, :], in_=x1_d[i_ * 128:(i_ + 1) * 128, :])))(i),
              w=(xr[i].whole,), key=f"xr{i % 2}")
            P.q["sync"][-1].waits.append(("dma", "x1w", P.dma_cnt["x1w"]))
        for i in range(8):
            E("dve", (lambda i_: (lambda e: e.scalar_tensor_tensor(
                out=ff.h[:, i_, :], in0=ff.h[:, i_, :], scalar=st.h[:, cG_rs + i_:cG_rs + i_ + 1], in1=xr[i_].h[:, :],
                op0=ALU.mult, op1=ALU.add)))(i), r=(ff.sub(i), brsG, xr[i].whole), w=(ff.sub(i),))
            E("sync", (lambda i_: (lambda e: e.dma_start(out=out_d[i_ * 128:(i_ + 1) * 128, :], in_=ff.h[:, i_, :])))(i),
              r=(ff.sub(i),), w=(), key="outw")
        fin = E("sync", None)
        fin.waits.append(("dma", "outw", P.dma_cnt["outw"]))
        assert wcur[0] == len(wpieces), (wcur[0], len(wpieces))


    except _Stop:
        pass

    fin_all = P.op("sync", None)
    for k_, v_ in P.dma_cnt.items():
        fin_all.waits.append(("dma", k_, v_))

    for e_ in ENGS:
        cnt = 0
        for ins in P.q[e_]:
            if ins.signal and not ins.is_dma:
                cnt += 1
                ins.value = cnt
    keys = sorted(P.dma_cnt.keys())
    sem_ctx = {}
    sems_eng = {e_: nc.alloc_semaphore(f"s_{e_}") for e_ in ENGS}
    sems_key = {k: nc.alloc_semaphore(f"d_{k}") for k in keys}

    def replay(ename, eng):
        waited = {}
        for ins in P.q[ename]:
            for w in ins.waits:
                if w[0] == "eng":
                    p = w[1]
                    sem, val = sems_eng[p.eng], p.value
                else:
                    sem, val = sems_key[w[1]], w[2]
                k = id(sem)
                if waited.get(k, 0) < val:
                    eng.wait_ge(sem, val)
                    waited[k] = val
            if ins.fn is None:
                continue
            bi = ins.fn(eng)
            if ins.is_dma:
                bi.then_inc(sems_key[ins.key], 16)
            elif ins.signal:
                bi.then_inc(sems_eng[ename], 1)

    with nc.Block() as block:
        @block.sync
        def _(e):
            replay("sync", e)

        @block.scalar
        def _(e):
            replay("act", e)

        @block.vector
        def _(e):
            replay("dve", e)

        @block.gpsimd
        def _(e):
            replay("pool", e)

        @block.tensor
        def _(e):
            replay("pe", e)

    stats = {e_: len(P.q[e_]) for e_ in ENGS}
    stats["sig"] = {e_: sum(1 for i in P.q[e_] if i.signal and not i.is_dma) for e_ in ENGS}
    return nc, stats


def _blocks_k(w, ncols_per_block):
    K, N = w.shape
    kc = K // 128
    nb = N // ncols_per_block
    a = w.reshape(kc, 128, nb, ncols_per_block).transpose(2, 1, 0, 3)
    return np.ascontiguousarray(a.reshape(nb, 128, kc * ncols_per_block))


def prepare_inputs(x, positions, pre_mix_norm, w_in, q_norm, w_uq, kv_norm, w_ukv, conv_w, conv_b, conv_ln_g,
                   conv_ln_b, conv_out_norm, attn_out_norm, w_out, post_mix_norm, pre_ffn_norm, w_gate, w_up,
                   w_down, post_ffn_norm):
    f = np.float32
    x = np.asarray(x, f)
    positions = np.asarray(positions, np.int32)
    w_in = np.asarray(w_in, f)[0]
    w_uq = np.asarray(w_uq, f)[0]
    w_ukv = np.asarray(w_ukv, f)[0]
    w_out = np.asarray(w_out, f)[0]
    w_gate = np.asarray(w_gate, f)[0]
    w_up = np.asarray(w_up, f)[0]
    w_down = np.asarray(w_down, f)[0]

    c1 = 2 * CONV_CH
    c2 = c1 + Q_LORA
    c3 = c2 + KV_LORA
    cols = []
    cols += list(range(c2, c3))
    cols += list(range(c3, c3 + 64)) + list(range(c3 + 32, c3 + 64)) + list(range(c3, c3 + 32))
    cols += list(range(c1, c2))
    for c in range(8):
        cols += list(range(CONV_CH + c * 128, CONV_CH + (c + 1) * 128))
        cols += list(range(c * 128, (c + 1) * 128))
    w_in_p = _blocks_k(w_in[:, cols], 128)
    cols = []
    for h in range(N_HEADS):
        b0 = h * 192
        cols += list(range(b0, b0 + 128))
    for h in range(N_HEADS):
        b0 = h * 192
        cols += list(range(b0 + 128, b0 + 192)) + list(range(b0 + 160, b0 + 192)) + list(range(b0 + 128, b0 + 160))
    w_uq_p = _blocks_k(w_uq[:, cols], 128)
    kcols = []
    vcols = []
    for h in range(N_HEADS):
        kcols += list(range(h * 256, h * 256 + 128))
        vcols += list(range(h * 256 + 128, h * 256 + 256))
    w_uk_p = _blocks_k(w_ukv[:, kcols], 128)
    wv = w_ukv[:, vcols]
    w_uv_p = np.ascontiguousarray(wv.reshape(2, 2, 128, 1024).transpose(0, 2, 1, 3).reshape(2, 128, 2048))
    w_out_p = np.ascontiguousarray(w_out.reshape(16, 128, 2048).transpose(1, 0, 2).reshape(128, 16 * 2048))
    g_p = _blocks_k(w_gate, 128)
    u_p = _blocks_k(w_up, 128)
    w_gu_p = np.ascontiguousarray(np.stack([g_p, u_p], axis=1).reshape(88, 128, 2048))
    wd = w_down.reshape(11, 4, 128, 4, 512)
    w_dn_p = np.ascontiguousarray(wd.transpose(3, 0, 2, 1, 4).reshape(44, 128, 2048))

    c_bf = np.zeros((128, 512), np.float32)
    c_bf[:, 0:128] = np.eye(128)
    c_bf[:, 128:256] = 1.0
    e64 = np.eye(64)
    c_bf[:, 256:384] = np.block([[e64, e64], [e64, e64]])
    kk = np.arange(128)[:, None]
    qq = np.arange(128)[None, :]
    c_bf[:, 384:512] = (qq >= kk).astype(np.float32)
    c_bf = c_bf.astype(ml_dtypes.bfloat16)

    cvec = np.zeros((128, NCV), f)
    cw = np.asarray(conv_w, f)[0]
    for c in range(8):
        cvec[:, CW + c * 31:CW + (c + 1) * 31] = cw[:, c * 128:(c + 1) * 128].T
    cvec[:, CB:CB + 8] = np.asarray(conv_b, f)[0].reshape(8, 128).T
    cvec[:, LG:LG + 8] = np.asarray(conv_ln_g, f)[0].reshape(8, 128).T
    cvec[:, LB:LB + 8] = np.asarray(conv_ln_b, f)[0].reshape(8, 128).T
    cvec[:, G2:G2 + 8] = np.asarray(conv_out_norm, f)[0].reshape(8, 128).T
    cvec[:, GQ:GQ + 6] = np.asarray(q_norm, f)[0].reshape(6, 128).T
    cvec[:, GKV:GKV + 4] = np.asarray(kv_norm, f)[0].reshape(4, 128).T
    inv_freq = (np.float32(10000.0) ** (-np.arange(0, 64, 2, dtype=np.float32) / np.float32(64))).astype(f)
    cvec[:, IFQ] = np.tile(inv_freq, 4)
    cvec[0:64, PHS] = np.float32(np.pi / 2)
    cvec[:, SGN] = 1.0
    cvec[64:96, SGN] = -1.0

    gbc = np.zeros((5, 128, 2048), f)
    gbc[0] = np.asarray(pre_mix_norm, f)[0][None, :]
    gbc[1] = np.asarray(post_mix_norm, f)[0][None, :]
    gbc[2] = np.asarray(pre_ffn_norm, f)[0][None, :]
    gbc[3] = np.asarray(post_ffn_norm, f)[0][None, :]
    gbc[4, :, 0:1024] = np.asarray(attn_out_norm, f)[0][None, :]

    shared = dict(c_bf=c_bf, gbc=gbc, w_in_p=w_in_p, w_uq_p=w_uq_p, w_uk_p=w_uk_p, w_uv_p=w_uv_p,
                  w_out_p=w_out_p, w_gu_p=w_gu_p, w_dn_p=w_dn_p)
    in_maps = []
    for core in range(8):
        b, half = core // 2, core % 2
        m = dict(shared)
        m["x_own"] = np.ascontiguousarray(x[b, half * TOWN:(half + 1) * TOWN])
        cvc = cvec.copy()
        pos = np.zeros((2048,), np.int32)
        if half == 1:
            m["x_prev"] = np.ascontiguousarray(x[b, 0:TOWN])
            pos[:] = positions[b, 0:2048]
            cvc[:, PFL] = 1.0
        else:
            m["x_prev"] = np.zeros((TOWN, D_MODEL), f)
            pos[1024:] = positions[b, 0:1024]
            cvc[:, PFL] = 0.0
        m["cvec"] = cvc
        m["pos_bc"] = np.ascontiguousarray(np.broadcast_to(pos[None, :], (128, 2048)))
        in_maps.append(m)
    return in_maps


_CACHE = {}


def kernel(**inputs):
    if "nc" not in _CACHE:
        _CACHE["nc"], _CACHE["stats"] = build_program()
    nc = _CACHE["nc"]
    in_maps = prepare_inputs(**inputs)
    res = run_bass_kernel_spmd(nc, in_maps, core_ids=list(range(8)))
    out = np.zeros((BATCH, SEQ, D_MODEL), np.float32)
    for core in range(8):
        b, half = core // 2, core % 2
        out[b, half * TOWN:(half + 1) * TOWN] = res.results[core]["out"]
    return out
```
